# Optimizing a Trainium2 kernel written in Bass

```python
import math
import functools
import jax
import jax.numpy as jnp
from jax import lax
import numpy as np

D_MODEL = 1024
BATCH = 16
SEQ = 2048
DEPTH = 1
DEC_BATCH = 32
DEC_SEQ = 1
PAST_LEN = 16384
PAGE_SIZE = 128

M_HEADS = 4
M_HEAD_DIM = 128
M_WIDTH = M_HEADS * M_HEAD_DIM
M_CONV = 4
M_CHUNK = 64
N_HEADS = 8
N_KV_HEADS = 2
N_GROUP = N_HEADS // N_KV_HEADS
N_HEAD_DIM = 64
N_WIDTH = N_HEADS * N_HEAD_DIM
N_KV_WIDTH = 2 * N_KV_HEADS * N_HEAD_DIM
CMP_BLOCK = 32
CMP_STRIDE = 16
SLC_BLOCK = 64
N_SELECT = 16
WINDOW = 512
WIN_QBLOCK = 128
SEL_QBLOCK = 32
ATTN_SCALE = N_HEAD_DIM ** -0.5
REL_BUCKETS = 32
REL_MAX_DIST = 128
PEER_HEADS = 8
PEER_KEYS = 128
PEER_EXPERTS = PEER_KEYS * PEER_KEYS
PEER_QDIM = 256
PEER_TOPK = 16
PEER_TOKEN_BLOCK = 128
DN_ALPHA = (2 * DEPTH) ** 0.25
DN_BETA = (8 * DEPTH) ** -0.25
LN_EPS = 1e-5
NEG = -1e30
FORCE_SCORE = 1e4

IN_WIDTHS = (M_WIDTH, M_WIDTH, M_WIDTH, M_HEADS, M_HEADS, N_WIDTH, N_KV_WIDTH, N_KV_WIDTH, N_KV_WIDTH, 3 * N_HEADS, D_MODEL, D_MODEL)
IN_DIM = sum(IN_WIDTHS)

kernel_name = 'hybrid_mlstm_nsa_peer_decode_step'


def layer_norm(x, g, b):
    xf = x.astype(jnp.float32)
    mu = xf.mean(-1, keepdims=True)
    var = jnp.square(xf - mu).mean(-1, keepdims=True)
    return ((xf - mu) * lax.rsqrt(var + LN_EPS) * g + b).astype(x.dtype)


def split_in(z):
    outs, off = [], 0
    for w in IN_WIDTHS:
        outs.append(z[..., off:off + w])
        off += w
    return outs


def pad_time(a, n):
    return jnp.pad(a, [(0, 0), (0, n)] + [(0, 0)] * (a.ndim - 2))


def to_blocks(a, qb):
    B, T = a.shape[:2]
    return jnp.swapaxes(a.reshape((B, T // qb, qb) + a.shape[2:]), 0, 1)


def from_blocks(a):
    a = jnp.swapaxes(a, 0, 1)
    return a.reshape((a.shape[0], a.shape[1] * a.shape[2]) + a.shape[3:])


def rel_bucket(dist):
    n = jnp.maximum(dist, 0)
    exact = REL_BUCKETS // 2
    nf = jnp.maximum(n, exact).astype(jnp.float32)
    large = exact + (jnp.log(nf / exact) / math.log(REL_MAX_DIST / exact) * (REL_BUCKETS - exact)).astype(jnp.int32)
    return jnp.where(n < exact, n, jnp.minimum(large, REL_BUCKETS - 1))


def rel_bias_lookup(rel_bias, dist):
    return rel_bias[rel_bucket(dist)].astype(jnp.float32).reshape(dist.shape + (N_KV_HEADS, N_GROUP))


def masked_softmax(s, mask):
    s = jnp.where(mask, s, NEG)
    e = jnp.where(mask, jnp.exp(s - s.max(-1, keepdims=True)), 0.0)
    return e / jnp.maximum(e.sum(-1, keepdims=True), 1e-30)


def attn_core(q, q_pos, kv, k_pos, mask, rel_bias):
    s = jnp.einsum('bthgd,bkhd->bthgk', q, kv[:, :, 0]).astype(jnp.float32) * ATTN_SCALE
    bias = jnp.transpose(rel_bias_lookup(rel_bias, q_pos[:, None] - k_pos[None, :]), (0, 2, 3, 1))
    p = masked_softmax(s + bias, mask[:, None, None, :])
    return jnp.einsum('bthgk,bkhd->bthgd', p.astype(kv.dtype), kv[:, :, 1]), p


def window_attn(q, q_pos, kv, k_pos, rel_bias):
    d = q_pos[:, None] - k_pos[None, :]
    mask = (d >= 0) & (d < WINDOW) & (k_pos[None, :] >= 0)
    return attn_core(q, q_pos, kv, k_pos, mask, rel_bias)[0]


def compress_blocks(kv, pe, w1, w2):
    B, T = kv.shape[:2]
    half = CMP_BLOCK // CMP_STRIDE
    nc = T // CMP_STRIDE
    ch = kv.reshape(B, nc, CMP_STRIDE, 2, N_KV_HEADS, N_HEAD_DIM)
    w1r = w1.reshape(2, half, CMP_STRIDE, N_HEAD_DIM, N_HEAD_DIM)
    per = pe.reshape(2, half, CMP_STRIDE, N_HEAD_DIM)
    proj = jnp.einsum('bnrchd,cprdo->bnpcho', ch, w1r)
    proj = proj + jnp.einsum('cprd,cprdo->pco', per, w1r)[:, :, None, :]
    nw = nc - half + 1
    hid = jax.nn.gelu(sum(proj[:, p:p + nw, p] for p in range(half)))
    return jnp.einsum('bnchi,cio->bncho', hid, w2)


def select_blocks(p_cmp, q_pos, n_sel):
    half = CMP_BLOCK // CMP_STRIDE
    pg = p_cmp.sum(axis=3)
    pads = [(0, 0)] * 3
    chunk = sum(jnp.pad(pg, pads + [(p, half - 1 - p)]) for p in range(half))
    ps = chunk.reshape(chunk.shape[:-1] + (n_sel, SLC_BLOCK // CMP_STRIDE)).sum(-1)
    blk = jnp.arange(n_sel, dtype=jnp.int32)[None, :]
    cur = (q_pos // SLC_BLOCK)[:, None]
    causal = blk * SLC_BLOCK <= q_pos[:, None]
    forced = (blk == 0) | (blk == cur) | (blk == cur - 1)
    score = jnp.where(causal[None, :, None, :], jnp.where(forced[None, :, None, :], FORCE_SCORE, ps), NEG)
    _, idx = lax.top_k(score, min(N_SELECT, n_sel))
    valid = idx * SLC_BLOCK <= q_pos[None, :, None, None]
    return idx, valid


def selected_attn(q, q_pos, idx, valid, kv_sel, rel_bias):
    B, Tq, H, K = idx.shape
    k_pos = idx[..., None] * SLC_BLOCK + jnp.arange(SLC_BLOCK, dtype=jnp.int32)
    qp = q_pos[None, :, None, None, None]
    mask = (valid[..., None] & (k_pos <= qp)).reshape(B, Tq, H, 1, K * SLC_BLOCK)
    s = jnp.einsum('bthgd,bthksd->bthgks', q, kv_sel[..., 0, :]).astype(jnp.float32) * ATTN_SCALE
    hidx = jnp.arange(N_KV_HEADS)[None, None, :, None, None]
    bias = rel_bias.reshape(REL_BUCKETS, N_KV_HEADS, N_GROUP)[rel_bucket(qp - k_pos), hidx].astype(jnp.float32)
    s = s + jnp.moveaxis(bias, -1, 3)
    p = masked_softmax(s.reshape(B, Tq, H, N_GROUP, K * SLC_BLOCK), mask).reshape(s.shape)
    return jnp.einsum('bthgks,bthksd->bthgd', p.astype(kv_sel.dtype), kv_sel[..., 1, :])


def combine_nsa(gates, o_cmp, o_slc, o_win):
    B, T = gates.shape[:2]
    g = jax.nn.sigmoid(gates.astype(jnp.float32)).reshape(B, T, 3, N_KV_HEADS, N_GROUP, 1)
    o = g[:, :, 0] * o_cmp + g[:, :, 1] * o_slc + g[:, :, 2] * o_win
    return o.reshape(B, T, N_WIDTH).astype(gates.dtype)


def nsa_prompt(q, gates, kv_cmp, kv_slc, kv_win, pe, w1, w2, rel_bias):
    B, T = q.shape[:2]
    q_pos = jnp.arange(T, dtype=jnp.int32)
    t_pad = -(-T // SLC_BLOCK) * SLC_BLOCK
    kc = compress_blocks(pad_time(kv_cmp, t_pad - T), pe, w1, w2)
    k_end = jnp.arange(kc.shape[1], dtype=jnp.int32) * CMP_STRIDE + (CMP_BLOCK - 1)
    o_cmp, p_cmp = attn_core(q, q_pos, kc, k_end, k_end[None, :] <= q_pos[:, None], rel_bias)
    n_sel = t_pad // SLC_BLOCK
    idx, valid = select_blocks(p_cmp, q_pos, n_sel)
    blocks = pad_time(kv_slc, t_pad - T).reshape(B, n_sel, SLC_BLOCK, 2, N_KV_HEADS, N_HEAD_DIM)
    bidx = jnp.arange(B)[:, None, None, None]
    hidx = jnp.arange(N_KV_HEADS)[None, None, :, None]
    qb = min(SEL_QBLOCK, T)

    def sel_block(args):
        q_b, pos_b, idx_b, val_b = args
        return selected_attn(q_b, pos_b, idx_b, val_b, blocks[bidx, idx_b, :, :, hidx], rel_bias)

    o_slc = from_blocks(lax.map(sel_block, (to_blocks(q, qb), q_pos.reshape(-1, qb), to_blocks(idx, qb), to_blocks(valid, qb))))
    wq = min(WIN_QBLOCK, T)
    kv_wp = jnp.pad(kv_win, [(0, 0), (WINDOW, 0), (0, 0), (0, 0), (0, 0)])

    def win_block(n):
        start = n * wq
        q_b = lax.dynamic_slice_in_dim(q, start, wq, axis=1)
        kv_b = lax.dynamic_slice_in_dim(kv_wp, start, wq + WINDOW, axis=1)
        pos_b = start + jnp.arange(wq, dtype=jnp.int32)
        kpos_b = start - WINDOW + jnp.arange(wq + WINDOW, dtype=jnp.int32)
        return window_attn(q_b, pos_b, kv_b, kpos_b, rel_bias)

    o_win = from_blocks(lax.map(win_block, jnp.arange(T // wq, dtype=jnp.int32)))
    return combine_nsa(gates, o_cmp, o_slc, o_win), kv_win[:, -min(WINDOW, T):]


def nsa_sample(q, gates, kv_cmp, kv_slc, kv_win, pool_cmp, pool_slc, win_buf, page_table, pe, w1, w2, rel_bias):
    B, T = q.shape[:2]
    past = page_table.shape[1] * PAGE_SIZE
    q_pos = past + jnp.arange(T, dtype=jnp.int32)
    total = past + T
    t_pad = -(-total // SLC_BLOCK) * SLC_BLOCK
    past_cmp = pool_cmp[page_table].reshape(B, past, 2, N_KV_HEADS, N_HEAD_DIM)
    full_cmp = pad_time(jnp.concatenate([past_cmp, kv_cmp], axis=1), t_pad - total)
    kc = compress_blocks(full_cmp, pe, w1, w2)
    k_end = jnp.arange(kc.shape[1], dtype=jnp.int32) * CMP_STRIDE + (CMP_BLOCK - 1)
    o_cmp, p_cmp = attn_core(q, q_pos, kc, k_end, k_end[None, :] <= q_pos[:, None], rel_bias)
    n_sel = t_pad // SLC_BLOCK
    idx, valid = select_blocks(p_cmp, q_pos, n_sel)
    bpp = PAGE_SIZE // SLC_BLOCK
    n_past_blk = past // SLC_BLOCK
    n_tail = n_sel - n_past_blk
    pool_blk = pool_slc.reshape(-1, SLC_BLOCK, 2, N_KV_HEADS, N_HEAD_DIM)
    tail = pad_time(kv_slc, n_tail * SLC_BLOCK - T).reshape(B, n_tail, SLC_BLOCK, 2, N_KV_HEADS, N_HEAD_DIM)
    bidx = jnp.arange(B)[:, None, None, None]
    hidx = jnp.arange(N_KV_HEADS)[None, None, :, None]
    pj = jnp.minimum(idx, n_past_blk - 1)
    phys = page_table[bidx, pj // bpp] * bpp + pj % bpp
    tj = jnp.clip(idx - n_past_blk, 0, n_tail - 1)
    kv_sel = jnp.where((idx < n_past_blk)[..., None, None, None], pool_blk[phys, :, :, hidx], tail[bidx, tj, :, :, hidx])
    o_slc = selected_attn(q, q_pos, idx, valid, kv_sel, rel_bias)
    wb = win_buf.shape[1]
    keys = jnp.concatenate([win_buf.astype(kv_win.dtype), kv_win], axis=1)
    k_pos = past - wb + jnp.arange(wb + T, dtype=jnp.int32)
    o_win = window_attn(q, q_pos, keys, k_pos, rel_bias)
    return combine_nsa(gates, o_cmp, o_slc, o_win), keys[:, T:]


def mlstm_scan(q, k, v, ig, lf, C0, n0, m0):
    B, T, H, D = q.shape
    L = min(M_CHUNK, T)
    nC = -(-T // L)
    tp = nC * L - T

    def prep(a, fill):
        a = jnp.pad(a, [(0, 0), (0, tp)] + [(0, 0)] * (a.ndim - 2), constant_values=fill)
        a = a.reshape((B, nC, L) + a.shape[2:])
        return jnp.swapaxes(jnp.moveaxis(a, 1, 0), 2, 3)

    tril = jnp.tril(jnp.ones((L, L), dtype=bool))

    def step(carry, xs):
        C, n, m = carry
        qc, kc, vc, igc, lfc = xs
        b = jnp.cumsum(lfc, axis=-1)
        logw = jnp.where(tril, b[..., :, None] - b[..., None, :] + igc[..., None, :], NEG)
        m_inter = b + m[..., None]
        m_t = jnp.maximum(m_inter, logw.max(-1))
        sw = jnp.einsum('bhtd,bhsd->bhts', qc, kc) * jnp.exp(logw - m_t[..., None])
        scale_in = jnp.exp(m_inter - m_t)
        num = scale_in[..., None] * jnp.einsum('bhvk,bhtk->bhtv', C, qc) + jnp.einsum('bhts,bhsv->bhtv', sw, vc)
        den = scale_in * jnp.einsum('bhk,bhtk->bht', n, qc) + sw.sum(-1)
        h = num / jnp.maximum(jnp.abs(den), jnp.exp(-m_t))[..., None]
        m_new = m_t[..., -1]
        wts = jnp.exp(b[..., -1:] - b + igc - m_new[..., None])
        decay = jnp.exp(b[..., -1] + m - m_new)
        C = decay[..., None, None] * C + jnp.einsum('bhs,bhsv,bhsk->bhvk', wts, vc, kc)
        n = decay[..., None] * n + jnp.einsum('bhs,bhsk->bhk', wts, kc)
        return (C, n, m_new), h

    (C, n, m), h = lax.scan(step, (C0, n0, m0), (prep(q, 0.0), prep(k, 0.0), prep(v, 0.0), prep(ig, NEG), prep(lf, 0.0)))
    h = jnp.moveaxis(jnp.swapaxes(h, 2, 3), 0, 1).reshape(B, nC * L, H, D)[:, :T]
    return h, C, n, m


def mlstm_branch(u, v, o_pre, i_pre, f_pre, conv_buf, C0, n0, m0, conv_w, conv_b, wq, wk, gate_bias, norm_g):
    B, T = u.shape[:2]
    f32 = jnp.float32
    full = jnp.concatenate([conv_buf.astype(u.dtype), u], axis=1)
    c = conv_b + sum(conv_w[j] * full[:, j:j + T] for j in range(M_CONV))
    c = jax.nn.silu(c).reshape(B, T, M_HEADS, M_HEAD_DIM)
    q = jnp.einsum('bthd,hde->bthe', c, wq).astype(f32)
    k = jnp.einsum('bthd,hde->bthe', c, wk).astype(f32) * (M_HEAD_DIM ** -0.5)
    vv = v.reshape(B, T, M_HEADS, M_HEAD_DIM).astype(f32)
    ig = i_pre.astype(f32) + gate_bias[0].astype(f32)
    lf = jax.nn.log_sigmoid(f_pre.astype(f32) + gate_bias[1].astype(f32))
    h, C, n, m = mlstm_scan(q, k, vv, ig, lf, C0.astype(f32), n0.astype(f32), m0.astype(f32))
    mu = h.mean(-1, keepdims=True)
    var = jnp.square(h - mu).mean(-1, keepdims=True)
    hn = (h - mu) * lax.rsqrt(var + LN_EPS) * norm_g.astype(f32).reshape(M_HEADS, M_HEAD_DIM)
    out = (jax.nn.sigmoid(o_pre.astype(f32)) * hn.reshape(B, T, M_WIDTH)).astype(u.dtype)
    return out, C, n, m, full[:, T:]


def peer_ffn(x, wq, subkeys, u_tab, v_tab):
    B, T, D = x.shape
    N = B * T
    blk = min(PEER_TOKEN_BLOCK, N)
    nb = -(-N // blk)
    xf = jnp.pad(x.reshape(N, D), [(0, nb * blk - N), (0, 0)]).reshape(nb, blk, D)
    half = PEER_QDIM // 2

    def one(xb):
        q = (xb @ wq).reshape(blk, PEER_HEADS, 2, half)
        s = jnp.einsum('nhpd,pkd->nhpk', q, subkeys).astype(jnp.float32)
        s1, i1 = lax.top_k(s[:, :, 0], PEER_TOPK)
        s2, i2 = lax.top_k(s[:, :, 1], PEER_TOPK)
        cand = (s1[..., :, None] + s2[..., None, :]).reshape(blk, PEER_HEADS, PEER_TOPK * PEER_TOPK)
        sc, ci = lax.top_k(cand, PEER_TOPK)
        e = jnp.take_along_axis(i1, ci // PEER_TOPK, -1) * PEER_KEYS + jnp.take_along_axis(i2, ci % PEER_TOPK, -1)
        g = jax.nn.softmax(sc, axis=-1)
        act = jax.nn.gelu(jnp.einsum('nhkd,nd->nhk', u_tab[e], xb).astype(jnp.float32))
        return jnp.einsum('nhk,nhkd->nd', (g * act).astype(v_tab.dtype), v_tab[e])

    return lax.map(one, xf).reshape(nb * blk, D)[:N].reshape(B, T, D).astype(x.dtype)


def trunk_layer(x, nsa_fn, conv_buf, C0, n0, m0, w_in, m_conv_w, m_conv_b, m_wq, m_wk, m_gate_bias, m_norm_g,
                w_up_m, w_up_n, w_out, ln1_g, ln1_b, ln2_g, ln2_b, peer_wq, peer_subkeys, peer_u, peer_v):
    B, T, _ = x.shape
    (u, v, o_pre, i_pre, f_pre, q_n, kv_c, kv_s, kv_w, g_n, gate_m, gate_n) = split_in(x @ w_in)
    m_out, C, n, m, conv_new = mlstm_branch(u, v, o_pre, i_pre, f_pre, conv_buf, C0, n0, m0,
                                            m_conv_w, m_conv_b, m_wq, m_wk, m_gate_bias, m_norm_g)
    kv_c = kv_c.reshape(B, T, 2, N_KV_HEADS, N_HEAD_DIM)
    kv_s = kv_s.reshape(B, T, 2, N_KV_HEADS, N_HEAD_DIM)
    kv_w = kv_w.reshape(B, T, 2, N_KV_HEADS, N_HEAD_DIM)
    n_out, win_state = nsa_fn(q_n.reshape(B, T, N_KV_HEADS, N_GROUP, N_HEAD_DIM), g_n, kv_c, kv_s, kv_w)
    mix = jax.nn.sigmoid(gate_m) * (m_out @ w_up_m) + jax.nn.sigmoid(gate_n) * (n_out @ w_up_n)
    x = layer_norm(DN_ALPHA * x + mix @ w_out, ln1_g, ln1_b)
    x = layer_norm(DN_ALPHA * x + peer_ffn(x, peer_wq, peer_subkeys, peer_u, peer_v), ln2_g, ln2_b)
    return x, (kv_c, kv_s, win_state, C, n, m, conv_new)


def setup_inputs(seed: int = 0) -> dict:
    key = jax.random.key(seed)
    kit = iter(jax.random.split(key, 48))
    f32 = jnp.float32

    def nrm(shape, scale):
        return jax.random.normal(next(kit), shape, f32) * scale

    n_pages = PAST_LEN // PAGE_SIZE
    n_phys = (5 * DEC_BATCH * n_pages + 3) // 4
    win_buf = min(WINDOW, PAST_LEN)
    page_table = jax.random.permutation(next(kit), n_phys)[:DEC_BATCH * n_pages].reshape(DEC_BATCH, n_pages).astype(jnp.int32)
    kv_tail = (2, N_KV_HEADS, N_HEAD_DIM)
    return {
        'x_prompt': nrm((BATCH, SEQ, D_MODEL), 1.0),
        'x_sample': nrm((DEC_BATCH, DEC_SEQ, D_MODEL), 1.0),
        'cache_cmp_kv': nrm((DEPTH, n_phys, PAGE_SIZE) + kv_tail, 1.0),
        'cache_slc_kv': nrm((DEPTH, n_phys, PAGE_SIZE) + kv_tail, 1.0),
        'page_table': page_table,
        'state_win_kv': nrm((DEPTH, DEC_BATCH, win_buf) + kv_tail, 1.0),
        'state_mlstm_C': nrm((DEPTH, DEC_BATCH, M_HEADS, M_HEAD_DIM, M_HEAD_DIM), M_HEAD_DIM ** -0.5),
        'state_mlstm_n': nrm((DEPTH, DEC_BATCH, M_HEADS, M_HEAD_DIM), M_HEAD_DIM ** -0.5),
        'state_mlstm_m': nrm((DEPTH, DEC_BATCH, M_HEADS), 0.5),
        'state_mlstm_conv': nrm((DEPTH, DEC_BATCH, M_CONV - 1, M_WIDTH), 1.0),
        'w_in': nrm((DEPTH, D_MODEL, IN_DIM), D_MODEL ** -0.5),
        'm_conv_w': nrm((DEPTH, M_CONV, M_WIDTH), 0.5),
        'm_conv_b': nrm((DEPTH, M_WIDTH), 0.01),
        'm_wq': nrm((DEPTH, M_HEADS, M_HEAD_DIM, M_HEAD_DIM), M_HEAD_DIM ** -0.5),
        'm_wk': nrm((DEPTH, M_HEADS, M_HEAD_DIM, M_HEAD_DIM), M_HEAD_DIM ** -0.5),
        'm_gate_bias': jnp.stack([nrm((DEPTH, M_HEADS), 0.1),
                                  jnp.linspace(3.0, 6.0, M_HEADS) + nrm((DEPTH, M_HEADS), 0.1)], axis=1),
        'm_norm_g': 1.0 + nrm((DEPTH, M_WIDTH), 0.01),
        'cmp_pe': nrm((DEPTH, 2, CMP_BLOCK, N_HEAD_DIM), 0.02),
        'cmp_w1': nrm((DEPTH, 2, CMP_BLOCK * N_HEAD_DIM, N_HEAD_DIM), (CMP_BLOCK * N_HEAD_DIM) ** -0.5),
        'cmp_w2': nrm((DEPTH, 2, N_HEAD_DIM, N_HEAD_DIM), N_HEAD_DIM ** -0.5),
        'rel_bias': nrm((REL_BUCKETS, N_HEADS), 0.1),
        'w_up_m': nrm((DEPTH, M_WIDTH, D_MODEL), M_WIDTH ** -0.5 * DN_BETA),
        'w_up_n': nrm((DEPTH, N_WIDTH, D_MODEL), N_WIDTH ** -0.5 * DN_BETA),
        'w_out': nrm((DEPTH, D_MODEL, D_MODEL), D_MODEL ** -0.5 * DN_BETA),
        'ln1_g': 1.0 + nrm((DEPTH, D_MODEL), 0.01),
        'ln1_b': nrm((DEPTH, D_MODEL), 0.01),
        'ln2_g': 1.0 + nrm((DEPTH, D_MODEL), 0.01),
        'ln2_b': nrm((DEPTH, D_MODEL), 0.01),
        'peer_wq': nrm((DEPTH, D_MODEL, PEER_HEADS * PEER_QDIM), D_MODEL ** -0.5),
        'peer_subkeys': nrm((DEPTH, 2, PEER_KEYS, PEER_QDIM // 2), (PEER_QDIM // 2) ** -0.5),
        'peer_u': nrm((DEPTH, PEER_EXPERTS, D_MODEL), D_MODEL ** -0.5),
        'peer_v': nrm((DEPTH, PEER_EXPERTS, D_MODEL), DN_BETA * PEER_HEADS ** -0.5),
    }


def reference(x_prompt, x_sample, cache_cmp_kv, cache_slc_kv, page_table, state_win_kv, state_mlstm_C, state_mlstm_n,
              state_mlstm_m, state_mlstm_conv, w_in, m_conv_w, m_conv_b, m_wq, m_wk, m_gate_bias, m_norm_g, cmp_pe,
              cmp_w1, cmp_w2, rel_bias, w_up_m, w_up_n, w_out, ln1_g, ln1_b, ln2_g, ln2_b, peer_wq, peer_subkeys,
              peer_u, peer_v):
    Bp = x_prompt.shape[0]
    xp, xs = x_prompt, x_sample
    states_p, states_s = [], []
    for l in range(DEPTH):
        shared = (w_in[l], m_conv_w[l], m_conv_b[l], m_wq[l], m_wk[l], m_gate_bias[l], m_norm_g[l],
                  w_up_m[l], w_up_n[l], w_out[l], ln1_g[l], ln1_b[l], ln2_g[l], ln2_b[l],
                  peer_wq[l], peer_subkeys[l], peer_u[l], peer_v[l])
        nsa_p = functools.partial(nsa_prompt, pe=cmp_pe[l], w1=cmp_w1[l], w2=cmp_w2[l], rel_bias=rel_bias)
        xp, st_p = trunk_layer(xp, nsa_p,
                               jnp.zeros((Bp, M_CONV - 1, M_WIDTH), xp.dtype),
                               jnp.zeros((Bp, M_HEADS, M_HEAD_DIM, M_HEAD_DIM), jnp.float32),
                               jnp.zeros((Bp, M_HEADS, M_HEAD_DIM), jnp.float32),
                               jnp.zeros((Bp, M_HEADS), jnp.float32), *shared)
        nsa_s = functools.partial(nsa_sample, pool_cmp=cache_cmp_kv[l], pool_slc=cache_slc_kv[l],
                                  win_buf=state_win_kv[l], page_table=page_table,
                                  pe=cmp_pe[l], w1=cmp_w1[l], w2=cmp_w2[l], rel_bias=rel_bias)
        xs, st_s = trunk_layer(xs, nsa_s, state_mlstm_conv[l], state_mlstm_C[l], state_mlstm_n[l],
                               state_mlstm_m[l], *shared)
        states_p.append(st_p)
        states_s.append(st_s)
    cmp_p, slc_p, win_p, C_p, n_p, m_p, conv_p = [jnp.stack(a) for a in zip(*states_p)]
    cmp_s, slc_s, win_s, C_s, n_s, m_s, conv_s = [jnp.stack(a) for a in zip(*states_s)]
    return (xp, xs, cmp_p, cmp_s, slc_p, slc_s, win_p, win_s, C_p, C_s, n_p, n_s, m_p, m_s, conv_p, conv_s)
```

```python
import contextlib
import os
import numpy as np
import concourse.bass as bass
import concourse.mybir as mybir
from concourse.bass_utils import run_bass_kernel_spmd

F32 = mybir.dt.float32
I32 = mybir.dt.int32
U32 = mybir.dt.uint32
BF16 = mybir.dt.bfloat16
AF = mybir.ActivationFunctionType
ALU = mybir.AluOpType
AX = mybir.AxisListType

NCORES = 8
D = 1024
SEQ = 2048
BP = 16
BS = 32
SPC = BP // NCORES
SSC = BS // NCORES
TP = SPC * SEQ
NT = TP // 128
IN_DIM = 4896
O_U, O_V, O_O, O_I, O_F, O_Q, O_KC, O_KS, O_KW, O_GN, O_GM, O_GNN = (
    0, 512, 1024, 1536, 1540, 1544, 2056, 2312, 2568, 2824, 2848, 3872)
COL_GROUPS = [(0, 512), (512, 512), (1024, 512), (1536, 8), (1544, 512), (2056, 512),
              (2568, 280), (2848, 512), (3360, 512), (3872, 512), (4384, 512)]

DEBUG = False
DBG = {}
ENGS = ("pe", "act", "dve", "pool", "sp")
N_DMA_SEMS = 12


class Prog:
    def __init__(self, nc, stack):
        self.nc = nc
        self.ops = {e: [] for e in ENGS}
        self.cnt = {e: 0 for e in ENGS}
        self.esem = {e: stack.enter_context(nc.semaphore("es_" + e)) for e in ENGS}
        self.dsem = {e: [stack.enter_context(nc.semaphore("ds_%s%d" % (e, i))) for i in range(N_DMA_SEMS)]
                     for e in ("sp", "pool", "act")}
        self.dval = {e: [0] * N_DMA_SEMS for e in ("sp", "pool", "act")}
        self.dnext = {e: 0 for e in ("sp", "pool", "act")}
        self.semobj = {}
        for e in ENGS:
            self.semobj["es_" + e] = self.esem[e]
        for e in self.dsem:
            for i, s in enumerate(self.dsem[e]):
                self.semobj["ds_%s%d" % (e, i)] = s
        self.lastw = {}
        self.readers = {}
        self.waited = {e: {} for e in ENGS}
        self.final_tokens = []

    def _deps(self, eng, reads, writes):
        deps = {}

        def add(tok, same_ok):
            if tok is None:
                return
            s, v = tok
            if s == "es_" + eng and not same_ok:
                return
            if deps.get(s, 0) < v:
                deps[s] = v

        for k in reads:
            add(self.lastw.get(k), eng != "pe")
        for k in writes:
            add(self.lastw.get(k), False)
            for s, v in self.readers.get(k, {}).items():
                add((s, v), False)
        out = []
        for s, v in deps.items():
            if self.waited[eng].get(s, 0) < v:
                self.waited[eng][s] = v
                out.append((s, v))
        return out

    def _commit(self, tok, reads, writes):
        for k in writes:
            self.lastw[k] = tok
            self.readers[k] = {}
        for k in reads:
            r = self.readers.setdefault(k, {})
            if r.get(tok[0], 0) < tok[1]:
                r[tok[0]] = tok[1]

    def op(self, eng, fn, reads=(), writes=()):
        waits = self._deps(eng, reads, writes)
        self.cnt[eng] += 1
        tok = ("es_" + eng, self.cnt[eng])
        self.ops[eng].append((waits, fn, ("es_" + eng, 1)))
        self._commit(tok, reads, writes)
        return tok

    def dma(self, eng, fn, reads=(), writes=(), final=False):
        i = self.dnext[eng]
        self.dnext[eng] = (i + 1) % N_DMA_SEMS
        sname = "ds_%s%d" % (eng, i)
        waits = self._deps(eng, reads, writes)
        prev = self.dval[eng][i]
        if prev > 0 and self.waited[eng].get(sname, 0) < prev:
            self.waited[eng][sname] = prev
            waits.append((sname, prev))
        self.dval[eng][i] += 16
        tok = (sname, self.dval[eng][i])
        self.ops[eng].append((waits, fn, (sname, 16)))
        self._commit(tok, reads, writes)
        if final:
            self.final_tokens.append(tok)
        return tok

    def bank(self):
        b = self._bank = (getattr(self, "_bank", -1) + 1) % 8
        return b

    def mm(self, out, lhsT, rhs, start, stop, r, w):
        return self.op("pe", lambda e: e.matmul(out=out, lhsT=lhsT, rhs=rhs, start=start, stop=stop), r, w)

    def tr(self, out, in_, ident, r, w):
        return self.op("pe", lambda e: e.transpose(out=out, in_=in_, identity=ident), r, w)

    def act(self, out, in_, func, r, w, bias=None, scale=None):
        kw = {}
        if bias is not None:
            kw["bias"] = bias
        if scale is not None:
            kw["scale"] = scale
        return self.op("act", lambda e: e.activation(out=out, in_=in_, func=func, **kw), r, w)

    def tt(self, eng, out, in0, in1, op, r, w):
        return self.op(eng, lambda e: e.tensor_tensor(out=out, in0=in0, in1=in1, op=op), r, w)

    def ts(self, eng, out, in0, s1, op0, r, w, s2=None, op1=None):
        if op1 is None:
            return self.op(eng, lambda e: e.tensor_scalar(out=out, in0=in0, scalar1=s1, scalar2=None, op0=op0), r, w)
        return self.op(eng, lambda e: e.tensor_scalar(out=out, in0=in0, scalar1=s1, scalar2=s2, op0=op0, op1=op1), r, w)

    def stt(self, eng, out, in0, scalar, in1, op0, op1, r, w):
        return self.op(eng, lambda e: e.scalar_tensor_tensor(out=out, in0=in0, scalar=scalar, in1=in1, op0=op0, op1=op1), r, w)

    def cp(self, eng, out, in_, r, w):
        if eng == "act":
            return self.op("act", lambda e: e.copy(out=out, in_=in_), r, w)
        return self.op(eng, lambda e: e.tensor_copy(out=out, in_=in_), r, w)

    def memset(self, eng, out, val, w):
        return self.op(eng, lambda e: e.memset(out, val), (), w)

    def ld(self, out, in_, w, r=(), eng="sp"):
        return self.dma(eng, lambda e: e.dma_start(out=out, in_=in_), r, w)

    def st(self, out, in_, r, w=(), eng="pool"):
        return self.dma(eng, lambda e: e.dma_start(out=out, in_=in_), r, w)

    def barrier(self):
        toks = []
        for e in ENGS:
            if self.cnt[e] > 0:
                toks.append(("es_" + e, self.cnt[e]))
        for e in self.dsem:
            for i in range(N_DMA_SEMS):
                if self.dval[e][i] > 0:
                    toks.append(("ds_%s%d" % (e, i), self.dval[e][i]))
        for e in ENGS:
            waits = []
            for s_, v in toks:
                if s_ == "es_" + e and e in ("pe", "sp"):
                    continue
                if self.waited[e].get(s_, 0) < v:
                    self.waited[e][s_] = v
                    waits.append((s_, v))
            if waits:
                self.ops[e].append((waits, None, None))
        self.lastw = {}
        self.readers = {}

    def emit(self, eng, eobj):
        for waits, fn, inc in self.ops[eng]:
            sname, amt = inc if inc is not None else (None, None)
            for s, v in waits:
                eobj.wait_ge(self.semobj[s], v)
            if fn is None:
                continue
            ins = fn(eobj)
            ins.then_inc(self.semobj[sname], amt)
        if eng == "sp":
            for e in self.dsem:
                for i in range(N_DMA_SEMS):
                    if self.dval[e][i] > 0:
                        eobj.wait_ge(self.dsem[e][i], self.dval[e][i])


def build_program():
    nc = bass.Bass("TRN2", target_bir_lowering=False)
    stack = contextlib.ExitStack()
    with stack:
        def din(name, shape, dt=F32):
            return nc.dram_tensor(name, list(shape), dt, kind="ExternalInput").ap()

        def dout(name, shape, dt=F32):
            return nc.dram_tensor(name, list(shape), dt, kind="ExternalOutput").ap()

        def dscr(name, shape, dt=F32):
            return nc.dram_tensor(name, list(shape), dt, kind="Internal").ap()

        def sb(name, shape, dt=F32):
            return stack.enter_context(nc.sbuf_tensor(name, list(shape), dt))

        def ps(name, shape, dt=F32):
            return stack.enter_context(nc.psum_tensor(name, list(shape), dt))

        x_p = din("x_p", [TP, D])
        x_s = din("x_s", [128, D])
        w_in = din("w_in", [D, IN_DIM])
        ident_d = din("ident", [128, 128])
        st_win = din("st_win", [SSC, 512, 256])
        st_conv = din("st_conv", [SSC, 3, 512])

        o_cmp_p = dout("o_cmp_p", [TP, 256])
        o_slc_p = dout("o_slc_p", [TP, 256])
        o_win_p = dout("o_win_p", [SPC, 512, 256])
        o_conv_p = dout("o_conv_p", [SPC, 3, 512])
        o_cmp_s = dout("o_cmp_s", [SSC, 256])
        o_slc_s = dout("o_slc_s", [SSC, 256])
        o_win_s = dout("o_win_s", [SSC, 512, 256])
        o_conv_s = dout("o_conv_s", [SSC, 3, 512])

        convw_d = din("convw_l", [128, 4, 4])
        convb_d = din("convb_l", [128, 4])
        wq_d = din("wq_l", [128, 4, 128])
        wk_d = din("wk_l", [128, 4, 128])
        gb_d = din("gb_l", [4, 2])
        normg_d = din("normg_rep", [128, 512])
        sel_d = din("sel_c", [4, 4, 128])
        tri_d = din("tri_c", [128, 128])
        o_C_p = dout("o_C_p", [SPC, 4, 128, 128])
        o_n_p = dout("o_n_p", [SPC, 4, 128])
        o_m_p = dout("o_m_p", [SPC, 4])
        MOUT = (dout if DEBUG else dscr)("MOUT", [TP + 128, 512])
        cbt_d = din("cbt", [SEQ, 8, 128])
        tz_d = din("tz", [128, 8, 2, 128])
        wz_d = din("wz", [128, 128])
        ebig_d = din("ebig", [32, SEQ])
        fvnc_d = din("fvnc", [128, 16, 2, 32])
        rowv_d = din("rowv", [128, 1])
        sh_d = din("shm", [128, 128])
        cbr_d = din("cbr", [128, 8])
        w1l_d = din("w1l", [128, 2, 16, 128])
        w1n_d = din("w1n", [128, 2, 16, 64])
        pel_d = din("pel", [128, 2, 16])
        w2d_d = din("w2d", [64, 2, 128])
        w2n_d = din("w2n", [64, 2, 64])
        NOUTD = (dout if DEBUG else dscr)("NOUTD", [TP + 128, 512])
        wupm_d = din("w_up_m", [512, D])
        wupn_d = din("w_up_n", [512, D])
        wout_d = din("w_out", [D, D])
        lnp_d = din("lnp", [128, 4, D])
        pwqt_d = din("pwq_t", [128, 16, 8, 128])
        skt_d = din("sk_t", [128, 2, 128])
        iota_d = din("iota16", [128, 16])
        peer_u_d = din("peer_u", [16384, D])
        peer_v_d = din("peer_v", [16384, D])
        X1D = (dout if DEBUG else dscr)("X1D", [TP + 128, D])
        y_out = dout("y_out", [TP + 128, D])
        U16D = dscr("U16D", [16384, D], BF16)
        V16D = dscr("V16D", [16384, D], BF16)
        stC_d = din("st_C", [SSC * 4, 128, 128])
        stn_d = din("st_n", [SSC, 512])
        stm_d = din("st_m", [SSC, 4])
        cw4_d = din("cw4", [4, 4, 512])
        cb4_d = din("cb4", [4, 512])
        gb4_d = din("gb4", [4, 8])
        ng16_d = din("ng16", [16, 128])
        o_C_s = dout("o_C_s", [SSC * 4, 128, 128])
        o_n_s = dout("o_n_s", [SSC, 512])
        o_m_s = dout("o_m_s", [SSC, 4])
        SCRB = dscr("SCRB", [16, 264])
        pool_cmp_d = din("pool_cmp", [5120 * 8, 4096])
        pool_slc_d = din("pool_slc", [5120 * 8, 4096])
        ptl_d = din("pt_l", [128, 4], I32)
        pt8_d = din("pt8", [8, 128], I32)
        cbs_d = din("cbs", [128, 8, 8])
        bw_d = din("bw", [128, 4, 8])
        t25_d = din("t25", [60, 2, 8, 64])
        fl255_d = din("fl255", [60, 1])
        ind_d = din("ind60", [60, 4])
        rb0_d = din("rb0", [4, 8])
        shd_d = din("shd", [128, 128])
        w2kt_d = din("w2kt", [64, 64])
        iota8_d = din("iota8", [128, 8])
        iot128_d = din("iot128", [8, 2, 128])
        SCRQ = dscr("SCRQ", [4, 512])
        OSD = dscr("OSD", [2, 4, 8, 65])
        SELD = dscr("SELD", [8, 256])
        SCRP = dscr("SCRP", [8, 15, 5])
        Z = dscr("Z", [TP + 128, IN_DIM])

        P = Prog(nc, stack)

        ARENA_F = 52800
        EI = sb("EI_i32", [128, 128], I32)[:, :]
        EI2 = sb("EI2_i32", [128, 128], I32)[:, :]
        IDXC = sb("IDXC_i32", [128, 32], I32)[:, :]
        IDXS = sb("IDXS_i32", [128, 8], I32)[:, :]
        arena = sb("arena", [128, ARENA_F])
        apos = [0]

        def alloc(n):
            o = apos[0]
            apos[0] += n
            assert apos[0] <= ARENA_F, ("arena overflow", apos[0])
            return arena[:, o:o + n]

        def alloc3(a, b):
            return alloc(a * b).rearrange("p (a b) -> p a b", a=a)

        ident = alloc(128)
        pbank = [ps("pb%d" % i, [128, 512]) for i in range(8)]
        tri = alloc(128)
        selc = alloc3(4, 128)
        persist_mark = apos[0]

        QT = 8
        xt = [alloc(D) for i in range(2)]
        xT = alloc3(8, QT * 128)
        wg = [alloc3(8, 512) for i in range(2)]
        zs = [alloc(512) for i in range(2)]

        P.dma("sp", lambda e: e.dma_start(out=ident, in_=ident_d[:, :]), writes=["ident"])
        P.ld(tri, tri_d[:, :], ["tri"])
        P.ld(selc[0:4], sel_d[:, :, :], ["selc"])

        n_tiles_all = NT + 1
        groups = [list(range(g, min(g + QT, n_tiles_all))) for g in range(0, n_tiles_all, QT)]
        xcnt = 0
        wcnt = 0
        zcnt = 0
        pcnt = 0
        for grp in groups:
            for li, ti in enumerate(grp):
                b = xcnt % 2
                xcnt += 1
                src = x_p[ti * 128:(ti + 1) * 128, :] if ti < NT else x_s[:, :]
                P.dma("sp", lambda e, b=b, src=src: e.dma_start(out=xt[b], in_=src),
                      writes=["xt%d" % b])
                for c in range(8):
                    pb = pcnt % 8
                    pcnt += 1
                    P.op("pe", lambda e, pb=pb, b=b, c=c: e.transpose(
                        out=pbank[pb][:, 0:128], in_=xt[b][:, c * 128:(c + 1) * 128], identity=ident),
                        reads=["xt%d" % b, "ident"], writes=["pb%d" % pb])
                    eng = "dve" if c % 2 == 0 else "act"
                    if eng == "dve":
                        P.op("dve", lambda e, pb=pb, c=c, li=li: e.tensor_copy(
                            out=xT[:, c, li * 128:(li + 1) * 128], in_=pbank[pb][:, 0:128]),
                            reads=["pb%d" % pb], writes=["xT_%d_%d" % (c, li)])
                    else:
                        P.op("act", lambda e, pb=pb, c=c, li=li: e.copy(
                            out=xT[:, c, li * 128:(li + 1) * 128], in_=pbank[pb][:, 0:128]),
                            reads=["pb%d" % pb], writes=["xT_%d_%d" % (c, li)])
            for (c0, cw) in COL_GROUPS:
                wb = wcnt % 2
                wcnt += 1
                P.dma("sp", lambda e, wb=wb, c0=c0, cw=cw: e.dma_start(
                    out=wg[wb][:, :, 0:cw],
                    in_=w_in[:, c0:c0 + cw].rearrange("(c p) n -> p c n", p=128)),
                    writes=["wg%d" % wb])
                for li, ti in enumerate(grp):
                    pb = pcnt % 8
                    pcnt += 1
                    for c in range(8):
                        P.op("pe", lambda e, pb=pb, c=c, li=li, wb=wb, cw=cw: e.matmul(
                            out=pbank[pb][:, 0:cw], lhsT=xT[:, c, li * 128:(li + 1) * 128],
                            rhs=wg[wb][:, c, 0:cw], start=(c == 0), stop=(c == 7)),
                            reads=["xT_%d_%d" % (c, li), "wg%d" % wb], writes=["pb%d" % pb])
                    zb = zcnt % 2
                    zcnt += 1
                    if zcnt % 2 == 0:
                        P.op("dve", lambda e, pb=pb, zb=zb, cw=cw: e.tensor_copy(
                            out=zs[zb][:, 0:cw], in_=pbank[pb][:, 0:cw]),
                            reads=["pb%d" % pb], writes=["zs%d" % zb])
                    else:
                        P.op("act", lambda e, pb=pb, zb=zb, cw=cw: e.copy(
                            out=zs[zb][:, 0:cw], in_=pbank[pb][:, 0:cw]),
                            reads=["pb%d" % pb], writes=["zs%d" % zb])
                    P.dma("pool", lambda e, zb=zb, ti=ti, c0=c0, cw=cw: e.dma_start(
                        out=Z[ti * 128:(ti + 1) * 128, c0:c0 + cw], in_=zs[zb][:, 0:cw]),
                        reads=["zs%d" % zb], writes=["Z_%d_%d" % (ti, c0)])

        ZALL = ["Z_%d_%d" % (ti, c0) for ti in range(NT + 1) for (c0, _) in COL_GROUPS]

        def d2d(dst, src, final=True):
            P.dma("sp", lambda e: e.dma_start(out=dst, in_=src), reads=ZALL, writes=[], final=final)

        d2d(o_cmp_p[:, :], Z[0:TP, O_KC:O_KC + 256])
        d2d(o_slc_p[:, :], Z[0:TP, O_KS:O_KS + 256])
        for s in range(SPC):
            d2d(o_win_p[s, :, :], Z[s * SEQ + SEQ - 512:(s + 1) * SEQ, O_KW:O_KW + 256])
            d2d(o_conv_p[s, :, :], Z[(s + 1) * SEQ - 3:(s + 1) * SEQ, O_U:O_U + 512])
        d2d(o_cmp_s[:, :], Z[TP:TP + SSC, O_KC:O_KC + 256])
        d2d(o_slc_s[:, :], Z[TP:TP + SSC, O_KS:O_KS + 256])
        for s in range(SSC):
            d2d(o_win_s[s, 0:511, :], st_win[s, 1:512, :])
            d2d(o_win_s[s, 511:512, :], Z[TP + s:TP + s + 1, O_KW:O_KW + 256])
            d2d(o_conv_s[s, 0:2, :], st_conv[s, 1:3, :])
            d2d(o_conv_s[s, 2:3, :], Z[TP + s:TP + s + 1, O_U:O_U + 512])


        P.barrier()
        apos[0] = persist_mark
        CONST = ["ident", "tri", "selc", "m_w"]
        convw = alloc3(4, 4)
        convb = alloc(4)
        wq = alloc3(4, 128)
        wk = alloc3(4, 128)
        gb = alloc(2)
        ngbf = alloc(1)
        normg = alloc(512)
        zero_row = alloc(SEQ)
        P.ld(convw, convw_d[:, :, :], ["m_w"])
        P.ld(convb, convb_d[:, :], ["m_w"])
        P.ld(wq, wq_d[:, :, :], ["m_w"])
        P.ld(wk, wk_d[:, :, :], ["m_w"])
        P.ld(gb[0:4], gb_d[:, :], ["gb"])
        P.ld(normg, normg_d[:, :], ["m_w"])
        P.memset("dve", zero_row, 0.0, ["zero_row"])
        P.ts("dve", ngbf[0:4], gb[0:4, 1:2], -1.0, ALU.mult, ["gb"], ["ngbf"])

        ZIF = alloc3(16, 8)
        IT = alloc(SEQ)
        FT = alloc(SEQ)
        Bc = alloc(SEQ)
        Ac = IT
        CMc = FT
        Mc = Bc
        NRc = alloc(SEQ)
        EMc = alloc(SEQ)
        aT = alloc3(16, 4)
        emT = alloc3(16, 4)
        Uh = alloc3(16, 128)
        UT = alloc(3 + SEQ)
        CT = alloc(SEQ)
        ACC = CT
        QTh = alloc(SEQ)
        KTh = alloc(SEQ)
        KTOK = alloc3(16, 128)
        V1 = alloc3(16, 129)
        OP = alloc3(16, 128)
        SIG = alloc3(16, 128)
        Rb = alloc(SEQ)
        WT = alloc3(16, 512)
        Eb = [alloc(512) for _ in range(2)]
        MO = alloc3(16, 128)
        wts = alloc(16)
        VW = alloc3(16, 128)
        Csb = alloc(128)
        nsb = alloc(128)
        ep = [dict(dd=alloc(1), rec=alloc(1), hq=alloc(128), st=alloc(8), ag=alloc(4), rstd=alloc(1),
                   hn=alloc(128)) for _ in range(2)]
        BN_S = 6

        P.memset("dve", UT[:, 0:3], 0.0, ["UT"])
        P.memset("dve", V1[:, :, 128:129], 1.0, ["V1ones"])
        ecnt = 0
        epc = 0
        for sq in range(SPC):
            r0 = sq * SEQ
            zrows = Z[r0:r0 + SEQ, :]
            P.ld(ZIF, zrows[:, O_I:O_I + 8].rearrange("(n p) c -> p n c", p=128), ["ZIF"])
            for which, dst, dkey in ((0, IT, "ITb"), (1, FT, "FTb")):
                for g in range(4):
                    pb = P.bank()
                    for j in range(4):
                        n = g * 4 + j
                        P.tr(pbank[pb][0:4, j * 128:(j + 1) * 128], ZIF[:, n, which * 4:which * 4 + 4], ident,
                             ["ZIF", "ident"], ["pb%d" % pb])
                    P.cp("dve", dst[0:4, g * 512:(g + 1) * 512], pbank[pb][0:4, :], ["pb%d" % pb], [dkey])
            ITk = ["ITb"]
            FTk = ["FTb"]
            P.ts("dve", IT[0:4], IT[0:4], gb[0:4, 0:1], ALU.add, ITk + ["gb"], ["ITb"])
            P.act(FT[0:4], FT[0:4], AF.Exp, FTk + ["ngbf"], ["FTb"], bias=ngbf[0:4], scale=-1.0)
            P.act(FT[0:4], FT[0:4], AF.Ln, ["FTb"], ["FTb"], bias=1.0, scale=1.0)
            P.ts("dve", FT[0:4], FT[0:4], -1.0, ALU.mult, ["FTb"], ["FTb"])
            P.op("dve", lambda e: e.tensor_tensor_scan(out=Bc[0:4], data0=FT[0:4], data1=zero_row[0:4], initial=0.0,
                                                      op0=ALU.add, op1=ALU.add), ["FTb", "zero_row"], ["Bb"])
            P.tt("dve", Ac[0:4], IT[0:4], Bc[0:4], ALU.subtract, ["ITb", "Bb"], ["ITb", "ITb"] + ITk)
            P.op("dve", lambda e: e.tensor_tensor_scan(out=CMc[0:4], data0=Ac[0:4], data1=Ac[0:4], initial=0.0,
                                                      op0=ALU.max, op1=ALU.max), ["ITb"], ["FTb", "FTb", "FTb", "FTb"] + FTk)
            P.ts("dve", NRc[0:4], CMc[0:4], -1.0, ALU.mult, ["FTb"], ["NRc"])
            P.tt("dve", Mc[0:4], Bc[0:4], CMc[0:4], ALU.add, ["Bb", "FTb", "ITb"], ["Bb", "Bb"])
            P.act(EMc[0:4], Mc[0:4], AF.Exp, ["Bb"], ["EMc"], scale=-1.0)
            P.st(o_m_p[sq, :].rearrange("(h o) -> h o", o=1), Mc[0:4, SEQ - 1:SEQ], ["Bb"])
            for src, skey, dst, dkey in ((Ac, "ITb", aT, "aT"), (EMc, "EMc", emT, "emT")):
                pb = P.bank()
                for n in range(16):
                    P.tr(pbank[pb][:, n * 4:(n + 1) * 4], src[0:4, n * 128:(n + 1) * 128], ident[0:4, 0:4],
                         [skey, "ident"], ["pb%d" % pb])
                P.cp("dve", dst, pbank[pb][:, 0:64].rearrange("p (n h) -> p n h", h=4), ["pb%d" % pb], [dkey])
            for h in range(4):
                hs = slice(h * 128, (h + 1) * 128)
                P.ld(Uh, zrows[:, O_U + h * 128:O_U + (h + 1) * 128].rearrange("(n p) c -> p n c", p=128), ["Uh"])
                P.ld(V1[:, :, 0:128], zrows[:, O_V + h * 128:O_V + (h + 1) * 128].rearrange("(n p) c -> p n c", p=128),
                     ["V1"])
                P.ld(OP, zrows[:, O_O + h * 128:O_O + (h + 1) * 128].rearrange("(n p) c -> p n c", p=128), ["OP"])
                P.act(SIG, OP, AF.Sigmoid, ["OP"], ["SIG"])
                for g in range(4):
                    pb = P.bank()
                    P.mm(pbank[pb][:, :], selc[0:4, h, :], NRc[0:4, g * 512:(g + 1) * 512], True, True,
                         ["selc", "NRc"], ["pb%d" % pb])
                    P.cp("act", Rb[:, g * 512:(g + 1) * 512], pbank[pb][:, :], ["pb%d" % pb], ["Rb%d" % g])
                for g in range(4):
                    pb = P.bank()
                    for j in range(4):
                        n = g * 4 + j
                        P.tr(pbank[pb][:, j * 128:(j + 1) * 128], Uh[:, n, :], ident, ["Uh", "ident"], ["pb%d" % pb])
                    P.cp("dve", UT[:, 3 + g * 512:3 + (g + 1) * 512], pbank[pb][:, :], ["pb%d" % pb], ["UT"])
                P.ts("dve", ACC, UT[:, 0:SEQ], convw[:, h, 0:1], ALU.mult, ["UT", "m_w"], ["CT", "CT"])
                for j in range(1, 4):
                    P.stt("dve", ACC, UT[:, j:j + SEQ], convw[:, h, j:j + 1], ACC, ALU.mult, ALU.add,
                          ["UT", "CT", "m_w"], ["CT"])
                P.act(CT, ACC, AF.Silu, ["CT", "m_w"], ["CT", "CT"], bias=convb[:, h:h + 1])
                for g in range(4):
                    pb = P.bank()
                    P.mm(pbank[pb][:, :], wq[:, h, :], CT[:, g * 512:(g + 1) * 512], True, True, ["m_w", "CT"],
                         ["pb%d" % pb])
                    P.cp("act", QTh[:, g * 512:(g + 1) * 512], pbank[pb][:, :], ["pb%d" % pb], ["QT%d" % g])
                    pb = P.bank()
                    P.mm(pbank[pb][:, :], wk[:, h, :], CT[:, g * 512:(g + 1) * 512], True, True, ["m_w", "CT"],
                         ["pb%d" % pb])
                    P.ts("dve", KTh[:, g * 512:(g + 1) * 512], pbank[pb][:, :], 128.0 ** -0.5, ALU.mult,
                         ["pb%d" % pb], ["KT"])
                    pb = P.bank()
                    for j in range(4):
                        n = g * 4 + j
                        P.mm(pbank[pb][:, j * 128:(j + 1) * 128], CT[:, n * 128:(n + 1) * 128], wk[:, h, :], True, True,
                             ["m_w", "CT"], ["pb%d" % pb])
                    P.ts("dve", KTOK[:, g * 4:(g + 1) * 4, :], pbank[pb][:, :].rearrange("p (j e) -> p j e", j=4),
                         128.0 ** -0.5, ALU.mult, ["pb%d" % pb], ["KTOK"])
                for tb in range(4):
                    nst = 4 * tb + 4
                    for st_ in range(nst):
                        p_ = max(0, st_ - 4 * tb)
                        c0 = p_ * 128
                        cs = slice(c0, 512)
                        gs = slice(tb * 512 + c0, (tb + 1) * 512)
                        pb = P.bank()
                        P.mm(pbank[pb][:, cs], KTh[:, st_ * 128:(st_ + 1) * 128], QTh[:, gs], True, True,
                             ["KT", "QT%d" % tb], ["pb%d" % pb])
                        eb = ecnt % 2
                        ecnt += 1
                        P.ts("dve", Eb[eb][:, cs], Rb[:, gs], aT[:, st_, h:h + 1], ALU.add, ["Rb%d" % tb, "aT"],
                             ["Eb%d" % eb], s2=0.0, op1=ALU.min)
                        P.act(Eb[eb][:, cs], Eb[eb][:, cs], AF.Exp, ["Eb%d" % eb], ["Eb%d" % eb])
                        if st_ >= 4 * tb:
                            P.tt("pool", Eb[eb][:, c0:c0 + 128], Eb[eb][:, c0:c0 + 128], tri, ALU.mult,
                                 ["Eb%d" % eb, "tri"], ["Eb%d" % eb])
                        P.tt("dve", WT[:, st_, cs], pbank[pb][:, cs], Eb[eb][:, cs], ALU.mult,
                             ["pb%d" % pb, "Eb%d" % eb], ["WT%d" % st_])
                    for sub in range(4):
                        tq = 4 * tb + sub
                        pb = P.bank()
                        for st_ in range(tq + 1):
                            P.mm(pbank[pb][:, 0:129], WT[:, st_, sub * 128:(sub + 1) * 128], V1[:, st_, :],
                                 st_ == 0, st_ == tq, ["WT%d" % st_, "V1", "V1ones"], ["pb%d" % pb])
                        E = ep[epc % 2]
                        ek = "ep%d" % (epc % 2)
                        epc += 1
                        P.act(E["dd"], pbank[pb][:, 128:129], AF.Abs, ["pb%d" % pb], [ek + "dd"])
                        P.ts("dve", E["dd"], E["dd"], emT[:, tq, h:h + 1], ALU.max, [ek + "dd", "emT"], [ek + "dd"])
                        P.op("dve", lambda e, E=E: e.reciprocal(out=E["rec"], in_=E["dd"]), [ek + "dd"], [ek + "rec"])
                        P.ts("dve", E["hq"], pbank[pb][:, 0:128], E["rec"], ALU.mult, ["pb%d" % pb, ek + "rec"],
                             [ek + "hq"])
                        P.op("dve", lambda e, E=E: e.bn_stats(out=E["st"][:, 0:BN_S], in_=E["hq"]), [ek + "hq"],
                             [ek + "st"])
                        P.op("dve", lambda e, E=E: e.bn_aggr(out=E["ag"][:, 0:2], in_=E["st"][:, 0:BN_S]), [ek + "st"],
                             [ek + "ag"])
                        P.act(E["rstd"], E["ag"][:, 1:2], AF.Sqrt, [ek + "ag"], [ek + "rstd"], bias=1e-5, scale=1.0)
                        P.op("dve", lambda e, E=E: e.reciprocal(out=E["rstd"], in_=E["rstd"]), [ek + "rstd"],
                             [ek + "rstd"])
                        P.ts("dve", E["hn"], E["hq"], E["ag"][:, 0:1], ALU.subtract, [ek + "hq", ek + "ag", ek + "rstd"],
                             [ek + "hn"], s2=E["rstd"], op1=ALU.mult)
                        P.tt("pool", E["hn"], E["hn"], normg[:, hs], ALU.mult, [ek + "hn", "m_w"], [ek + "hn"])
                        P.tt("pool", MO[:, tq, :], E["hn"], SIG[:, tq, :], ALU.mult, [ek + "hn", "SIG"], ["MO"])
                P.st(MOUT[r0:r0 + SEQ, hs].rearrange("(n p) c -> p n c", p=128), MO, ["MO"], ["MOUT"])
                P.act(wts, aT[:, :, h], AF.Exp, ["aT", "Rb3"], ["wts"], bias=Rb[:, SEQ - 1:SEQ])
                P.tt("dve", VW, V1[:, :, 0:128], wts.unsqueeze(2).to_broadcast([128, 16, 128]), ALU.mult,
                     ["V1", "wts"], ["VW"])
                pb = P.bank()
                for st_ in range(16):
                    P.mm(pbank[pb][:, 0:128], VW[:, st_, :], KTOK[:, st_, :], st_ == 0, st_ == 15, ["VW", "KTOK"],
                         ["pb%d" % pb])
                P.cp("act", Csb, pbank[pb][:, 0:128], ["pb%d" % pb], ["Csb"])
                P.st(o_C_p[sq, h, :, :], Csb, ["Csb"])
                pb = P.bank()
                for st_ in range(16):
                    P.mm(pbank[pb][0:1, 0:128], wts[:, st_:st_ + 1], KTOK[:, st_, :], st_ == 0, st_ == 15,
                         ["wts", "KTOK"], ["pb%d" % pb])
                P.cp("act", nsb[0:1], pbank[pb][0:1, 0:128], ["pb%d" % pb], ["nsb"])
                P.st(o_n_p[sq:sq + 1, h, :], nsb[0:1], ["nsb"])


        P.barrier()
        apos[0] = persist_mark
        BIGM = 30000.0
        QT8 = alloc3(4, SEQ)
        KTz = [[alloc(SEQ) for _ in range(2)] for _ in range(2)]
        V1n = [alloc3(16, 65) for _ in range(2)]
        KCTz = alloc(2 * 2 * 128).rearrange("p (k f n) -> p k f n", k=2, f=2)
        KCV = alloc3(2, 64)
        TZs = alloc(8 * 2 * 128).rearrange("p (h k j) -> p h k j", h=8, k=2)
        WZ = alloc(128)
        SHM = alloc(128)
        EBIG = alloc(SEQ)
        SELM1 = [alloc(SEQ) for _ in range(2)]
        XTc = SELM1
        FVNC = alloc(16 * 2 * 32).rearrange("p (n k b) -> p n k b", n=16, k=2)
        ROWV = alloc(1)
        CBR = alloc(8)
        W2Z = alloc3(2, 128)
        W2N = alloc3(2, 64)
        PET = alloc(128)
        ONES = alloc(128)
        GS = alloc3(16, 24)
        NOUT = alloc3(4, 512)
        stg = [alloc3(16, 128) for _ in range(2)]
        CB = [alloc3(8, 128) for _ in range(2)]
        Sx = alloc3(8, 128)
        Pn = alloc3(8, 128)
        PnT = Sx
        sm8 = alloc(8)
        rs8 = alloc(8)
        PG = alloc3(2, 128)
        PSs = alloc3(2, 32)
        S012 = alloc3(2, 32)
        SC = alloc3(2, 32)
        M8 = alloc(8)
        SCW = alloc(32)
        SEL = alloc3(2, 32)
        PAB = alloc(128)
        XG = alloc(64)
        TG = alloc(64)
        HID = alloc(64)
        HT = alloc(128)
        epn = [dict(r=alloc(1)) for _ in range(4)]
        PTb = alloc3(16, 512)
        W1L = PTb[:, 0:8, :].rearrange("p a b -> p (a b)").rearrange("p (c r o) -> p c r o", c=2, r=16)
        W1N = PTb[:, 8:12, :].rearrange("p a b -> p (a b)").rearrange("p (c j o) -> p c j o", c=2, j=16)
        PEL = PTb[:, 12, 0:32].rearrange("p (c j) -> p c j", c=2)
        PTK = ["PT%d" % i for i in range(16)]

        P.ld(TZs, tz_d[:, :, :, :], ["TZ"])
        P.ld(WZ, wz_d[:, :], ["ncst"])
        P.ld(SHM, sh_d[:, :], ["ncst"])
        P.ld(EBIG[0:32], ebig_d[:, :], ["ncst"])
        P.ld(FVNC, fvnc_d[:, :, :, :], ["ncst"])
        P.ld(ROWV, rowv_d[:, :], ["ncst"])
        P.ld(CBR, cbr_d[:, :], ["CBR"])
        P.ld(W2Z[0:64], w2d_d[:, :, :], ["ncst"])
        P.ld(W2N[0:64], w2n_d[:, :, :], ["ncst"])
        P.memset("dve", ONES, 1.0, ["ONES"])
        for h in range(8):
            P.ts("dve", TZs[:, h, :, :], TZs[:, h, :, :], CBR[:, h:h + 1], ALU.subtract, ["TZ", "CBR"], ["TZ"])
        for br in range(2):
            P.memset("dve", V1n[br][:, :, 64:65], 1.0, ["V1n1"])
        cbcnt = 0
        encnt = 0
        K3 = int(os.environ.get("K3STOP", "9"))

        def gelu_from(xsb, tmp, out, rk, wk_):
            P.tt("dve", tmp, xsb, xsb, ALU.mult, rk, [wk_ + "t"])
            P.ts("dve", tmp, tmp, 0.044715, ALU.mult, [wk_ + "t"], [wk_ + "t"], s2=1.0, op1=ALU.add)
            P.tt("dve", tmp, tmp, xsb, ALU.mult, [wk_ + "t"] + rk, [wk_ + "t"])
            P.act(tmp, tmp, AF.Sigmoid, [wk_ + "t"], [wk_ + "t"], scale=1.5957691216057308)
            P.tt("dve", out, tmp, xsb, ALU.mult, [wk_ + "t"] + rk, [wk_])

        def tr_block(src3, dst, wkeys, scale=None, rmajor=False):
            for g in range(4):
                pb = P.bank()
                for j in range(4):
                    P.tr(pbank[pb][:, j * 128:(j + 1) * 128], src3[:, g * 4 + j, :], ident, ["stgX", "ident"],
                         ["pb%d" % pb])
                if rmajor:
                    P.cp("act", dst.rearrange("p (r n) -> p r n", r=16)[:, :, g * 32:(g + 1) * 32],
                         pbank[pb][:, :].rearrange("p (n r) -> p r n", r=16), ["pb%d" % pb], wkeys)
                elif scale is not None:
                    P.ts("dve", dst[:, g * 512:(g + 1) * 512], pbank[pb][:, :], scale, ALU.mult, ["pb%d" % pb], wkeys)
                else:
                    P.cp("act", dst[:, g * 512:(g + 1) * 512], pbank[pb][:, :], ["pb%d" % pb], wkeys)

        for sq in range(SPC if K3 >= 2 else 0):
            r0 = sq * SEQ
            zrows = Z[r0:r0 + SEQ, :]

            def zt(c0, w):
                return zrows[:, c0:c0 + w].rearrange("(n p) c -> p n c", p=128)

            for pr in range(4):
                P.ld(stg[0], zt(O_Q + pr * 128, 128), ["stgX"])
                tr_block(stg[0], QT8[:, pr, :], ["QT8"], scale=0.125)
            for c in range(2):
                P.ld(stg[0], zt(O_KC + c * 128, 128), ["stgX"])
                tr_block(stg[0], XTc[c], ["XTc", "SELM1_0", "SELM1_1"], rmajor=True)
            P.ld(GS, zt(O_GN, 24), ["GSraw"])
            P.act(GS, GS, AF.Sigmoid, ["GSraw"], ["GS", "GSraw"])
            P.ld(W1L, w1l_d[:, :, :, :], ["W1L"] + PTK)
            P.ld(W1N, w1n_d[:, :, :, :], ["W1N"] + PTK)
            P.ld(PEL, pel_d[:, :, :], ["PEL"] + PTK)
            for c in range(2):
                pb = P.bank()
                for j in range(16):
                    P.mm(pbank[pb][0:1, 0:64], PEL[:, c, j:j + 1], W1N[:, c, j, :], j == 0, j == 15, ["PEL", "W1N"],
                         ["pb%d" % pb])
                P.cp("act", PET[0:1, c * 64:(c + 1) * 64], pbank[pb][0:1, 0:64], ["pb%d" % pb], ["PET"])
            for c in range(2):
                for k in range(2):
                    rows = slice(k * 64, (k + 1) * 64)
                    xv = XTc[c][rows, :].rearrange("p (r n) -> p r n", r=16)
                    pb = P.bank()
                    for r in range(16):
                        P.mm(pbank[pb][:, 0:128], xv[:, r, :], W1L[rows, c, r, :], r == 0, r == 15, ["XTc", "W1L"],
                             ["pb%d" % pb])
                    P.cp("act", PAB, pbank[pb][:, 0:128], ["pb%d" % pb], ["PAB"])
                    pb = P.bank()
                    P.mm(pbank[pb][:, 0:64], ident, PAB[:, 0:64], True, False, ["ident", "PAB"], ["pb%d" % pb])
                    P.mm(pbank[pb][:, 0:64], SHM, PAB[:, 64:128], False, False, ["ncst", "PAB"], ["pb%d" % pb])
                    P.mm(pbank[pb][:, 0:64], ONES[0:1, :], PET[0:1, c * 64:(c + 1) * 64], False, True, ["ONES", "PET"],
                         ["pb%d" % pb])
                    P.cp("act", XG, pbank[pb][:, 0:64], ["pb%d" % pb], ["XG"])
                    gelu_from(XG, TG, HID, ["XG"], "HID")
                    pb = P.bank()
                    P.tr(pbank[pb][0:64, 0:128], HID, ident, ["HID", "ident"], ["pb%d" % pb])
                    P.cp("act", HT[0:64], pbank[pb][0:64, 0:128], ["pb%d" % pb], ["HT"])
                    if c == 0:
                        for f in range(2):
                            pb = P.bank()
                            P.mm(pbank[pb][:, 0:128], W2Z[0:64, f, :], HT[0:64], True, True, ["ncst", "HT"],
                                 ["pb%d" % pb])
                            P.cp("act", KCTz[:, k, f, :], pbank[pb][:, 0:128], ["pb%d" % pb], ["KCT"])
                    else:
                        pb = P.bank()
                        P.mm(pbank[pb][:, 0:64], HT[0:64], W2N[0:64, c, :], True, True, ["ncst", "HT"], ["pb%d" % pb])
                        P.cp("act", KCV[:, k, :], pbank[pb][:, 0:64], ["pb%d" % pb], ["KCV"])

            def attn_block(br, h, tb, st_list):
                nonlocal encnt
                pr, half, kvh = h // 2, h % 2, h // 4
                hl = h % 4
                active = {}
                for st_ in st_list:
                    subs = [sub for sub in range(4) if 0 <= (4 * tb + sub - st_) <= (4 if br == 1 else 10 ** 6)]
                    if not subs:
                        continue
                    lo, hi = subs[0], subs[-1] + 1
                    active[st_] = (lo, hi)
                    cs = slice(lo * 128, hi * 128)
                    gs = slice(tb * 512 + lo * 128, tb * 512 + hi * 128)
                    pb = P.bank()
                    extra = []
                    for sub in subs:
                        dd = 4 * tb + sub - st_
                        if dd == 0:
                            extra.append((sub, TZs[:, h, 0, :], "TZ"))
                        elif dd == 1:
                            extra.append((sub, TZs[:, h, 1, :], "TZ"))
                        elif dd == 4 and br == 1:
                            extra.append((sub, WZ, "ncst"))
                    nmm = 1 + (1 if br == 0 else 0) + len(extra)
                    i_mm = 0
                    P.mm(pbank[pb][:, cs], KTz[br][half][:, st_ * 128:(st_ + 1) * 128], QT8[:, pr, gs],
                         True, nmm == 1, ["KTz", "QT8"], ["pb%d" % pb])
                    i_mm += 1
                    if br == 0:
                        P.mm(pbank[pb][:, cs], EBIG[0:32, st_ * 128:(st_ + 1) * 128], SELM1[kvh][0:32, gs],
                             False, i_mm == nmm - 1, ["ncst", "SELM1_%d" % kvh], ["pb%d" % pb])
                        i_mm += 1
                    for (sub, tile_, key) in extra:
                        P.mm(pbank[pb][:, sub * 128:(sub + 1) * 128], ident, tile_, False, i_mm == nmm - 1,
                             ["ident", key], ["pb%d" % pb])
                        i_mm += 1
                    P.act(PTb[:, st_, cs], pbank[pb][:, cs], AF.Exp, ["pb%d" % pb, "CBR"], ["PT%d" % st_],
                          bias=CBR[:, h:h + 1])
                for sub in range(4):
                    tq = 4 * tb + sub
                    sts = [st_ for st_ in st_list if st_ in active and active[st_][0] <= sub < active[st_][1]]
                    pb = P.bank()
                    for i, st_ in enumerate(sts):
                        P.mm(pbank[pb][:, 0:65], PTb[:, st_, sub * 128:(sub + 1) * 128], V1n[br][:, st_, :],
                             i == 0, i == len(sts) - 1, ["PT%d" % st_, "V1n", "V1n1"], ["pb%d" % pb])
                    E = epn[encnt % 4]
                    ek = "epn%d" % (encnt % 4)
                    encnt += 1
                    P.ts("dve", E["r"], pbank[pb][:, 64:65], 1e-30, ALU.max, ["pb%d" % pb], [ek])
                    P.op("dve", lambda e, E=E: e.reciprocal(out=E["r"], in_=E["r"]), [ek], [ek])
                    gcol = (1 + br) * 8 + h
                    P.tt("dve", E["r"], E["r"], GS[:, tq, gcol:gcol + 1], ALU.mult, [ek, "GS"], [ek])
                    P.stt("dve", NOUT[:, sub, hl * 64:(hl + 1) * 64], pbank[pb][:, 0:64], E["r"],
                          NOUT[:, sub, hl * 64:(hl + 1) * 64], ALU.mult, ALU.add, ["pb%d" % pb, ek, "NOUT"], ["NOUT"])

            for tb in range(4 if K3 >= 4 else 0):
                for sub in range(4):
                    tq = 4 * tb + sub
                    cb_ = cbcnt % 2
                    cbcnt += 1
                    P.ld(CB[cb_], cbt_d[tq * 128:(tq + 1) * 128, :, :], ["CB%d" % cb_])
                    for hg in range(2):
                        pb = P.bank()
                        for hh in range(4):
                            h = hg * 4 + hh
                            pr, half, kvh = h // 2, h % 2, h // 4
                            P.mm(pbank[pb][:, hh * 128:(hh + 1) * 128], QT8[:, pr, tq * 128:(tq + 1) * 128],
                                 KCTz[:, kvh, half, :], True, True, ["QT8", "KCT"], ["pb%d" % pb])
                        P.tt("dve", Sx[:, hg * 4:(hg + 1) * 4, :],
                             pbank[pb][:, :].rearrange("p (h n) -> p h n", h=4), CB[cb_][:, hg * 4:(hg + 1) * 4, :],
                             ALU.add, ["pb%d" % pb, "CB%d" % cb_], ["Sx", "PnT"])
                    P.op("dve", lambda e: e.reduce_max(out=sm8, in_=Sx, axis=AX.X), ["Sx"], ["sm8"])
                    P.tt("dve", Sx, Sx, sm8.unsqueeze(2).to_broadcast([128, 8, 128]), ALU.subtract, ["Sx", "sm8"], ["Sx"])
                    P.act(Pn, Sx, AF.Exp, ["Sx"], ["Pn"])
                    P.op("dve", lambda e: e.reduce_sum(out=sm8, in_=Pn, axis=AX.X), ["Pn"], ["sm8"])
                    P.op("dve", lambda e: e.reciprocal(out=rs8, in_=sm8), ["sm8"], ["rs8"])
                    if tq == 0:
                        P.ts("dve", rs8, rs8, ROWV[:, 0:1], ALU.mult, ["rs8", "ncst"], ["rs8"])
                    P.tt("dve", Pn, Pn, rs8.unsqueeze(2).to_broadcast([128, 8, 128]), ALU.mult, ["Pn", "rs8"], ["Pn"])
                    for hg in range(2):
                        pb = P.bank()
                        for hh in range(4):
                            h = hg * 4 + hh
                            P.tr(pbank[pb][:, hh * 128:(hh + 1) * 128], Pn[:, h, :], ident, ["Pn", "ident"],
                                 ["pb%d" % pb])
                        P.cp("act", PnT[:, hg * 4:(hg + 1) * 4, :], pbank[pb][:, :].rearrange("p (h n) -> p h n", h=4),
                             ["pb%d" % pb], ["PnT", "Sx"])
                    pb = P.bank()
                    for h in range(8):
                        P.mm(pbank[pb][:, h * 64:(h + 1) * 64], PnT[:, h, :], KCV[:, h // 4, :], True, True,
                             ["PnT", "KCV"], ["pb%d" % pb])
                    P.tt("dve", NOUT[:, sub, :].rearrange("p (h d) -> p h d", h=8),
                         pbank[pb][:, :].rearrange("p (h d) -> p h d", h=8),
                         GS[:, tq, 0:8].unsqueeze(2).to_broadcast([128, 8, 64]), ALU.mult, ["pb%d" % pb, "GS"], ["NOUT"])
                    P.op("dve", lambda e: e.tensor_reduce(out=PG, in_=Pn.rearrange("p (k g) n -> p k n g", k=2),
                                                         axis=AX.X, op=ALU.add), ["Pn"], ["PG"])
                    pg4 = PG.rearrange("p k (b r) -> p k b r", r=4)
                    P.op("dve", lambda e, pg4=pg4: e.tensor_reduce(out=S012, in_=pg4[:, :, :, 0:3], axis=AX.X, op=ALU.add),
                         ["PG"], ["S012"])
                    P.stt("dve", PSs, S012, 2.0, pg4[:, :, :, 3], ALU.mult, ALU.add, ["S012", "PG"], ["PSs"])
                    P.tt("dve", PSs[:, :, 1:32], PSs[:, :, 1:32], pg4[:, :, 0:31, 3], ALU.add, ["PSs", "PG"], ["PSs"])
                    fv = FVNC[:, tq, 0, :]
                    ncm = FVNC[:, tq, 1, :]
                    for k in range(2):
                        P.tt("dve", SC[:, k, :], PSs[:, k, :], fv, ALU.max, ["PSs", "ncst"], ["SC"])
                        P.tt("dve", SC[:, k, :], SC[:, k, :], ncm, ALU.add, ["SC", "ncst"], ["SC"])
                        P.op("dve", lambda e, k=k: e.max(out=M8, in_=SC[:, k, :]), ["SC"], ["M8"])
                        P.op("dve", lambda e, k=k: e.match_replace(out=SCW, in_to_replace=M8, in_values=SC[:, k, :],
                                                                  imm_value=-1e9), ["SC", "M8"], ["SCW"])
                        P.op("dve", lambda e: e.max(out=M8, in_=SCW), ["SCW"], ["M8"])
                        P.ts("dve", SEL[:, k, :], SC[:, k, :], M8[:, 7:8], ALU.is_ge, ["SC", "M8"], ["SEL"], s2=-1.0,
                             op1=ALU.add)
                        pb = P.bank()
                        P.tr(pbank[pb][0:32, 0:128], SEL[:, k, :], ident, ["SEL", "ident"], ["pb%d" % pb])
                        P.cp("act", SELM1[k][0:32, tq * 128:(tq + 1) * 128], pbank[pb][0:32, 0:128], ["pb%d" % pb],
                             ["SELM1_%d" % k, "XTc"])
                P.st(NOUTD[r0 + tb * 512:r0 + (tb + 1) * 512, :].rearrange("(n p) c -> p n c", p=128), NOUT,
                     ["NOUT"], ["NOUTD"])
            for kvh in range(2 if K3 >= 5 else 0):
                for br, obase in ((0, O_KS), (1, O_KW)):
                    for half in range(2):
                        P.memset("dve", stg[half], 0.0, ["stgX"])
                        P.ld(stg[half][:, :, half * 64:(half + 1) * 64], zt(obase + kvh * 64, 64), ["stgX"])
                        tr_block(stg[half], KTz[br][half], ["KTz"])
                    P.ld(V1n[br][:, :, 0:64], zt(obase + 128 + kvh * 64, 64), ["V1n"])
                for tb in range(4):
                    nsl = NOUT[:, :, 0:256]
                    dsl = NOUTD[r0 + tb * 512:r0 + (tb + 1) * 512, kvh * 256:(kvh + 1) * 256].rearrange(
                        "(n p) c -> p n c", p=128)
                    P.ld(nsl, dsl, ["NOUT"], r=["NOUTD"])
                    for g in range(4):
                        h = kvh * 4 + g
                        if K3 != 7:
                            attn_block(0, h, tb, list(range(0, 4 * tb + 4)))
                        if K3 != 6:
                            attn_block(1, h, tb, list(range(max(0, 4 * tb - 4), 4 * tb + 4)))
                    P.st(dsl, nsl, ["NOUT"], ["NOUTD"])

        def top16(src, scr, vals, idx_u, nelem, key):
            P.op("dve", lambda e: e.max(out=vals[:, 0:8], in_=src), [key], [key + "v"])
            P.op("dve", lambda e: e.max_index(out=idx_u[:, 0:8], in_max=vals[:, 0:8], in_values=src),
                 [key, key + "v"], [key + "i"])
            P.op("dve", lambda e: e.match_replace(out=scr, in_to_replace=vals[:, 0:8], in_values=src,
                                                  imm_value=-1e30), [key, key + "v"], [key + "s"])
            P.op("dve", lambda e: e.max(out=vals[:, 8:16], in_=scr), [key + "s"], [key + "v"])
            P.op("dve", lambda e: e.max_index(out=idx_u[:, 8:16], in_max=vals[:, 8:16], in_values=scr),
                 [key + "s", key + "v"], [key + "i"])

        P.barrier()
        apos[0] = persist_mark
        zt0 = alloc(512)
        P.memset("dve", zt0, 0.0, ["zt0"])
        P.st(MOUT[TP:TP + 128, :], zt0, ["zt0"], ["MOUT_s"])
        P.st(NOUTD[TP:TP + 128, :], zt0, ["zt0"], ["NOUTD_s"])
        wq5 = alloc3(4, 128)
        wk5 = alloc3(4, 128)
        P.ld(wq5, wq_d[:, :, :], ["w5s"])
        P.ld(wk5, wk_d[:, :, :], ["w5s"])
        ZS = alloc(1544)
        CB3 = alloc3(3, 512)
        N0 = alloc(512)
        M0 = alloc(4)
        CW4 = alloc3(4, 512)
        CB4 = alloc(512)
        GB4 = alloc(8)
        NG16 = alloc(128)
        OPS = alloc(128)
        C4 = alloc(512)
        T4 = alloc(512)
        CTs = alloc3(4, 4)
        Q4 = alloc(512)
        K4 = alloc(512)
        S4 = {n: alloc(4) for n in ("IG", "XF", "LF", "MI", "MT", "SWS", "SCI", "EMT", "QK", "NQ", "SW", "DEN", "RD", "T")}
        PACK = alloc3(4, 264)
        C0s = alloc3(16, 128)
        BCT = alloc3(16, 264)
        VTs = alloc3(4, 4)
        T1 = alloc3(16, 128)
        CQ = alloc(16)
        NUM = alloc(16)
        WV = alloc(16)
        HTs = alloc(16)
        HB = alloc(128)
        NN = alloc(512)
        zs4 = Z[TP:TP + SSC, :]
        P.ld(ZS[0:4], zs4[:, 0:1544], ["ZS"])
        P.ld(CB3[0:4], st_conv[:, :, :], ["s5in"])
        P.ld(N0[0:4], stn_d[:, :], ["s5in"])
        P.ld(M0[0:4], stm_d[:, :], ["s5in"])
        P.ld(CW4[0:4], cw4_d[:, :, :], ["s5in"])
        P.ld(CB4[0:4], cb4_d[:, :], ["s5in"])
        P.ld(GB4[0:4], gb4_d[:, :], ["s5in"])
        P.ld(NG16[0:16], ng16_d[:, :], ["s5in"])
        for b in range(SSC):
            P.ld(OPS[4 * b:4 * b + 4], Z[TP + b, O_O:O_O + 512].rearrange("(h v) -> h v", h=4), ["OPS"])
        P.ld(C0s, stC_d.rearrange("a v k -> v a k"), ["C0s"])
        P.tt("dve", C4[0:4], CW4[0:4, 3, :], ZS[0:4, O_U:O_U + 512], ALU.mult, ["s5in", "ZS"], ["C4"])
        for j in range(3):
            P.tt("dve", T4[0:4], CW4[0:4, j, :], CB3[0:4, j, :], ALU.mult, ["s5in"], ["T4"])
            P.tt("dve", C4[0:4], C4[0:4], T4[0:4], ALU.add, ["C4", "T4"], ["C4"])
        P.tt("dve", C4[0:4], C4[0:4], CB4[0:4], ALU.add, ["C4", "s5in"], ["C4"])
        P.act(C4[0:4], C4[0:4], AF.Silu, ["C4"], ["C4"])
        pb = P.bank()
        for h in range(4):
            P.tr(pbank[pb][:, h * 4:(h + 1) * 4], C4[0:4, h * 128:(h + 1) * 128], ident[0:4, 0:4], ["C4", "ident"],
                 ["pb%d" % pb])
        P.cp("act", CTs, pbank[pb][:, 0:16].rearrange("p (h b) -> p h b", h=4), ["pb%d" % pb], ["CTs"])
        pq = P.bank()
        for h in range(4):
            P.mm(pbank[pq][0:4, h * 128:(h + 1) * 128], CTs[:, h, :], wq5[:, h, :], True, True, ["CTs", "w5s"],
                 ["pb%d" % pq])
        P.cp("act", Q4[0:4], pbank[pq][0:4, :], ["pb%d" % pq], ["Q4"])
        pk = P.bank()
        for h in range(4):
            P.mm(pbank[pk][0:4, h * 128:(h + 1) * 128], CTs[:, h, :], wk5[:, h, :], True, True, ["CTs", "w5s"],
                 ["pb%d" % pk])
        P.ts("dve", K4[0:4], pbank[pk][0:4, :], 128.0 ** -0.5, ALU.mult, ["pb%d" % pk], ["K4"])
        A_ = {n: v[0:4] for n, v in S4.items()}
        P.tt("dve", A_["IG"], ZS[0:4, O_I:O_I + 4], GB4[0:4, 0:4], ALU.add, ["ZS", "s5in"], ["IG"])
        P.tt("dve", A_["XF"], ZS[0:4, O_F:O_F + 4], GB4[0:4, 4:8], ALU.add, ["ZS", "s5in"], ["XF"])
        P.act(A_["LF"], A_["XF"], AF.Exp, ["XF"], ["LF"], scale=-1.0)
        P.act(A_["LF"], A_["LF"], AF.Ln, ["LF"], ["LF"], bias=1.0, scale=1.0)
        P.stt("dve", A_["MI"], A_["LF"], -1.0, M0[0:4], ALU.mult, ALU.add, ["LF", "s5in"], ["MI"])
        P.tt("dve", A_["MT"], A_["MI"], A_["IG"], ALU.max, ["MI", "IG"], ["MT"])
        P.tt("dve", A_["T"], A_["IG"], A_["MT"], ALU.subtract, ["IG", "MT"], ["T"])
        P.act(A_["SWS"], A_["T"], AF.Exp, ["T"], ["SWS"])
        P.tt("dve", A_["T"], A_["MI"], A_["MT"], ALU.subtract, ["MI", "MT", "SWS"], ["T"])
        P.act(A_["SCI"], A_["T"], AF.Exp, ["T"], ["SCI"])
        P.act(A_["EMT"], A_["MT"], AF.Exp, ["MT"], ["EMT"], scale=-1.0)
        P.tt("dve", T4[0:4], Q4[0:4], K4[0:4], ALU.mult, ["Q4", "K4"], ["T4"])
        P.op("dve", lambda e: e.reduce_sum(out=A_["QK"], in_=T4[0:4].rearrange("p (h e) -> p h e", h=4), axis=AX.X),
             ["T4"], ["QK"])
        P.tt("dve", T4[0:4], Q4[0:4], N0[0:4], ALU.mult, ["Q4", "s5in", "QK"], ["T4"])
        P.op("dve", lambda e: e.reduce_sum(out=A_["NQ"], in_=T4[0:4].rearrange("p (h e) -> p h e", h=4), axis=AX.X),
             ["T4"], ["NQ"])
        P.tt("dve", A_["SW"], A_["QK"], A_["SWS"], ALU.mult, ["QK", "SWS"], ["SW"])
        P.tt("dve", A_["DEN"], A_["SCI"], A_["NQ"], ALU.mult, ["SCI", "NQ"], ["DEN"])
        P.tt("dve", A_["DEN"], A_["DEN"], A_["SW"], ALU.add, ["DEN", "SW"], ["DEN"])
        P.act(A_["DEN"], A_["DEN"], AF.Abs, ["DEN"], ["DEN"])
        P.tt("dve", A_["DEN"], A_["DEN"], A_["EMT"], ALU.max, ["DEN", "EMT"], ["DEN"])
        P.op("dve", lambda e: e.reciprocal(out=A_["RD"], in_=A_["DEN"]), ["DEN"], ["RD"])
        P.cp("dve", PACK[0:4, :, 0:128], Q4[0:4].rearrange("p (h e) -> p h e", h=4), ["Q4"], ["PACK"])
        P.cp("dve", PACK[0:4, :, 128:256], K4[0:4].rearrange("p (h e) -> p h e", h=4), ["K4"], ["PACK"])
        for i, n in enumerate(("SCI", "SWS", "SW", "RD")):
            P.cp("dve", PACK[0:4, :, 256 + i:257 + i], A_[n].unsqueeze(2), [n], ["PACK"])
        P.st(SCRB.rearrange("(b h) x -> b h x", h=4), PACK[0:4], ["PACK"], ["SCRB"], eng="sp")
        P.ld(BCT, bass.AP(SCRB.tensor, 0, [[0, 128], [264, 16], [1, 264]]), ["BCT"], r=["SCRB"])
        pb = P.bank()
        for h in range(4):
            P.tr(pbank[pb][:, h * 4:(h + 1) * 4], ZS[0:4, O_V + h * 128:O_V + (h + 1) * 128], ident[0:4, 0:4],
                 ["ZS", "ident"], ["pb%d" % pb])
        P.cp("act", VTs, pbank[pb][:, 0:16].rearrange("p (h b) -> p h b", h=4), ["pb%d" % pb], ["VTs"])
        VTbh = VTs.rearrange("p h b -> p b h")
        sc = lambda i: BCT[:, :, 256 + i].rearrange("p (b h) -> p b h", h=4)
        P.tt("dve", T1, C0s, BCT[:, :, 0:128], ALU.mult, ["C0s", "BCT"], ["T1"])
        P.op("dve", lambda e: e.reduce_sum(out=CQ, in_=T1, axis=AX.X), ["T1"], ["CQ"])
        CQ3 = CQ.rearrange("p (b h) -> p b h", h=4)
        NUM3 = NUM.rearrange("p (b h) -> p b h", h=4)
        WV3 = WV.rearrange("p (b h) -> p b h", h=4)
        HT3 = HTs.rearrange("p (b h) -> p b h", h=4)
        P.tt("dve", NUM3, CQ3, sc(0), ALU.mult, ["CQ", "BCT"], ["NUM"])
        P.tt("dve", WV3, VTbh, sc(2), ALU.mult, ["VTs", "BCT"], ["WV"])
        P.tt("dve", NUM3, NUM3, WV3, ALU.add, ["NUM", "WV"], ["NUM"])
        P.tt("dve", HT3, NUM3, sc(3), ALU.mult, ["NUM", "BCT"], ["HTs"])
        pb = P.bank()
        P.tr(pbank[pb][0:16, 0:128], HTs, ident, ["HTs", "ident"], ["pb%d" % pb])
        P.cp("act", HB[0:16], pbank[pb][0:16, 0:128], ["pb%d" % pb], ["HB"])
        st5 = alloc(8)
        ag5 = alloc(4)
        rs5 = alloc(1)
        P.op("dve", lambda e: e.bn_stats(out=st5[0:16, 0:6], in_=HB[0:16]), ["HB"], ["st5"])
        P.op("dve", lambda e: e.bn_aggr(out=ag5[0:16, 0:2], in_=st5[0:16, 0:6]), ["st5"], ["ag5"])
        P.act(rs5[0:16], ag5[0:16, 1:2], AF.Sqrt, ["ag5"], ["rs5"], bias=1e-5, scale=1.0)
        P.op("dve", lambda e: e.reciprocal(out=rs5[0:16], in_=rs5[0:16]), ["rs5"], ["rs5"])
        P.ts("dve", HB[0:16], HB[0:16], ag5[0:16, 0:1], ALU.subtract, ["HB", "ag5", "rs5"], ["HB"], s2=rs5[0:16],
             op1=ALU.mult)
        P.tt("dve", HB[0:16], HB[0:16], NG16[0:16], ALU.mult, ["HB", "s5in"], ["HB"])
        P.act(OPS[0:16], OPS[0:16], AF.Sigmoid, ["OPS"], ["OPS"])
        P.tt("dve", HB[0:16], HB[0:16], OPS[0:16], ALU.mult, ["HB", "OPS"], ["HB"])
        for b in range(SSC):
            P.st(MOUT[TP + b, :].rearrange("(h v) -> h v", h=4), HB[4 * b:4 * b + 4], ["HB"], ["MOUT_s"], eng="sp")
        P.tt("dve", WV3, VTbh, sc(1), ALU.mult, ["VTs", "BCT", "NUM"], ["WV"])
        P.tt("dve", T1, BCT[:, :, 128:256], WV.unsqueeze(2).to_broadcast([128, 16, 128]), ALU.mult, ["BCT", "WV", "CQ"],
             ["T1"])
        P.tt("dve", C0s, C0s, BCT[:, :, 256:257].to_broadcast([128, 16, 128]), ALU.mult, ["C0s", "BCT"], ["C0s"])
        P.tt("dve", C0s, C0s, T1, ALU.add, ["C0s", "T1"], ["C0s"])
        P.st(o_C_s.rearrange("a v k -> v a k"), C0s, ["C0s"], eng="sp")
        N03 = N0[0:4].rearrange("p (h e) -> p h e", h=4)
        NN3 = NN[0:4].rearrange("p (h e) -> p h e", h=4)
        K43 = K4[0:4].rearrange("p (h e) -> p h e", h=4)
        P.tt("dve", NN3, N03, A_["SCI"].unsqueeze(2).to_broadcast([4, 4, 128]), ALU.mult, ["s5in", "SCI"], ["NN"])
        P.tt("dve", T4[0:4].rearrange("p (h e) -> p h e", h=4), K43, A_["SWS"].unsqueeze(2).to_broadcast([4, 4, 128]),
             ALU.mult, ["K4", "SWS", "NQ"], ["T4"])
        P.tt("dve", NN[0:4], NN[0:4], T4[0:4], ALU.add, ["NN", "T4"], ["NN"])
        P.st(o_n_s[:, :], NN[0:4], ["NN"], eng="sp")
        P.st(o_m_s[:, :], A_["MT"], ["MT"], eng="sp")

        P.barrier()
        apos[0] = persist_mark
        K6 = int(os.environ.get("K6STOP", "9"))
        W1Ls = alloc(2 * 16 * 128).rearrange("p (c r o) -> p c r o", c=2, r=16)
        W1Ns = alloc(2 * 16 * 64).rearrange("p (c j o) -> p c j o", c=2, j=16)
        PELs = alloc3(2, 16)
        W2Ns = alloc3(2, 64)
        W2KT = alloc(64)
        SHMs = alloc(128)
        SHD = alloc(128)
        ONESM = alloc(128)
        CBS = alloc3(8, 8)
        BWt = alloc3(4, 8)
        T25 = alloc(2 * 8 * 64).rearrange("p (a h t) -> p a h t", a=2, h=8)
        CBR6 = alloc(8)
        FL255 = alloc(1)
        IND = alloc(4)
        RB0 = alloc(8)
        IOTA8 = alloc(8)
        IOT128 = alloc3(2, 128)
        PETs = alloc(128)
        PETB = alloc3(2, 64)
        PT4i = alloc(4).bitcast(I32)
        PT8i = alloc(128).bitcast(I32)
        PTF = alloc(4)
        PTROW = alloc(128)
        IDXf = alloc3(4, 8)
        P.ld(W1Ls, w1l_d[:, :, :, :], ["c6"])
        P.ld(W1Ns, w1n_d[:, :, :, :], ["c6"])
        P.ld(PELs, pel_d[:, :, :], ["c6"])
        P.ld(W2Ns[0:64], w2n_d[:, :, :], ["c6"])
        P.ld(W2KT[0:64], w2kt_d[:, :], ["c6"])
        P.ld(SHMs, sh_d[:, :], ["c6"])
        P.ld(SHD, shd_d[:, :], ["c6"])
        P.ld(CBS, cbs_d[:, :, :], ["c6"])
        P.ld(BWt, bw_d[:, :, :], ["c6"])
        P.ld(T25[0:60], t25_d[:, :, :, :], ["c6"])
        P.ld(CBR6, cbr_d[:, :], ["c6"])
        P.ld(FL255[0:60], fl255_d[:, :], ["c6"])
        P.ld(IND[0:60], ind_d[:, :], ["c6"])
        P.ld(RB0[0:4], rb0_d[:, :], ["c6"])
        P.ld(IOTA8, iota8_d[:, :], ["c6"])
        P.ld(IOT128[0:8], iot128_d[:, :, :], ["c6"])
        P.ld(PT4i, ptl_d[:, :], ["c6"])
        P.ld(PT8i[0:8], pt8_d[:, :], ["c6"])
        P.memset("dve", ONESM, 1.0, ["ONESM"])
        P.cp("dve", PTF, PT4i, ["c6"], ["PTF"])
        P.cp("dve", PTROW[0:8], PT8i[0:8], ["c6"], ["PTROW"])
        P.stt("dve", IDXf, PTF.unsqueeze(2).to_broadcast([128, 4, 8]), 8.0,
              IOTA8.unsqueeze(1).to_broadcast([128, 4, 8]), ALU.mult, ALU.add, ["PTF", "c6"], ["IDXf"])
        P.cp("dve", IDXC.rearrange("p (b n) -> p b n", b=4), IDXf, ["IDXf"], ["IDXC"])
        for c in range(2):
            pb = P.bank()
            for j in range(16):
                P.mm(pbank[pb][0:1, 0:64], PELs[:, c, j:j + 1], W1Ns[:, c, j, :], j == 0, j == 15, ["c6"], ["pb%d" % pb])
            P.cp("act", PETs[0:1, c * 64:(c + 1) * 64], pbank[pb][0:1, 0:64], ["pb%d" % pb], ["PETs"])
        pb = P.bank()
        P.mm(pbank[pb][:, 0:128], ONESM[0:1, :], PETs[0:1, :], True, True, ["ONESM", "PETs"], ["pb%d" % pb])
        P.cp("act", PETB, pbank[pb][:, 0:128].rearrange("p (c o) -> p c o", c=2), ["pb%d" % pb], ["PETB"])
        ZN = alloc(1304)
        P.ld(ZN[0:4], Z[TP:TP + SSC, O_Q:O_Q + 1304], ["ZN"])
        QS = alloc(512)
        P.ts("dve", QS[0:4], ZN[0:4, 0:512], 0.125, ALU.mult, ["ZN"], ["QS"])
        QTS = alloc3(4, 8)
        pb = P.bank()
        for h in range(8):
            P.tr(pbank[pb][0:64, h * 4:(h + 1) * 4], QS[0:4, h * 64:(h + 1) * 64], ident[0:4, 0:4], ["QS", "ident"],
                 ["pb%d" % pb])
        P.cp("act", QTS[0:64].rearrange("p b h -> p h b"), pbank[pb][0:64, 0:32].rearrange("p (h b) -> p h b", h=8),
             ["pb%d" % pb], ["QTS"])
        QW8 = alloc(64)
        for b in range(SSC):
            pb = P.bank()
            P.mm(pbank[pb][0:8, 0:64], QTS[0:64, b, :], W2KT[0:64, :], True, True, ["QTS", "c6"], ["pb%d" % pb])
            P.cp("act", QW8[0:8], pbank[pb][0:8, 0:64], ["pb%d" % pb], ["QW8"])
            P.st(SCRQ[b, :].rearrange("(h i) -> h i", h=8), QW8[0:8], ["QW8"], ["SCRQ"], eng="sp")
        QWB = alloc3(4, 512)
        P.ld(QWB, bass.AP(SCRQ.tensor, 0, [[0, 128], [512, 4], [1, 512]]), ["QWB"], r=["SCRQ"])
        G = [alloc(4096) for _ in range(2)]
        XTs = alloc(32 * 128).rearrange("p (q n) -> p q n", q=32)
        PABs = alloc(8 * 4 * 128).rearrange("p (n k o) -> p n k o", n=8, k=4)
        HPRE = alloc(8 * 4 * 64).rearrange("p (n k o) -> p n k o", n=8, k=4)
        HTMP = alloc(8 * 4 * 64).rearrange("p (n k o) -> p n k o", n=8, k=4)
        HIDs = alloc(8 * 4 * 64).rearrange("p (n k o) -> p n k o", n=8, k=4)
        TMPs = alloc(1024)
        SCs = alloc3(8, 8)
        RS = alloc(8)
        RT = alloc(8)
        PGs = alloc3(8, 2)
        S3 = alloc(2)
        PSB = alloc3(2, 2)
        U4 = alloc(64)
        UT = alloc(4)
        OC = alloc(65)
        WK = alloc3(4, 256)
        QB128 = alloc(512)
        SWs = alloc3(4, 8)
        gcnt6 = 0
        for b in range(SSC if K6 >= 1 else 0):
            for n_ in range(8):
                gb = gcnt6 % 2
                gcnt6 += 1
                P.dma("pool", lambda e, gb=gb, col=b * 8 + n_: e.indirect_dma_start(
                    out=G[gb], out_offset=None, in_=pool_cmp_d[:, :],
                    in_offset=bass.IndirectOffsetOnAxis(ap=IDXC[:, col:col + 1], axis=0)), ["IDXC"], ["G%d" % gb])
                G3 = G[gb].rearrange("p (r x) -> p r x", r=16)
                for g8 in range(8):
                    pb = P.bank()
                    for j in range(4):
                        q_ = g8 * 4 + j
                        r, c = q_ // 2, q_ % 2
                        P.tr(pbank[pb][:, j * 128:(j + 1) * 128], G3[:, r, c * 128:(c + 1) * 128], ident,
                             ["G%d" % gb, "ident"], ["pb%d" % pb])
                    P.cp("act" if g8 % 2 else "dve", XTs[:, g8 * 4:(g8 + 1) * 4, :],
                         pbank[pb][:, :].rearrange("p (q n) -> p q n", q=4), ["pb%d" % pb], ["XTs"])
                for c in range(2):
                    for k in range(2):
                        rows = slice(k * 64, (k + 1) * 64)
                        pb = P.bank()
                        for r in range(16):
                            P.mm(pbank[pb][:, 0:128], XTs[rows, r * 2 + c, :], W1Ls[rows, c, r, :], r == 0, r == 15,
                                 ["XTs", "c6"], ["pb%d" % pb])
                        P.cp("act", PABs[:, n_, c * 2 + k, :], pbank[pb][:, 0:128], ["pb%d" % pb], ["PABs"])
            P.tt("dve", HPRE[:, 0:7], PABs[:, 0:7, :, 0:64], PABs[:, 1:8, :, 64:128], ALU.add, ["PABs"], ["HPRE"])
            pb = P.bank()
            P.mm(pbank[pb][:, 0:256], SHMs, PABs[:, 0, :, 64:128], True, True, ["c6", "PABs"], ["pb%d" % pb])
            P.tt("dve", HPRE[:, 7], PABs[:, 7, :, 0:64], pbank[pb][:, 0:256].rearrange("p (k o) -> p k o", k=4), ALU.add,
                 ["PABs", "pb%d" % pb], ["HPRE"])
            for c in range(2):
                P.tt("dve", HPRE[:, :, 2 * c:2 * c + 2, :], HPRE[:, :, 2 * c:2 * c + 2, :],
                     PETB[:, c, :].unsqueeze(1).unsqueeze(1).to_broadcast([128, 8, 2, 64]), ALU.add, ["HPRE", "PETB"],
                     ["HPRE"])
            gelu_from(HPRE, HTMP, HIDs, ["HPRE"], "HIDs")
            for h in range(8):
                kvh = h // 4
                P.tt("dve", TMPs[:, 0:512].rearrange("p (n i) -> p n i", n=8), HIDs[:, :, kvh, :],
                     QWB[:, b, h * 64:(h + 1) * 64].unsqueeze(1).to_broadcast([128, 8, 64]), ALU.mult,
                     ["HIDs", "QWB"], ["TMPs"])
                P.op("dve", lambda e, h=h: e.reduce_sum(out=SCs[:, :, h], in_=TMPs[:, 0:512].rearrange(
                    "p (n i) -> p n i", n=8), axis=AX.X), ["TMPs"], ["SCs"])
            P.tt("dve", SCs, SCs, CBS, ALU.add, ["SCs", "c6"], ["SCs"])
            P.act(SCs, SCs, AF.Exp, ["SCs"], ["SCs"])
            P.op("dve", lambda e: e.reduce_sum(out=RS, in_=SCs.rearrange("p n h -> p h n"), axis=AX.X), ["SCs"], ["RS"])
            pb = P.bank()
            P.mm(pbank[pb][:, 0:8], ONESM, RS, True, True, ["ONESM", "RS"], ["pb%d" % pb])
            P.op("dve", lambda e, pb=pb: e.reciprocal(out=RT, in_=pbank[pb][:, 0:8]), ["pb%d" % pb], ["RT"])
            P.tt("dve", SCs, SCs, RT.unsqueeze(1).to_broadcast([128, 8, 8]), ALU.mult, ["SCs", "RT"], ["SCs"])
            for kvh in range(2):
                pb = P.bank()
                for n_ in range(8):
                    P.mm(pbank[pb][0:4, 0:64], SCs[:, n_, kvh * 4:(kvh + 1) * 4], HIDs[:, n_, 2 + kvh, :], n_ == 0,
                         n_ == 7, ["SCs", "HIDs"], ["pb%d" % pb])
                P.cp("act", U4[0:4], pbank[pb][0:4, 0:64], ["pb%d" % pb], ["U4"])
                pb = P.bank()
                P.tr(pbank[pb][0:64, 0:4], U4[0:4, 0:64], ident[0:4, 0:4], ["U4", "ident"], ["pb%d" % pb])
                P.cp("act", UT[0:64], pbank[pb][0:64, 0:4], ["pb%d" % pb], ["UT"])
                pb = P.bank()
                P.mm(pbank[pb][0:4, 0:64], UT[0:64, 0:4], W2Ns[0:64, 1, :], True, True, ["UT", "c6"], ["pb%d" % pb])
                P.cp("act", OC[0:4, 0:64], pbank[pb][0:4, 0:64], ["pb%d" % pb], ["OC"])
                P.st(OSD[0, b, kvh * 4:(kvh + 1) * 4, 0:64], OC[0:4, 0:64], ["OC"], ["OSD"], eng="sp")
            P.op("dve", lambda e: e.tensor_reduce(out=PGs, in_=SCs.rearrange("p n (k g) -> p n k g", k=2), axis=AX.X,
                                                 op=ALU.add), ["SCs"], ["PGs"])
            pb = P.bank()
            P.mm(pbank[pb][:, 0:2], SHD, PGs[:, 7, :], True, True, ["c6", "PGs"], ["pb%d" % pb])
            for eo in range(2):
                o_ = eo * 4
                P.tt("dve", S3, PGs[:, o_ + 0, :], PGs[:, o_ + 1, :], ALU.add, ["PGs"], ["S3"])
                P.tt("dve", S3, S3, PGs[:, o_ + 2, :], ALU.add, ["S3", "PGs"], ["S3"])
                P.stt("dve", PSB[:, :, eo], S3, 2.0, PGs[:, 3, :], ALU.mult, ALU.add, ["S3", "PGs"], ["PSB"])
                if eo == 0:
                    P.tt("dve", PSB[:, :, 0], PSB[:, :, 0], pbank[pb][:, 0:2], ALU.add, ["PSB", "pb%d" % pb], ["PSB"])
                else:
                    P.tt("dve", PSB[:, :, 1], PSB[:, :, 1], PGs[:, 7, :], ALU.add, ["PSB", "PGs"], ["PSB"])
            P.st(SELD[b * 2:b * 2 + 2, :].rearrange("k (p e) -> p k e", e=2), PSB, ["PSB"], ["SELD"], eng="sp")
            P.ld(WK, st_win[b, :, :].rearrange("(c p) x -> p c x", p=128), ["WK"])
            P.ld(QB128, bass.AP(Z.tensor, (TP + b) * IN_DIM + O_Q, [[0, 128], [1, 512]]), ["QB128"])
            P.ts("dve", QB128, QB128, 0.125, ALU.mult, ["QB128"], ["QB128"])
            for h in range(8):
                kvh = h // 4
                P.tt("dve", TMPs[:, 0:256].rearrange("p (c i) -> p c i", c=4), WK[:, :, kvh * 64:(kvh + 1) * 64],
                     QB128[:, h * 64:(h + 1) * 64].unsqueeze(1).to_broadcast([128, 4, 64]), ALU.mult, ["WK", "QB128"],
                     ["TMPs"])
                P.op("dve", lambda e, h=h: e.reduce_sum(out=SWs[:, :, h], in_=TMPs[:, 0:256].rearrange(
                    "p (c i) -> p c i", c=4), axis=AX.X), ["TMPs"], ["SWs"])
            P.tt("dve", SWs, SWs, BWt, ALU.add, ["SWs", "c6"], ["SWs"])
            P.act(SWs, SWs, AF.Exp, ["SWs"], ["SWs"])
            for kvh in range(2):
                pn_ = P.bank()
                for c in range(4):
                    P.mm(pbank[pn_][0:4, 0:64], SWs[:, c, kvh * 4:(kvh + 1) * 4],
                         WK[:, c, 128 + kvh * 64:128 + (kvh + 1) * 64], c == 0, c == 3, ["SWs", "WK"], ["pb%d" % pn_])
                pd_ = P.bank()
                for c in range(4):
                    P.mm(pbank[pd_][0:4, 0:1], SWs[:, c, kvh * 4:(kvh + 1) * 4], ONESM[:, 0:1], c == 0, c == 3,
                         ["SWs", "ONESM"], ["pb%d" % pd_])
                P.cp("act", OC[0:4, 0:64], pbank[pn_][0:4, 0:64], ["pb%d" % pn_], ["OC"])
                P.cp("act", OC[0:4, 64:65], pbank[pd_][0:4, 0:1], ["pb%d" % pd_], ["OC"])
                P.st(OSD[1, b, kvh * 4:(kvh + 1) * 4, :], OC[0:4, :], ["OC"], ["OSD"], eng="sp")

        if K6 >= 2:
            SELIN = alloc(256)
            SELW = alloc(256)
            V16s = alloc(16)
            I16s = alloc(16).bitcast(U32)
            BLK = alloc(16)
            HALF = alloc(16)
            PAR = alloc(16)
            PGID = alloc(16)
            PHYS = alloc(16)
            OH6 = alloc3(15, 128)
            PQ5 = alloc3(15, 5)
            P.ld(SELIN[0:8], SELD[:, :], ["SELIN"], r=["SELD"])
            P.memset("dve", SELIN[0:8, 0:1], -1e9, ["SELIN"])
            P.memset("dve", SELIN[0:8, 255:256], -1e9, ["SELIN"])
            top16(SELIN[0:8], SELW[0:8], V16s[0:8], I16s[0:8], 256, "SELIN")
            P.cp("dve", BLK[0:8], I16s[0:8], ["SELINi"], ["BLK"])
            P.memset("dve", BLK[0:8, 13:14], 0.0, ["BLK"])
            P.memset("dve", BLK[0:8, 14:15], 255.0, ["BLK"])
            B15 = BLK[0:8, 0:15]
            P.tt("dve", OH6[0:8], B15.unsqueeze(2).to_broadcast([8, 15, 128]),
                 IOT128[0:8, 1, :].unsqueeze(1).to_broadcast([8, 15, 128]), ALU.is_ge, ["BLK", "c6"], ["OH6"])
            P.op("dve", lambda e: e.tensor_reduce(out=HALF[0:8, 0:15], in_=OH6[0:8], axis=AX.X, op=ALU.add), ["OH6"],
                 ["HALF"])
            P.stt("dve", PAR[0:8, 0:15], HALF[0:8, 0:15], -2.0, B15, ALU.mult, ALU.add, ["HALF", "BLK"], ["PAR"])
            P.tt("dve", OH6[0:8], HALF[0:8, 0:15].unsqueeze(2).to_broadcast([8, 15, 128]),
                 IOT128[0:8, 0, :].unsqueeze(1).to_broadcast([8, 15, 128]), ALU.is_equal, ["HALF", "c6"], ["OH6"])
            P.tt("dve", OH6[0:8], OH6[0:8], PTROW[0:8].unsqueeze(1).to_broadcast([8, 15, 128]), ALU.mult,
                 ["OH6", "PTROW"], ["OH6"])
            P.op("dve", lambda e: e.tensor_reduce(out=PGID[0:8, 0:15], in_=OH6[0:8], axis=AX.X, op=ALU.add), ["OH6"],
                 ["PGID"])
            P.stt("dve", PHYS[0:8, 0:15], PGID[0:8, 0:15], 2.0, PAR[0:8, 0:15], ALU.mult, ALU.add, ["PGID", "PAR"],
                  ["PHYS"])
            for qd in range(4):
                P.ts("dve", PQ5[0:8, :, qd], PHYS[0:8, 0:15], 4.0, ALU.mult, ["PHYS"], ["PQ5"], s2=float(qd),
                     op1=ALU.add)
            P.cp("dve", PQ5[0:8, :, 4], B15, ["BLK"], ["PQ5"])
            P.st(SCRP[:, :, :], PQ5[0:8], ["PQ5"], ["SCRP"], eng="sp")
            IQf = [alloc(5) for _ in range(2)]
            P.memset("dve", IDXS, 0, ["IDXS"])
            for kvh in range(2):
                for b in range(SSC):
                    P.ld(IQf[kvh][15 * b:15 * b + 15], SCRP[b * 2 + kvh, :, :], ["IQf%d" % kvh], r=["SCRP"])
                P.cp("dve", IDXS[0:60, kvh * 4:(kvh + 1) * 4], IQf[kvh][0:60, 0:4], ["IQf%d" % kvh], ["IDXS"])
            QB60 = alloc(512)
            for b in range(SSC):
                P.ld(QB60[15 * b:15 * b + 15], bass.AP(Z.tensor, (TP + b) * IN_DIM + O_Q, [[0, 15], [1, 512]]), ["QB60"])
            P.ts("dve", QB60[0:60], QB60[0:60], 0.125, ALU.mult, ["QB60"], ["QB60"])
            FL254 = alloc(1)
            W0 = alloc(1)
            BIAS = alloc3(4, 64)
            SS6 = alloc3(4, 64)
            PVD = alloc(260)
            PVt = alloc(64)
            DNt = alloc(4)
            SLCR = [alloc(260) for _ in range(2)]
            for kvh in range(2):
                hs = slice(kvh * 4, (kvh + 1) * 4)
                P.ts("dve", FL254[0:60], IQf[kvh][0:60, 4:5], 254.0, ALU.is_equal, ["IQf%d" % kvh], ["FL254"])
                P.tt("dve", W0[0:60], FL254[0:60], FL255[0:60], ALU.add, ["FL254", "c6"], ["W0"])
                P.ts("dve", W0[0:60], W0[0:60], -1.0, ALU.mult, ["W0"], ["W0"], s2=1.0, op1=ALU.add)
                P.ts("dve", BIAS[0:60], T25[0:60, 0, hs, :], FL254[0:60], ALU.mult, ["c6", "FL254"], ["BIAS"])
                P.stt("dve", BIAS[0:60], T25[0:60, 1, hs, :], FL255[0:60], BIAS[0:60], ALU.mult, ALU.add,
                      ["c6", "BIAS"], ["BIAS"])
                P.stt("dve", BIAS[0:60], CBR6[0:60, hs].unsqueeze(2).to_broadcast([60, 4, 64]), W0[0:60], BIAS[0:60],
                      ALU.mult, ALU.add, ["c6", "W0", "BIAS"], ["BIAS"])
                P.memset("dve", PVD[0:60], 0.0, ["PVD"])
                for qd in range(4):
                    gb = gcnt6 % 2
                    gcnt6 += 1
                    P.dma("pool", lambda e, gb=gb, col=kvh * 4 + qd: e.indirect_dma_start(
                        out=G[gb], out_offset=None, in_=pool_slc_d[:, :],
                        in_offset=bass.IndirectOffsetOnAxis(ap=IDXS[:, col:col + 1], axis=0)), ["IDXS"],
                        ["G%d" % gb])
                    GQ3 = G[gb][0:60].rearrange("p (t x) -> p t x", t=16)
                    ts_ = slice(qd * 16, (qd + 1) * 16)
                    for g in range(4):
                        h = kvh * 4 + g
                        P.tt("dve", TMPs[0:60].rearrange("p (t d) -> p t d", t=16), GQ3[:, :, kvh * 64:(kvh + 1) * 64],
                             QB60[0:60, h * 64:(h + 1) * 64].unsqueeze(1).to_broadcast([60, 16, 64]), ALU.mult,
                             ["G%d" % gb, "QB60"], ["TMPs"])
                        P.op("dve", lambda e, g=g, ts_=ts_: e.reduce_sum(out=SS6[0:60, g, ts_], in_=TMPs[0:60].rearrange(
                            "p (t d) -> p t d", t=16), axis=AX.X), ["TMPs"], ["SS6"])
                    P.tt("dve", SS6[0:60, :, ts_], SS6[0:60, :, ts_], BIAS[0:60, :, ts_], ALU.add, ["SS6", "BIAS"], ["SS6"])
                    P.act(SS6[0:60, :, ts_], SS6[0:60, :, ts_], AF.Exp, ["SS6"], ["SS6"])
                    P.op("dve", lambda e, ts_=ts_: e.reduce_sum(out=DNt[0:60], in_=SS6[0:60, :, ts_], axis=AX.X), ["SS6"],
                         ["DNt"])
                    P.tt("dve", PVD[0:60, 256:260], PVD[0:60, 256:260], DNt[0:60], ALU.add, ["PVD", "DNt"], ["PVD"])
                    for g in range(4):
                        P.tt("dve", TMPs[0:60].rearrange("p (d t) -> p d t", d=64),
                             GQ3[:, :, 128 + kvh * 64:128 + (kvh + 1) * 64].rearrange("p t d -> p d t"),
                             SS6[0:60, g, ts_].unsqueeze(1).to_broadcast([60, 64, 16]), ALU.mult, ["G%d" % gb, "SS6"],
                             ["TMPs"])
                        P.op("dve", lambda e: e.reduce_sum(out=PVt[0:60], in_=TMPs[0:60].rearrange(
                            "p (d t) -> p d t", d=64), axis=AX.X), ["TMPs"], ["PVt"])
                        P.tt("dve", PVD[0:60, g * 64:(g + 1) * 64], PVD[0:60, g * 64:(g + 1) * 64], PVt[0:60], ALU.add,
                             ["PVD", "PVt"], ["PVD"])
                pb = P.bank()
                P.mm(pbank[pb][0:4, 0:260], IND[0:60, :], PVD[0:60, :], True, True, ["c6", "PVD"], ["pb%d" % pb])
                P.cp("act", SLCR[kvh][0:4], pbank[pb][0:4, 0:260], ["pb%d" % pb], ["SLCR%d" % kvh])
            OSB = alloc(2 * 8 * 65).rearrange("p (a h x) -> p a h x", a=2, h=8)
            for a in range(2):
                P.ld(OSB[0:4, a], OSD[a, :, :, :], ["OSB"], r=["OSD"])
            GS4 = alloc(24)
            P.act(GS4[0:4], ZN[0:4, 1280:1304], AF.Sigmoid, ["ZN"], ["GS4"])
            QS3 = QS[0:4].rearrange("p (h d) -> p h d", h=8)
            NOS = alloc(512)
            NOS3 = NOS[0:4].rearrange("p (h d) -> p h d", h=8)
            TL = alloc(512)
            TL3 = TL[0:4].rearrange("p (h d) -> p h d", h=8)
            PTL = alloc(8)
            DEN8 = alloc(8)
            P.tt("dve", NOS3, OSB[0:4, 0, :, 0:64], GS4[0:4, 0:8].unsqueeze(2).to_broadcast([4, 8, 64]), ALU.mult,
                 ["OSB", "GS4"], ["NOS"])
            for br, kbase in ((1, 768), (2, 1024)):
                for kvh in range(2):
                    P.tt("dve", TL3[:, kvh * 4:(kvh + 1) * 4, :], QS3[:, kvh * 4:(kvh + 1) * 4, :],
                         ZN[0:4, kbase + kvh * 64:kbase + (kvh + 1) * 64].unsqueeze(1).to_broadcast([4, 4, 64]),
                         ALU.mult, ["QS", "ZN", "TLr"], ["TL"])
                P.op("dve", lambda e: e.reduce_sum(out=PTL[0:4], in_=TL3, axis=AX.X), ["TL"], ["PTL"])
                P.tt("dve", PTL[0:4], PTL[0:4], RB0[0:4], ALU.add, ["PTL", "c6"], ["PTL"])
                P.act(PTL[0:4], PTL[0:4], AF.Exp, ["PTL"], ["PTL"])
                for kvh in range(2):
                    hs = slice(kvh * 4, (kvh + 1) * 4)
                    if br == 1:
                        num_src = SLCR[kvh][0:4, 0:256].rearrange("p (g d) -> p g d", g=4)
                        den_src = SLCR[kvh][0:4, 256:260]
                        rk = ["SLCR%d" % kvh]
                    else:
                        num_src = OSB[0:4, 1, hs, 0:64]
                        den_src = OSB[0:4, 1, hs, 64]
                        rk = ["OSB"]
                    P.tt("dve", DEN8[0:4, hs], den_src, PTL[0:4, hs], ALU.add, rk + ["PTL"], ["DEN8"])
                    P.tt("dve", TL3[:, hs, :], PTL[0:4, hs].unsqueeze(2).to_broadcast([4, 4, 64]),
                         ZN[0:4, kbase + 128 + kvh * 64:kbase + 128 + (kvh + 1) * 64].unsqueeze(1).to_broadcast([4, 4, 64]),
                         ALU.mult, ["PTL", "ZN", "PTL"], ["TL"])
                    P.tt("dve", TL3[:, hs, :], TL3[:, hs, :], num_src, ALU.add, ["TL"] + rk, ["TL"])
                P.ts("dve", DEN8[0:4], DEN8[0:4], 1e-30, ALU.max, ["DEN8"], ["DEN8"])
                P.op("dve", lambda e: e.reciprocal(out=DEN8[0:4], in_=DEN8[0:4]), ["DEN8"], ["DEN8"])
                P.tt("dve", DEN8[0:4], DEN8[0:4], GS4[0:4, br * 8:(br + 1) * 8], ALU.mult, ["DEN8", "GS4"], ["DEN8"])
                P.tt("dve", TL3, TL3, DEN8[0:4].unsqueeze(2).to_broadcast([4, 8, 64]), ALU.mult, ["TL", "DEN8"], ["TL"])
                P.tt("dve", NOS[0:4], NOS[0:4], TL[0:4], ALU.add, ["NOS", "TL"], ["NOS", "TLr"])
            P.st(NOUTD[TP:TP + SSC, :], NOS[0:4], ["NOS"], ["NOUTD_s"], eng="sp")

        P.barrier()
        apos[0] = persist_mark
        K5 = int(os.environ.get("K5STOP", "9"))
        ALPHA = 2.0 ** 0.25
        WUM = alloc3(4, D)
        WUN = alloc3(4, D)
        WO = alloc3(8, D)
        LNP = alloc3(4, D)
        P.ld(WUM, wupm_d.rearrange("(c p) n -> p c n", p=128), ["w4"])
        P.ld(WUN, wupn_d.rearrange("(c p) n -> p c n", p=128), ["w4"])
        P.ld(WO, wout_d.rearrange("(c p) n -> p c n", p=128), ["w4"])
        P.ld(LNP, lnp_d[:, :, :], ["w4"])
        NB = 2
        XB = [alloc(D) for _ in range(NB)]
        MOB = [alloc(512) for _ in range(NB)]
        NOB = [alloc(512) for _ in range(NB)]
        GMB = [alloc(D) for _ in range(NB)]
        GNB = [alloc(D) for _ in range(NB)]
        MOT = alloc3(4, 128)
        NOTt = alloc3(4, 128)
        MIX = alloc(D)
        MIXT = alloc3(8, 128)
        RB = alloc(D)
        X1B = [alloc(D) for _ in range(NB)]
        STT = alloc(16)
        AGG = alloc(4)
        RSTD = alloc(1)

        def layer_norm(src, dst, gi, rk, wk_):
            for i in range(2):
                P.op("dve", lambda e, i=i: e.bn_stats(out=STT[:, i * 6:(i + 1) * 6], in_=src[:, i * 512:(i + 1) * 512]),
                     rk, ["STT"])
            P.op("dve", lambda e: e.bn_aggr(out=AGG[:, 0:2], in_=STT[:, 0:12].rearrange("p (a b) -> p a b", a=2)),
                 ["STT"], ["AGG"])
            P.act(RSTD, AGG[:, 1:2], AF.Sqrt, ["AGG"], ["RSTD"], bias=1e-5, scale=1.0)
            P.op("dve", lambda e: e.reciprocal(out=RSTD, in_=RSTD), ["RSTD"], ["RSTD"])
            P.ts("dve", dst, src, AGG[:, 0:1], ALU.subtract, rk + ["AGG", "RSTD"], wk_, s2=RSTD, op1=ALU.mult)
            P.tt("pool", dst, dst, LNP[:, gi, :], ALU.mult, wk_ + ["w4"], wk_)
            P.tt("pool", dst, dst, LNP[:, gi + 1, :], ALU.add, wk_ + ["w4"], wk_)

        for ti in range(NT + 1 if K5 >= 1 else 0):
            b_ = ti % NB
            rows = slice(ti * 128, (ti + 1) * 128)
            xsrc = x_p[rows, :] if ti < NT else x_s[:, :]
            P.ld(XB[b_], xsrc, ["XB%d" % b_])
            P.ld(MOB[b_], MOUT[rows, :], ["MOB%d" % b_], r=["MOUT", "MOUT_s"])
            P.ld(NOB[b_], NOUTD[rows, :], ["NOB%d" % b_], r=["NOUTD", "NOUTD_s"])
            P.ld(GMB[b_], Z[rows, O_GM:O_GM + D], ["GMB%d" % b_])
            P.ld(GNB[b_], Z[rows, O_GNN:O_GNN + D], ["GNB%d" % b_])
            for src, skey, dst, dkey in ((MOB[b_], "MOB%d" % b_, MOT, "MOT"), (NOB[b_], "NOB%d" % b_, NOTt, "NOT")):
                pb = P.bank()
                for c in range(4):
                    P.tr(pbank[pb][:, c * 128:(c + 1) * 128], src[:, c * 128:(c + 1) * 128], ident, [skey, "ident"],
                         ["pb%d" % pb])
                P.cp("act", dst, pbank[pb][:, :].rearrange("p (c t) -> p c t", c=4), ["pb%d" % pb], [dkey])
            P.act(GMB[b_], GMB[b_], AF.Sigmoid, ["GMB%d" % b_], ["GMB%d" % b_])
            P.act(GNB[b_], GNB[b_], AF.Sigmoid, ["GNB%d" % b_], ["GNB%d" % b_])
            for nb in range(2):
                ns = slice(nb * 512, (nb + 1) * 512)
                pa = P.bank()
                for c in range(4):
                    P.mm(pbank[pa][:, :], MOT[:, c, :], WUM[:, c, ns], c == 0, c == 3, ["MOT", "w4"], ["pb%d" % pa])
                pn = P.bank()
                for c in range(4):
                    P.mm(pbank[pn][:, :], NOTt[:, c, :], WUN[:, c, ns], c == 0, c == 3, ["NOT", "w4"], ["pb%d" % pn])
                P.tt("dve", MIX[:, ns], pbank[pa][:, :], GMB[b_][:, ns], ALU.mult, ["pb%d" % pa, "GMB%d" % b_], ["MIX"])
                P.tt("dve", GNB[b_][:, ns], pbank[pn][:, :], GNB[b_][:, ns], ALU.mult, ["pb%d" % pn, "GNB%d" % b_],
                     ["GNB%d" % b_])
                P.tt("pool", MIX[:, ns], MIX[:, ns], GNB[b_][:, ns], ALU.add, ["MIX", "GNB%d" % b_], ["MIX"])
            for g in range(2):
                pb = P.bank()
                for j in range(4):
                    c = g * 4 + j
                    P.tr(pbank[pb][:, j * 128:(j + 1) * 128], MIX[:, c * 128:(c + 1) * 128], ident, ["MIX", "ident"],
                         ["pb%d" % pb])
                P.cp("act", MIXT[:, g * 4:(g + 1) * 4, :], pbank[pb][:, :].rearrange("p (c t) -> p c t", c=4),
                     ["pb%d" % pb], ["MIXT"])
            for nb in range(2):
                ns = slice(nb * 512, (nb + 1) * 512)
                pb = P.bank()
                for c in range(8):
                    P.mm(pbank[pb][:, :], MIXT[:, c, :], WO[:, c, ns], c == 0, c == 7, ["MIXT", "w4"], ["pb%d" % pb])
                P.stt("dve", RB[:, ns], XB[b_][:, ns], ALPHA, pbank[pb][:, :], ALU.mult, ALU.add,
                      ["XB%d" % b_, "pb%d" % pb], ["RB"])
            layer_norm(RB, X1B[b_], 0, ["RB"], ["X1B%d" % b_])
            P.st(X1D[rows, :], X1B[b_], ["X1B%d" % b_], ["X1D_%d" % ti])

        P.barrier()
        apos[0] = persist_mark
        LN2 = alloc3(2, D)
        WPR = alloc3(8, 2048)
        IOTA = alloc(16)
        P.ld(LN2, lnp_d[:, 2:4, :], ["w5"])
        IOTA2 = alloc(16)
        P.ld(IOTA, iota_d[:, :], ["w5"])
        P.ts("dve", IOTA2, IOTA, 16.0, ALU.mult, ["w5"], ["w5"], s2=16.0, op1=ALU.add)
        cmark = apos[0]
        CV32 = [alloc(4096) for _ in range(3)]
        CV16 = [alloc(2048).bitcast(BF16) for _ in range(3)]
        ccnt = 0
        for src_d, dst_d in ((peer_u_d, U16D), (peer_v_d, V16D)):
            for ch in range(32):
                cb_ = ccnt % 3
                ccnt += 1
                rs_ = slice(ch * 512, (ch + 1) * 512)
                P.ld(CV32[cb_], src_d[rs_, :].rearrange("(p r) n -> p (r n)", p=128), ["CV32_%d" % cb_])
                eng = ("dve", "act", "pool")[cb_]
                P.cp(eng, CV16[cb_], CV32[cb_], ["CV32_%d" % cb_], ["CV16_%d" % cb_])
                P.st(dst_d[rs_, :].rearrange("(p r) n -> p (r n)", p=128), CV16[cb_], ["CV16_%d" % cb_], ["T16"], eng="sp")
        P.barrier()
        apos[0] = cmark
        PWT = alloc(16 * 8 * 128).rearrange("p (a c m) -> p a c m", a=16, c=8)
        SKT = alloc3(2, 128)
        P.ld(PWT, pwqt_d[:, :, :, :], ["PWT"])
        P.ld(SKT, skt_d[:, :, :], ["SKT"])
        for c in range(8):
            for g in range(4):
                pb = P.bank()
                for j in range(4):
                    hp = g * 4 + j
                    P.mm(pbank[pb][:, j * 128:(j + 1) * 128], PWT[:, hp, c, :], SKT[:, hp % 2, :], True, True,
                         ["PWT", "SKT"], ["pb%d" % pb])
                P.cp("act" if g % 2 else "dve", WPR[:, c, g * 512:(g + 1) * 512], pbank[pb][:, :], ["pb%d" % pb], ["w5"])
        P.barrier()
        apos[0] -= 16 * 8 * 128 + 256
        X1 = [alloc(D) for _ in range(2)]
        X1T = alloc3(8, 128)
        SS = alloc3(16, 128)
        SS2 = alloc3(16, 128)
        V16 = alloc3(16, 16)
        I16 = alloc3(16, 16)
        I16u = I16.bitcast(U32)
        I16f = alloc3(16, 16)
        CAND = alloc3(8, 256)
        CAND2 = alloc3(8, 256)
        VC = alloc3(8, 16)
        ICu = alloc3(8, 16).bitcast(U32)
        ICf = alloc3(8, 16)
        AIX = alloc3(8, 16)
        BIX = alloc3(8, 16)
        OH = alloc(8 * 16 * 16).rearrange("p (h k a) -> p h k a", h=8, k=16)
        I1S = alloc3(8, 16)
        I2S = alloc3(8, 16)
        EF = alloc(128)
        GWs = [alloc3(8, 16) for _ in range(2)]
        sm8b = alloc(8)
        APREs = [alloc(128) for _ in range(2)]
        GT = alloc(128)
        WCOLs = [alloc(128) for _ in range(2)]
        NGU = 8
        NGV = 8
        UBu = [alloc(D // 2).bitcast(BF16) for _ in range(NGU)]
        UBv = [alloc(D // 2).bitcast(BF16) for _ in range(NGV)]
        JUNK = alloc(D)
        RB2 = alloc(D)
        EIs = [EI, EI2]
        DG = [alloc(64).bitcast(BF16) for _ in range(4)]
        YB = alloc(D)
        gcnt = 0
        dcnt = 0

        def stage_a(ti):
            b_ = ti % 2
            rows = slice(ti * 128, (ti + 1) * 128)
            P.ld(X1[b_], X1D[rows, :], ["X1_%d" % b_], r=["X1D_%d" % ti])
            for g in range(2):
                pb = P.bank()
                for j in range(4):
                    c = g * 4 + j
                    P.tr(pbank[pb][:, j * 128:(j + 1) * 128], X1[b_][:, c * 128:(c + 1) * 128], ident,
                         ["X1_%d" % b_, "ident"], ["pb%d" % pb])
                P.cp("act", X1T[:, g * 4:(g + 1) * 4, :], pbank[pb][:, :].rearrange("p (c t) -> p c t", c=4),
                     ["pb%d" % pb], ["X1T"])
            for g in range(4):
                pb = P.bank()
                for c in range(8):
                    P.mm(pbank[pb][:, :], X1T[:, c, :], WPR[:, c, g * 512:(g + 1) * 512], c == 0, c == 7,
                         ["X1T", "w5"], ["pb%d" % pb])
                P.cp("act", SS[:, g * 4:(g + 1) * 4, :], pbank[pb][:, :].rearrange("p (a k) -> p a k", a=4),
                     ["pb%d" % pb], ["SS%d" % g])
            for hp in range(16):
                top16(SS[:, hp, :], SS2[:, hp, :], V16[:, hp, :], I16u[:, hp, :], 128, "SS%d" % (hp // 4))
            VK = ["SS%dv" % g for g in range(4)]
            IK = ["SS%di" % g for g in range(4)]
            V4 = V16.rearrange("p (h q) k -> p h q k", q=2)
            P.tt("dve", CAND.rearrange("p h (a b) -> p h a b", a=16),
                 V4[:, :, 0, :].unsqueeze(3).to_broadcast([128, 8, 16, 16]),
                 V4[:, :, 1, :].unsqueeze(2).to_broadcast([128, 8, 16, 16]), ALU.add, VK, ["CAND"])
            for h in range(8):
                top16(CAND[:, h, :], CAND2[:, h, :], VC[:, h, :], ICu[:, h, :], 256, "CAND")
            P.cp("dve", ICf, ICu, ["CANDi"], ["ICf"])
            P.cp("dve", I16f, I16u, IK, ["I16f"])
            P.tt("dve", OH, ICf.unsqueeze(3).to_broadcast([128, 8, 16, 16]),
                 IOTA2.unsqueeze(1).unsqueeze(1).to_broadcast([128, 8, 16, 16]), ALU.is_ge, ["ICf", "w5"], ["OH"])
            P.op("dve", lambda e: e.tensor_reduce(out=AIX, in_=OH, axis=AX.X, op=ALU.add), ["OH"], ["AIX"])
            P.stt("dve", BIX, AIX, -16.0, ICf, ALU.mult, ALU.add, ["AIX", "ICf"], ["BIX"])
            I4 = I16f.rearrange("p (h q) k -> p h q k", q=2)
            iota_b = IOTA.unsqueeze(1).unsqueeze(1).to_broadcast([128, 8, 16, 16])
            for sel_ix, src_i, dst in ((AIX, I4[:, :, 0, :], I1S), (BIX, I4[:, :, 1, :], I2S)):
                P.tt("dve", OH, sel_ix.unsqueeze(3).to_broadcast([128, 8, 16, 16]), iota_b, ALU.is_equal,
                     ["AIX", "BIX", "w5"], ["OH"])
                P.tt("dve", OH, OH, src_i.unsqueeze(2).to_broadcast([128, 8, 16, 16]), ALU.mult, ["OH", "I16f"], ["OH"])
                P.op("dve", lambda e, dst=dst: e.tensor_reduce(out=dst, in_=OH, axis=AX.X, op=ALU.add), ["OH"],
                     ["I12S"])
            P.stt("dve", EF.rearrange("p (h k) -> p h k", h=8), I1S, 128.0, I2S, ALU.mult, ALU.add, ["I12S"], ["EF"])
            P.cp("dve", EIs[b_], EF, ["EF"], ["EI%d" % b_])
            gw = GWs[b_]
            P.tt("dve", gw, VC, VC[:, :, 0:1].to_broadcast([128, 8, 16]), ALU.subtract, ["CANDv"], ["GW%d" % b_])
            P.act(gw, gw, AF.Exp, ["GW%d" % b_], ["GW%d" % b_])
            P.op("dve", lambda e: e.reduce_sum(out=sm8b, in_=gw, axis=AX.X), ["GW%d" % b_], ["sm8b"])
            P.op("dve", lambda e: e.reciprocal(out=sm8b, in_=sm8b), ["sm8b"], ["sm8b"])
            P.tt("dve", gw, gw, sm8b.unsqueeze(2).to_broadcast([128, 8, 16]), ALU.mult, ["GW%d" % b_, "sm8b"],
                 ["GW%d" % b_])

        ucnt = [0]
        vcnt = [0]
        dgc = [0]

        def col_u(ti, j):
            b_ = ti % 2
            ub = ucnt[0] % NGU
            ucnt[0] += 1
            P.dma("pool", lambda e: e.indirect_dma_start(
                out=UBu[ub], out_offset=None, in_=U16D[:, :],
                in_offset=bass.IndirectOffsetOnAxis(ap=EIs[b_][:, j:j + 1], axis=0)), ["EI%d" % b_], ["UBu%d" % ub])
            P.op("dve", lambda e: e.scalar_tensor_tensor(
                out=JUNK, in0=UBu[ub], scalar=1.0, in1=X1[b_], op0=ALU.mult, op1=ALU.mult,
                accum_out=APREs[b_][:, j:j + 1]), ["UBu%d" % ub, "X1_%d" % b_], ["JUNK", "APRE%d" % b_])

        def finish_u(ti):
            b_ = ti % 2
            gelu_from(APREs[b_], GT, WCOLs[b_], ["APRE%d" % b_], "WCOL%d" % b_)
            P.tt("dve", WCOLs[b_], WCOLs[b_], GWs[b_].rearrange("p h k -> p (h k)"), ALU.mult,
                 ["WCOL%d" % b_, "GW%d" % b_], ["WCOL%d" % b_])

        def col_v(ti, j, p0, p1):
            b_ = ti % 2
            vb = vcnt[0] % NGV
            vcnt[0] += 1
            P.dma("pool", lambda e: e.indirect_dma_start(
                out=UBv[vb], out_offset=None, in_=V16D[:, :],
                in_offset=bass.IndirectOffsetOnAxis(ap=EIs[b_][:, j:j + 1], axis=0)), ["EI%d" % b_], ["UBv%d" % vb])
            dg = dgc[0] % 4
            dgc[0] += 1
            P.act(DG[dg], ident, AF.Copy, ["ident", "WCOL%d" % b_], ["DG%d" % dg], scale=WCOLs[b_][:, j:j + 1])
            P.mm(pbank[p0][:, :], DG[dg], UBv[vb][:, 0:512], j == 0, j == 127, ["DG%d" % dg, "UBv%d" % vb],
                 ["pb%d" % p0])
            P.mm(pbank[p1][:, :], DG[dg], UBv[vb][:, 512:1024], j == 0, j == 127, ["DG%d" % dg, "UBv%d" % vb],
                 ["pb%d" % p1])

        def finish_v(ti, p0, p1):
            b_ = ti % 2
            rows = slice(ti * 128, (ti + 1) * 128)
            P.stt("dve", RB2[:, 0:512], X1[b_][:, 0:512], ALPHA, pbank[p0][:, :], ALU.mult, ALU.add,
                  ["X1_%d" % b_, "pb%d" % p0], ["RB2"])
            P.stt("dve", RB2[:, 512:1024], X1[b_][:, 512:1024], ALPHA, pbank[p1][:, :], ALU.mult, ALU.add,
                  ["X1_%d" % b_, "pb%d" % p1], ["RB2"])
            layer_norm(RB2, YB, 0, ["RB2"], ["YB"])
            P.st(y_out[rows, :], YB, ["YB"], eng="sp")

        LNP = LN2
        ntile = NT + 1 if K5 >= 2 else 0
        if ntile:
            stage_a(0)
            for j in range(128):
                col_u(0, j)
            finish_u(0)
        for ti in range(ntile):
            nxt = ti + 1 < ntile
            if nxt:
                stage_a(ti + 1)
            p0 = P.bank()
            p1 = P.bank()
            LEAD = 112
            for k in range(128 + LEAD):
                if k < 128:
                    col_v(ti, k, p0, p1)
                if nxt and k >= LEAD:
                    col_u(ti + 1, k - LEAD)
            finish_v(ti, p0, p1)
            if nxt:
                finish_u(ti + 1)

        with nc.Block() as block:
            @block.tensor
            def _(e):
                P.emit("pe", e)

            @block.scalar
            def _(e):
                P.emit("act", e)

            @block.vector
            def _(e):
                P.emit("dve", e)

            @block.gpsimd
            def _(e):
                P.emit("pool", e)

            @block.sync
            def _(e):
                P.emit("sp", e)
    return nc


_NC_CACHE = {}


def kernel(x_prompt, x_sample, cache_cmp_kv, cache_slc_kv, page_table, state_win_kv, state_mlstm_C,
           state_mlstm_n, state_mlstm_m, state_mlstm_conv, w_in, m_conv_w, m_conv_b, m_wq, m_wk,
           m_gate_bias, m_norm_g, cmp_pe, cmp_w1, cmp_w2, rel_bias, w_up_m, w_up_n, w_out, ln1_g, ln1_b,
           ln2_g, ln2_b, peer_wq, peer_subkeys, peer_u, peer_v):
    f32 = np.float32
    if "nc" not in _NC_CACHE:
        _NC_CACHE["nc"] = build_program()
    nc = _NC_CACHE["nc"]
    xp = np.ascontiguousarray(np.asarray(x_prompt, f32)).reshape(BP * SEQ, D)
    xs = np.asarray(x_sample, f32).reshape(BS, D)
    ident = np.eye(128, dtype=f32)
    w_in0 = np.ascontiguousarray(np.asarray(w_in, f32)[0])
    stw = np.asarray(state_win_kv, f32)[0].reshape(BS, 512, 256)
    stc = np.asarray(state_mlstm_conv, f32)[0]
    cw = np.asarray(m_conv_w, f32)[0]
    convw_l = np.ascontiguousarray(np.transpose(cw.reshape(4, 4, 128), (2, 1, 0)))
    convb_l = np.ascontiguousarray(np.asarray(m_conv_b, f32)[0].reshape(4, 128).T)
    wq_l = np.ascontiguousarray(np.transpose(np.asarray(m_wq, f32)[0], (1, 0, 2)))
    wk_l = np.ascontiguousarray(np.transpose(np.asarray(m_wk, f32)[0], (1, 0, 2)))
    gb_l = np.ascontiguousarray(np.asarray(m_gate_bias, f32)[0].T)
    normg_rep = np.ascontiguousarray(np.broadcast_to(np.asarray(m_norm_g, f32)[0][None, :], (128, 512)))
    sel_c = np.zeros((4, 4, 128), f32)
    for h in range(4):
        sel_c[h, h, :] = 1.0
    tri_c = np.triu(np.ones((128, 128), f32))
    relb = np.asarray(rel_bias, f32)
    dist = np.arange(0, 4096)
    nf = np.maximum(dist, 16).astype(f32)
    large = 16 + (np.log(nf / f32(16)) / f32(np.log(8.0)) * f32(16)).astype(np.int32)
    bucket = np.where(dist < 16, dist, np.minimum(large, 31)).astype(np.int64)
    NEGM = f32(-30000.0)
    tt_ = np.arange(SEQ)[:, None]
    nn_ = np.arange(128)[None, :]
    dcm = tt_ - 16 * nn_ - 31
    vcm = (dcm >= 0) & (nn_ <= 126)
    cbt = np.where(vcm[:, None, :], relb[bucket[np.maximum(dcm, 0)]].transpose(0, 2, 1), NEGM).astype(f32)
    ii = np.arange(128)[:, None]
    jj = np.arange(128)[None, :]
    tz = np.empty((128, 8, 2, 128), f32)
    d0 = jj - ii
    tz[:, :, 0, :] = np.where((d0 >= 0)[:, None, :], relb[bucket[np.maximum(d0, 0)]].transpose(0, 2, 1), NEGM)
    tz[:, :, 1, :] = relb[bucket[128 + d0]].transpose(0, 2, 1)
    wz = np.where(jj < ii, f32(0), NEGM).astype(f32)
    ebig = (np.arange(SEQ)[None, :] // 64 == np.arange(32)[:, None]).astype(f32) * f32(30000.0)
    tq_ = (np.arange(16)[None, :, None] * 128 + np.arange(128)[:, None, None])
    bb_ = np.arange(32)[None, None, :]
    cur = tq_ // 64
    fvnc = np.empty((128, 16, 2, 32), f32)
    fvnc[:, :, 0, :] = np.where((bb_ == 0) | (bb_ == cur) | (bb_ == cur - 1), f32(1e4), f32(0))
    fvnc[:, :, 1, :] = np.where(bb_ * 64 <= tq_, f32(0), NEGM)
    rowv = (np.arange(128) >= 31).astype(f32).reshape(128, 1)
    shm = (ii == jj + 1).astype(f32)
    cbr = np.ascontiguousarray(np.broadcast_to(relb[31][None, :], (128, 8))).astype(f32)
    w1 = np.asarray(cmp_w1, f32)[0]
    w1r = w1.reshape(2, 2, 16, 64, 64)
    w1l_h = np.transpose(w1r, (3, 0, 2, 1, 4)).reshape(64, 2, 16, 128)
    w1l = np.ascontiguousarray(np.concatenate([w1l_h, w1l_h], axis=0))
    w1n = np.ascontiguousarray(np.transpose(w1.reshape(2, 16, 128, 64), (2, 0, 1, 3)))
    pel = np.ascontiguousarray(np.transpose(np.asarray(cmp_pe, f32)[0].reshape(2, 16, 128), (2, 0, 1)))
    w2 = np.asarray(cmp_w2, f32)[0]
    w2n = np.ascontiguousarray(np.transpose(w2, (1, 0, 2)))
    w2d = np.zeros((64, 2, 128), f32)
    w2d[:, 0, 0:64] = w2[0]
    w2d[:, 1, 64:128] = w2[0]
    lnp = np.ascontiguousarray(np.broadcast_to(np.stack([np.asarray(a, f32)[0] for a in (ln1_g, ln1_b, ln2_g, ln2_b)])[None],
                                               (128, 4, D)))
    pwq_t = np.ascontiguousarray(np.asarray(peer_wq, f32)[0].reshape(8, 128, 16, 128).transpose(3, 2, 0, 1))
    sk_t = np.ascontiguousarray(np.asarray(peer_subkeys, f32)[0].transpose(2, 0, 1))
    iota16 = np.ascontiguousarray(np.broadcast_to(np.arange(16, dtype=f32)[None, :], (128, 16)))
    cw4 = np.ascontiguousarray(np.broadcast_to(cw[None], (4, 4, 512)))
    cb4 = np.ascontiguousarray(np.broadcast_to(np.asarray(m_conv_b, f32)[0][None], (4, 512)))
    gb4 = np.ascontiguousarray(np.broadcast_to(np.asarray(m_gate_bias, f32)[0].reshape(1, 8), (4, 8)))
    ng16 = np.ascontiguousarray(np.tile(np.asarray(m_norm_g, f32)[0].reshape(4, 128), (4, 1)))
    stC = np.asarray(state_mlstm_C, f32)[0].reshape(BS * 4, 128, 128)
    stn = np.asarray(state_mlstm_n, f32)[0].reshape(BS, 512)
    stm = np.asarray(state_mlstm_m, f32)[0]
    n6 = np.arange(128)[:, None] * 8 + np.arange(8)[None, :]
    d6 = 16353 - 16 * n6
    cbs = np.where((n6 <= 1022)[:, :, None], relb[bucket[np.clip(d6, 0, 4095)]], NEGM).astype(f32)
    i6 = np.arange(4)[None, :] * 128 + np.arange(128)[:, None]
    bw = np.where((i6 >= 1)[:, :, None], relb[bucket[np.clip(512 - i6, 0, 4095)]], NEGM).astype(f32)
    tok = np.arange(64)
    t25 = np.empty((60, 2, 8, 64), f32)
    t25[:, 0] = relb[bucket[128 - tok]].T[None]
    t25[:, 1] = relb[bucket[64 - tok]].T[None]
    fl255 = (np.arange(60) % 15 == 14).astype(f32).reshape(60, 1)
    ind60 = (np.arange(60)[:, None] // 15 == np.arange(4)[None, :]).astype(f32)
    rb0 = np.ascontiguousarray(np.broadcast_to(relb[0][None, :], (4, 8))).astype(f32)
    shd = np.ascontiguousarray(shm.T)
    w2kt = np.ascontiguousarray(w2[0].T)
    iota8 = np.ascontiguousarray(np.broadcast_to(np.arange(8, dtype=f32)[None, :], (128, 8)))
    iot128 = np.empty((8, 2, 128), f32)
    iot128[:, 0] = np.arange(128, dtype=f32)[None]
    iot128[:, 1] = 2.0 * (np.arange(128, dtype=f32)[None] + 1.0)
    pool_c = np.asarray(cache_cmp_kv, f32)[0].reshape(5120 * 8, 4096)
    pool_s = np.asarray(cache_slc_kv, f32)[0].reshape(5120 * 8, 4096)
    ptab = np.asarray(page_table, np.int32)
    shared = {"pool_cmp": pool_c, "pool_slc": pool_s, "cbs": cbs, "bw": bw, "t25": t25, "fl255": fl255, "ind60": ind60,
              "rb0": rb0, "shd": shd, "w2kt": w2kt, "iota8": iota8, "iot128": iot128, "cw4": cw4, "cb4": cb4, "gb4": gb4, "ng16": ng16, "w_up_m": np.ascontiguousarray(np.asarray(w_up_m, f32)[0]), "w_up_n": np.ascontiguousarray(np.asarray(w_up_n, f32)[0]),
              "w_out": np.ascontiguousarray(np.asarray(w_out, f32)[0]), "lnp": lnp, "pwq_t": pwq_t, "sk_t": sk_t,
              "iota16": iota16, "peer_u": np.ascontiguousarray(np.asarray(peer_u, f32)[0]),
              "peer_v": np.ascontiguousarray(np.asarray(peer_v, f32)[0]),
              "cbt": cbt, "tz": tz, "wz": wz, "ebig": ebig, "fvnc": fvnc, "rowv": rowv, "shm": shm, "cbr": cbr,
              "w1l": w1l, "w1n": w1n, "pel": pel, "w2d": w2d, "w2n": w2n,
              "w_in": w_in0, "ident": ident, "convw_l": convw_l, "convb_l": convb_l, "wq_l": wq_l, "wk_l": wk_l,
              "gb_l": gb_l, "normg_rep": normg_rep, "sel_c": sel_c, "tri_c": tri_c}
    in_maps = []
    for c in range(NCORES):
        xs_pad = np.zeros((128, D), f32)
        xs_pad[:SSC] = xs[c * SSC:(c + 1) * SSC]
        in_maps.append({
            **shared,
            "x_p": xp[c * TP:(c + 1) * TP],
            "x_s": xs_pad,
            "st_win": np.ascontiguousarray(stw[c * SSC:(c + 1) * SSC]),
            "st_conv": np.ascontiguousarray(stc[c * SSC:(c + 1) * SSC]),
            "st_C": np.ascontiguousarray(stC[c * SSC * 4:(c + 1) * SSC * 4]),
            "st_n": np.ascontiguousarray(stn[c * SSC:(c + 1) * SSC]),
            "st_m": np.ascontiguousarray(stm[c * SSC:(c + 1) * SSC]),
            "pt_l": np.ascontiguousarray(ptab[c * SSC:(c + 1) * SSC].T),
            "pt8": np.ascontiguousarray(np.repeat(ptab[c * SSC:(c + 1) * SSC], 2, axis=0)),
        })
    res = run_bass_kernel_spmd(nc, in_maps, core_ids=list(range(NCORES)))
    R = res.results
    if DEBUG:
        DBG["R"] = R

    def cat(name):
        return np.concatenate([np.asarray(r[name]) for r in R], axis=0)

    kvt = (2, 2, 64)
    y_p = np.concatenate([np.asarray(r["y_out"])[:TP] for r in R], axis=0).reshape(BP, SEQ, D)
    y_s = np.concatenate([np.asarray(r["y_out"])[TP:TP + SSC] for r in R], axis=0).reshape(BS, 1, D)
    cmp_p = cat("o_cmp_p").reshape((1, BP, SEQ) + kvt)
    cmp_s = cat("o_cmp_s").reshape((1, BS, 1) + kvt)
    slc_p = cat("o_slc_p").reshape((1, BP, SEQ) + kvt)
    slc_s = cat("o_slc_s").reshape((1, BS, 1) + kvt)
    win_p = cat("o_win_p").reshape((1, BP, 512) + kvt)
    win_s = cat("o_win_s").reshape((1, BS, 512) + kvt)
    C_p = cat("o_C_p").reshape(1, BP, 4, 128, 128)
    C_s = cat("o_C_s").reshape(1, BS, 4, 128, 128)
    n_p = cat("o_n_p").reshape(1, BP, 4, 128)
    n_s = cat("o_n_s").reshape(1, BS, 4, 128)
    m_p = cat("o_m_p").reshape(1, BP, 4)
    m_s = cat("o_m_s").reshape(1, BS, 4)
    conv_p = cat("o_conv_p").reshape(1, BP, 3, 512)
    conv_s = cat("o_conv_s").reshape(1, BS, 3, 512)
    return (y_p, y_s, cmp_p, cmp_s, slc_p, slc_s, win_p, win_s, C_p, C_s, n_p, n_s, m_p, m_s, conv_p, conv_s)
```

```python
import contextlib
import os
import numpy as np
import concourse.bass as bass
import concourse.mybir as mybir
from concourse.bass_utils import run_bass_kernel_spmd

F32 = mybir.dt.float32
I32 = mybir.dt.int32
U32 = mybir.dt.uint32
BF16 = mybir.dt.bfloat16
AF = mybir.ActivationFunctionType
ALU = mybir.AluOpType
AX = mybir.AxisListType

NCORES = 8
D = 1024
SEQ = 2048
BP = 16
BS = 32
SPC = BP // NCORES
SSC = BS // NCORES
TP = SPC * SEQ
NT = TP // 128
IN_DIM = 4896
O_U, O_V, O_O, O_I, O_F, O_Q, O_KC, O_KS, O_KW, O_GN, O_GM, O_GNN = (
    0, 512, 1024, 1536, 1540, 1544, 2056, 2312, 2568, 2824, 2848, 3872)
COL_GROUPS = [(0, 512), (512, 512), (1024, 512), (1536, 8), (1544, 512), (2056, 512),
              (2568, 280), (2848, 512), (3360, 512), (3872, 512), (4384, 512)]

DEBUG = False
DBG = {}
ENGS = ("pe", "act", "dve", "pool", "sp")
N_DMA_SEMS = 12


class Prog:
    def __init__(self, nc, stack):
        self.nc = nc
        self.ops = {e: [] for e in ENGS}
        self.cnt = {e: 0 for e in ENGS}
        self.esem = {e: stack.enter_context(nc.semaphore("es_" + e)) for e in ENGS}
        self.dsem = {e: [stack.enter_context(nc.semaphore("ds_%s%d" % (e, i))) for i in range(N_DMA_SEMS)]
                     for e in ("sp", "pool", "act")}
        self.dval = {e: [0] * N_DMA_SEMS for e in ("sp", "pool", "act")}
        self.dnext = {e: 0 for e in ("sp", "pool", "act")}
        self.semobj = {}
        for e in ENGS:
            self.semobj["es_" + e] = self.esem[e]
        for e in self.dsem:
            for i, s in enumerate(self.dsem[e]):
                self.semobj["ds_%s%d" % (e, i)] = s
        self.lastw = {}
        self.readers = {}
        self.waited = {e: {} for e in ENGS}
        self.final_tokens = []

    def _deps(self, eng, reads, writes):
        deps = {}

        def add(tok, same_ok):
            if tok is None:
                return
            s, v = tok
            if s == "es_" + eng and not same_ok:
                return
            if deps.get(s, 0) < v:
                deps[s] = v

        for k in reads:
            add(self.lastw.get(k), eng != "pe")
        for k in writes:
            add(self.lastw.get(k), False)
            for s, v in self.readers.get(k, {}).items():
                add((s, v), False)
        out = []
        for s, v in deps.items():
            if self.waited[eng].get(s, 0) < v:
                self.waited[eng][s] = v
                out.append((s, v))
        return out

    def _commit(self, tok, reads, writes):
        for k in writes:
            self.lastw[k] = tok
            self.readers[k] = {}
        for k in reads:
            r = self.readers.setdefault(k, {})
            if r.get(tok[0], 0) < tok[1]:
                r[tok[0]] = tok[1]

    def op(self, eng, fn, reads=(), writes=()):
        waits = self._deps(eng, reads, writes)
        self.cnt[eng] += 1
        tok = ("es_" + eng, self.cnt[eng])
        self.ops[eng].append((waits, fn, ("es_" + eng, 1)))
        self._commit(tok, reads, writes)
        return tok

    def dma(self, eng, fn, reads=(), writes=(), final=False):
        i = self.dnext[eng]
        self.dnext[eng] = (i + 1) % N_DMA_SEMS
        sname = "ds_%s%d" % (eng, i)
        waits = self._deps(eng, reads, writes)
        prev = self.dval[eng][i]
        if prev > 0 and self.waited[eng].get(sname, 0) < prev:
            self.waited[eng][sname] = prev
            waits.append((sname, prev))
        self.dval[eng][i] += 16
        tok = (sname, self.dval[eng][i])
        self.ops[eng].append((waits, fn, (sname, 16)))
        self._commit(tok, reads, writes)
        if final:
            self.final_tokens.append(tok)
        return tok

    def bank(self):
        b = self._bank = (getattr(self, "_bank", -1) + 1) % 8
        return b

    def mm(self, out, lhsT, rhs, start, stop, r, w):
        return self.op("pe", lambda e: e.matmul(out=out, lhsT=lhsT, rhs=rhs, start=start, stop=stop), r, w)

    def tr(self, out, in_, ident, r, w):
        return self.op("pe", lambda e: e.transpose(out=out, in_=in_, identity=ident), r, w)

    def act(self, out, in_, func, r, w, bias=None, scale=None):
        kw = {}
        if bias is not None:
            kw["bias"] = bias
        if scale is not None:
            kw["scale"] = scale
        return self.op("act", lambda e: e.activation(out=out, in_=in_, func=func, **kw), r, w)

    def tt(self, eng, out, in0, in1, op, r, w):
        return self.op(eng, lambda e: e.tensor_tensor(out=out, in0=in0, in1=in1, op=op), r, w)

    def ts(self, eng, out, in0, s1, op0, r, w, s2=None, op1=None):
        if op1 is None:
            return self.op(eng, lambda e: e.tensor_scalar(out=out, in0=in0, scalar1=s1, scalar2=None, op0=op0), r, w)
        return self.op(eng, lambda e: e.tensor_scalar(out=out, in0=in0, scalar1=s1, scalar2=s2, op0=op0, op1=op1), r, w)

    def stt(self, eng, out, in0, scalar, in1, op0, op1, r, w):
        return self.op(eng, lambda e: e.scalar_tensor_tensor(out=out, in0=in0, scalar=scalar, in1=in1, op0=op0, op1=op1), r, w)

    def cp(self, eng, out, in_, r, w):
        if eng == "act":
            return self.op("act", lambda e: e.copy(out=out, in_=in_), r, w)
        return self.op(eng, lambda e: e.tensor_copy(out=out, in_=in_), r, w)

    def memset(self, eng, out, val, w):
        return self.op(eng, lambda e: e.memset(out, val), (), w)

    def ld(self, out, in_, w, r=(), eng="sp"):
        return self.dma(eng, lambda e: e.dma_start(out=out, in_=in_), r, w)

    def st(self, out, in_, r, w=(), eng="pool"):
        return self.dma(eng, lambda e: e.dma_start(out=out, in_=in_), r, w)

    def barrier(self):
        toks = []
        for e in ENGS:
            if self.cnt[e] > 0:
                toks.append(("es_" + e, self.cnt[e]))
        for e in self.dsem:
            for i in range(N_DMA_SEMS):
                if self.dval[e][i] > 0:
                    toks.append(("ds_%s%d" % (e, i), self.dval[e][i]))
        for e in ENGS:
            waits = []
            for s_, v in toks:
                if s_ == "es_" + e and e in ("pe", "sp"):
                    continue
                if self.waited[e].get(s_, 0) < v:
                    self.waited[e][s_] = v
                    waits.append((s_, v))
            if waits:
                self.ops[e].append((waits, None, None))
        self.lastw = {}
        self.readers = {}

    def emit(self, eng, eobj):
        for waits, fn, inc in self.ops[eng]:
            sname, amt = inc if inc is not None else (None, None)
            for s, v in waits:
                eobj.wait_ge(self.semobj[s], v)
            if fn is None:
                continue
            ins = fn(eobj)
            ins.then_inc(self.semobj[sname], amt)
        if eng == "sp":
            for e in self.dsem:
                for i in range(N_DMA_SEMS):
                    if self.dval[e][i] > 0:
                        eobj.wait_ge(self.dsem[e][i], self.dval[e][i])


def build_program():
    nc = bass.Bass("TRN2", target_bir_lowering=False)
    stack = contextlib.ExitStack()
    with stack:
        def din(name, shape, dt=F32):
            return nc.dram_tensor(name, list(shape), dt, kind="ExternalInput").ap()

        def dout(name, shape, dt=F32):
            return nc.dram_tensor(name, list(shape), dt, kind="ExternalOutput").ap()

        def dscr(name, shape, dt=F32):
            return nc.dram_tensor(name, list(shape), dt, kind="Internal").ap()

        def sb(name, shape, dt=F32):
            return stack.enter_context(nc.sbuf_tensor(name, list(shape), dt))

        def ps(name, shape, dt=F32):
            return stack.enter_context(nc.psum_tensor(name, list(shape), dt))

        x_p = din("x_p", [TP, D])
        x_s = din("x_s", [128, D])
        w_in = din("w_in", [D, IN_DIM])
        ident_d = din("ident", [128, 128])
        st_win = din("st_win", [SSC, 512, 256])
        st_conv = din("st_conv", [SSC, 3, 512])

        o_cmp_p = dout("o_cmp_p", [TP, 256])
        o_slc_p = dout("o_slc_p", [TP, 256])
        o_win_p = dout("o_win_p", [SPC, 512, 256])
        o_conv_p = dout("o_conv_p", [SPC, 3, 512])
        o_cmp_s = dout("o_cmp_s", [SSC, 256])
        o_slc_s = dout("o_slc_s", [SSC, 256])
        o_win_s = dout("o_win_s", [SSC, 512, 256])
        o_conv_s = dout("o_conv_s", [SSC, 3, 512])

        convw_d = din("convw_l", [128, 4, 4])
        convb_d = din("convb_l", [128, 4])
        wq_d = din("wq_l", [128, 4, 128])
        wk_d = din("wk_l", [128, 4, 128])
        gb_d = din("gb_l", [4, 2])
        normg_d = din("normg_rep", [128, 512])
        sel_d = din("sel_c", [4, 4, 128])
        tri_d = din("tri_c", [128, 128])
        o_C_p = dout("o_C_p", [SPC, 4, 128, 128])
        o_n_p = dout("o_n_p", [SPC, 4, 128])
        o_m_p = dout("o_m_p", [SPC, 4])
        MOUT = (dout if DEBUG else dscr)("MOUT", [TP + 128, 512])
        cbt_d = din("cbt", [SEQ, 8, 128])
        tz_d = din("tz", [128, 8, 2, 128])
        wz_d = din("wz", [128, 128])
        ebig_d = din("ebig", [32, SEQ])
        fvnc_d = din("fvnc", [128, 16, 2, 32])
        rowv_d = din("rowv", [128, 1])
        sh_d = din("shm", [128, 128])
        cbr_d = din("cbr", [128, 8])
        w1l_d = din("w1l", [128, 2, 16, 128])
        w1n_d = din("w1n", [128, 2, 16, 64])
        pel_d = din("pel", [128, 2, 16])
        w2d_d = din("w2d", [64, 2, 128])
        w2n_d = din("w2n", [64, 2, 64])
        NOUTD = (dout if DEBUG else dscr)("NOUTD", [TP + 128, 512])
        wupm_d = din("w_up_m", [512, D])
        wupn_d = din("w_up_n", [512, D])
        wout_d = din("w_out", [D, D])
        lnp_d = din("lnp", [128, 4, D])
        pwqt_d = din("pwq_t", [128, 16, 8, 128])
        skt_d = din("sk_t", [128, 2, 128])
        iota_d = din("iota16", [128, 16])
        peer_u_d = din("peer_u", [16384, D])
        peer_v_d = din("peer_v", [16384, D])
        X1D = (dout if DEBUG else dscr)("X1D", [TP + 128, D])
        y_out = dout("y_out", [TP + 128, D])
        U16D = dscr("U16D", [16384, D], BF16)
        V16D = dscr("V16D", [16384, D], BF16)
        stC_d = din("st_C", [SSC * 4, 128, 128])
        stn_d = din("st_n", [SSC, 512])
        stm_d = din("st_m", [SSC, 4])
        cw4_d = din("cw4", [4, 4, 512])
        cb4_d = din("cb4", [4, 512])
        gb4_d = din("gb4", [4, 8])
        ng16_d = din("ng16", [16, 128])
        o_C_s = dout("o_C_s", [SSC * 4, 128, 128])
        o_n_s = dout("o_n_s", [SSC, 512])
        o_m_s = dout("o_m_s", [SSC, 4])
        SCRB = dscr("SCRB", [16, 264])
        pool_cmp_d = din("pool_cmp", [5120 * 8, 4096])
        pool_slc_d = din("pool_slc", [5120 * 8, 4096])
        ptl_d = din("pt_l", [128, 4], I32)
        pt8_d = din("pt8", [8, 128], I32)
        cbs_d = din("cbs", [128, 8, 8])
        bw_d = din("bw", [128, 4, 8])
        t25_d = din("t25", [60, 2, 8, 64])
        fl255_d = din("fl255", [60, 1])
        ind_d = din("ind60", [60, 4])
        rb0_d = din("rb0", [4, 8])
        shd_d = din("shd", [128, 128])
        w2kt_d = din("w2kt", [64, 64])
        iota8_d = din("iota8", [128, 8])
        iot128_d = din("iot128", [8, 2, 128])
        SCRQ = dscr("SCRQ", [4, 512])
        OSD = dscr("OSD", [2, 4, 8, 65])
        SELD = dscr("SELD", [8, 256])
        SCRP = dscr("SCRP", [8, 15, 5])
        Z = dscr("Z", [TP + 128, IN_DIM])

        P = Prog(nc, stack)

        ARENA_F = 52800
        EI = sb("EI_i32", [128, 128], I32)[:, :]
        EI2 = sb("EI2_i32", [128, 128], I32)[:, :]
        IDXC = sb("IDXC_i32", [128, 32], I32)[:, :]
        IDXS = sb("IDXS_i32", [128, 8], I32)[:, :]
        arena = sb("arena", [128, ARENA_F])
        apos = [0]

        def alloc(n):
            o = apos[0]
            apos[0] += n
            assert apos[0] <= ARENA_F, ("arena overflow", apos[0])
            return arena[:, o:o + n]

        def alloc3(a, b):
            return alloc(a * b).rearrange("p (a b) -> p a b", a=a)

        ident = alloc(128)
        pbank = [ps("pb%d" % i, [128, 512]) for i in range(8)]
        tri = alloc(128)
        selc = alloc3(4, 128)
        persist_mark = apos[0]

        QT = 8
        xt = [alloc(D) for i in range(2)]
        xT = alloc3(8, QT * 128)
        wg = [alloc3(8, 512) for i in range(2)]
        zs = [alloc(512) for i in range(2)]

        P.dma("sp", lambda e: e.dma_start(out=ident, in_=ident_d[:, :]), writes=["ident"])
        P.ld(tri, tri_d[:, :], ["tri"])
        P.ld(selc[0:4], sel_d[:, :, :], ["selc"])

        n_tiles_all = NT + 1
        groups = [list(range(g, min(g + QT, n_tiles_all))) for g in range(0, n_tiles_all, QT)]
        xcnt = 0
        wcnt = 0
        zcnt = 0
        pcnt = 0
        for grp in groups:
            for li, ti in enumerate(grp):
                b = xcnt % 2
                xcnt += 1
                src = x_p[ti * 128:(ti + 1) * 128, :] if ti < NT else x_s[:, :]
                P.dma("sp", lambda e, b=b, src=src: e.dma_start(out=xt[b], in_=src),
                      writes=["xt%d" % b])
                for c in range(8):
                    pb = pcnt % 8
                    pcnt += 1
                    P.op("pe", lambda e, pb=pb, b=b, c=c: e.transpose(
                        out=pbank[pb][:, 0:128], in_=xt[b][:, c * 128:(c + 1) * 128], identity=ident),
                        reads=["xt%d" % b, "ident"], writes=["pb%d" % pb])
                    eng = "dve" if c % 2 == 0 else "act"
                    if eng == "dve":
                        P.op("dve", lambda e, pb=pb, c=c, li=li: e.tensor_copy(
                            out=xT[:, c, li * 128:(li + 1) * 128], in_=pbank[pb][:, 0:128]),
                            reads=["pb%d" % pb], writes=["xT_%d_%d" % (c, li)])
                    else:
                        P.op("act", lambda e, pb=pb, c=c, li=li: e.copy(
                            out=xT[:, c, li * 128:(li + 1) * 128], in_=pbank[pb][:, 0:128]),
                            reads=["pb%d" % pb], writes=["xT_%d_%d" % (c, li)])
            for (c0, cw) in COL_GROUPS:
                wb = wcnt % 2
                wcnt += 1
                P.dma("sp", lambda e, wb=wb, c0=c0, cw=cw: e.dma_start(
                    out=wg[wb][:, :, 0:cw],
                    in_=w_in[:, c0:c0 + cw].rearrange("(c p) n -> p c n", p=128)),
                    writes=["wg%d" % wb])
                for li, ti in enumerate(grp):
                    pb = pcnt % 8
                    pcnt += 1
                    for c in range(8):
                        P.op("pe", lambda e, pb=pb, c=c, li=li, wb=wb, cw=cw: e.matmul(
                            out=pbank[pb][:, 0:cw], lhsT=xT[:, c, li * 128:(li + 1) * 128],
                            rhs=wg[wb][:, c, 0:cw], start=(c == 0), stop=(c == 7)),
                            reads=["xT_%d_%d" % (c, li), "wg%d" % wb], writes=["pb%d" % pb])
                    zb = zcnt % 2
                    zcnt += 1
                    if zcnt % 2 == 0:
                        P.op("dve", lambda e, pb=pb, zb=zb, cw=cw: e.tensor_copy(
                            out=zs[zb][:, 0:cw], in_=pbank[pb][:, 0:cw]),
                            reads=["pb%d" % pb], writes=["zs%d" % zb])
                    else:
                        P.op("act", lambda e, pb=pb, zb=zb, cw=cw: e.copy(
                            out=zs[zb][:, 0:cw], in_=pbank[pb][:, 0:cw]),
                            reads=["pb%d" % pb], writes=["zs%d" % zb])
                    P.dma("pool", lambda e, zb=zb, ti=ti, c0=c0, cw=cw: e.dma_start(
                        out=Z[ti * 128:(ti + 1) * 128, c0:c0 + cw], in_=zs[zb][:, 0:cw]),
                        reads=["zs%d" % zb], writes=["Z_%d_%d" % (ti, c0)])

        ZALL = ["Z_%d_%d" % (ti, c0) for ti in range(NT + 1) for (c0, _) in COL_GROUPS]

        def d2d(dst, src, final=True):
            P.dma("sp", lambda e: e.dma_start(out=dst, in_=src), reads=ZALL, writes=[], final=final)

        d2d(o_cmp_p[:, :], Z[0:TP, O_KC:O_KC + 256])
        d2d(o_slc_p[:, :], Z[0:TP, O_KS:O_KS + 256])
        for s in range(SPC):
            d2d(o_win_p[s, :, :], Z[s * SEQ + SEQ - 512:(s + 1) * SEQ, O_KW:O_KW + 256])
            d2d(o_conv_p[s, :, :], Z[(s + 1) * SEQ - 3:(s + 1) * SEQ, O_U:O_U + 512])
        d2d(o_cmp_s[:, :], Z[TP:TP + SSC, O_KC:O_KC + 256])
        d2d(o_slc_s[:, :], Z[TP:TP + SSC, O_KS:O_KS + 256])
        for s in range(SSC):
            d2d(o_win_s[s, 0:511, :], st_win[s, 1:512, :])
            d2d(o_win_s[s, 511:512, :], Z[TP + s:TP + s + 1, O_KW:O_KW + 256])
            d2d(o_conv_s[s, 0:2, :], st_conv[s, 1:3, :])
            d2d(o_conv_s[s, 2:3, :], Z[TP + s:TP + s + 1, O_U:O_U + 512])


        P.barrier()
        apos[0] = persist_mark
        CONST = ["ident", "tri", "selc", "m_w"]
        convw = alloc3(4, 4)
        convb = alloc(4)
        wq = alloc3(4, 128)
        wk = alloc3(4, 128)
        gb = alloc(2)
        ngbf = alloc(1)
        normg = alloc(512)
        zero_row = alloc(SEQ)
        P.ld(convw, convw_d[:, :, :], ["m_w"])
        P.ld(convb, convb_d[:, :], ["m_w"])
        P.ld(wq, wq_d[:, :, :], ["m_w"])
        P.ld(wk, wk_d[:, :, :], ["m_w"])
        P.ld(gb[0:4], gb_d[:, :], ["gb"])
        P.ld(normg, normg_d[:, :], ["m_w"])
        P.memset("dve", zero_row, 0.0, ["zero_row"])
        P.ts("dve", ngbf[0:4], gb[0:4, 1:2], -1.0, ALU.mult, ["gb"], ["ngbf"])

        ZIF = alloc3(16, 8)
        IT = alloc(SEQ)
        FT = alloc(SEQ)
        Bc = alloc(SEQ)
        Ac = IT
        CMc = FT
        Mc = Bc
        NRc = alloc(SEQ)
        EMc = alloc(SEQ)
        aT = alloc3(16, 4)
        emT = alloc3(16, 4)
        Uh = alloc3(16, 128)
        UT = alloc(3 + SEQ)
        CT = alloc(SEQ)
        ACC = CT
        QTh = alloc(SEQ)
        KTh = alloc(SEQ)
        KTOK = alloc3(16, 128)
        V1 = alloc3(16, 129)
        OP = alloc3(16, 128)
        SIG = alloc3(16, 128)
        Rb = alloc(SEQ)
        WT = alloc3(16, 512)
        Eb = [alloc(512) for _ in range(2)]
        MO = alloc3(16, 128)
        wts = alloc(16)
        VW = alloc3(16, 128)
        Csb = alloc(128)
        nsb = alloc(128)
        ep = [dict(dd=alloc(1), rec=alloc(1), hq=alloc(128), st=alloc(8), ag=alloc(4), rstd=alloc(1),
                   hn=alloc(128)) for _ in range(2)]
        BN_S = 6

        P.memset("dve", UT[:, 0:3], 0.0, ["UT"])
        P.memset("dve", V1[:, :, 128:129], 1.0, ["V1ones"])
        ecnt = 0
        epc = 0
        for sq in range(SPC):
            r0 = sq * SEQ
            zrows = Z[r0:r0 + SEQ, :]
            P.ld(ZIF, zrows[:, O_I:O_I + 8].rearrange("(n p) c -> p n c", p=128), ["ZIF"])
            for which, dst, dkey in ((0, IT, "ITb"), (1, FT, "FTb")):
                for g in range(4):
                    pb = P.bank()
                    for j in range(4):
                        n = g * 4 + j
                        P.tr(pbank[pb][0:4, j * 128:(j + 1) * 128], ZIF[:, n, which * 4:which * 4 + 4], ident,
                             ["ZIF", "ident"], ["pb%d" % pb])
                    P.cp("dve", dst[0:4, g * 512:(g + 1) * 512], pbank[pb][0:4, :], ["pb%d" % pb], [dkey])
            ITk = ["ITb"]
            FTk = ["FTb"]
            P.ts("dve", IT[0:4], IT[0:4], gb[0:4, 0:1], ALU.add, ITk + ["gb"], ["ITb"])
            P.act(FT[0:4], FT[0:4], AF.Exp, FTk + ["ngbf"], ["FTb"], bias=ngbf[0:4], scale=-1.0)
            P.act(FT[0:4], FT[0:4], AF.Ln, ["FTb"], ["FTb"], bias=1.0, scale=1.0)
            P.ts("dve", FT[0:4], FT[0:4], -1.0, ALU.mult, ["FTb"], ["FTb"])
            P.op("dve", lambda e: e.tensor_tensor_scan(out=Bc[0:4], data0=FT[0:4], data1=zero_row[0:4], initial=0.0,
                                                      op0=ALU.add, op1=ALU.add), ["FTb", "zero_row"], ["Bb"])
            P.tt("dve", Ac[0:4], IT[0:4], Bc[0:4], ALU.subtract, ["ITb", "Bb"], ["ITb", "ITb"] + ITk)
            P.op("dve", lambda e: e.tensor_tensor_scan(out=CMc[0:4], data0=Ac[0:4], data1=Ac[0:4], initial=0.0,
                                                      op0=ALU.max, op1=ALU.max), ["ITb"], ["FTb", "FTb", "FTb", "FTb"] + FTk)
            P.ts("dve", NRc[0:4], CMc[0:4], -1.0, ALU.mult, ["FTb"], ["NRc"])
            P.tt("dve", Mc[0:4], Bc[0:4], CMc[0:4], ALU.add, ["Bb", "FTb", "ITb"], ["Bb", "Bb"])
            P.act(EMc[0:4], Mc[0:4], AF.Exp, ["Bb"], ["EMc"], scale=-1.0)
            P.st(o_m_p[sq, :].rearrange("(h o) -> h o", o=1), Mc[0:4, SEQ - 1:SEQ], ["Bb"])
            for src, skey, dst, dkey in ((Ac, "ITb", aT, "aT"), (EMc, "EMc", emT, "emT")):
                pb = P.bank()
                for n in range(16):
                    P.tr(pbank[pb][:, n * 4:(n + 1) * 4], src[0:4, n * 128:(n + 1) * 128], ident[0:4, 0:4],
                         [skey, "ident"], ["pb%d" % pb])
                P.cp("dve", dst, pbank[pb][:, 0:64].rearrange("p (n h) -> p n h", h=4), ["pb%d" % pb], [dkey])
            for h in range(4):
                hs = slice(h * 128, (h + 1) * 128)
                P.ld(Uh, zrows[:, O_U + h * 128:O_U + (h + 1) * 128].rearrange("(n p) c -> p n c", p=128), ["Uh"])
                P.ld(V1[:, :, 0:128], zrows[:, O_V + h * 128:O_V + (h + 1) * 128].rearrange("(n p) c -> p n c", p=128),
                     ["V1"])
                P.ld(OP, zrows[:, O_O + h * 128:O_O + (h + 1) * 128].rearrange("(n p) c -> p n c", p=128), ["OP"])
                P.act(SIG, OP, AF.Sigmoid, ["OP"], ["SIG"])
                for g in range(4):
                    pb = P.bank()
                    P.mm(pbank[pb][:, :], selc[0:4, h, :], NRc[0:4, g * 512:(g + 1) * 512], True, True,
                         ["selc", "NRc"], ["pb%d" % pb])
                    P.cp("act", Rb[:, g * 512:(g + 1) * 512], pbank[pb][:, :], ["pb%d" % pb], ["Rb%d" % g])
                for g in range(4):
                    pb = P.bank()
                    for j in range(4):
                        n = g * 4 + j
                        P.tr(pbank[pb][:, j * 128:(j + 1) * 128], Uh[:, n, :], ident, ["Uh", "ident"], ["pb%d" % pb])
                    P.cp("dve", UT[:, 3 + g * 512:3 + (g + 1) * 512], pbank[pb][:, :], ["pb%d" % pb], ["UT"])
                P.ts("dve", ACC, UT[:, 0:SEQ], convw[:, h, 0:1], ALU.mult, ["UT", "m_w"], ["CT", "CT"])
                for j in range(1, 4):
                    P.stt("dve", ACC, UT[:, j:j + SEQ], convw[:, h, j:j + 1], ACC, ALU.mult, ALU.add,
                          ["UT", "CT", "m_w"], ["CT"])
                P.act(CT, ACC, AF.Silu, ["CT", "m_w"], ["CT", "CT"], bias=convb[:, h:h + 1])
                for g in range(4):
                    pb = P.bank()
                    P.mm(pbank[pb][:, :], wq[:, h, :], CT[:, g * 512:(g + 1) * 512], True, True, ["m_w", "CT"],
                         ["pb%d" % pb])
                    P.cp("act", QTh[:, g * 512:(g + 1) * 512], pbank[pb][:, :], ["pb%d" % pb], ["QT%d" % g])
                    pb = P.bank()
                    P.mm(pbank[pb][:, :], wk[:, h, :], CT[:, g * 512:(g + 1) * 512], True, True, ["m_w", "CT"],
                         ["pb%d" % pb])
                    P.ts("dve", KTh[:, g * 512:(g + 1) * 512], pbank[pb][:, :], 128.0 ** -0.5, ALU.mult,
                         ["pb%d" % pb], ["KT"])
                    pb = P.bank()
                    for j in range(4):
                        n = g * 4 + j
                        P.mm(pbank[pb][:, j * 128:(j + 1) * 128], CT[:, n * 128:(n + 1) * 128], wk[:, h, :], True, True,
                             ["m_w", "CT"], ["pb%d" % pb])
                    P.ts("dve", KTOK[:, g * 4:(g + 1) * 4, :], pbank[pb][:, :].rearrange("p (j e) -> p j e", j=4),
                         128.0 ** -0.5, ALU.mult, ["pb%d" % pb], ["KTOK"])
                for tb in range(4):
                    nst = 4 * tb + 4
                    for st_ in range(nst):
                        p_ = max(0, st_ - 4 * tb)
                        c0 = p_ * 128
                        cs = slice(c0, 512)
                        gs = slice(tb * 512 + c0, (tb + 1) * 512)
                        pb = P.bank()
                        P.mm(pbank[pb][:, cs], KTh[:, st_ * 128:(st_ + 1) * 128], QTh[:, gs], True, True,
                             ["KT", "QT%d" % tb], ["pb%d" % pb])
                        eb = ecnt % 2
                        ecnt += 1
                        P.ts("dve", Eb[eb][:, cs], Rb[:, gs], aT[:, st_, h:h + 1], ALU.add, ["Rb%d" % tb, "aT"],
                             ["Eb%d" % eb], s2=0.0, op1=ALU.min)
                        P.act(Eb[eb][:, cs], Eb[eb][:, cs], AF.Exp, ["Eb%d" % eb], ["Eb%d" % eb])
                        if st_ >= 4 * tb:
                            P.tt("pool", Eb[eb][:, c0:c0 + 128], Eb[eb][:, c0:c0 + 128], tri, ALU.mult,
                                 ["Eb%d" % eb, "tri"], ["Eb%d" % eb])
                        P.tt("dve", WT[:, st_, cs], pbank[pb][:, cs], Eb[eb][:, cs], ALU.mult,
                             ["pb%d" % pb, "Eb%d" % eb], ["WT%d" % st_])
                    for sub in range(4):
                        tq = 4 * tb + sub
                        pb = P.bank()
                        for st_ in range(tq + 1):
                            P.mm(pbank[pb][:, 0:129], WT[:, st_, sub * 128:(sub + 1) * 128], V1[:, st_, :],
                                 st_ == 0, st_ == tq, ["WT%d" % st_, "V1", "V1ones"], ["pb%d" % pb])
                        E = ep[epc % 2]
                        ek = "ep%d" % (epc % 2)
                        epc += 1
                        P.act(E["dd"], pbank[pb][:, 128:129], AF.Abs, ["pb%d" % pb], [ek + "dd"])
                        P.ts("dve", E["dd"], E["dd"], emT[:, tq, h:h + 1], ALU.max, [ek + "dd", "emT"], [ek + "dd"])
                        P.op("dve", lambda e, E=E: e.reciprocal(out=E["rec"], in_=E["dd"]), [ek + "dd"], [ek + "rec"])
                        P.ts("dve", E["hq"], pbank[pb][:, 0:128], E["rec"], ALU.mult, ["pb%d" % pb, ek + "rec"],
                             [ek + "hq"])
                        P.op("dve", lambda e, E=E: e.bn_stats(out=E["st"][:, 0:BN_S], in_=E["hq"]), [ek + "hq"],
                             [ek + "st"])
                        P.op("dve", lambda e, E=E: e.bn_aggr(out=E["ag"][:, 0:2], in_=E["st"][:, 0:BN_S]), [ek + "st"],
                             [ek + "ag"])
                        P.act(E["rstd"], E["ag"][:, 1:2], AF.Sqrt, [ek + "ag"], [ek + "rstd"], bias=1e-5, scale=1.0)
                        P.op("dve", lambda e, E=E: e.reciprocal(out=E["rstd"], in_=E["rstd"]), [ek + "rstd"],
                             [ek + "rstd"])
                        P.ts("dve", E["hn"], E["hq"], E["ag"][:, 0:1], ALU.subtract, [ek + "hq", ek + "ag", ek + "rstd"],
                             [ek + "hn"], s2=E["rstd"], op1=ALU.mult)
                        P.tt("pool", E["hn"], E["hn"], normg[:, hs], ALU.mult, [ek + "hn", "m_w"], [ek + "hn"])
                        P.tt("pool", MO[:, tq, :], E["hn"], SIG[:, tq, :], ALU.mult, [ek + "hn", "SIG"], ["MO"])
                P.st(MOUT[r0:r0 + SEQ, hs].rearrange("(n p) c -> p n c", p=128), MO, ["MO"], ["MOUT"])
                P.act(wts, aT[:, :, h], AF.Exp, ["aT", "Rb3"], ["wts"], bias=Rb[:, SEQ - 1:SEQ])
                P.tt("dve", VW, V1[:, :, 0:128], wts.unsqueeze(2).to_broadcast([128, 16, 128]), ALU.mult,
                     ["V1", "wts"], ["VW"])
                pb = P.bank()
                for st_ in range(16):
                    P.mm(pbank[pb][:, 0:128], VW[:, st_, :], KTOK[:, st_, :], st_ == 0, st_ == 15, ["VW", "KTOK"],
                         ["pb%d" % pb])
                P.cp("act", Csb, pbank[pb][:, 0:128], ["pb%d" % pb], ["Csb"])
                P.st(o_C_p[sq, h, :, :], Csb, ["Csb"])
                pb = P.bank()
                for st_ in range(16):
                    P.mm(pbank[pb][0:1, 0:128], wts[:, st_:st_ + 1], KTOK[:, st_, :], st_ == 0, st_ == 15,
                         ["wts", "KTOK"], ["pb%d" % pb])
                P.cp("act", nsb[0:1], pbank[pb][0:1, 0:128], ["pb%d" % pb], ["nsb"])
                P.st(o_n_p[sq:sq + 1, h, :], nsb[0:1], ["nsb"])


        P.barrier()
        apos[0] = persist_mark
        BIGM = 30000.0
        QT8 = alloc3(4, SEQ)
        KTz = [[alloc(SEQ) for _ in range(2)] for _ in range(2)]
        V1n = [alloc3(16, 65) for _ in range(2)]
        KCTz = alloc(2 * 2 * 128).rearrange("p (k f n) -> p k f n", k=2, f=2)
        KCV = alloc3(2, 64)
        TZs = alloc(8 * 2 * 128).rearrange("p (h k j) -> p h k j", h=8, k=2)
        WZ = alloc(128)
        SHM = alloc(128)
        EBIG = alloc(SEQ)
        SELM1 = [alloc(SEQ) for _ in range(2)]
        XTc = SELM1
        FVNC = alloc(16 * 2 * 32).rearrange("p (n k b) -> p n k b", n=16, k=2)
        ROWV = alloc(1)
        CBR = alloc(8)
        W2Z = alloc3(2, 128)
        W2N = alloc3(2, 64)
        PET = alloc(128)
        ONES = alloc(128)
        GS = alloc3(16, 24)
        NOUT = alloc3(4, 512)
        stg = [alloc3(16, 128) for _ in range(2)]
        CB = [alloc3(8, 128) for _ in range(2)]
        Sx = alloc3(8, 128)
        Pn = alloc3(8, 128)
        PnT = Sx
        sm8 = alloc(8)
        rs8 = alloc(8)
        PG = alloc3(2, 128)
        PSs = alloc3(2, 32)
        S012 = alloc3(2, 32)
        SC = alloc3(2, 32)
        M8 = alloc(8)
        SCW = alloc(32)
        SEL = alloc3(2, 32)
        PAB = alloc(128)
        XG = alloc(64)
        TG = alloc(64)
        HID = alloc(64)
        HT = alloc(128)
        epn = [dict(r=alloc(1)) for _ in range(4)]
        PTb = alloc3(16, 512)
        W1L = PTb[:, 0:8, :].rearrange("p a b -> p (a b)").rearrange("p (c r o) -> p c r o", c=2, r=16)
        W1N = PTb[:, 8:12, :].rearrange("p a b -> p (a b)").rearrange("p (c j o) -> p c j o", c=2, j=16)
        PEL = PTb[:, 12, 0:32].rearrange("p (c j) -> p c j", c=2)
        PTK = ["PT%d" % i for i in range(16)]

        P.ld(TZs, tz_d[:, :, :, :], ["TZ"])
        P.ld(WZ, wz_d[:, :], ["ncst"])
        P.ld(SHM, sh_d[:, :], ["ncst"])
        P.ld(EBIG[0:32], ebig_d[:, :], ["ncst"])
        P.ld(FVNC, fvnc_d[:, :, :, :], ["ncst"])
        P.ld(ROWV, rowv_d[:, :], ["ncst"])
        P.ld(CBR, cbr_d[:, :], ["CBR"])
        P.ld(W2Z[0:64], w2d_d[:, :, :], ["ncst"])
        P.ld(W2N[0:64], w2n_d[:, :, :], ["ncst"])
        P.memset("dve", ONES, 1.0, ["ONES"])
        for h in range(8):
            P.ts("dve", TZs[:, h, :, :], TZs[:, h, :, :], CBR[:, h:h + 1], ALU.subtract, ["TZ", "CBR"], ["TZ"])
        for br in range(2):
            P.memset("dve", V1n[br][:, :, 64:65], 1.0, ["V1n1"])
        cbcnt = 0
        encnt = 0
        K3 = int(os.environ.get("K3STOP", "9"))

        def gelu_from(xsb, tmp, out, rk, wk_):
            P.tt("dve", tmp, xsb, xsb, ALU.mult, rk, [wk_ + "t"])
            P.ts("dve", tmp, tmp, 0.044715, ALU.mult, [wk_ + "t"], [wk_ + "t"], s2=1.0, op1=ALU.add)
            P.tt("dve", tmp, tmp, xsb, ALU.mult, [wk_ + "t"] + rk, [wk_ + "t"])
            P.act(tmp, tmp, AF.Sigmoid, [wk_ + "t"], [wk_ + "t"], scale=1.5957691216057308)
            P.tt("dve", out, tmp, xsb, ALU.mult, [wk_ + "t"] + rk, [wk_])

        def tr_block(src3, dst, wkeys, scale=None, rmajor=False):
            for g in range(4):
                pb = P.bank()
                for j in range(4):
                    P.tr(pbank[pb][:, j * 128:(j + 1) * 128], src3[:, g * 4 + j, :], ident, ["stgX", "ident"],
                         ["pb%d" % pb])
                if rmajor:
                    P.cp("act", dst.rearrange("p (r n) -> p r n", r=16)[:, :, g * 32:(g + 1) * 32],
                         pbank[pb][:, :].rearrange("p (n r) -> p r n", r=16), ["pb%d" % pb], wkeys)
                elif scale is not None:
                    P.ts("dve", dst[:, g * 512:(g + 1) * 512], pbank[pb][:, :], scale, ALU.mult, ["pb%d" % pb], wkeys)
                else:
                    P.cp("act", dst[:, g * 512:(g + 1) * 512], pbank[pb][:, :], ["pb%d" % pb], wkeys)

        for sq in range(SPC if K3 >= 2 else 0):
            r0 = sq * SEQ
            zrows = Z[r0:r0 + SEQ, :]

            def zt(c0, w):
                return zrows[:, c0:c0 + w].rearrange("(n p) c -> p n c", p=128)

            for pr in range(4):
                P.ld(stg[0], zt(O_Q + pr * 128, 128), ["stgX"])
                tr_block(stg[0], QT8[:, pr, :], ["QT8"], scale=0.125)
            for c in range(2):
                P.ld(stg[0], zt(O_KC + c * 128, 128), ["stgX"])
                tr_block(stg[0], XTc[c], ["XTc", "SELM1_0", "SELM1_1"], rmajor=True)
            P.ld(GS, zt(O_GN, 24), ["GSraw"])
            P.act(GS, GS, AF.Sigmoid, ["GSraw"], ["GS", "GSraw"])
            P.ld(W1L, w1l_d[:, :, :, :], ["W1L"] + PTK)
            P.ld(W1N, w1n_d[:, :, :, :], ["W1N"] + PTK)
            P.ld(PEL, pel_d[:, :, :], ["PEL"] + PTK)
            for c in range(2):
                pb = P.bank()
                for j in range(16):
                    P.mm(pbank[pb][0:1, 0:64], PEL[:, c, j:j + 1], W1N[:, c, j, :], j == 0, j == 15, ["PEL", "W1N"],
                         ["pb%d" % pb])
                P.cp("act", PET[0:1, c * 64:(c + 1) * 64], pbank[pb][0:1, 0:64], ["pb%d" % pb], ["PET"])
            for c in range(2):
                for k in range(2):
                    rows = slice(k * 64, (k + 1) * 64)
                    xv = XTc[c][rows, :].rearrange("p (r n) -> p r n", r=16)
                    pb = P.bank()
                    for r in range(16):
                        P.mm(pbank[pb][:, 0:128], xv[:, r, :], W1L[rows, c, r, :], r == 0, r == 15, ["XTc", "W1L"],
                             ["pb%d" % pb])
                    P.cp("act", PAB, pbank[pb][:, 0:128], ["pb%d" % pb], ["PAB"])
                    pb = P.bank()
                    P.mm(pbank[pb][:, 0:64], ident, PAB[:, 0:64], True, False, ["ident", "PAB"], ["pb%d" % pb])
                    P.mm(pbank[pb][:, 0:64], SHM, PAB[:, 64:128], False, False, ["ncst", "PAB"], ["pb%d" % pb])
                    P.mm(pbank[pb][:, 0:64], ONES[0:1, :], PET[0:1, c * 64:(c + 1) * 64], False, True, ["ONES", "PET"],
                         ["pb%d" % pb])
                    P.cp("act", XG, pbank[pb][:, 0:64], ["pb%d" % pb], ["XG"])
                    gelu_from(XG, TG, HID, ["XG"], "HID")
                    pb = P.bank()
                    P.tr(pbank[pb][0:64, 0:128], HID, ident, ["HID", "ident"], ["pb%d" % pb])
                    P.cp("act", HT[0:64], pbank[pb][0:64, 0:128], ["pb%d" % pb], ["HT"])
                    if c == 0:
                        for f in range(2):
                            pb = P.bank()
                            P.mm(pbank[pb][:, 0:128], W2Z[0:64, f, :], HT[0:64], True, True, ["ncst", "HT"],
                                 ["pb%d" % pb])
                            P.cp("act", KCTz[:, k, f, :], pbank[pb][:, 0:128], ["pb%d" % pb], ["KCT"])
                    else:
                        pb = P.bank()
                        P.mm(pbank[pb][:, 0:64], HT[0:64], W2N[0:64, c, :], True, True, ["ncst", "HT"], ["pb%d" % pb])
                        P.cp("act", KCV[:, k, :], pbank[pb][:, 0:64], ["pb%d" % pb], ["KCV"])

            def attn_block(br, h, tb, st_list):
                nonlocal encnt
                pr, half, kvh = h // 2, h % 2, h // 4
                hl = h % 4
                active = {}
                for st_ in st_list:
                    subs = [sub for sub in range(4) if 0 <= (4 * tb + sub - st_) <= (4 if br == 1 else 10 ** 6)]
                    if not subs:
                        continue
                    lo, hi = subs[0], subs[-1] + 1
                    active[st_] = (lo, hi)
                    cs = slice(lo * 128, hi * 128)
                    gs = slice(tb * 512 + lo * 128, tb * 512 + hi * 128)
                    pb = P.bank()
                    extra = []
                    for sub in subs:
                        dd = 4 * tb + sub - st_
                        if dd == 0:
                            extra.append((sub, TZs[:, h, 0, :], "TZ"))
                        elif dd == 1:
                            extra.append((sub, TZs[:, h, 1, :], "TZ"))
                        elif dd == 4 and br == 1:
                            extra.append((sub, WZ, "ncst"))
                    nmm = 1 + (1 if br == 0 else 0) + len(extra)
                    i_mm = 0
                    P.mm(pbank[pb][:, cs], KTz[br][half][:, st_ * 128:(st_ + 1) * 128], QT8[:, pr, gs],
                         True, nmm == 1, ["KTz", "QT8"], ["pb%d" % pb])
                    i_mm += 1
                    if br == 0:
                        P.mm(pbank[pb][:, cs], EBIG[0:32, st_ * 128:(st_ + 1) * 128], SELM1[kvh][0:32, gs],
                             False, i_mm == nmm - 1, ["ncst", "SELM1_%d" % kvh], ["pb%d" % pb])
                        i_mm += 1
                    for (sub, tile_, key) in extra:
                        P.mm(pbank[pb][:, sub * 128:(sub + 1) * 128], ident, tile_, False, i_mm == nmm - 1,
                             ["ident", key], ["pb%d" % pb])
                        i_mm += 1
                    P.act(PTb[:, st_, cs], pbank[pb][:, cs], AF.Exp, ["pb%d" % pb, "CBR"], ["PT%d" % st_],
                          bias=CBR[:, h:h + 1])
                for sub in range(4):
                    tq = 4 * tb + sub
                    sts = [st_ for st_ in st_list if st_ in active and active[st_][0] <= sub < active[st_][1]]
                    pb = P.bank()
                    for i, st_ in enumerate(sts):
                        P.mm(pbank[pb][:, 0:65], PTb[:, st_, sub * 128:(sub + 1) * 128], V1n[br][:, st_, :],
                             i == 0, i == len(sts) - 1, ["PT%d" % st_, "V1n", "V1n1"], ["pb%d" % pb])
                    E = epn[encnt % 4]
                    ek = "epn%d" % (encnt % 4)
                    encnt += 1
                    P.ts("dve", E["r"], pbank[pb][:, 64:65], 1e-30, ALU.max, ["pb%d" % pb], [ek])
                    P.op("dve", lambda e, E=E: e.reciprocal(out=E["r"], in_=E["r"]), [ek], [ek])
                    gcol = (1 + br) * 8 + h
                    P.tt("dve", E["r"], E["r"], GS[:, tq, gcol:gcol + 1], ALU.mult, [ek, "GS"], [ek])
                    P.stt("dve", NOUT[:, sub, hl * 64:(hl + 1) * 64], pbank[pb][:, 0:64], E["r"],
                          NOUT[:, sub, hl * 64:(hl + 1) * 64], ALU.mult, ALU.add, ["pb%d" % pb, ek, "NOUT"], ["NOUT"])

            for tb in range(4 if K3 >= 4 else 0):
                for sub in range(4):
                    tq = 4 * tb + sub
                    cb_ = cbcnt % 2
                    cbcnt += 1
                    P.ld(CB[cb_], cbt_d[tq * 128:(tq + 1) * 128, :, :], ["CB%d" % cb_])
                    for hg in range(2):
                        pb = P.bank()
                        for hh in range(4):
                            h = hg * 4 + hh
                            pr, half, kvh = h // 2, h % 2, h // 4
                            P.mm(pbank[pb][:, hh * 128:(hh + 1) * 128], QT8[:, pr, tq * 128:(tq + 1) * 128],
                                 KCTz[:, kvh, half, :], True, True, ["QT8", "KCT"], ["pb%d" % pb])
                        P.tt("dve", Sx[:, hg * 4:(hg + 1) * 4, :],
                             pbank[pb][:, :].rearrange("p (h n) -> p h n", h=4), CB[cb_][:, hg * 4:(hg + 1) * 4, :],
                             ALU.add, ["pb%d" % pb, "CB%d" % cb_], ["Sx", "PnT"])
                    P.op("dve", lambda e: e.reduce_max(out=sm8, in_=Sx, axis=AX.X), ["Sx"], ["sm8"])
                    P.tt("dve", Sx, Sx, sm8.unsqueeze(2).to_broadcast([128, 8, 128]), ALU.subtract, ["Sx", "sm8"], ["Sx"])
                    P.act(Pn, Sx, AF.Exp, ["Sx"], ["Pn"])
                    P.op("dve", lambda e: e.reduce_sum(out=sm8, in_=Pn, axis=AX.X), ["Pn"], ["sm8"])
                    P.op("dve", lambda e: e.reciprocal(out=rs8, in_=sm8), ["sm8"], ["rs8"])
                    if tq == 0:
                        P.ts("dve", rs8, rs8, ROWV[:, 0:1], ALU.mult, ["rs8", "ncst"], ["rs8"])
                    P.tt("dve", Pn, Pn, rs8.unsqueeze(2).to_broadcast([128, 8, 128]), ALU.mult, ["Pn", "rs8"], ["Pn"])
                    for hg in range(2):
                        pb = P.bank()
                        for hh in range(4):
                            h = hg * 4 + hh
                            P.tr(pbank[pb][:, hh * 128:(hh + 1) * 128], Pn[:, h, :], ident, ["Pn", "ident"],
                                 ["pb%d" % pb])
                        P.cp("act", PnT[:, hg * 4:(hg + 1) * 4, :], pbank[pb][:, :].rearrange("p (h n) -> p h n", h=4),
                             ["pb%d" % pb], ["PnT", "Sx"])
                    pb = P.bank()
                    for h in range(8):
                        P.mm(pbank[pb][:, h * 64:(h + 1) * 64], PnT[:, h, :], KCV[:, h // 4, :], True, True,
                             ["PnT", "KCV"], ["pb%d" % pb])
                    P.tt("dve", NOUT[:, sub, :].rearrange("p (h d) -> p h d", h=8),
                         pbank[pb][:, :].rearrange("p (h d) -> p h d", h=8),
                         GS[:, tq, 0:8].unsqueeze(2).to_broadcast([128, 8, 64]), ALU.mult, ["pb%d" % pb, "GS"], ["NOUT"])
                    P.op("dve", lambda e: e.tensor_reduce(out=PG, in_=Pn.rearrange("p (k g) n -> p k n g", k=2),
                                                         axis=AX.X, op=ALU.add), ["Pn"], ["PG"])
                    pg4 = PG.rearrange("p k (b r) -> p k b r", r=4)
                    P.op("dve", lambda e, pg4=pg4: e.tensor_reduce(out=S012, in_=pg4[:, :, :, 0:3], axis=AX.X, op=ALU.add),
                         ["PG"], ["S012"])
                    P.stt("dve", PSs, S012, 2.0, pg4[:, :, :, 3], ALU.mult, ALU.add, ["S012", "PG"], ["PSs"])
                    P.tt("dve", PSs[:, :, 1:32], PSs[:, :, 1:32], pg4[:, :, 0:31, 3], ALU.add, ["PSs", "PG"], ["PSs"])
                    fv = FVNC[:, tq, 0, :]
                    ncm = FVNC[:, tq, 1, :]
                    for k in range(2):
                        P.tt("dve", SC[:, k, :], PSs[:, k, :], fv, ALU.max, ["PSs", "ncst"], ["SC"])
                        P.tt("dve", SC[:, k, :], SC[:, k, :], ncm, ALU.add, ["SC", "ncst"], ["SC"])
                        P.op("dve", lambda e, k=k: e.max(out=M8, in_=SC[:, k, :]), ["SC"], ["M8"])
                        P.op("dve", lambda e, k=k: e.match_replace(out=SCW, in_to_replace=M8, in_values=SC[:, k, :],
                                                                  imm_value=-1e9), ["SC", "M8"], ["SCW"])
                        P.op("dve", lambda e: e.max(out=M8, in_=SCW), ["SCW"], ["M8"])
                        P.ts("dve", SEL[:, k, :], SC[:, k, :], M8[:, 7:8], ALU.is_ge, ["SC", "M8"], ["SEL"], s2=-1.0,
                             op1=ALU.add)
                        pb = P.bank()
                        P.tr(pbank[pb][0:32, 0:128], SEL[:, k, :], ident, ["SEL", "ident"], ["pb%d" % pb])
                        P.cp("act", SELM1[k][0:32, tq * 128:(tq + 1) * 128], pbank[pb][0:32, 0:128], ["pb%d" % pb],
                             ["SELM1_%d" % k, "XTc"])
                P.st(NOUTD[r0 + tb * 512:r0 + (tb + 1) * 512, :].rearrange("(n p) c -> p n c", p=128), NOUT,
                     ["NOUT"], ["NOUTD"])
            for kvh in range(2 if K3 >= 5 else 0):
                for br, obase in ((0, O_KS), (1, O_KW)):
                    for half in range(2):
                        P.memset("dve", stg[half], 0.0, ["stgX"])
                        P.ld(stg[half][:, :, half * 64:(half + 1) * 64], zt(obase + kvh * 64, 64), ["stgX"])
                        tr_block(stg[half], KTz[br][half], ["KTz"])
                    P.ld(V1n[br][:, :, 0:64], zt(obase + 128 + kvh * 64, 64), ["V1n"])
                for tb in range(4):
                    nsl = NOUT[:, :, 0:256]
                    dsl = NOUTD[r0 + tb * 512:r0 + (tb + 1) * 512, kvh * 256:(kvh + 1) * 256].rearrange(
                        "(n p) c -> p n c", p=128)
                    P.ld(nsl, dsl, ["NOUT"], r=["NOUTD"])
                    for g in range(4):
                        h = kvh * 4 + g
                        if K3 != 7:
                            attn_block(0, h, tb, list(range(0, 4 * tb + 4)))
                        if K3 != 6:
                            attn_block(1, h, tb, list(range(max(0, 4 * tb - 4), 4 * tb + 4)))
                    P.st(dsl, nsl, ["NOUT"], ["NOUTD"])

        def top16(src, scr, vals, idx_u, nelem, key):
            P.op("dve", lambda e: e.max(out=vals[:, 0:8], in_=src), [key], [key + "v"])
            P.op("dve", lambda e: e.max_index(out=idx_u[:, 0:8], in_max=vals[:, 0:8], in_values=src),
                 [key, key + "v"], [key + "i"])
            P.op("dve", lambda e: e.match_replace(out=scr, in_to_replace=vals[:, 0:8], in_values=src,
                                                  imm_value=-1e30), [key, key + "v"], [key + "s"])
            P.op("dve", lambda e: e.max(out=vals[:, 8:16], in_=scr), [key + "s"], [key + "v"])
            P.op("dve", lambda e: e.max_index(out=idx_u[:, 8:16], in_max=vals[:, 8:16], in_values=scr),
                 [key + "s", key + "v"], [key + "i"])

        P.barrier()
        apos[0] = persist_mark
        zt0 = alloc(512)
        P.memset("dve", zt0, 0.0, ["zt0"])
        P.st(MOUT[TP:TP + 128, :], zt0, ["zt0"], ["MOUT_s"])
        P.st(NOUTD[TP:TP + 128, :], zt0, ["zt0"], ["NOUTD_s"])
        wq5 = alloc3(4, 128)
        wk5 = alloc3(4, 128)
        P.ld(wq5, wq_d[:, :, :], ["w5s"])
        P.ld(wk5, wk_d[:, :, :], ["w5s"])
        ZS = alloc(1544)
        CB3 = alloc3(3, 512)
        N0 = alloc(512)
        M0 = alloc(4)
        CW4 = alloc3(4, 512)
        CB4 = alloc(512)
        GB4 = alloc(8)
        NG16 = alloc(128)
        OPS = alloc(128)
        C4 = alloc(512)
        T4 = alloc(512)
        CTs = alloc3(4, 4)
        Q4 = alloc(512)
        K4 = alloc(512)
        S4 = {n: alloc(4) for n in ("IG", "XF", "LF", "MI", "MT", "SWS", "SCI", "EMT", "QK", "NQ", "SW", "DEN", "RD", "T")}
        PACK = alloc3(4, 264)
        C0s = alloc3(16, 128)
        BCT = alloc3(16, 264)
        VTs = alloc3(4, 4)
        T1 = alloc3(16, 128)
        CQ = alloc(16)
        NUM = alloc(16)
        WV = alloc(16)
        HTs = alloc(16)
        HB = alloc(128)
        NN = alloc(512)
        zs4 = Z[TP:TP + SSC, :]
        P.ld(ZS[0:4], zs4[:, 0:1544], ["ZS"])
        P.ld(CB3[0:4], st_conv[:, :, :], ["s5in"])
        P.ld(N0[0:4], stn_d[:, :], ["s5in"])
        P.ld(M0[0:4], stm_d[:, :], ["s5in"])
        P.ld(CW4[0:4], cw4_d[:, :, :], ["s5in"])
        P.ld(CB4[0:4], cb4_d[:, :], ["s5in"])
        P.ld(GB4[0:4], gb4_d[:, :], ["s5in"])
        P.ld(NG16[0:16], ng16_d[:, :], ["s5in"])
        for b in range(SSC):
            P.ld(OPS[4 * b:4 * b + 4], Z[TP + b, O_O:O_O + 512].rearrange("(h v) -> h v", h=4), ["OPS"])
        P.ld(C0s, stC_d.rearrange("a v k -> v a k"), ["C0s"])
        P.tt("dve", C4[0:4], CW4[0:4, 3, :], ZS[0:4, O_U:O_U + 512], ALU.mult, ["s5in", "ZS"], ["C4"])
        for j in range(3):
            P.tt("dve", T4[0:4], CW4[0:4, j, :], CB3[0:4, j, :], ALU.mult, ["s5in"], ["T4"])
            P.tt("dve", C4[0:4], C4[0:4], T4[0:4], ALU.add, ["C4", "T4"], ["C4"])
        P.tt("dve", C4[0:4], C4[0:4], CB4[0:4], ALU.add, ["C4", "s5in"], ["C4"])
        P.act(C4[0:4], C4[0:4], AF.Silu, ["C4"], ["C4"])
        pb = P.bank()
        for h in range(4):
            P.tr(pbank[pb][:, h * 4:(h + 1) * 4], C4[0:4, h * 128:(h + 1) * 128], ident[0:4, 0:4], ["C4", "ident"],
                 ["pb%d" % pb])
        P.cp("act", CTs, pbank[pb][:, 0:16].rearrange("p (h b) -> p h b", h=4), ["pb%d" % pb], ["CTs"])
        pq = P.bank()
        for h in range(4):
            P.mm(pbank[pq][0:4, h * 128:(h + 1) * 128], CTs[:, h, :], wq5[:, h, :], True, True, ["CTs", "w5s"],
                 ["pb%d" % pq])
        P.cp("act", Q4[0:4], pbank[pq][0:4, :], ["pb%d" % pq], ["Q4"])
        pk = P.bank()
        for h in range(4):
            P.mm(pbank[pk][0:4, h * 128:(h + 1) * 128], CTs[:, h, :], wk5[:, h, :], True, True, ["CTs", "w5s"],
                 ["pb%d" % pk])
        P.ts("dve", K4[0:4], pbank[pk][0:4, :], 128.0 ** -0.5, ALU.mult, ["pb%d" % pk], ["K4"])
        A_ = {n: v[0:4] for n, v in S4.items()}
        P.tt("dve", A_["IG"], ZS[0:4, O_I:O_I + 4], GB4[0:4, 0:4], ALU.add, ["ZS", "s5in"], ["IG"])
        P.tt("dve", A_["XF"], ZS[0:4, O_F:O_F + 4], GB4[0:4, 4:8], ALU.add, ["ZS", "s5in"], ["XF"])
        P.act(A_["LF"], A_["XF"], AF.Exp, ["XF"], ["LF"], scale=-1.0)
        P.act(A_["LF"], A_["LF"], AF.Ln, ["LF"], ["LF"], bias=1.0, scale=1.0)
        P.stt("dve", A_["MI"], A_["LF"], -1.0, M0[0:4], ALU.mult, ALU.add, ["LF", "s5in"], ["MI"])
        P.tt("dve", A_["MT"], A_["MI"], A_["IG"], ALU.max, ["MI", "IG"], ["MT"])
        P.tt("dve", A_["T"], A_["IG"], A_["MT"], ALU.subtract, ["IG", "MT"], ["T"])
        P.act(A_["SWS"], A_["T"], AF.Exp, ["T"], ["SWS"])
        P.tt("dve", A_["T"], A_["MI"], A_["MT"], ALU.subtract, ["MI", "MT", "SWS"], ["T"])
        P.act(A_["SCI"], A_["T"], AF.Exp, ["T"], ["SCI"])
        P.act(A_["EMT"], A_["MT"], AF.Exp, ["MT"], ["EMT"], scale=-1.0)
        P.tt("dve", T4[0:4], Q4[0:4], K4[0:4], ALU.mult, ["Q4", "K4"], ["T4"])
        P.op("dve", lambda e: e.reduce_sum(out=A_["QK"], in_=T4[0:4].rearrange("p (h e) -> p h e", h=4), axis=AX.X),
             ["T4"], ["QK"])
        P.tt("dve", T4[0:4], Q4[0:4], N0[0:4], ALU.mult, ["Q4", "s5in", "QK"], ["T4"])
        P.op("dve", lambda e: e.reduce_sum(out=A_["NQ"], in_=T4[0:4].rearrange("p (h e) -> p h e", h=4), axis=AX.X),
             ["T4"], ["NQ"])
        P.tt("dve", A_["SW"], A_["QK"], A_["SWS"], ALU.mult, ["QK", "SWS"], ["SW"])
        P.tt("dve", A_["DEN"], A_["SCI"], A_["NQ"], ALU.mult, ["SCI", "NQ"], ["DEN"])
        P.tt("dve", A_["DEN"], A_["DEN"], A_["SW"], ALU.add, ["DEN", "SW"], ["DEN"])
        P.act(A_["DEN"], A_["DEN"], AF.Abs, ["DEN"], ["DEN"])
        P.tt("dve", A_["DEN"], A_["DEN"], A_["EMT"], ALU.max, ["DEN", "EMT"], ["DEN"])
        P.op("dve", lambda e: e.reciprocal(out=A_["RD"], in_=A_["DEN"]), ["DEN"], ["RD"])
        P.cp("dve", PACK[0:4, :, 0:128], Q4[0:4].rearrange("p (h e) -> p h e", h=4), ["Q4"], ["PACK"])
        P.cp("dve", PACK[0:4, :, 128:256], K4[0:4].rearrange("p (h e) -> p h e", h=4), ["K4"], ["PACK"])
        for i, n in enumerate(("SCI", "SWS", "SW", "RD")):
            P.cp("dve", PACK[0:4, :, 256 + i:257 + i], A_[n].unsqueeze(2), [n], ["PACK"])
        P.st(SCRB.rearrange("(b h) x -> b h x", h=4), PACK[0:4], ["PACK"], ["SCRB"], eng="sp")
        P.ld(BCT, bass.AP(SCRB.tensor, 0, [[0, 128], [264, 16], [1, 264]]), ["BCT"], r=["SCRB"])
        pb = P.bank()
        for h in range(4):
            P.tr(pbank[pb][:, h * 4:(h + 1) * 4], ZS[0:4, O_V + h * 128:O_V + (h + 1) * 128], ident[0:4, 0:4],
                 ["ZS", "ident"], ["pb%d" % pb])
        P.cp("act", VTs, pbank[pb][:, 0:16].rearrange("p (h b) -> p h b", h=4), ["pb%d" % pb], ["VTs"])
        VTbh = VTs.rearrange("p h b -> p b h")
        sc = lambda i: BCT[:, :, 256 + i].rearrange("p (b h) -> p b h", h=4)
        P.tt("dve", T1, C0s, BCT[:, :, 0:128], ALU.mult, ["C0s", "BCT"], ["T1"])
        P.op("dve", lambda e: e.reduce_sum(out=CQ, in_=T1, axis=AX.X), ["T1"], ["CQ"])
        CQ3 = CQ.rearrange("p (b h) -> p b h", h=4)
        NUM3 = NUM.rearrange("p (b h) -> p b h", h=4)
        WV3 = WV.rearrange("p (b h) -> p b h", h=4)
        HT3 = HTs.rearrange("p (b h) -> p b h", h=4)
        P.tt("dve", NUM3, CQ3, sc(0), ALU.mult, ["CQ", "BCT"], ["NUM"])
        P.tt("dve", WV3, VTbh, sc(2), ALU.mult, ["VTs", "BCT"], ["WV"])
        P.tt("dve", NUM3, NUM3, WV3, ALU.add, ["NUM", "WV"], ["NUM"])
        P.tt("dve", HT3, NUM3, sc(3), ALU.mult, ["NUM", "BCT"], ["HTs"])
        pb = P.bank()
        P.tr(pbank[pb][0:16, 0:128], HTs, ident, ["HTs", "ident"], ["pb%d" % pb])
        P.cp("act", HB[0:16], pbank[pb][0:16, 0:128], ["pb%d" % pb], ["HB"])
        st5 = alloc(8)
        ag5 = alloc(4)
        rs5 = alloc(1)
        P.op("dve", lambda e: e.bn_stats(out=st5[0:16, 0:6], in_=HB[0:16]), ["HB"], ["st5"])
        P.op("dve", lambda e: e.bn_aggr(out=ag5[0:16, 0:2], in_=st5[0:16, 0:6]), ["st5"], ["ag5"])
        P.act(rs5[0:16], ag5[0:16, 1:2], AF.Sqrt, ["ag5"], ["rs5"], bias=1e-5, scale=1.0)
        P.op("dve", lambda e: e.reciprocal(out=rs5[0:16], in_=rs5[0:16]), ["rs5"], ["rs5"])
        P.ts("dve", HB[0:16], HB[0:16], ag5[0:16, 0:1], ALU.subtract, ["HB", "ag5", "rs5"], ["HB"], s2=rs5[0:16],
             op1=ALU.mult)
        P.tt("dve", HB[0:16], HB[0:16], NG16[0:16], ALU.mult, ["HB", "s5in"], ["HB"])
        P.act(OPS[0:16], OPS[0:16], AF.Sigmoid, ["OPS"], ["OPS"])
        P.tt("dve", HB[0:16], HB[0:16], OPS[0:16], ALU.mult, ["HB", "OPS"], ["HB"])
        for b in range(SSC):
            P.st(MOUT[TP + b, :].rearrange("(h v) -> h v", h=4), HB[4 * b:4 * b + 4], ["HB"], ["MOUT_s"], eng="sp")
        P.tt("dve", WV3, VTbh, sc(1), ALU.mult, ["VTs", "BCT", "NUM"], ["WV"])
        P.tt("dve", T1, BCT[:, :, 128:256], WV.unsqueeze(2).to_broadcast([128, 16, 128]), ALU.mult, ["BCT", "WV", "CQ"],
             ["T1"])
        P.tt("dve", C0s, C0s, BCT[:, :, 256:257].to_broadcast([128, 16, 128]), ALU.mult, ["C0s", "BCT"], ["C0s"])
        P.tt("dve", C0s, C0s, T1, ALU.add, ["C0s", "T1"], ["C0s"])
        P.st(o_C_s.rearrange("a v k -> v a k"), C0s, ["C0s"], eng="sp")
        N03 = N0[0:4].rearrange("p (h e) -> p h e", h=4)
        NN3 = NN[0:4].rearrange("p (h e) -> p h e", h=4)
        K43 = K4[0:4].rearrange("p (h e) -> p h e", h=4)
        P.tt("dve", NN3, N03, A_["SCI"].unsqueeze(2).to_broadcast([4, 4, 128]), ALU.mult, ["s5in", "SCI"], ["NN"])
        P.tt("dve", T4[0:4].rearrange("p (h e) -> p h e", h=4), K43, A_["SWS"].unsqueeze(2).to_broadcast([4, 4, 128]),
             ALU.mult, ["K4", "SWS", "NQ"], ["T4"])
        P.tt("dve", NN[0:4], NN[0:4], T4[0:4], ALU.add, ["NN", "T4"], ["NN"])
        P.st(o_n_s[:, :], NN[0:4], ["NN"], eng="sp")
        P.st(o_m_s[:, :], A_["MT"], ["MT"], eng="sp")

        P.barrier()
        apos[0] = persist_mark
        K6 = int(os.environ.get("K6STOP", "9"))
        W1Ls = alloc(2 * 16 * 128).rearrange("p (c r o) -> p c r o", c=2, r=16)
        W1Ns = alloc(2 * 16 * 64).rearrange("p (c j o) -> p c j o", c=2, j=16)
        PELs = alloc3(2, 16)
        W2Ns = alloc3(2, 64)
        W2KT = alloc(64)
        SHMs = alloc(128)
        SHD = alloc(128)
        ONESM = alloc(128)
        CBS = alloc3(8, 8)
        BWt = alloc3(4, 8)
        T25 = alloc(2 * 8 * 64).rearrange("p (a h t) -> p a h t", a=2, h=8)
        CBR6 = alloc(8)
        FL255 = alloc(1)
        IND = alloc(4)
        RB0 = alloc(8)
        IOTA8 = alloc(8)
        IOT128 = alloc3(2, 128)
        PETs = alloc(128)
        PETB = alloc3(2, 64)
        PT4i = alloc(4).bitcast(I32)
        PT8i = alloc(128).bitcast(I32)
        PTF = alloc(4)
        PTROW = alloc(128)
        IDXf = alloc3(4, 8)
        P.ld(W1Ls, w1l_d[:, :, :, :], ["c6"])
        P.ld(W1Ns, w1n_d[:, :, :, :], ["c6"])
        P.ld(PELs, pel_d[:, :, :], ["c6"])
        P.ld(W2Ns[0:64], w2n_d[:, :, :], ["c6"])
        P.ld(W2KT[0:64], w2kt_d[:, :], ["c6"])
        P.ld(SHMs, sh_d[:, :], ["c6"])
        P.ld(SHD, shd_d[:, :], ["c6"])
        P.ld(CBS, cbs_d[:, :, :], ["c6"])
        P.ld(BWt, bw_d[:, :, :], ["c6"])
        P.ld(T25[0:60], t25_d[:, :, :, :], ["c6"])
        P.ld(CBR6, cbr_d[:, :], ["c6"])
        P.ld(FL255[0:60], fl255_d[:, :], ["c6"])
        P.ld(IND[0:60], ind_d[:, :], ["c6"])
        P.ld(RB0[0:4], rb0_d[:, :], ["c6"])
        P.ld(IOTA8, iota8_d[:, :], ["c6"])
        P.ld(IOT128[0:8], iot128_d[:, :, :], ["c6"])
        P.ld(PT4i, ptl_d[:, :], ["c6"])
        P.ld(PT8i[0:8], pt8_d[:, :], ["c6"])
        P.memset("dve", ONESM, 1.0, ["ONESM"])
        P.cp("dve", PTF, PT4i, ["c6"], ["PTF"])
        P.cp("dve", PTROW[0:8], PT8i[0:8], ["c6"], ["PTROW"])
        P.stt("dve", IDXf, PTF.unsqueeze(2).to_broadcast([128, 4, 8]), 8.0,
              IOTA8.unsqueeze(1).to_broadcast([128, 4, 8]), ALU.mult, ALU.add, ["PTF", "c6"], ["IDXf"])
        P.cp("dve", IDXC.rearrange("p (b n) -> p b n", b=4), IDXf, ["IDXf"], ["IDXC"])
        for c in range(2):
            pb = P.bank()
            for j in range(16):
                P.mm(pbank[pb][0:1, 0:64], PELs[:, c, j:j + 1], W1Ns[:, c, j, :], j == 0, j == 15, ["c6"], ["pb%d" % pb])
            P.cp("act", PETs[0:1, c * 64:(c + 1) * 64], pbank[pb][0:1, 0:64], ["pb%d" % pb], ["PETs"])
        pb = P.bank()
        P.mm(pbank[pb][:, 0:128], ONESM[0:1, :], PETs[0:1, :], True, True, ["ONESM", "PETs"], ["pb%d" % pb])
        P.cp("act", PETB, pbank[pb][:, 0:128].rearrange("p (c o) -> p c o", c=2), ["pb%d" % pb], ["PETB"])
        ZN = alloc(1304)
        P.ld(ZN[0:4], Z[TP:TP + SSC, O_Q:O_Q + 1304], ["ZN"])
        QS = alloc(512)
        P.ts("dve", QS[0:4], ZN[0:4, 0:512], 0.125, ALU.mult, ["ZN"], ["QS"])
        QTS = alloc3(4, 8)
        pb = P.bank()
        for h in range(8):
            P.tr(pbank[pb][0:64, h * 4:(h + 1) * 4], QS[0:4, h * 64:(h + 1) * 64], ident[0:4, 0:4], ["QS", "ident"],
                 ["pb%d" % pb])
        P.cp("act", QTS[0:64].rearrange("p b h -> p h b"), pbank[pb][0:64, 0:32].rearrange("p (h b) -> p h b", h=8),
             ["pb%d" % pb], ["QTS"])
        QW8 = alloc(64)
        for b in range(SSC):
            pb = P.bank()
            P.mm(pbank[pb][0:8, 0:64], QTS[0:64, b, :], W2KT[0:64, :], True, True, ["QTS", "c6"], ["pb%d" % pb])
            P.cp("act", QW8[0:8], pbank[pb][0:8, 0:64], ["pb%d" % pb], ["QW8"])
            P.st(SCRQ[b, :].rearrange("(h i) -> h i", h=8), QW8[0:8], ["QW8"], ["SCRQ"], eng="sp")
        QWB = alloc3(4, 512)
        P.ld(QWB, bass.AP(SCRQ.tensor, 0, [[0, 128], [512, 4], [1, 512]]), ["QWB"], r=["SCRQ"])
        G = [alloc(4096) for _ in range(2)]
        XTs = alloc(32 * 128).rearrange("p (q n) -> p q n", q=32)
        PABs = alloc(8 * 4 * 128).rearrange("p (n k o) -> p n k o", n=8, k=4)
        HPRE = alloc(8 * 4 * 64).rearrange("p (n k o) -> p n k o", n=8, k=4)
        HTMP = alloc(8 * 4 * 64).rearrange("p (n k o) -> p n k o", n=8, k=4)
        HIDs = alloc(8 * 4 * 64).rearrange("p (n k o) -> p n k o", n=8, k=4)
        TMPs = alloc(1024)
        SCs = alloc3(8, 8)
        RS = alloc(8)
        RT = alloc(8)
        PGs = alloc3(8, 2)
        S3 = alloc(2)
        PSB = alloc3(2, 2)
        U4 = alloc(64)
        UT = alloc(4)
        OC = alloc(65)
        WK = alloc3(4, 256)
        QB128 = alloc(512)
        SWs = alloc3(4, 8)
        gcnt6 = 0
        for b in range(SSC if K6 >= 1 else 0):
            for n_ in range(8):
                gb = gcnt6 % 2
                gcnt6 += 1
                P.dma("pool", lambda e, gb=gb, col=b * 8 + n_: e.indirect_dma_start(
                    out=G[gb], out_offset=None, in_=pool_cmp_d[:, :],
                    in_offset=bass.IndirectOffsetOnAxis(ap=IDXC[:, col:col + 1], axis=0)), ["IDXC"], ["G%d" % gb])
                G3 = G[gb].rearrange("p (r x) -> p r x", r=16)
                for g8 in range(8):
                    pb = P.bank()
                    for j in range(4):
                        q_ = g8 * 4 + j
                        r, c = q_ // 2, q_ % 2
                        P.tr(pbank[pb][:, j * 128:(j + 1) * 128], G3[:, r, c * 128:(c + 1) * 128], ident,
                             ["G%d" % gb, "ident"], ["pb%d" % pb])
                    P.cp("act" if g8 % 2 else "dve", XTs[:, g8 * 4:(g8 + 1) * 4, :],
                         pbank[pb][:, :].rearrange("p (q n) -> p q n", q=4), ["pb%d" % pb], ["XTs"])
                for c in range(2):
                    for k in range(2):
                        rows = slice(k * 64, (k + 1) * 64)
                        pb = P.bank()
                        for r in range(16):
                            P.mm(pbank[pb][:, 0:128], XTs[rows, r * 2 + c, :], W1Ls[rows, c, r, :], r == 0, r == 15,
                                 ["XTs", "c6"], ["pb%d" % pb])
                        P.cp("act", PABs[:, n_, c * 2 + k, :], pbank[pb][:, 0:128], ["pb%d" % pb], ["PABs"])
            P.tt("dve", HPRE[:, 0:7], PABs[:, 0:7, :, 0:64], PABs[:, 1:8, :, 64:128], ALU.add, ["PABs"], ["HPRE"])
            pb = P.bank()
            P.mm(pbank[pb][:, 0:256], SHMs, PABs[:, 0, :, 64:128], True, True, ["c6", "PABs"], ["pb%d" % pb])
            P.tt("dve", HPRE[:, 7], PABs[:, 7, :, 0:64], pbank[pb][:, 0:256].rearrange("p (k o) -> p k o", k=4), ALU.add,
                 ["PABs", "pb%d" % pb], ["HPRE"])
            for c in range(2):
                P.tt("dve", HPRE[:, :, 2 * c:2 * c + 2, :], HPRE[:, :, 2 * c:2 * c + 2, :],
                     PETB[:, c, :].unsqueeze(1).unsqueeze(1).to_broadcast([128, 8, 2, 64]), ALU.add, ["HPRE", "PETB"],
                     ["HPRE"])
            gelu_from(HPRE, HTMP, HIDs, ["HPRE"], "HIDs")
            for h in range(8):
                kvh = h // 4
                P.tt("dve", TMPs[:, 0:512].rearrange("p (n i) -> p n i", n=8), HIDs[:, :, kvh, :],
                     QWB[:, b, h * 64:(h + 1) * 64].unsqueeze(1).to_broadcast([128, 8, 64]), ALU.mult,
                     ["HIDs", "QWB"], ["TMPs"])
                P.op("dve", lambda e, h=h: e.reduce_sum(out=SCs[:, :, h], in_=TMPs[:, 0:512].rearrange(
                    "p (n i) -> p n i", n=8), axis=AX.X), ["TMPs"], ["SCs"])
            P.tt("dve", SCs, SCs, CBS, ALU.add, ["SCs", "c6"], ["SCs"])
            P.act(SCs, SCs, AF.Exp, ["SCs"], ["SCs"])
            P.op("dve", lambda e: e.reduce_sum(out=RS, in_=SCs.rearrange("p n h -> p h n"), axis=AX.X), ["SCs"], ["RS"])
            pb = P.bank()
            P.mm(pbank[pb][:, 0:8], ONESM, RS, True, True, ["ONESM", "RS"], ["pb%d" % pb])
            P.op("dve", lambda e, pb=pb: e.reciprocal(out=RT, in_=pbank[pb][:, 0:8]), ["pb%d" % pb], ["RT"])
            P.tt("dve", SCs, SCs, RT.unsqueeze(1).to_broadcast([128, 8, 8]), ALU.mult, ["SCs", "RT"], ["SCs"])
            for kvh in range(2):
                pb = P.bank()
                for n_ in range(8):
                    P.mm(pbank[pb][0:4, 0:64], SCs[:, n_, kvh * 4:(kvh + 1) * 4], HIDs[:, n_, 2 + kvh, :], n_ == 0,
                         n_ == 7, ["SCs", "HIDs"], ["pb%d" % pb])
                P.cp("act", U4[0:4], pbank[pb][0:4, 0:64], ["pb%d" % pb], ["U4"])
                pb = P.bank()
                P.tr(pbank[pb][0:64, 0:4], U4[0:4, 0:64], ident[0:4, 0:4], ["U4", "ident"], ["pb%d" % pb])
                P.cp("act", UT[0:64], pbank[pb][0:64, 0:4], ["pb%d" % pb], ["UT"])
                pb = P.bank()
                P.mm(pbank[pb][0:4, 0:64], UT[0:64, 0:4], W2Ns[0:64, 1, :], True, True, ["UT", "c6"], ["pb%d" % pb])
                P.cp("act", OC[0:4, 0:64], pbank[pb][0:4, 0:64], ["pb%d" % pb], ["OC"])
                P.st(OSD[0, b, kvh * 4:(kvh + 1) * 4, 0:64], OC[0:4, 0:64], ["OC"], ["OSD"], eng="sp")
            P.op("dve", lambda e: e.tensor_reduce(out=PGs, in_=SCs.rearrange("p n (k g) -> p n k g", k=2), axis=AX.X,
                                                 op=ALU.add), ["SCs"], ["PGs"])
            pb = P.bank()
            P.mm(pbank[pb][:, 0:2], SHD, PGs[:, 7, :], True, True, ["c6", "PGs"], ["pb%d" % pb])
            for eo in range(2):
                o_ = eo * 4
                P.tt("dve", S3, PGs[:, o_ + 0, :], PGs[:, o_ + 1, :], ALU.add, ["PGs"], ["S3"])
                P.tt("dve", S3, S3, PGs[:, o_ + 2, :], ALU.add, ["S3", "PGs"], ["S3"])
                P.stt("dve", PSB[:, :, eo], S3, 2.0, PGs[:, 3, :], ALU.mult, ALU.add, ["S3", "PGs"], ["PSB"])
                if eo == 0:
                    P.tt("dve", PSB[:, :, 0], PSB[:, :, 0], pbank[pb][:, 0:2], ALU.add, ["PSB", "pb%d" % pb], ["PSB"])
                else:
                    P.tt("dve", PSB[:, :, 1], PSB[:, :, 1], PGs[:, 7, :], ALU.add, ["PSB", "PGs"], ["PSB"])
            P.st(SELD[b * 2:b * 2 + 2, :].rearrange("k (p e) -> p k e", e=2), PSB, ["PSB"], ["SELD"], eng="sp")
            P.ld(WK, st_win[b, :, :].rearrange("(c p) x -> p c x", p=128), ["WK"])
            P.ld(QB128, bass.AP(Z.tensor, (TP + b) * IN_DIM + O_Q, [[0, 128], [1, 512]]), ["QB128"])
            P.ts("dve", QB128, QB128, 0.125, ALU.mult, ["QB128"], ["QB128"])
            for h in range(8):
                kvh = h // 4
                P.tt("dve", TMPs[:, 0:256].rearrange("p (c i) -> p c i", c=4), WK[:, :, kvh * 64:(kvh + 1) * 64],
                     QB128[:, h * 64:(h + 1) * 64].unsqueeze(1).to_broadcast([128, 4, 64]), ALU.mult, ["WK", "QB128"],
                     ["TMPs"])
                P.op("dve", lambda e, h=h: e.reduce_sum(out=SWs[:, :, h], in_=TMPs[:, 0:256].rearrange(
                    "p (c i) -> p c i", c=4), axis=AX.X), ["TMPs"], ["SWs"])
            P.tt("dve", SWs, SWs, BWt, ALU.add, ["SWs", "c6"], ["SWs"])
            P.act(SWs, SWs, AF.Exp, ["SWs"], ["SWs"])
            for kvh in range(2):
                pn_ = P.bank()
                for c in range(4):
                    P.mm(pbank[pn_][0:4, 0:64], SWs[:, c, kvh * 4:(kvh + 1) * 4],
                         WK[:, c, 128 + kvh * 64:128 + (kvh + 1) * 64], c == 0, c == 3, ["SWs", "WK"], ["pb%d" % pn_])
                pd_ = P.bank()
                for c in range(4):
                    P.mm(pbank[pd_][0:4, 0:1], SWs[:, c, kvh * 4:(kvh + 1) * 4], ONESM[:, 0:1], c == 0, c == 3,
                         ["SWs", "ONESM"], ["pb%d" % pd_])
                P.cp("act", OC[0:4, 0:64], pbank[pn_][0:4, 0:64], ["pb%d" % pn_], ["OC"])
                P.cp("act", OC[0:4, 64:65], pbank[pd_][0:4, 0:1], ["pb%d" % pd_], ["OC"])
                P.st(OSD[1, b, kvh * 4:(kvh + 1) * 4, :], OC[0:4, :], ["OC"], ["OSD"], eng="sp")

        if K6 >= 2:
            SELIN = alloc(256)
            SELW = alloc(256)
            V16s = alloc(16)
            I16s = alloc(16).bitcast(U32)
            BLK = alloc(16)
            HALF = alloc(16)
            PAR = alloc(16)
            PGID = alloc(16)
            PHYS = alloc(16)
            OH6 = alloc3(15, 128)
            PQ5 = alloc3(15, 5)
            P.ld(SELIN[0:8], SELD[:, :], ["SELIN"], r=["SELD"])
            P.memset("dve", SELIN[0:8, 0:1], -1e9, ["SELIN"])
            P.memset("dve", SELIN[0:8, 255:256], -1e9, ["SELIN"])
            top16(SELIN[0:8], SELW[0:8], V16s[0:8], I16s[0:8], 256, "SELIN")
            P.cp("dve", BLK[0:8], I16s[0:8], ["SELINi"], ["BLK"])
            P.memset("dve", BLK[0:8, 13:14], 0.0, ["BLK"])
            P.memset("dve", BLK[0:8, 14:15], 255.0, ["BLK"])
            B15 = BLK[0:8, 0:15]
            P.tt("dve", OH6[0:8], B15.unsqueeze(2).to_broadcast([8, 15, 128]),
                 IOT128[0:8, 1, :].unsqueeze(1).to_broadcast([8, 15, 128]), ALU.is_ge, ["BLK", "c6"], ["OH6"])
            P.op("dve", lambda e: e.tensor_reduce(out=HALF[0:8, 0:15], in_=OH6[0:8], axis=AX.X, op=ALU.add), ["OH6"],
                 ["HALF"])
            P.stt("dve", PAR[0:8, 0:15], HALF[0:8, 0:15], -2.0, B15, ALU.mult, ALU.add, ["HALF", "BLK"], ["PAR"])
            P.tt("dve", OH6[0:8], HALF[0:8, 0:15].unsqueeze(2).to_broadcast([8, 15, 128]),
                 IOT128[0:8, 0, :].unsqueeze(1).to_broadcast([8, 15, 128]), ALU.is_equal, ["HALF", "c6"], ["OH6"])
            P.tt("dve", OH6[0:8], OH6[0:8], PTROW[0:8].unsqueeze(1).to_broadcast([8, 15, 128]), ALU.mult,
                 ["OH6", "PTROW"], ["OH6"])
            P.op("dve", lambda e: e.tensor_reduce(out=PGID[0:8, 0:15], in_=OH6[0:8], axis=AX.X, op=ALU.add), ["OH6"],
                 ["PGID"])
            P.stt("dve", PHYS[0:8, 0:15], PGID[0:8, 0:15], 2.0, PAR[0:8, 0:15], ALU.mult, ALU.add, ["PGID", "PAR"],
                  ["PHYS"])
            for qd in range(4):
                P.ts("dve", PQ5[0:8, :, qd], PHYS[0:8, 0:15], 4.0, ALU.mult, ["PHYS"], ["PQ5"], s2=float(qd),
                     op1=ALU.add)
            P.cp("dve", PQ5[0:8, :, 4], B15, ["BLK"], ["PQ5"])
            P.st(SCRP[:, :, :], PQ5[0:8], ["PQ5"], ["SCRP"], eng="sp")
            IQf = [alloc(5) for _ in range(2)]
            P.memset("dve", IDXS, 0, ["IDXS"])
            for kvh in range(2):
                for b in range(SSC):
                    P.ld(IQf[kvh][15 * b:15 * b + 15], SCRP[b * 2 + kvh, :, :], ["IQf%d" % kvh], r=["SCRP"])
                P.cp("dve", IDXS[0:60, kvh * 4:(kvh + 1) * 4], IQf[kvh][0:60, 0:4], ["IQf%d" % kvh], ["IDXS"])
            QB60 = alloc(512)
            for b in range(SSC):
                P.ld(QB60[15 * b:15 * b + 15], bass.AP(Z.tensor, (TP + b) * IN_DIM + O_Q, [[0, 15], [1, 512]]), ["QB60"])
            P.ts("dve", QB60[0:60], QB60[0:60], 0.125, ALU.mult, ["QB60"], ["QB60"])
            FL254 = alloc(1)
            W0 = alloc(1)
            BIAS = alloc3(4, 64)
            SS6 = alloc3(4, 64)
            PVD = alloc(260)
            PVt = alloc(64)
            DNt = alloc(4)
            SLCR = [alloc(260) for _ in range(2)]
            for kvh in range(2):
                hs = slice(kvh * 4, (kvh + 1) * 4)
                P.ts("dve", FL254[0:60], IQf[kvh][0:60, 4:5], 254.0, ALU.is_equal, ["IQf%d" % kvh], ["FL254"])
                P.tt("dve", W0[0:60], FL254[0:60], FL255[0:60], ALU.add, ["FL254", "c6"], ["W0"])
                P.ts("dve", W0[0:60], W0[0:60], -1.0, ALU.mult, ["W0"], ["W0"], s2=1.0, op1=ALU.add)
                P.ts("dve", BIAS[0:60], T25[0:60, 0, hs, :], FL254[0:60], ALU.mult, ["c6", "FL254"], ["BIAS"])
                P.stt("dve", BIAS[0:60], T25[0:60, 1, hs, :], FL255[0:60], BIAS[0:60], ALU.mult, ALU.add,
                      ["c6", "BIAS"], ["BIAS"])
                P.stt("dve", BIAS[0:60], CBR6[0:60, hs].unsqueeze(2).to_broadcast([60, 4, 64]), W0[0:60], BIAS[0:60],
                      ALU.mult, ALU.add, ["c6", "W0", "BIAS"], ["BIAS"])
                P.memset("dve", PVD[0:60], 0.0, ["PVD"])
                for qd in range(4):
                    gb = gcnt6 % 2
                    gcnt6 += 1
                    P.dma("pool", lambda e, gb=gb, col=kvh * 4 + qd: e.indirect_dma_start(
                        out=G[gb], out_offset=None, in_=pool_slc_d[:, :],
                        in_offset=bass.IndirectOffsetOnAxis(ap=IDXS[:, col:col + 1], axis=0)), ["IDXS"],
                        ["G%d" % gb])
                    GQ3 = G[gb][0:60].rearrange("p (t x) -> p t x", t=16)
                    ts_ = slice(qd * 16, (qd + 1) * 16)
                    for g in range(4):
                        h = kvh * 4 + g
                        P.tt("dve", TMPs[0:60].rearrange("p (t d) -> p t d", t=16), GQ3[:, :, kvh * 64:(kvh + 1) * 64],
                             QB60[0:60, h * 64:(h + 1) * 64].unsqueeze(1).to_broadcast([60, 16, 64]), ALU.mult,
                             ["G%d" % gb, "QB60"], ["TMPs"])
                        P.op("dve", lambda e, g=g, ts_=ts_: e.reduce_sum(out=SS6[0:60, g, ts_], in_=TMPs[0:60].rearrange(
                            "p (t d) -> p t d", t=16), axis=AX.X), ["TMPs"], ["SS6"])
                    P.tt("dve", SS6[0:60, :, ts_], SS6[0:60, :, ts_], BIAS[0:60, :, ts_], ALU.add, ["SS6", "BIAS"], ["SS6"])
                    P.act(SS6[0:60, :, ts_], SS6[0:60, :, ts_], AF.Exp, ["SS6"], ["SS6"])
                    P.op("dve", lambda e, ts_=ts_: e.reduce_sum(out=DNt[0:60], in_=SS6[0:60, :, ts_], axis=AX.X), ["SS6"],
                         ["DNt"])
                    P.tt("dve", PVD[0:60, 256:260], PVD[0:60, 256:260], DNt[0:60], ALU.add, ["PVD", "DNt"], ["PVD"])
                    for g in range(4):
                        P.tt("dve", TMPs[0:60].rearrange("p (d t) -> p d t", d=64),
                             GQ3[:, :, 128 + kvh * 64:128 + (kvh + 1) * 64].rearrange("p t d -> p d t"),
                             SS6[0:60, g, ts_].unsqueeze(1).to_broadcast([60, 64, 16]), ALU.mult, ["G%d" % gb, "SS6"],
                             ["TMPs"])
                        P.op("dve", lambda e: e.reduce_sum(out=PVt[0:60], in_=TMPs[0:60].rearrange(
                            "p (d t) -> p d t", d=64), axis=AX.X), ["TMPs"], ["PVt"])
                        P.tt("dve", PVD[0:60, g * 64:(g + 1) * 64], PVD[0:60, g * 64:(g + 1) * 64], PVt[0:60], ALU.add,
                             ["PVD", "PVt"], ["PVD"])
                pb = P.bank()
                P.mm(pbank[pb][0:4, 0:260], IND[0:60, :], PVD[0:60, :], True, True, ["c6", "PVD"], ["pb%d" % pb])
                P.cp("act", SLCR[kvh][0:4], pbank[pb][0:4, 0:260], ["pb%d" % pb], ["SLCR%d" % kvh])
            OSB = alloc(2 * 8 * 65).rearrange("p (a h x) -> p a h x", a=2, h=8)
            for a in range(2):
                P.ld(OSB[0:4, a], OSD[a, :, :, :], ["OSB"], r=["OSD"])
            GS4 = alloc(24)
            P.act(GS4[0:4], ZN[0:4, 1280:1304], AF.Sigmoid, ["ZN"], ["GS4"])
            QS3 = QS[0:4].rearrange("p (h d) -> p h d", h=8)
            NOS = alloc(512)
            NOS3 = NOS[0:4].rearrange("p (h d) -> p h d", h=8)
            TL = alloc(512)
            TL3 = TL[0:4].rearrange("p (h d) -> p h d", h=8)
            PTL = alloc(8)
            DEN8 = alloc(8)
            P.tt("dve", NOS3, OSB[0:4, 0, :, 0:64], GS4[0:4, 0:8].unsqueeze(2).to_broadcast([4, 8, 64]), ALU.mult,
                 ["OSB", "GS4"], ["NOS"])
            for br, kbase in ((1, 768), (2, 1024)):
                for kvh in range(2):
                    P.tt("dve", TL3[:, kvh * 4:(kvh + 1) * 4, :], QS3[:, kvh * 4:(kvh + 1) * 4, :],
                         ZN[0:4, kbase + kvh * 64:kbase + (kvh + 1) * 64].unsqueeze(1).to_broadcast([4, 4, 64]),
                         ALU.mult, ["QS", "ZN", "TLr"], ["TL"])
                P.op("dve", lambda e: e.reduce_sum(out=PTL[0:4], in_=TL3, axis=AX.X), ["TL"], ["PTL"])
                P.tt("dve", PTL[0:4], PTL[0:4], RB0[0:4], ALU.add, ["PTL", "c6"], ["PTL"])
                P.act(PTL[0:4], PTL[0:4], AF.Exp, ["PTL"], ["PTL"])
                for kvh in range(2):
                    hs = slice(kvh * 4, (kvh + 1) * 4)
                    if br == 1:
                        num_src = SLCR[kvh][0:4, 0:256].rearrange("p (g d) -> p g d", g=4)
                        den_src = SLCR[kvh][0:4, 256:260]
                        rk = ["SLCR%d" % kvh]
                    else:
                        num_src = OSB[0:4, 1, hs, 0:64]
                        den_src = OSB[0:4, 1, hs, 64]
                        rk = ["OSB"]
                    P.tt("dve", DEN8[0:4, hs], den_src, PTL[0:4, hs], ALU.add, rk + ["PTL"], ["DEN8"])
                    P.tt("dve", TL3[:, hs, :], PTL[0:4, hs].unsqueeze(2).to_broadcast([4, 4, 64]),
                         ZN[0:4, kbase + 128 + kvh * 64:kbase + 128 + (kvh + 1) * 64].unsqueeze(1).to_broadcast([4, 4, 64]),
                         ALU.mult, ["PTL", "ZN", "PTL"], ["TL"])
                    P.tt("dve", TL3[:, hs, :], TL3[:, hs, :], num_src, ALU.add, ["TL"] + rk, ["TL"])
                P.ts("dve", DEN8[0:4], DEN8[0:4], 1e-30, ALU.max, ["DEN8"], ["DEN8"])
                P.op("dve", lambda e: e.reciprocal(out=DEN8[0:4], in_=DEN8[0:4]), ["DEN8"], ["DEN8"])
                P.tt("dve", DEN8[0:4], DEN8[0:4], GS4[0:4, br * 8:(br + 1) * 8], ALU.mult, ["DEN8", "GS4"], ["DEN8"])
                P.tt("dve", TL3, TL3, DEN8[0:4].unsqueeze(2).to_broadcast([4, 8, 64]), ALU.mult, ["TL", "DEN8"], ["TL"])
                P.tt("dve", NOS[0:4], NOS[0:4], TL[0:4], ALU.add, ["NOS", "TL"], ["NOS", "TLr"])
            P.st(NOUTD[TP:TP + SSC, :], NOS[0:4], ["NOS"], ["NOUTD_s"], eng="sp")

        P.barrier()
        apos[0] = persist_mark
        K5 = int(os.environ.get("K5STOP", "9"))
        ALPHA = 2.0 ** 0.25
        WUM = alloc3(4, D)
        WUN = alloc3(4, D)
        WO = alloc3(8, D)
        LNP = alloc3(4, D)
        P.ld(WUM, wupm_d.rearrange("(c p) n -> p c n", p=128), ["w4"])
        P.ld(WUN, wupn_d.rearrange("(c p) n -> p c n", p=128), ["w4"])
        P.ld(WO, wout_d.rearrange("(c p) n -> p c n", p=128), ["w4"])
        P.ld(LNP, lnp_d[:, :, :], ["w4"])
        NB = 2
        XB = [alloc(D) for _ in range(NB)]
        MOB = [alloc(512) for _ in range(NB)]
        NOB = [alloc(512) for _ in range(NB)]
        GMB = [alloc(D) for _ in range(NB)]
        GNB = [alloc(D) for _ in range(NB)]
        MOT = alloc3(4, 128)
        NOTt = alloc3(4, 128)
        MIX = alloc(D)
        MIXT = alloc3(8, 128)
        RB = alloc(D)
        X1B = [alloc(D) for _ in range(NB)]
        STT = alloc(16)
        AGG = alloc(4)
        RSTD = alloc(1)

        def layer_norm(src, dst, gi, rk, wk_):
            for i in range(2):
                P.op("dve", lambda e, i=i: e.bn_stats(out=STT[:, i * 6:(i + 1) * 6], in_=src[:, i * 512:(i + 1) * 512]),
                     rk, ["STT"])
            P.op("dve", lambda e: e.bn_aggr(out=AGG[:, 0:2], in_=STT[:, 0:12].rearrange("p (a b) -> p a b", a=2)),
                 ["STT"], ["AGG"])
            P.act(RSTD, AGG[:, 1:2], AF.Sqrt, ["AGG"], ["RSTD"], bias=1e-5, scale=1.0)
            P.op("dve", lambda e: e.reciprocal(out=RSTD, in_=RSTD), ["RSTD"], ["RSTD"])
            P.ts("dve", dst, src, AGG[:, 0:1], ALU.subtract, rk + ["AGG", "RSTD"], wk_, s2=RSTD, op1=ALU.mult)
            P.tt("pool", dst, dst, LNP[:, gi, :], ALU.mult, wk_ + ["w4"], wk_)
            P.tt("pool", dst, dst, LNP[:, gi + 1, :], ALU.add, wk_ + ["w4"], wk_)

        for ti in range(NT + 1 if K5 >= 1 else 0):
            b_ = ti % NB
            rows = slice(ti * 128, (ti + 1) * 128)
            xsrc = x_p[rows, :] if ti < NT else x_s[:, :]
            P.ld(XB[b_], xsrc, ["XB%d" % b_])
            P.ld(MOB[b_], MOUT[rows, :], ["MOB%d" % b_], r=["MOUT", "MOUT_s"])
            P.ld(NOB[b_], NOUTD[rows, :], ["NOB%d" % b_], r=["NOUTD", "NOUTD_s"])
            P.ld(GMB[b_], Z[rows, O_GM:O_GM + D], ["GMB%d" % b_])
            P.ld(GNB[b_], Z[rows, O_GNN:O_GNN + D], ["GNB%d" % b_])
            for src, skey, dst, dkey in ((MOB[b_], "MOB%d" % b_, MOT, "MOT"), (NOB[b_], "NOB%d" % b_, NOTt, "NOT")):
                pb = P.bank()
                for c in range(4):
                    P.tr(pbank[pb][:, c * 128:(c + 1) * 128], src[:, c * 128:(c + 1) * 128], ident, [skey, "ident"],
                         ["pb%d" % pb])
                P.cp("act", dst, pbank[pb][:, :].rearrange("p (c t) -> p c t", c=4), ["pb%d" % pb], [dkey])
            P.act(GMB[b_], GMB[b_], AF.Sigmoid, ["GMB%d" % b_], ["GMB%d" % b_])
            P.act(GNB[b_], GNB[b_], AF.Sigmoid, ["GNB%d" % b_], ["GNB%d" % b_])
            for nb in range(2):
                ns = slice(nb * 512, (nb + 1) * 512)
                pa = P.bank()
                for c in range(4):
                    P.mm(pbank[pa][:, :], MOT[:, c, :], WUM[:, c, ns], c == 0, c == 3, ["MOT", "w4"], ["pb%d" % pa])
                pn = P.bank()
                for c in range(4):
                    P.mm(pbank[pn][:, :], NOTt[:, c, :], WUN[:, c, ns], c == 0, c == 3, ["NOT", "w4"], ["pb%d" % pn])
                P.tt("dve", MIX[:, ns], pbank[pa][:, :], GMB[b_][:, ns], ALU.mult, ["pb%d" % pa, "GMB%d" % b_], ["MIX"])
                P.tt("dve", GNB[b_][:, ns], pbank[pn][:, :], GNB[b_][:, ns], ALU.mult, ["pb%d" % pn, "GNB%d" % b_],
                     ["GNB%d" % b_])
                P.tt("pool", MIX[:, ns], MIX[:, ns], GNB[b_][:, ns], ALU.add, ["MIX", "GNB%d" % b_], ["MIX"])
            for g in range(2):
                pb = P.bank()
                for j in range(4):
                    c = g * 4 + j
                    P.tr(pbank[pb][:, j * 128:(j + 1) * 128], MIX[:, c * 128:(c + 1) * 128], ident, ["MIX", "ident"],
                         ["pb%d" % pb])
                P.cp("act", MIXT[:, g * 4:(g + 1) * 4, :], pbank[pb][:, :].rearrange("p (c t) -> p c t", c=4),
                     ["pb%d" % pb], ["MIXT"])
            for nb in range(2):
                ns = slice(nb * 512, (nb + 1) * 512)
                pb = P.bank()
                for c in range(8):
                    P.mm(pbank[pb][:, :], MIXT[:, c, :], WO[:, c, ns], c == 0, c == 7, ["MIXT", "w4"], ["pb%d" % pb])
                P.stt("dve", RB[:, ns], XB[b_][:, ns], ALPHA, pbank[pb][:, :], ALU.mult, ALU.add,
                      ["XB%d" % b_, "pb%d" % pb], ["RB"])
            layer_norm(RB, X1B[b_], 0, ["RB"], ["X1B%d" % b_])
            P.st(X1D[rows, :], X1B[b_], ["X1B%d" % b_], ["X1D_%d" % ti])

        P.barrier()
        apos[0] = persist_mark
        LN2 = alloc3(2, D)
        WPR = alloc3(8, 2048)
        IOTA = alloc(16)
        P.ld(LN2, lnp_d[:, 2:4, :], ["w5"])
        IOTA2 = alloc(16)
        P.ld(IOTA, iota_d[:, :], ["w5"])
        P.ts("dve", IOTA2, IOTA, 16.0, ALU.mult, ["w5"], ["w5"], s2=16.0, op1=ALU.add)
        cmark = apos[0]
        CV32 = [alloc(4096) for _ in range(3)]
        CV16 = [alloc(2048).bitcast(BF16) for _ in range(3)]
        ccnt = 0
        for src_d, dst_d in ((peer_u_d, U16D), (peer_v_d, V16D)):
            for ch in range(32):
                cb_ = ccnt % 3
                ccnt += 1
                rs_ = slice(ch * 512, (ch + 1) * 512)
                P.ld(CV32[cb_], src_d[rs_, :].rearrange("(p r) n -> p (r n)", p=128), ["CV32_%d" % cb_])
                eng = ("dve", "act", "pool")[cb_]
                P.cp(eng, CV16[cb_], CV32[cb_], ["CV32_%d" % cb_], ["CV16_%d" % cb_])
                P.st(dst_d[rs_, :].rearrange("(p r) n -> p (r n)", p=128), CV16[cb_], ["CV16_%d" % cb_], ["T16"], eng="sp")
        P.barrier()
        apos[0] = cmark
        PWT = alloc(16 * 8 * 128).rearrange("p (a c m) -> p a c m", a=16, c=8)
        SKT = alloc3(2, 128)
        P.ld(PWT, pwqt_d[:, :, :, :], ["PWT"])
        P.ld(SKT, skt_d[:, :, :], ["SKT"])
        for c in range(8):
            for g in range(4):
                pb = P.bank()
                for j in range(4):
                    hp = g * 4 + j
                    P.mm(pbank[pb][:, j * 128:(j + 1) * 128], PWT[:, hp, c, :], SKT[:, hp % 2, :], True, True,
                         ["PWT", "SKT"], ["pb%d" % pb])
                P.cp("act" if g % 2 else "dve", WPR[:, c, g * 512:(g + 1) * 512], pbank[pb][:, :], ["pb%d" % pb], ["w5"])
        P.barrier()
        apos[0] -= 16 * 8 * 128 + 256
        X1 = [alloc(D) for _ in range(2)]
        X1T = alloc3(8, 128)
        SS = alloc3(16, 128)
        SS2 = alloc3(16, 128)
        V16 = alloc3(16, 16)
        I16 = alloc3(16, 16)
        I16u = I16.bitcast(U32)
        I16f = alloc3(16, 16)
        CAND = alloc3(8, 256)
        CAND2 = alloc3(8, 256)
        VC = alloc3(8, 16)
        ICu = alloc3(8, 16).bitcast(U32)
        ICf = alloc3(8, 16)
        AIX = alloc3(8, 16)
        BIX = alloc3(8, 16)
        OH = alloc(8 * 16 * 16).rearrange("p (h k a) -> p h k a", h=8, k=16)
        I1S = alloc3(8, 16)
        I2S = alloc3(8, 16)
        EF = alloc(128)
        GWs = [alloc3(8, 16) for _ in range(2)]
        sm8b = alloc(8)
        APREs = [alloc(128) for _ in range(2)]
        GT = alloc(128)
        WCOLs = [alloc(128) for _ in range(2)]
        NGU = 8
        NGV = 8
        UBu = [alloc(D // 2).bitcast(BF16) for _ in range(NGU)]
        UBv = [alloc(D // 2).bitcast(BF16) for _ in range(NGV)]
        JUNK = alloc(D)
        RB2 = alloc(D)
        EIs = [EI, EI2]
        DG = [alloc(64).bitcast(BF16) for _ in range(4)]
        YB = alloc(D)
        gcnt = 0
        dcnt = 0

        def stage_a(ti):
            b_ = ti % 2
            rows = slice(ti * 128, (ti + 1) * 128)
            P.ld(X1[b_], X1D[rows, :], ["X1_%d" % b_], r=["X1D_%d" % ti])
            for g in range(2):
                pb = P.bank()
                for j in range(4):
                    c = g * 4 + j
                    P.tr(pbank[pb][:, j * 128:(j + 1) * 128], X1[b_][:, c * 128:(c + 1) * 128], ident,
                         ["X1_%d" % b_, "ident"], ["pb%d" % pb])
                P.cp("act", X1T[:, g * 4:(g + 1) * 4, :], pbank[pb][:, :].rearrange("p (c t) -> p c t", c=4),
                     ["pb%d" % pb], ["X1T"])
            for g in range(4):
                pb = P.bank()
                for c in range(8):
                    P.mm(pbank[pb][:, :], X1T[:, c, :], WPR[:, c, g * 512:(g + 1) * 512], c == 0, c == 7,
                         ["X1T", "w5"], ["pb%d" % pb])
                P.cp("act", SS[:, g * 4:(g + 1) * 4, :], pbank[pb][:, :].rearrange("p (a k) -> p a k", a=4),
                     ["pb%d" % pb], ["SS%d" % g])
            for hp in range(16):
                top16(SS[:, hp, :], SS2[:, hp, :], V16[:, hp, :], I16u[:, hp, :], 128, "SS%d" % (hp // 4))
            VK = ["SS%dv" % g for g in range(4)]
            IK = ["SS%di" % g for g in range(4)]
            V4 = V16.rearrange("p (h q) k -> p h q k", q=2)
            P.tt("dve", CAND.rearrange("p h (a b) -> p h a b", a=16),
                 V4[:, :, 0, :].unsqueeze(3).to_broadcast([128, 8, 16, 16]),
                 V4[:, :, 1, :].unsqueeze(2).to_broadcast([128, 8, 16, 16]), ALU.add, VK, ["CAND"])
            for h in range(8):
                top16(CAND[:, h, :], CAND2[:, h, :], VC[:, h, :], ICu[:, h, :], 256, "CAND")
            P.cp("dve", ICf, ICu, ["CANDi"], ["ICf"])
            P.cp("dve", I16f, I16u, IK, ["I16f"])
            P.tt("dve", OH, ICf.unsqueeze(3).to_broadcast([128, 8, 16, 16]),
                 IOTA2.unsqueeze(1).unsqueeze(1).to_broadcast([128, 8, 16, 16]), ALU.is_ge, ["ICf", "w5"], ["OH"])
            P.op("dve", lambda e: e.tensor_reduce(out=AIX, in_=OH, axis=AX.X, op=ALU.add), ["OH"], ["AIX"])
            P.stt("dve", BIX, AIX, -16.0, ICf, ALU.mult, ALU.add, ["AIX", "ICf"], ["BIX"])
            I4 = I16f.rearrange("p (h q) k -> p h q k", q=2)
            iota_b = IOTA.unsqueeze(1).unsqueeze(1).to_broadcast([128, 8, 16, 16])
            for sel_ix, src_i, dst in ((AIX, I4[:, :, 0, :], I1S), (BIX, I4[:, :, 1, :], I2S)):
                P.tt("dve", OH, sel_ix.unsqueeze(3).to_broadcast([128, 8, 16, 16]), iota_b, ALU.is_equal,
                     ["AIX", "BIX", "w5"], ["OH"])
                P.tt("dve", OH, OH, src_i.unsqueeze(2).to_broadcast([128, 8, 16, 16]), ALU.mult, ["OH", "I16f"], ["OH"])
                P.op("dve", lambda e, dst=dst: e.tensor_reduce(out=dst, in_=OH, axis=AX.X, op=ALU.add), ["OH"],
                     ["I12S"])
            P.stt("dve", EF.rearrange("p (h k) -> p h k", h=8), I1S, 128.0, I2S, ALU.mult, ALU.add, ["I12S"], ["EF"])
            P.cp("dve", EIs[b_], EF, ["EF"], ["EI%d" % b_])
            gw = GWs[b_]
            P.tt("dve", gw, VC, VC[:, :, 0:1].to_broadcast([128, 8, 16]), ALU.subtract, ["CANDv"], ["GW%d" % b_])

        ucnt = [0]
        vcnt = [0]
        dgc = [0]

        def col_u(ti, j):
            b_ = ti % 2
            ub = ucnt[0] % NGU
            ucnt[0] += 1
            P.dma("pool", lambda e: e.indirect_dma_start(
                out=UBu[ub], out_offset=None, in_=U16D[:, :],
                in_offset=bass.IndirectOffsetOnAxis(ap=EIs[b_][:, j:j + 1], axis=0)), ["EI%d" % b_], ["UBu%d" % ub])
            P.op("dve", lambda e: e.scalar_tensor_tensor(
                out=JUNK, in0=UBu[ub], scalar=1.0, in1=X1[b_], op0=ALU.mult, op1=ALU.mult,
                accum_out=APREs[b_][:, j:j + 1]), ["UBu%d" % ub, "X1_%d" % b_], ["JUNK", "APRE%d" % b_])

        def finish_u(ti):
            b_ = ti % 2
            gw = GWs[b_]
            P.act(gw, gw, AF.Exp, ["GW%d" % b_], ["GW%d" % b_])
            P.op("dve", lambda e: e.reduce_sum(out=sm8b, in_=gw, axis=AX.X), ["GW%d" % b_], ["sm8b"])
            P.op("dve", lambda e: e.reciprocal(out=sm8b, in_=sm8b), ["sm8b"], ["sm8b"])
            P.tt("dve", gw, gw, sm8b.unsqueeze(2).to_broadcast([128, 8, 16]), ALU.mult, ["GW%d" % b_, "sm8b"],
                 ["GW%d" % b_])
            gelu_from(APREs[b_], GT, WCOLs[b_], ["APRE%d" % b_], "WCOL%d" % b_)
            P.tt("dve", WCOLs[b_], WCOLs[b_], GWs[b_].rearrange("p h k -> p (h k)"), ALU.mult,
                 ["WCOL%d" % b_, "GW%d" % b_], ["WCOL%d" % b_])

        def col_v(ti, j, p0, p1):
            b_ = ti % 2
            vb = vcnt[0] % NGV
            vcnt[0] += 1
            P.dma("pool", lambda e: e.indirect_dma_start(
                out=UBv[vb], out_offset=None, in_=V16D[:, :],
                in_offset=bass.IndirectOffsetOnAxis(ap=EIs[b_][:, j:j + 1], axis=0)), ["EI%d" % b_], ["UBv%d" % vb])
            dg = dgc[0] % 4
            dgc[0] += 1
            P.act(DG[dg], ident, AF.Copy, ["ident", "WCOL%d" % b_], ["DG%d" % dg], scale=WCOLs[b_][:, j:j + 1])
            P.mm(pbank[p0][:, :], DG[dg], UBv[vb][:, 0:512], j == 0, j == 127, ["DG%d" % dg, "UBv%d" % vb],
                 ["pb%d" % p0])
            P.mm(pbank[p1][:, :], DG[dg], UBv[vb][:, 512:1024], j == 0, j == 127, ["DG%d" % dg, "UBv%d" % vb],
                 ["pb%d" % p1])

        def finish_v(ti, p0, p1):
            b_ = ti % 2
            rows = slice(ti * 128, (ti + 1) * 128)
            P.stt("dve", RB2[:, 0:512], X1[b_][:, 0:512], ALPHA, pbank[p0][:, :], ALU.mult, ALU.add,
                  ["X1_%d" % b_, "pb%d" % p0], ["RB2"])
            P.stt("dve", RB2[:, 512:1024], X1[b_][:, 512:1024], ALPHA, pbank[p1][:, :], ALU.mult, ALU.add,
                  ["X1_%d" % b_, "pb%d" % p1], ["RB2"])
            layer_norm(RB2, YB, 0, ["RB2"], ["YB"])
            P.st(y_out[rows, :], YB, ["YB"], eng="sp")

        LNP = LN2
        ntile = NT + 1 if K5 >= 2 else 0
        if ntile:
            stage_a(0)
            for j in range(128):
                col_u(0, j)
            finish_u(0)
        for ti in range(ntile):
            nxt = ti + 1 < ntile
            if nxt:
                stage_a(ti + 1)
            p0 = P.bank()
            p1 = P.bank()
            LEAD = 112
            for k in range(128 + LEAD):
                if k < 128:
                    col_v(ti, k, p0, p1)
                if nxt and k >= LEAD:
                    col_u(ti + 1, k - LEAD)
            finish_v(ti, p0, p1)
            if nxt:
                finish_u(ti + 1)

        with nc.Block() as block:
            @block.tensor
            def _(e):
                P.emit("pe", e)

            @block.scalar
            def _(e):
                P.emit("act", e)

            @block.vector
            def _(e):
                P.emit("dve", e)

            @block.gpsimd
            def _(e):
                P.emit("pool", e)

            @block.sync
            def _(e):
                P.emit("sp", e)
    return nc


_NC_CACHE = {}


def kernel(x_prompt, x_sample, cache_cmp_kv, cache_slc_kv, page_table, state_win_kv, state_mlstm_C,
           state_mlstm_n, state_mlstm_m, state_mlstm_conv, w_in, m_conv_w, m_conv_b, m_wq, m_wk,
           m_gate_bias, m_norm_g, cmp_pe, cmp_w1, cmp_w2, rel_bias, w_up_m, w_up_n, w_out, ln1_g, ln1_b,
           ln2_g, ln2_b, peer_wq, peer_subkeys, peer_u, peer_v):
    f32 = np.float32
    if "nc" not in _NC_CACHE:
        _NC_CACHE["nc"] = build_program()
    nc = _NC_CACHE["nc"]
    xp = np.ascontiguousarray(np.asarray(x_prompt, f32)).reshape(BP * SEQ, D)
    xs = np.asarray(x_sample, f32).reshape(BS, D)
    ident = np.eye(128, dtype=f32)
    w_in0 = np.ascontiguousarray(np.asarray(w_in, f32)[0])
    stw = np.asarray(state_win_kv, f32)[0].reshape(BS, 512, 256)
    stc = np.asarray(state_mlstm_conv, f32)[0]
    cw = np.asarray(m_conv_w, f32)[0]
    convw_l = np.ascontiguousarray(np.transpose(cw.reshape(4, 4, 128), (2, 1, 0)))
    convb_l = np.ascontiguousarray(np.asarray(m_conv_b, f32)[0].reshape(4, 128).T)
    wq_l = np.ascontiguousarray(np.transpose(np.asarray(m_wq, f32)[0], (1, 0, 2)))
    wk_l = np.ascontiguousarray(np.transpose(np.asarray(m_wk, f32)[0], (1, 0, 2)))
    gb_l = np.ascontiguousarray(np.asarray(m_gate_bias, f32)[0].T)
    normg_rep = np.ascontiguousarray(np.broadcast_to(np.asarray(m_norm_g, f32)[0][None, :], (128, 512)))
    sel_c = np.zeros((4, 4, 128), f32)
    for h in range(4):
        sel_c[h, h, :] = 1.0
    tri_c = np.triu(np.ones((128, 128), f32))
    relb = np.asarray(rel_bias, f32)
    dist = np.arange(0, 4096)
    nf = np.maximum(dist, 16).astype(f32)
    large = 16 + (np.log(nf / f32(16)) / f32(np.log(8.0)) * f32(16)).astype(np.int32)
    bucket = np.where(dist < 16, dist, np.minimum(large, 31)).astype(np.int64)
    NEGM = f32(-30000.0)
    tt_ = np.arange(SEQ)[:, None]
    nn_ = np.arange(128)[None, :]
    dcm = tt_ - 16 * nn_ - 31
    vcm = (dcm >= 0) & (nn_ <= 126)
    cbt = np.where(vcm[:, None, :], relb[bucket[np.maximum(dcm, 0)]].transpose(0, 2, 1), NEGM).astype(f32)
    ii = np.arange(128)[:, None]
    jj = np.arange(128)[None, :]
    tz = np.empty((128, 8, 2, 128), f32)
    d0 = jj - ii
    tz[:, :, 0, :] = np.where((d0 >= 0)[:, None, :], relb[bucket[np.maximum(d0, 0)]].transpose(0, 2, 1), NEGM)
    tz[:, :, 1, :] = relb[bucket[128 + d0]].transpose(0, 2, 1)
    wz = np.where(jj < ii, f32(0), NEGM).astype(f32)
    ebig = (np.arange(SEQ)[None, :] // 64 == np.arange(32)[:, None]).astype(f32) * f32(30000.0)
    tq_ = (np.arange(16)[None, :, None] * 128 + np.arange(128)[:, None, None])
    bb_ = np.arange(32)[None, None, :]
    cur = tq_ // 64
    fvnc = np.empty((128, 16, 2, 32), f32)
    fvnc[:, :, 0, :] = np.where((bb_ == 0) | (bb_ == cur) | (bb_ == cur - 1), f32(1e4), f32(0))
    fvnc[:, :, 1, :] = np.where(bb_ * 64 <= tq_, f32(0), NEGM)
    rowv = (np.arange(128) >= 31).astype(f32).reshape(128, 1)
    shm = (ii == jj + 1).astype(f32)
    cbr = np.ascontiguousarray(np.broadcast_to(relb[31][None, :], (128, 8))).astype(f32)
    w1 = np.asarray(cmp_w1, f32)[0]
    w1r = w1.reshape(2, 2, 16, 64, 64)
    w1l_h = np.transpose(w1r, (3, 0, 2, 1, 4)).reshape(64, 2, 16, 128)
    w1l = np.ascontiguousarray(np.concatenate([w1l_h, w1l_h], axis=0))
    w1n = np.ascontiguousarray(np.transpose(w1.reshape(2, 16, 128, 64), (2, 0, 1, 3)))
    pel = np.ascontiguousarray(np.transpose(np.asarray(cmp_pe, f32)[0].reshape(2, 16, 128), (2, 0, 1)))
    w2 = np.asarray(cmp_w2, f32)[0]
    w2n = np.ascontiguousarray(np.transpose(w2, (1, 0, 2)))
    w2d = np.zeros((64, 2, 128), f32)
    w2d[:, 0, 0:64] = w2[0]
    w2d[:, 1, 64:128] = w2[0]
    lnp = np.ascontiguousarray(np.broadcast_to(np.stack([np.asarray(a, f32)[0] for a in (ln1_g, ln1_b, ln2_g, ln2_b)])[None],
                                               (128, 4, D)))
    pwq_t = np.ascontiguousarray(np.asarray(peer_wq, f32)[0].reshape(8, 128, 16, 128).transpose(3, 2, 0, 1))
    sk_t = np.ascontiguousarray(np.asarray(peer_subkeys, f32)[0].transpose(2, 0, 1))
    iota16 = np.ascontiguousarray(np.broadcast_to(np.arange(16, dtype=f32)[None, :], (128, 16)))
    cw4 = np.ascontiguousarray(np.broadcast_to(cw[None], (4, 4, 512)))
    cb4 = np.ascontiguousarray(np.broadcast_to(np.asarray(m_conv_b, f32)[0][None], (4, 512)))
    gb4 = np.ascontiguousarray(np.broadcast_to(np.asarray(m_gate_bias, f32)[0].reshape(1, 8), (4, 8)))
    ng16 = np.ascontiguousarray(np.tile(np.asarray(m_norm_g, f32)[0].reshape(4, 128), (4, 1)))
    stC = np.asarray(state_mlstm_C, f32)[0].reshape(BS * 4, 128, 128)
    stn = np.asarray(state_mlstm_n, f32)[0].reshape(BS, 512)
    stm = np.asarray(state_mlstm_m, f32)[0]
    n6 = np.arange(128)[:, None] * 8 + np.arange(8)[None, :]
    d6 = 16353 - 16 * n6
    cbs = np.where((n6 <= 1022)[:, :, None], relb[bucket[np.clip(d6, 0, 4095)]], NEGM).astype(f32)
    i6 = np.arange(4)[None, :] * 128 + np.arange(128)[:, None]
    bw = np.where((i6 >= 1)[:, :, None], relb[bucket[np.clip(512 - i6, 0, 4095)]], NEGM).astype(f32)
    tok = np.arange(64)
    t25 = np.empty((60, 2, 8, 64), f32)
    t25[:, 0] = relb[bucket[128 - tok]].T[None]
    t25[:, 1] = relb[bucket[64 - tok]].T[None]
    fl255 = (np.arange(60) % 15 == 14).astype(f32).reshape(60, 1)
    ind60 = (np.arange(60)[:, None] // 15 == np.arange(4)[None, :]).astype(f32)
    rb0 = np.ascontiguousarray(np.broadcast_to(relb[0][None, :], (4, 8))).astype(f32)
    shd = np.ascontiguousarray(shm.T)
    w2kt = np.ascontiguousarray(w2[0].T)
    iota8 = np.ascontiguousarray(np.broadcast_to(np.arange(8, dtype=f32)[None, :], (128, 8)))
    iot128 = np.empty((8, 2, 128), f32)
    iot128[:, 0] = np.arange(128, dtype=f32)[None]
    iot128[:, 1] = 2.0 * (np.arange(128, dtype=f32)[None] + 1.0)
    pool_c = np.asarray(cache_cmp_kv, f32)[0].reshape(5120 * 8, 4096)
    pool_s = np.asarray(cache_slc_kv, f32)[0].reshape(5120 * 8, 4096)
    ptab = np.asarray(page_table, np.int32)
    shared = {"pool_cmp": pool_c, "pool_slc": pool_s, "cbs": cbs, "bw": bw, "t25": t25, "fl255": fl255, "ind60": ind60,
              "rb0": rb0, "shd": shd, "w2kt": w2kt, "iota8": iota8, "iot128": iot128, "cw4": cw4, "cb4": cb4, "gb4": gb4, "ng16": ng16, "w_up_m": np.ascontiguousarray(np.asarray(w_up_m, f32)[0]), "w_up_n": np.ascontiguousarray(np.asarray(w_up_n, f32)[0]),
              "w_out": np.ascontiguousarray(np.asarray(w_out, f32)[0]), "lnp": lnp, "pwq_t": pwq_t, "sk_t": sk_t,
              "iota16": iota16, "peer_u": np.ascontiguousarray(np.asarray(peer_u, f32)[0]),
              "peer_v": np.ascontiguousarray(np.asarray(peer_v, f32)[0]),
              "cbt": cbt, "tz": tz, "wz": wz, "ebig": ebig, "fvnc": fvnc, "rowv": rowv, "shm": shm, "cbr": cbr,
              "w1l": w1l, "w1n": w1n, "pel": pel, "w2d": w2d, "w2n": w2n,
              "w_in": w_in0, "ident": ident, "convw_l": convw_l, "convb_l": convb_l, "wq_l": wq_l, "wk_l": wk_l,
              "gb_l": gb_l, "normg_rep": normg_rep, "sel_c": sel_c, "tri_c": tri_c}
    in_maps = []
    for c in range(NCORES):
        xs_pad = np.zeros((128, D), f32)
        xs_pad[:SSC] = xs[c * SSC:(c + 1) * SSC]
        in_maps.append({
            **shared,
            "x_p": xp[c * TP:(c + 1) * TP],
            "x_s": xs_pad,
            "st_win": np.ascontiguousarray(stw[c * SSC:(c + 1) * SSC]),
            "st_conv": np.ascontiguousarray(stc[c * SSC:(c + 1) * SSC]),
            "st_C": np.ascontiguousarray(stC[c * SSC * 4:(c + 1) * SSC * 4]),
            "st_n": np.ascontiguousarray(stn[c * SSC:(c + 1) * SSC]),
            "st_m": np.ascontiguousarray(stm[c * SSC:(c + 1) * SSC]),
            "pt_l": np.ascontiguousarray(ptab[c * SSC:(c + 1) * SSC].T),
            "pt8": np.ascontiguousarray(np.repeat(ptab[c * SSC:(c + 1) * SSC], 2, axis=0)),
        })
    res = run_bass_kernel_spmd(nc, in_maps, core_ids=list(range(NCORES)))
    R = res.results
    if DEBUG:
        DBG["R"] = R

    def cat(name):
        return np.concatenate([np.asarray(r[name]) for r in R], axis=0)

    kvt = (2, 2, 64)
    y_p = np.concatenate([np.asarray(r["y_out"])[:TP] for r in R], axis=0).reshape(BP, SEQ, D)
    y_s = np.concatenate([np.asarray(r["y_out"])[TP:TP + SSC] for r in R], axis=0).reshape(BS, 1, D)
    cmp_p = cat("o_cmp_p").reshape((1, BP, SEQ) + kvt)
    cmp_s = cat("o_cmp_s").reshape((1, BS, 1) + kvt)
    slc_p = cat("o_slc_p").reshape((1, BP, SEQ) + kvt)
    slc_s = cat("o_slc_s").reshape((1, BS, 1) + kvt)
    win_p = cat("o_win_p").reshape((1, BP, 512) + kvt)
    win_s = cat("o_win_s").reshape((1, BS, 512) + kvt)
    C_p = cat("o_C_p").reshape(1, BP, 4, 128, 128)
    C_s = cat("o_C_s").reshape(1, BS, 4, 128, 128)
    n_p = cat("o_n_p").reshape(1, BP, 4, 128)
    n_s = cat("o_n_s").reshape(1, BS, 4, 128)
    m_p = cat("o_m_p").reshape(1, BP, 4)
    m_s = cat("o_m_s").reshape(1, BS, 4)
    conv_p = cat("o_conv_p").reshape(1, BP, 3, 512)
    conv_s = cat("o_conv_s").reshape(1, BS, 3, 512)
    return (y_p, y_s, cmp_p, cmp_s, slc_p, slc_s, win_p, win_s, C_p, C_s, n_p, n_s, m_p, m_s, conv_p, conv_s)
```

```python
import contextlib
import os
import numpy as np
import concourse.bass as bass
import concourse.mybir as mybir
from concourse.bass_utils import run_bass_kernel_spmd

F32 = mybir.dt.float32
I32 = mybir.dt.int32
U32 = mybir.dt.uint32
BF16 = mybir.dt.bfloat16
AF = mybir.ActivationFunctionType
ALU = mybir.AluOpType
AX = mybir.AxisListType

NCORES = 8
D = 1024
SEQ = 2048
BP = 16
BS = 32
SPC = BP // NCORES
SSC = BS // NCORES
TP = SPC * SEQ
NT = TP // 128
IN_DIM = 4896
O_U, O_V, O_O, O_I, O_F, O_Q, O_KC, O_KS, O_KW, O_GN, O_GM, O_GNN = (
    0, 512, 1024, 1536, 1540, 1544, 2056, 2312, 2568, 2824, 2848, 3872)
COL_GROUPS = [(0, 512), (512, 512), (1024, 512), (1536, 8), (1544, 512), (2056, 512),
              (2568, 280), (2848, 512), (3360, 512), (3872, 512), (4384, 512)]

DEBUG = False
DBG = {}
ENGS = ("pe", "act", "dve", "pool", "sp")
N_DMA_SEMS = 12


class Prog:
    def __init__(self, nc, stack):
        self.nc = nc
        self.ops = {e: [] for e in ENGS}
        self.cnt = {e: 0 for e in ENGS}
        self.esem = {e: stack.enter_context(nc.semaphore("es_" + e)) for e in ENGS}
        self.dsem = {e: [stack.enter_context(nc.semaphore("ds_%s%d" % (e, i))) for i in range(N_DMA_SEMS)]
                     for e in ("sp", "pool", "act")}
        self.dval = {e: [0] * N_DMA_SEMS for e in ("sp", "pool", "act")}
        self.dnext = {e: 0 for e in ("sp", "pool", "act")}
        self.semobj = {}
        for e in ENGS:
            self.semobj["es_" + e] = self.esem[e]
        for e in self.dsem:
            for i, s in enumerate(self.dsem[e]):
                self.semobj["ds_%s%d" % (e, i)] = s
        self.lastw = {}
        self.readers = {}
        self.waited = {e: {} for e in ENGS}
        self.final_tokens = []

    def _deps(self, eng, reads, writes):
        deps = {}

        def add(tok, same_ok):
            if tok is None:
                return
            s, v = tok
            if s == "es_" + eng and not same_ok:
                return
            if deps.get(s, 0) < v:
                deps[s] = v

        for k in reads:
            add(self.lastw.get(k), eng != "pe")
        for k in writes:
            add(self.lastw.get(k), False)
            for s, v in self.readers.get(k, {}).items():
                add((s, v), False)
        out = []
        for s, v in deps.items():
            if self.waited[eng].get(s, 0) < v:
                self.waited[eng][s] = v
                out.append((s, v))
        return out

    def _commit(self, tok, reads, writes):
        for k in writes:
            self.lastw[k] = tok
            self.readers[k] = {}
        for k in reads:
            r = self.readers.setdefault(k, {})
            if r.get(tok[0], 0) < tok[1]:
                r[tok[0]] = tok[1]

    def op(self, eng, fn, reads=(), writes=()):
        waits = self._deps(eng, reads, writes)
        self.cnt[eng] += 1
        tok = ("es_" + eng, self.cnt[eng])
        self.ops[eng].append((waits, fn, ("es_" + eng, 1)))
        self._commit(tok, reads, writes)
        return tok

    def dma(self, eng, fn, reads=(), writes=(), final=False):
        i = self.dnext[eng]
        self.dnext[eng] = (i + 1) % N_DMA_SEMS
        sname = "ds_%s%d" % (eng, i)
        waits = self._deps(eng, reads, writes)
        prev = self.dval[eng][i]
        if prev > 0 and self.waited[eng].get(sname, 0) < prev:
            self.waited[eng][sname] = prev
            waits.append((sname, prev))
        self.dval[eng][i] += 16
        tok = (sname, self.dval[eng][i])
        self.ops[eng].append((waits, fn, (sname, 16)))
        self._commit(tok, reads, writes)
        if final:
            self.final_tokens.append(tok)
        return tok

    def bank(self):
        b = self._bank = (getattr(self, "_bank", -1) + 1) % 8
        return b

    def mm(self, out, lhsT, rhs, start, stop, r, w):
        return self.op("pe", lambda e: e.matmul(out=out, lhsT=lhsT, rhs=rhs, start=start, stop=stop), r, w)

    def tr(self, out, in_, ident, r, w):
        return self.op("pe", lambda e: e.transpose(out=out, in_=in_, identity=ident), r, w)

    def act(self, out, in_, func, r, w, bias=None, scale=None):
        kw = {}
        if bias is not None:
            kw["bias"] = bias
        if scale is not None:
            kw["scale"] = scale
        return self.op("act", lambda e: e.activation(out=out, in_=in_, func=func, **kw), r, w)

    def tt(self, eng, out, in0, in1, op, r, w):
        return self.op(eng, lambda e: e.tensor_tensor(out=out, in0=in0, in1=in1, op=op), r, w)

    def ts(self, eng, out, in0, s1, op0, r, w, s2=None, op1=None):
        if op1 is None:
            return self.op(eng, lambda e: e.tensor_scalar(out=out, in0=in0, scalar1=s1, scalar2=None, op0=op0), r, w)
        return self.op(eng, lambda e: e.tensor_scalar(out=out, in0=in0, scalar1=s1, scalar2=s2, op0=op0, op1=op1), r, w)

    def stt(self, eng, out, in0, scalar, in1, op0, op1, r, w):
        return self.op(eng, lambda e: e.scalar_tensor_tensor(out=out, in0=in0, scalar=scalar, in1=in1, op0=op0, op1=op1), r, w)

    def cp(self, eng, out, in_, r, w):
        if eng == "act":
            return self.op("act", lambda e: e.copy(out=out, in_=in_), r, w)
        return self.op(eng, lambda e: e.tensor_copy(out=out, in_=in_), r, w)

    def memset(self, eng, out, val, w):
        return self.op(eng, lambda e: e.memset(out, val), (), w)

    def ld(self, out, in_, w, r=(), eng="sp"):
        return self.dma(eng, lambda e: e.dma_start(out=out, in_=in_), r, w)

    def st(self, out, in_, r, w=(), eng="pool"):
        return self.dma(eng, lambda e: e.dma_start(out=out, in_=in_), r, w)

    def barrier(self):
        toks = []
        for e in ENGS:
            if self.cnt[e] > 0:
                toks.append(("es_" + e, self.cnt[e]))
        for e in self.dsem:
            for i in range(N_DMA_SEMS):
                if self.dval[e][i] > 0:
                    toks.append(("ds_%s%d" % (e, i), self.dval[e][i]))
        for e in ENGS:
            waits = []
            for s_, v in toks:
                if s_ == "es_" + e and e in ("pe", "sp"):
                    continue
                if self.waited[e].get(s_, 0) < v:
                    self.waited[e][s_] = v
                    waits.append((s_, v))
            if waits:
                self.ops[e].append((waits, None, None))
        self.lastw = {}
        self.readers = {}

    def emit(self, eng, eobj):
        for waits, fn, inc in self.ops[eng]:
            sname, amt = inc if inc is not None else (None, None)
            for s, v in waits:
                eobj.wait_ge(self.semobj[s], v)
            if fn is None:
                continue
            ins = fn(eobj)
            ins.then_inc(self.semobj[sname], amt)
        if eng == "sp":
            for e in self.dsem:
                for i in range(N_DMA_SEMS):
                    if self.dval[e][i] > 0:
                        eobj.wait_ge(self.dsem[e][i], self.dval[e][i])


def build_program():
    nc = bass.Bass("TRN2", target_bir_lowering=False)
    stack = contextlib.ExitStack()
    with stack:
        def din(name, shape, dt=F32):
            return nc.dram_tensor(name, list(shape), dt, kind="ExternalInput").ap()

        def dout(name, shape, dt=F32):
            return nc.dram_tensor(name, list(shape), dt, kind="ExternalOutput").ap()

        def dscr(name, shape, dt=F32):
            return nc.dram_tensor(name, list(shape), dt, kind="Internal").ap()

        def sb(name, shape, dt=F32):
            return stack.enter_context(nc.sbuf_tensor(name, list(shape), dt))

        def ps(name, shape, dt=F32):
            return stack.enter_context(nc.psum_tensor(name, list(shape), dt))

        x_p = din("x_p", [TP, D])
        x_s = din("x_s", [128, D])
        w_in = din("w_in", [D, IN_DIM])
        ident_d = din("ident", [128, 128])
        st_win = din("st_win", [SSC, 512, 256])
        st_conv = din("st_conv", [SSC, 3, 512])

        o_cmp_p = dout("o_cmp_p", [TP, 256])
        o_slc_p = dout("o_slc_p", [TP, 256])
        o_win_p = dout("o_win_p", [SPC, 512, 256])
        o_conv_p = dout("o_conv_p", [SPC, 3, 512])
        o_cmp_s = dout("o_cmp_s", [SSC, 256])
        o_slc_s = dout("o_slc_s", [SSC, 256])
        o_win_s = dout("o_win_s", [SSC, 512, 256])
        o_conv_s = dout("o_conv_s", [SSC, 3, 512])

        convw_d = din("convw_l", [128, 4, 4])
        convb_d = din("convb_l", [128, 4])
        wq_d = din("wq_l", [128, 4, 128])
        wk_d = din("wk_l", [128, 4, 128])
        gb_d = din("gb_l", [4, 2])
        normg_d = din("normg_rep", [128, 512])
        sel_d = din("sel_c", [4, 4, 128])
        tri_d = din("tri_c", [128, 128])
        o_C_p = dout("o_C_p", [SPC, 4, 128, 128])
        o_n_p = dout("o_n_p", [SPC, 4, 128])
        o_m_p = dout("o_m_p", [SPC, 4])
        MOUT = (dout if DEBUG else dscr)("MOUT", [TP + 128, 512])
        cbt_d = din("cbt", [SEQ, 8, 128])
        tz_d = din("tz", [128, 8, 2, 128])
        wz_d = din("wz", [128, 128])
        ebig_d = din("ebig", [32, SEQ])
        fvnc_d = din("fvnc", [128, 16, 2, 32])
        rowv_d = din("rowv", [128, 1])
        sh_d = din("shm", [128, 128])
        cbr_d = din("cbr", [128, 8])
        w1l_d = din("w1l", [128, 2, 16, 128])
        w1n_d = din("w1n", [128, 2, 16, 64])
        pel_d = din("pel", [128, 2, 16])
        w2d_d = din("w2d", [64, 2, 128])
        w2n_d = din("w2n", [64, 2, 64])
        NOUTD = (dout if DEBUG else dscr)("NOUTD", [TP + 128, 512])
        wupm_d = din("w_up_m", [512, D])
        wupn_d = din("w_up_n", [512, D])
        wout_d = din("w_out", [D, D])
        lnp_d = din("lnp", [128, 4, D])
        pwqt_d = din("pwq_t", [128, 16, 8, 128])
        skt_d = din("sk_t", [128, 2, 128])
        iota_d = din("iota16", [128, 16])
        peer_u_d = din("peer_u", [16384, D])
        peer_v_d = din("peer_v", [16384, D])
        X1D = (dout if DEBUG else dscr)("X1D", [TP + 128, D])
        y_out = dout("y_out", [TP + 128, D])
        U16D = dscr("U16D", [16384, D], BF16)
        V16D = dscr("V16D", [16384, D], BF16)
        stC_d = din("st_C", [SSC * 4, 128, 128])
        stn_d = din("st_n", [SSC, 512])
        stm_d = din("st_m", [SSC, 4])
        cw4_d = din("cw4", [4, 4, 512])
        cb4_d = din("cb4", [4, 512])
        gb4_d = din("gb4", [4, 8])
        ng16_d = din("ng16", [16, 128])
        o_C_s = dout("o_C_s", [SSC * 4, 128, 128])
        o_n_s = dout("o_n_s", [SSC, 512])
        o_m_s = dout("o_m_s", [SSC, 4])
        SCRB = dscr("SCRB", [16, 264])
        pool_cmp_d = din("pool_cmp", [5120 * 8, 4096])
        pool_slc_d = din("pool_slc", [5120 * 8, 4096])
        ptl_d = din("pt_l", [128, 4], I32)
        pt8_d = din("pt8", [8, 128], I32)
        cbs_d = din("cbs", [128, 8, 8])
        bw_d = din("bw", [128, 4, 8])
        t25_d = din("t25", [60, 2, 8, 64])
        fl255_d = din("fl255", [60, 1])
        ind_d = din("ind60", [60, 4])
        rb0_d = din("rb0", [4, 8])
        shd_d = din("shd", [128, 128])
        w2kt_d = din("w2kt", [64, 64])
        iota8_d = din("iota8", [128, 8])
        iot128_d = din("iot128", [8, 2, 128])
        SCRQ = dscr("SCRQ", [4, 512])
        OSD = dscr("OSD", [2, 4, 8, 65])
        SELD = dscr("SELD", [8, 256])
        SCRP = dscr("SCRP", [8, 15, 5])
        Z = dscr("Z", [TP + 128, IN_DIM])

        P = Prog(nc, stack)

        ARENA_F = 52800
        EI = sb("EI_i32", [128, 128], I32)[:, :]
        EI2 = sb("EI2_i32", [128, 128], I32)[:, :]
        IDXC = sb("IDXC_i32", [128, 32], I32)[:, :]
        IDXS = sb("IDXS_i32", [128, 8], I32)[:, :]
        arena = sb("arena", [128, ARENA_F])
        apos = [0]

        def alloc(n):
            o = apos[0]
            apos[0] += n
            assert apos[0] <= ARENA_F, ("arena overflow", apos[0])
            return arena[:, o:o + n]

        def alloc3(a, b):
            return alloc(a * b).rearrange("p (a b) -> p a b", a=a)

        ident = alloc(128)
        pbank = [ps("pb%d" % i, [128, 512]) for i in range(8)]
        tri = alloc(128)
        selc = alloc3(4, 128)
        persist_mark = apos[0]

        QT = 8
        xt = [alloc(D) for i in range(2)]
        xT = alloc3(8, QT * 128)
        wg = [alloc3(8, 512) for i in range(2)]
        zs = [alloc(512) for i in range(2)]

        P.dma("sp", lambda e: e.dma_start(out=ident, in_=ident_d[:, :]), writes=["ident"])
        P.ld(tri, tri_d[:, :], ["tri"])
        P.ld(selc[0:4], sel_d[:, :, :], ["selc"])

        n_tiles_all = NT + 1
        groups = [list(range(g, min(g + QT, n_tiles_all))) for g in range(0, n_tiles_all, QT)]
        xcnt = 0
        wcnt = 0
        zcnt = 0
        pcnt = 0
        for grp in groups:
            for li, ti in enumerate(grp):
                b = xcnt % 2
                xcnt += 1
                src = x_p[ti * 128:(ti + 1) * 128, :] if ti < NT else x_s[:, :]
                P.dma("sp", lambda e, b=b, src=src: e.dma_start(out=xt[b], in_=src),
                      writes=["xt%d" % b])
                for c in range(8):
                    pb = pcnt % 8
                    pcnt += 1
                    P.op("pe", lambda e, pb=pb, b=b, c=c: e.transpose(
                        out=pbank[pb][:, 0:128], in_=xt[b][:, c * 128:(c + 1) * 128], identity=ident),
                        reads=["xt%d" % b, "ident"], writes=["pb%d" % pb])
                    eng = "dve" if c % 2 == 0 else "act"
                    if eng == "dve":
                        P.op("dve", lambda e, pb=pb, c=c, li=li: e.tensor_copy(
                            out=xT[:, c, li * 128:(li + 1) * 128], in_=pbank[pb][:, 0:128]),
                            reads=["pb%d" % pb], writes=["xT_%d_%d" % (c, li)])
                    else:
                        P.op("act", lambda e, pb=pb, c=c, li=li: e.copy(
                            out=xT[:, c, li * 128:(li + 1) * 128], in_=pbank[pb][:, 0:128]),
                            reads=["pb%d" % pb], writes=["xT_%d_%d" % (c, li)])
            for (c0, cw) in COL_GROUPS:
                wb = wcnt % 2
                wcnt += 1
                P.dma("sp", lambda e, wb=wb, c0=c0, cw=cw: e.dma_start(
                    out=wg[wb][:, :, 0:cw],
                    in_=w_in[:, c0:c0 + cw].rearrange("(c p) n -> p c n", p=128)),
                    writes=["wg%d" % wb])
                for li, ti in enumerate(grp):
                    pb = pcnt % 8
                    pcnt += 1
                    for c in range(8):
                        P.op("pe", lambda e, pb=pb, c=c, li=li, wb=wb, cw=cw: e.matmul(
                            out=pbank[pb][:, 0:cw], lhsT=xT[:, c, li * 128:(li + 1) * 128],
                            rhs=wg[wb][:, c, 0:cw], start=(c == 0), stop=(c == 7)),
                            reads=["xT_%d_%d" % (c, li), "wg%d" % wb], writes=["pb%d" % pb])
                    zb = zcnt % 2
                    zcnt += 1
                    if zcnt % 2 == 0:
                        P.op("dve", lambda e, pb=pb, zb=zb, cw=cw: e.tensor_copy(
                            out=zs[zb][:, 0:cw], in_=pbank[pb][:, 0:cw]),
                            reads=["pb%d" % pb], writes=["zs%d" % zb])
                    else:
                        P.op("act", lambda e, pb=pb, zb=zb, cw=cw: e.copy(
                            out=zs[zb][:, 0:cw], in_=pbank[pb][:, 0:cw]),
                            reads=["pb%d" % pb], writes=["zs%d" % zb])
                    P.dma("pool", lambda e, zb=zb, ti=ti, c0=c0, cw=cw: e.dma_start(
                        out=Z[ti * 128:(ti + 1) * 128, c0:c0 + cw], in_=zs[zb][:, 0:cw]),
                        reads=["zs%d" % zb], writes=["Z_%d_%d" % (ti, c0)])

        ZALL = ["Z_%d_%d" % (ti, c0) for ti in range(NT + 1) for (c0, _) in COL_GROUPS]

        def d2d(dst, src, final=True):
            P.dma("sp", lambda e: e.dma_start(out=dst, in_=src), reads=ZALL, writes=[], final=final)

        d2d(o_cmp_p[:, :], Z[0:TP, O_KC:O_KC + 256])
        d2d(o_slc_p[:, :], Z[0:TP, O_KS:O_KS + 256])
        for s in range(SPC):
            d2d(o_win_p[s, :, :], Z[s * SEQ + SEQ - 512:(s + 1) * SEQ, O_KW:O_KW + 256])
            d2d(o_conv_p[s, :, :], Z[(s + 1) * SEQ - 3:(s + 1) * SEQ, O_U:O_U + 512])
        d2d(o_cmp_s[:, :], Z[TP:TP + SSC, O_KC:O_KC + 256])
        d2d(o_slc_s[:, :], Z[TP:TP + SSC, O_KS:O_KS + 256])
        for s in range(SSC):
            d2d(o_win_s[s, 0:511, :], st_win[s, 1:512, :])
            d2d(o_win_s[s, 511:512, :], Z[TP + s:TP + s + 1, O_KW:O_KW + 256])
            d2d(o_conv_s[s, 0:2, :], st_conv[s, 1:3, :])
            d2d(o_conv_s[s, 2:3, :], Z[TP + s:TP + s + 1, O_U:O_U + 512])


        P.barrier()
        apos[0] = persist_mark
        CONST = ["ident", "tri", "selc", "m_w"]
        convw = alloc3(4, 4)
        convb = alloc(4)
        wq = alloc3(4, 128)
        wk = alloc3(4, 128)
        gb = alloc(2)
        ngbf = alloc(1)
        normg = alloc(512)
        zero_row = alloc(SEQ)
        P.ld(convw, convw_d[:, :, :], ["m_w"])
        P.ld(convb, convb_d[:, :], ["m_w"])
        P.ld(wq, wq_d[:, :, :], ["m_w"])
        P.ld(wk, wk_d[:, :, :], ["m_w"])
        P.ld(gb[0:4], gb_d[:, :], ["gb"])
        P.ld(normg, normg_d[:, :], ["m_w"])
        P.memset("dve", zero_row, 0.0, ["zero_row"])
        P.ts("dve", ngbf[0:4], gb[0:4, 1:2], -1.0, ALU.mult, ["gb"], ["ngbf"])

        ZIF = alloc3(16, 8)
        IT = alloc(SEQ)
        FT = alloc(SEQ)
        Bc = alloc(SEQ)
        Ac = IT
        CMc = FT
        Mc = Bc
        NRc = alloc(SEQ)
        EMc = alloc(SEQ)
        aT = alloc3(16, 4)
        emT = alloc3(16, 4)
        Uh = alloc3(16, 128)
        UT = alloc(3 + SEQ)
        CT = alloc(SEQ)
        ACC = CT
        QTh = alloc(SEQ)
        KTh = alloc(SEQ)
        KTOK = alloc3(16, 128)
        V1 = alloc3(16, 129)
        OP = alloc3(16, 128)
        SIG = alloc3(16, 128)
        Rb = alloc(SEQ)
        WT = alloc3(16, 512)
        Eb = [alloc(512) for _ in range(2)]
        MO = alloc3(16, 128)
        wts = alloc(16)
        VW = alloc3(16, 128)
        Csb = alloc(128)
        nsb = alloc(128)
        ep = [dict(dd=alloc(1), rec=alloc(1), hq=alloc(128), st=alloc(8), ag=alloc(4), rstd=alloc(1),
                   hn=alloc(128)) for _ in range(2)]
        BN_S = 6

        P.memset("dve", UT[:, 0:3], 0.0, ["UT"])
        P.memset("dve", V1[:, :, 128:129], 1.0, ["V1ones"])
        ecnt = 0
        epc = 0
        for sq in range(SPC):
            r0 = sq * SEQ
            zrows = Z[r0:r0 + SEQ, :]
            P.ld(ZIF, zrows[:, O_I:O_I + 8].rearrange("(n p) c -> p n c", p=128), ["ZIF"])
            for which, dst, dkey in ((0, IT, "ITb"), (1, FT, "FTb")):
                for g in range(4):
                    pb = P.bank()
                    for j in range(4):
                        n = g * 4 + j
                        P.tr(pbank[pb][0:4, j * 128:(j + 1) * 128], ZIF[:, n, which * 4:which * 4 + 4], ident,
                             ["ZIF", "ident"], ["pb%d" % pb])
                    P.cp("dve", dst[0:4, g * 512:(g + 1) * 512], pbank[pb][0:4, :], ["pb%d" % pb], [dkey])
            ITk = ["ITb"]
            FTk = ["FTb"]
            P.ts("dve", IT[0:4], IT[0:4], gb[0:4, 0:1], ALU.add, ITk + ["gb"], ["ITb"])
            P.act(FT[0:4], FT[0:4], AF.Exp, FTk + ["ngbf"], ["FTb"], bias=ngbf[0:4], scale=-1.0)
            P.act(FT[0:4], FT[0:4], AF.Ln, ["FTb"], ["FTb"], bias=1.0, scale=1.0)
            P.ts("dve", FT[0:4], FT[0:4], -1.0, ALU.mult, ["FTb"], ["FTb"])
            P.op("dve", lambda e: e.tensor_tensor_scan(out=Bc[0:4], data0=FT[0:4], data1=zero_row[0:4], initial=0.0,
                                                      op0=ALU.add, op1=ALU.add), ["FTb", "zero_row"], ["Bb"])
            P.tt("dve", Ac[0:4], IT[0:4], Bc[0:4], ALU.subtract, ["ITb", "Bb"], ["ITb", "ITb"] + ITk)
            P.op("dve", lambda e: e.tensor_tensor_scan(out=CMc[0:4], data0=Ac[0:4], data1=Ac[0:4], initial=0.0,
                                                      op0=ALU.max, op1=ALU.max), ["ITb"], ["FTb", "FTb", "FTb", "FTb"] + FTk)
            P.ts("dve", NRc[0:4], CMc[0:4], -1.0, ALU.mult, ["FTb"], ["NRc"])
            P.tt("dve", Mc[0:4], Bc[0:4], CMc[0:4], ALU.add, ["Bb", "FTb", "ITb"], ["Bb", "Bb"])
            P.act(EMc[0:4], Mc[0:4], AF.Exp, ["Bb"], ["EMc"], scale=-1.0)
            P.st(o_m_p[sq, :].rearrange("(h o) -> h o", o=1), Mc[0:4, SEQ - 1:SEQ], ["Bb"])
            for src, skey, dst, dkey in ((Ac, "ITb", aT, "aT"), (EMc, "EMc", emT, "emT")):
                pb = P.bank()
                for n in range(16):
                    P.tr(pbank[pb][:, n * 4:(n + 1) * 4], src[0:4, n * 128:(n + 1) * 128], ident[0:4, 0:4],
                         [skey, "ident"], ["pb%d" % pb])
                P.cp("dve", dst, pbank[pb][:, 0:64].rearrange("p (n h) -> p n h", h=4), ["pb%d" % pb], [dkey])
            for h in range(4):
                hs = slice(h * 128, (h + 1) * 128)
                P.ld(Uh, zrows[:, O_U + h * 128:O_U + (h + 1) * 128].rearrange("(n p) c -> p n c", p=128), ["Uh"])
                P.ld(V1[:, :, 0:128], zrows[:, O_V + h * 128:O_V + (h + 1) * 128].rearrange("(n p) c -> p n c", p=128),
                     ["V1"])
                P.ld(OP, zrows[:, O_O + h * 128:O_O + (h + 1) * 128].rearrange("(n p) c -> p n c", p=128), ["OP"])
                P.act(SIG, OP, AF.Sigmoid, ["OP"], ["SIG"])
                for g in range(4):
                    pb = P.bank()
                    P.mm(pbank[pb][:, :], selc[0:4, h, :], NRc[0:4, g * 512:(g + 1) * 512], True, True,
                         ["selc", "NRc"], ["pb%d" % pb])
                    P.cp("act", Rb[:, g * 512:(g + 1) * 512], pbank[pb][:, :], ["pb%d" % pb], ["Rb%d" % g])
                for g in range(4):
                    pb = P.bank()
                    for j in range(4):
                        n = g * 4 + j
                        P.tr(pbank[pb][:, j * 128:(j + 1) * 128], Uh[:, n, :], ident, ["Uh", "ident"], ["pb%d" % pb])
                    P.cp("dve", UT[:, 3 + g * 512:3 + (g + 1) * 512], pbank[pb][:, :], ["pb%d" % pb], ["UT"])
                P.ts("dve", ACC, UT[:, 0:SEQ], convw[:, h, 0:1], ALU.mult, ["UT", "m_w"], ["CT", "CT"])
                for j in range(1, 4):
                    P.stt("dve", ACC, UT[:, j:j + SEQ], convw[:, h, j:j + 1], ACC, ALU.mult, ALU.add,
                          ["UT", "CT", "m_w"], ["CT"])
                P.act(CT, ACC, AF.Silu, ["CT", "m_w"], ["CT", "CT"], bias=convb[:, h:h + 1])
                for g in range(4):
                    pb = P.bank()
                    P.mm(pbank[pb][:, :], wq[:, h, :], CT[:, g * 512:(g + 1) * 512], True, True, ["m_w", "CT"],
                         ["pb%d" % pb])
                    P.cp("act", QTh[:, g * 512:(g + 1) * 512], pbank[pb][:, :], ["pb%d" % pb], ["QT%d" % g])
                    pb = P.bank()
                    P.mm(pbank[pb][:, :], wk[:, h, :], CT[:, g * 512:(g + 1) * 512], True, True, ["m_w", "CT"],
                         ["pb%d" % pb])
                    P.ts("dve", KTh[:, g * 512:(g + 1) * 512], pbank[pb][:, :], 128.0 ** -0.5, ALU.mult,
                         ["pb%d" % pb], ["KT"])
                    pb = P.bank()
                    for j in range(4):
                        n = g * 4 + j
                        P.mm(pbank[pb][:, j * 128:(j + 1) * 128], CT[:, n * 128:(n + 1) * 128], wk[:, h, :], True, True,
                             ["m_w", "CT"], ["pb%d" % pb])
                    P.ts("dve", KTOK[:, g * 4:(g + 1) * 4, :], pbank[pb][:, :].rearrange("p (j e) -> p j e", j=4),
                         128.0 ** -0.5, ALU.mult, ["pb%d" % pb], ["KTOK"])
                for tb in range(4):
                    nst = 4 * tb + 4
                    for st_ in range(nst):
                        p_ = max(0, st_ - 4 * tb)
                        c0 = p_ * 128
                        cs = slice(c0, 512)
                        gs = slice(tb * 512 + c0, (tb + 1) * 512)
                        pb = P.bank()
                        P.mm(pbank[pb][:, cs], KTh[:, st_ * 128:(st_ + 1) * 128], QTh[:, gs], True, True,
                             ["KT", "QT%d" % tb], ["pb%d" % pb])
                        eb = ecnt % 2
                        ecnt += 1
                        P.ts("dve", Eb[eb][:, cs], Rb[:, gs], aT[:, st_, h:h + 1], ALU.add, ["Rb%d" % tb, "aT"],
                             ["Eb%d" % eb], s2=0.0, op1=ALU.min)
                        P.act(Eb[eb][:, cs], Eb[eb][:, cs], AF.Exp, ["Eb%d" % eb], ["Eb%d" % eb])
                        if st_ >= 4 * tb:
                            P.tt("pool", Eb[eb][:, c0:c0 + 128], Eb[eb][:, c0:c0 + 128], tri, ALU.mult,
                                 ["Eb%d" % eb, "tri"], ["Eb%d" % eb])
                        P.tt("dve", WT[:, st_, cs], pbank[pb][:, cs], Eb[eb][:, cs], ALU.mult,
                             ["pb%d" % pb, "Eb%d" % eb], ["WT%d" % st_])
                    for sub in range(4):
                        tq = 4 * tb + sub
                        pb = P.bank()
                        for st_ in range(tq + 1):
                            P.mm(pbank[pb][:, 0:129], WT[:, st_, sub * 128:(sub + 1) * 128], V1[:, st_, :],
                                 st_ == 0, st_ == tq, ["WT%d" % st_, "V1", "V1ones"], ["pb%d" % pb])
                        E = ep[epc % 2]
                        ek = "ep%d" % (epc % 2)
                        epc += 1
                        P.act(E["dd"], pbank[pb][:, 128:129], AF.Abs, ["pb%d" % pb], [ek + "dd"])
                        P.ts("dve", E["dd"], E["dd"], emT[:, tq, h:h + 1], ALU.max, [ek + "dd", "emT"], [ek + "dd"])
                        P.op("dve", lambda e, E=E: e.reciprocal(out=E["rec"], in_=E["dd"]), [ek + "dd"], [ek + "rec"])
                        P.ts("dve", E["hq"], pbank[pb][:, 0:128], E["rec"], ALU.mult, ["pb%d" % pb, ek + "rec"],
                             [ek + "hq"])
                        P.op("dve", lambda e, E=E: e.bn_stats(out=E["st"][:, 0:BN_S], in_=E["hq"]), [ek + "hq"],
                             [ek + "st"])
                        P.op("dve", lambda e, E=E: e.bn_aggr(out=E["ag"][:, 0:2], in_=E["st"][:, 0:BN_S]), [ek + "st"],
                             [ek + "ag"])
                        P.act(E["rstd"], E["ag"][:, 1:2], AF.Sqrt, [ek + "ag"], [ek + "rstd"], bias=1e-5, scale=1.0)
                        P.op("dve", lambda e, E=E: e.reciprocal(out=E["rstd"], in_=E["rstd"]), [ek + "rstd"],
                             [ek + "rstd"])
                        P.ts("dve", E["hn"], E["hq"], E["ag"][:, 0:1], ALU.subtract, [ek + "hq", ek + "ag", ek + "rstd"],
                             [ek + "hn"], s2=E["rstd"], op1=ALU.mult)
                        P.tt("pool", E["hn"], E["hn"], normg[:, hs], ALU.mult, [ek + "hn", "m_w"], [ek + "hn"])
                        P.tt("pool", MO[:, tq, :], E["hn"], SIG[:, tq, :], ALU.mult, [ek + "hn", "SIG"], ["MO"])
                P.st(MOUT[r0:r0 + SEQ, hs].rearrange("(n p) c -> p n c", p=128), MO, ["MO"], ["MOUT"])
                P.act(wts, aT[:, :, h], AF.Exp, ["aT", "Rb3"], ["wts"], bias=Rb[:, SEQ - 1:SEQ])
                P.tt("dve", VW, V1[:, :, 0:128], wts.unsqueeze(2).to_broadcast([128, 16, 128]), ALU.mult,
                     ["V1", "wts"], ["VW"])
                pb = P.bank()
                for st_ in range(16):
                    P.mm(pbank[pb][:, 0:128], VW[:, st_, :], KTOK[:, st_, :], st_ == 0, st_ == 15, ["VW", "KTOK"],
                         ["pb%d" % pb])
                P.cp("act", Csb, pbank[pb][:, 0:128], ["pb%d" % pb], ["Csb"])
                P.st(o_C_p[sq, h, :, :], Csb, ["Csb"])
                pb = P.bank()
                for st_ in range(16):
                    P.mm(pbank[pb][0:1, 0:128], wts[:, st_:st_ + 1], KTOK[:, st_, :], st_ == 0, st_ == 15,
                         ["wts", "KTOK"], ["pb%d" % pb])
                P.cp("act", nsb[0:1], pbank[pb][0:1, 0:128], ["pb%d" % pb], ["nsb"])
                P.st(o_n_p[sq:sq + 1, h, :], nsb[0:1], ["nsb"])


        P.barrier()
        apos[0] = persist_mark
        BIGM = 30000.0
        QT8 = alloc3(4, SEQ)
        KTz = [[alloc(SEQ) for _ in range(2)] for _ in range(2)]
        V1n = [alloc3(16, 65) for _ in range(2)]
        KCTz = alloc(2 * 2 * 128).rearrange("p (k f n) -> p k f n", k=2, f=2)
        KCV = alloc3(2, 64)
        TZs = alloc(8 * 2 * 128).rearrange("p (h k j) -> p h k j", h=8, k=2)
        WZ = alloc(128)
        SHM = alloc(128)
        EBIG = alloc(SEQ)
        SELM1 = [alloc(SEQ) for _ in range(2)]
        XTc = SELM1
        FVNC = alloc(16 * 2 * 32).rearrange("p (n k b) -> p n k b", n=16, k=2)
        ROWV = alloc(1)
        CBR = alloc(8)
        W2Z = alloc3(2, 128)
        W2N = alloc3(2, 64)
        PET = alloc(128)
        ONES = alloc(128)
        GS = alloc3(16, 24)
        NOUT = alloc3(4, 512)
        stg = [alloc3(16, 128) for _ in range(2)]
        CB = [alloc3(8, 128) for _ in range(2)]
        Sx = alloc3(8, 128)
        Pn = alloc3(8, 128)
        PnT = Sx
        sm8 = alloc(8)
        rs8 = alloc(8)
        PG = alloc3(2, 128)
        PSs = alloc3(2, 32)
        S012 = alloc3(2, 32)
        SC = alloc3(2, 32)
        M8 = alloc(8)
        SCW = alloc(32)
        SEL = alloc3(2, 32)
        PAB = alloc(128)
        XG = alloc(64)
        TG = alloc(64)
        HID = alloc(64)
        HT = alloc(128)
        epn = [dict(r=alloc(1)) for _ in range(4)]
        PTb = alloc3(16, 512)
        W1L = PTb[:, 0:8, :].rearrange("p a b -> p (a b)").rearrange("p (c r o) -> p c r o", c=2, r=16)
        W1N = PTb[:, 8:12, :].rearrange("p a b -> p (a b)").rearrange("p (c j o) -> p c j o", c=2, j=16)
        PEL = PTb[:, 12, 0:32].rearrange("p (c j) -> p c j", c=2)
        PTK = ["PT%d" % i for i in range(16)]

        P.ld(TZs, tz_d[:, :, :, :], ["TZ"])
        P.ld(WZ, wz_d[:, :], ["ncst"])
        P.ld(SHM, sh_d[:, :], ["ncst"])
        P.ld(EBIG[0:32], ebig_d[:, :], ["ncst"])
        P.ld(FVNC, fvnc_d[:, :, :, :], ["ncst"])
        P.ld(ROWV, rowv_d[:, :], ["ncst"])
        P.ld(CBR, cbr_d[:, :], ["CBR"])
        P.ld(W2Z[0:64], w2d_d[:, :, :], ["ncst"])
        P.ld(W2N[0:64], w2n_d[:, :, :], ["ncst"])
        P.memset("dve", ONES, 1.0, ["ONES"])
        for h in range(8):
            P.ts("dve", TZs[:, h, :, :], TZs[:, h, :, :], CBR[:, h:h + 1], ALU.subtract, ["TZ", "CBR"], ["TZ"])
        for br in range(2):
            P.memset("dve", V1n[br][:, :, 64:65], 1.0, ["V1n1"])
        cbcnt = 0
        encnt = 0
        K3 = int(os.environ.get("K3STOP", "9"))

        def gelu_from(xsb, tmp, out, rk, wk_):
            P.tt("dve", tmp, xsb, xsb, ALU.mult, rk, [wk_ + "t"])
            P.ts("dve", tmp, tmp, 0.044715, ALU.mult, [wk_ + "t"], [wk_ + "t"], s2=1.0, op1=ALU.add)
            P.tt("dve", tmp, tmp, xsb, ALU.mult, [wk_ + "t"] + rk, [wk_ + "t"])
            P.act(tmp, tmp, AF.Sigmoid, [wk_ + "t"], [wk_ + "t"], scale=1.5957691216057308)
            P.tt("dve", out, tmp, xsb, ALU.mult, [wk_ + "t"] + rk, [wk_])

        def tr_block(src3, dst, wkeys, scale=None, rmajor=False):
            for g in range(4):
                pb = P.bank()
                for j in range(4):
                    P.tr(pbank[pb][:, j * 128:(j + 1) * 128], src3[:, g * 4 + j, :], ident, ["stgX", "ident"],
                         ["pb%d" % pb])
                if rmajor:
                    P.cp("act", dst.rearrange("p (r n) -> p r n", r=16)[:, :, g * 32:(g + 1) * 32],
                         pbank[pb][:, :].rearrange("p (n r) -> p r n", r=16), ["pb%d" % pb], wkeys)
                elif scale is not None:
                    P.ts("dve", dst[:, g * 512:(g + 1) * 512], pbank[pb][:, :], scale, ALU.mult, ["pb%d" % pb], wkeys)
                else:
                    P.cp("act", dst[:, g * 512:(g + 1) * 512], pbank[pb][:, :], ["pb%d" % pb], wkeys)

        for sq in range(SPC if K3 >= 2 else 0):
            r0 = sq * SEQ
            zrows = Z[r0:r0 + SEQ, :]

            def zt(c0, w):
                return zrows[:, c0:c0 + w].rearrange("(n p) c -> p n c", p=128)

            for pr in range(4):
                P.ld(stg[0], zt(O_Q + pr * 128, 128), ["stgX"])
                tr_block(stg[0], QT8[:, pr, :], ["QT8"], scale=0.125)
            for c in range(2):
                P.ld(stg[0], zt(O_KC + c * 128, 128), ["stgX"])
                tr_block(stg[0], XTc[c], ["XTc", "SELM1_0", "SELM1_1"], rmajor=True)
            P.ld(GS, zt(O_GN, 24), ["GSraw"])
            P.act(GS, GS, AF.Sigmoid, ["GSraw"], ["GS", "GSraw"])
            P.ld(W1L, w1l_d[:, :, :, :], ["W1L"] + PTK)
            P.ld(W1N, w1n_d[:, :, :, :], ["W1N"] + PTK)
            P.ld(PEL, pel_d[:, :, :], ["PEL"] + PTK)
            for c in range(2):
                pb = P.bank()
                for j in range(16):
                    P.mm(pbank[pb][0:1, 0:64], PEL[:, c, j:j + 1], W1N[:, c, j, :], j == 0, j == 15, ["PEL", "W1N"],
                         ["pb%d" % pb])
                P.cp("act", PET[0:1, c * 64:(c + 1) * 64], pbank[pb][0:1, 0:64], ["pb%d" % pb], ["PET"])
            for c in range(2):
                for k in range(2):
                    rows = slice(k * 64, (k + 1) * 64)
                    xv = XTc[c][rows, :].rearrange("p (r n) -> p r n", r=16)
                    pb = P.bank()
                    for r in range(16):
                        P.mm(pbank[pb][:, 0:128], xv[:, r, :], W1L[rows, c, r, :], r == 0, r == 15, ["XTc", "W1L"],
                             ["pb%d" % pb])
                    P.cp("act", PAB, pbank[pb][:, 0:128], ["pb%d" % pb], ["PAB"])
                    pb = P.bank()
                    P.mm(pbank[pb][:, 0:64], ident, PAB[:, 0:64], True, False, ["ident", "PAB"], ["pb%d" % pb])
                    P.mm(pbank[pb][:, 0:64], SHM, PAB[:, 64:128], False, False, ["ncst", "PAB"], ["pb%d" % pb])
                    P.mm(pbank[pb][:, 0:64], ONES[0:1, :], PET[0:1, c * 64:(c + 1) * 64], False, True, ["ONES", "PET"],
                         ["pb%d" % pb])
                    P.cp("act", XG, pbank[pb][:, 0:64], ["pb%d" % pb], ["XG"])
                    gelu_from(XG, TG, HID, ["XG"], "HID")
                    pb = P.bank()
                    P.tr(pbank[pb][0:64, 0:128], HID, ident, ["HID", "ident"], ["pb%d" % pb])
                    P.cp("act", HT[0:64], pbank[pb][0:64, 0:128], ["pb%d" % pb], ["HT"])
                    if c == 0:
                        for f in range(2):
                            pb = P.bank()
                            P.mm(pbank[pb][:, 0:128], W2Z[0:64, f, :], HT[0:64], True, True, ["ncst", "HT"],
                                 ["pb%d" % pb])
                            P.cp("act", KCTz[:, k, f, :], pbank[pb][:, 0:128], ["pb%d" % pb], ["KCT"])
                    else:
                        pb = P.bank()
                        P.mm(pbank[pb][:, 0:64], HT[0:64], W2N[0:64, c, :], True, True, ["ncst", "HT"], ["pb%d" % pb])
                        P.cp("act", KCV[:, k, :], pbank[pb][:, 0:64], ["pb%d" % pb], ["KCV"])

            def attn_block(br, h, tb, st_list):
                nonlocal encnt
                pr, half, kvh = h // 2, h % 2, h // 4
                hl = h % 4
                active = {}
                for st_ in st_list:
                    subs = [sub for sub in range(4) if 0 <= (4 * tb + sub - st_) <= (4 if br == 1 else 10 ** 6)]
                    if not subs:
                        continue
                    lo, hi = subs[0], subs[-1] + 1
                    active[st_] = (lo, hi)
                    cs = slice(lo * 128, hi * 128)
                    gs = slice(tb * 512 + lo * 128, tb * 512 + hi * 128)
                    pb = P.bank()
                    extra = []
                    for sub in subs:
                        dd = 4 * tb + sub - st_
                        if dd == 0:
                            extra.append((sub, TZs[:, h, 0, :], "TZ"))
                        elif dd == 1:
                            extra.append((sub, TZs[:, h, 1, :], "TZ"))
                        elif dd == 4 and br == 1:
                            extra.append((sub, WZ, "ncst"))
                    nmm = 1 + (1 if br == 0 else 0) + len(extra)
                    i_mm = 0
                    P.mm(pbank[pb][:, cs], KTz[br][half][:, st_ * 128:(st_ + 1) * 128], QT8[:, pr, gs],
                         True, nmm == 1, ["KTz", "QT8"], ["pb%d" % pb])
                    i_mm += 1
                    if br == 0:
                        P.mm(pbank[pb][:, cs], EBIG[0:32, st_ * 128:(st_ + 1) * 128], SELM1[kvh][0:32, gs],
                             False, i_mm == nmm - 1, ["ncst", "SELM1_%d" % kvh], ["pb%d" % pb])
                        i_mm += 1
                    for (sub, tile_, key) in extra:
                        P.mm(pbank[pb][:, sub * 128:(sub + 1) * 128], ident, tile_, False, i_mm == nmm - 1,
                             ["ident", key], ["pb%d" % pb])
                        i_mm += 1
                    P.act(PTb[:, st_, cs], pbank[pb][:, cs], AF.Exp, ["pb%d" % pb, "CBR"], ["PT%d" % st_],
                          bias=CBR[:, h:h + 1])
                for sub in range(4):
                    tq = 4 * tb + sub
                    sts = [st_ for st_ in st_list if st_ in active and active[st_][0] <= sub < active[st_][1]]
                    pb = P.bank()
                    for i, st_ in enumerate(sts):
                        P.mm(pbank[pb][:, 0:65], PTb[:, st_, sub * 128:(sub + 1) * 128], V1n[br][:, st_, :],
                             i == 0, i == len(sts) - 1, ["PT%d" % st_, "V1n", "V1n1"], ["pb%d" % pb])
                    E = epn[encnt % 4]
                    ek = "epn%d" % (encnt % 4)
                    encnt += 1
                    P.ts("dve", E["r"], pbank[pb][:, 64:65], 1e-30, ALU.max, ["pb%d" % pb], [ek])
                    P.op("dve", lambda e, E=E: e.reciprocal(out=E["r"], in_=E["r"]), [ek], [ek])
                    gcol = (1 + br) * 8 + h
                    P.tt("dve", E["r"], E["r"], GS[:, tq, gcol:gcol + 1], ALU.mult, [ek, "GS"], [ek])
                    P.stt("dve", NOUT[:, sub, hl * 64:(hl + 1) * 64], pbank[pb][:, 0:64], E["r"],
                          NOUT[:, sub, hl * 64:(hl + 1) * 64], ALU.mult, ALU.add, ["pb%d" % pb, ek, "NOUT"], ["NOUT"])

            for tb in range(4 if K3 >= 4 else 0):
                for sub in range(4):
                    tq = 4 * tb + sub
                    cb_ = cbcnt % 2
                    cbcnt += 1
                    P.ld(CB[cb_], cbt_d[tq * 128:(tq + 1) * 128, :, :], ["CB%d" % cb_])
                    for hg in range(2):
                        pb = P.bank()
                        for hh in range(4):
                            h = hg * 4 + hh
                            pr, half, kvh = h // 2, h % 2, h // 4
                            P.mm(pbank[pb][:, hh * 128:(hh + 1) * 128], QT8[:, pr, tq * 128:(tq + 1) * 128],
                                 KCTz[:, kvh, half, :], True, True, ["QT8", "KCT"], ["pb%d" % pb])
                        P.tt("dve", Sx[:, hg * 4:(hg + 1) * 4, :],
                             pbank[pb][:, :].rearrange("p (h n) -> p h n", h=4), CB[cb_][:, hg * 4:(hg + 1) * 4, :],
                             ALU.add, ["pb%d" % pb, "CB%d" % cb_], ["Sx", "PnT"])
                    P.op("dve", lambda e: e.reduce_max(out=sm8, in_=Sx, axis=AX.X), ["Sx"], ["sm8"])
                    P.tt("dve", Sx, Sx, sm8.unsqueeze(2).to_broadcast([128, 8, 128]), ALU.subtract, ["Sx", "sm8"], ["Sx"])
                    P.act(Pn, Sx, AF.Exp, ["Sx"], ["Pn"])
                    P.op("dve", lambda e: e.reduce_sum(out=sm8, in_=Pn, axis=AX.X), ["Pn"], ["sm8"])
                    P.op("dve", lambda e: e.reciprocal(out=rs8, in_=sm8), ["sm8"], ["rs8"])
                    if tq == 0:
                        P.ts("dve", rs8, rs8, ROWV[:, 0:1], ALU.mult, ["rs8", "ncst"], ["rs8"])
                    P.tt("dve", Pn, Pn, rs8.unsqueeze(2).to_broadcast([128, 8, 128]), ALU.mult, ["Pn", "rs8"], ["Pn"])
                    for hg in range(2):
                        pb = P.bank()
                        for hh in range(4):
                            h = hg * 4 + hh
                            P.tr(pbank[pb][:, hh * 128:(hh + 1) * 128], Pn[:, h, :], ident, ["Pn", "ident"],
                                 ["pb%d" % pb])
                        P.cp("act", PnT[:, hg * 4:(hg + 1) * 4, :], pbank[pb][:, :].rearrange("p (h n) -> p h n", h=4),
                             ["pb%d" % pb], ["PnT", "Sx"])
                    pb = P.bank()
                    for h in range(8):
                        P.mm(pbank[pb][:, h * 64:(h + 1) * 64], PnT[:, h, :], KCV[:, h // 4, :], True, True,
                             ["PnT", "KCV"], ["pb%d" % pb])
                    P.tt("dve", NOUT[:, sub, :].rearrange("p (h d) -> p h d", h=8),
                         pbank[pb][:, :].rearrange("p (h d) -> p h d", h=8),
                         GS[:, tq, 0:8].unsqueeze(2).to_broadcast([128, 8, 64]), ALU.mult, ["pb%d" % pb, "GS"], ["NOUT"])
                    P.op("dve", lambda e: e.tensor_reduce(out=PG, in_=Pn.rearrange("p (k g) n -> p k n g", k=2),
                                                         axis=AX.X, op=ALU.add), ["Pn"], ["PG"])
                    pg4 = PG.rearrange("p k (b r) -> p k b r", r=4)
                    P.op("dve", lambda e, pg4=pg4: e.tensor_reduce(out=S012, in_=pg4[:, :, :, 0:3], axis=AX.X, op=ALU.add),
                         ["PG"], ["S012"])
                    P.stt("dve", PSs, S012, 2.0, pg4[:, :, :, 3], ALU.mult, ALU.add, ["S012", "PG"], ["PSs"])
                    P.tt("dve", PSs[:, :, 1:32], PSs[:, :, 1:32], pg4[:, :, 0:31, 3], ALU.add, ["PSs", "PG"], ["PSs"])
                    fv = FVNC[:, tq, 0, :]
                    ncm = FVNC[:, tq, 1, :]
                    for k in range(2):
                        P.tt("dve", SC[:, k, :], PSs[:, k, :], fv, ALU.max, ["PSs", "ncst"], ["SC"])
                        P.tt("dve", SC[:, k, :], SC[:, k, :], ncm, ALU.add, ["SC", "ncst"], ["SC"])
                        P.op("dve", lambda e, k=k: e.max(out=M8, in_=SC[:, k, :]), ["SC"], ["M8"])
                        P.op("dve", lambda e, k=k: e.match_replace(out=SCW, in_to_replace=M8, in_values=SC[:, k, :],
                                                                  imm_value=-1e9), ["SC", "M8"], ["SCW"])
                        P.op("dve", lambda e: e.max(out=M8, in_=SCW), ["SCW"], ["M8"])
                        P.ts("dve", SEL[:, k, :], SC[:, k, :], M8[:, 7:8], ALU.is_ge, ["SC", "M8"], ["SEL"], s2=-1.0,
                             op1=ALU.add)
                        pb = P.bank()
                        P.tr(pbank[pb][0:32, 0:128], SEL[:, k, :], ident, ["SEL", "ident"], ["pb%d" % pb])
                        P.cp("act", SELM1[k][0:32, tq * 128:(tq + 1) * 128], pbank[pb][0:32, 0:128], ["pb%d" % pb],
                             ["SELM1_%d" % k, "XTc"])
                P.st(NOUTD[r0 + tb * 512:r0 + (tb + 1) * 512, :].rearrange("(n p) c -> p n c", p=128), NOUT,
                     ["NOUT"], ["NOUTD"])
            for kvh in range(2 if K3 >= 5 else 0):
                for br, obase in ((0, O_KS), (1, O_KW)):
                    for half in range(2):
                        P.memset("dve", stg[half], 0.0, ["stgX"])
                        P.ld(stg[half][:, :, half * 64:(half + 1) * 64], zt(obase + kvh * 64, 64), ["stgX"])
                        tr_block(stg[half], KTz[br][half], ["KTz"])
                    P.ld(V1n[br][:, :, 0:64], zt(obase + 128 + kvh * 64, 64), ["V1n"])
                for tb in range(4):
                    nsl = NOUT[:, :, 0:256]
                    dsl = NOUTD[r0 + tb * 512:r0 + (tb + 1) * 512, kvh * 256:(kvh + 1) * 256].rearrange(
                        "(n p) c -> p n c", p=128)
                    P.ld(nsl, dsl, ["NOUT"], r=["NOUTD"])
                    for g in range(4):
                        h = kvh * 4 + g
                        if K3 != 7:
                            attn_block(0, h, tb, list(range(0, 4 * tb + 4)))
                        if K3 != 6:
                            attn_block(1, h, tb, list(range(max(0, 4 * tb - 4), 4 * tb + 4)))
                    P.st(dsl, nsl, ["NOUT"], ["NOUTD"])

        def top16(src, scr, vals, idx_u, nelem, key):
            P.op("dve", lambda e: e.max(out=vals[:, 0:8], in_=src), [key], [key + "v"])
            P.op("dve", lambda e: e.max_index(out=idx_u[:, 0:8], in_max=vals[:, 0:8], in_values=src),
                 [key, key + "v"], [key + "i"])
            P.op("dve", lambda e: e.match_replace(out=scr, in_to_replace=vals[:, 0:8], in_values=src,
                                                  imm_value=-1e30), [key, key + "v"], [key + "s"])
            P.op("dve", lambda e: e.max(out=vals[:, 8:16], in_=scr), [key + "s"], [key + "v"])
            P.op("dve", lambda e: e.max_index(out=idx_u[:, 8:16], in_max=vals[:, 8:16], in_values=scr),
                 [key + "s", key + "v"], [key + "i"])

        P.barrier()
        apos[0] = persist_mark
        zt0 = alloc(512)
        P.memset("dve", zt0, 0.0, ["zt0"])
        P.st(MOUT[TP:TP + 128, :], zt0, ["zt0"], ["MOUT_s"])
        P.st(NOUTD[TP:TP + 128, :], zt0, ["zt0"], ["NOUTD_s"])
        wq5 = alloc3(4, 128)
        wk5 = alloc3(4, 128)
        P.ld(wq5, wq_d[:, :, :], ["w5s"])
        P.ld(wk5, wk_d[:, :, :], ["w5s"])
        ZS = alloc(1544)
        CB3 = alloc3(3, 512)
        N0 = alloc(512)
        M0 = alloc(4)
        CW4 = alloc3(4, 512)
        CB4 = alloc(512)
        GB4 = alloc(8)
        NG16 = alloc(128)
        OPS = alloc(128)
        C4 = alloc(512)
        T4 = alloc(512)
        CTs = alloc3(4, 4)
        Q4 = alloc(512)
        K4 = alloc(512)
        S4 = {n: alloc(4) for n in ("IG", "XF", "LF", "MI", "MT", "SWS", "SCI", "EMT", "QK", "NQ", "SW", "DEN", "RD", "T")}
        PACK = alloc3(4, 264)
        C0s = alloc3(16, 128)
        BCT = alloc3(16, 264)
        VTs = alloc3(4, 4)
        T1 = alloc3(16, 128)
        CQ = alloc(16)
        NUM = alloc(16)
        WV = alloc(16)
        HTs = alloc(16)
        HB = alloc(128)
        NN = alloc(512)
        zs4 = Z[TP:TP + SSC, :]
        P.ld(ZS[0:4], zs4[:, 0:1544], ["ZS"])
        P.ld(CB3[0:4], st_conv[:, :, :], ["s5in"])
        P.ld(N0[0:4], stn_d[:, :], ["s5in"])
        P.ld(M0[0:4], stm_d[:, :], ["s5in"])
        P.ld(CW4[0:4], cw4_d[:, :, :], ["s5in"])
        P.ld(CB4[0:4], cb4_d[:, :], ["s5in"])
        P.ld(GB4[0:4], gb4_d[:, :], ["s5in"])
        P.ld(NG16[0:16], ng16_d[:, :], ["s5in"])
        for b in range(SSC):
            P.ld(OPS[4 * b:4 * b + 4], Z[TP + b, O_O:O_O + 512].rearrange("(h v) -> h v", h=4), ["OPS"])
        P.ld(C0s, stC_d.rearrange("a v k -> v a k"), ["C0s"])
        P.tt("dve", C4[0:4], CW4[0:4, 3, :], ZS[0:4, O_U:O_U + 512], ALU.mult, ["s5in", "ZS"], ["C4"])
        for j in range(3):
            P.tt("dve", T4[0:4], CW4[0:4, j, :], CB3[0:4, j, :], ALU.mult, ["s5in"], ["T4"])
            P.tt("dve", C4[0:4], C4[0:4], T4[0:4], ALU.add, ["C4", "T4"], ["C4"])
        P.tt("dve", C4[0:4], C4[0:4], CB4[0:4], ALU.add, ["C4", "s5in"], ["C4"])
        P.act(C4[0:4], C4[0:4], AF.Silu, ["C4"], ["C4"])
        pb = P.bank()
        for h in range(4):
            P.tr(pbank[pb][:, h * 4:(h + 1) * 4], C4[0:4, h * 128:(h + 1) * 128], ident[0:4, 0:4], ["C4", "ident"],
                 ["pb%d" % pb])
        P.cp("act", CTs, pbank[pb][:, 0:16].rearrange("p (h b) -> p h b", h=4), ["pb%d" % pb], ["CTs"])
        pq = P.bank()
        for h in range(4):
            P.mm(pbank[pq][0:4, h * 128:(h + 1) * 128], CTs[:, h, :], wq5[:, h, :], True, True, ["CTs", "w5s"],
                 ["pb%d" % pq])
        P.cp("act", Q4[0:4], pbank[pq][0:4, :], ["pb%d" % pq], ["Q4"])
        pk = P.bank()
        for h in range(4):
            P.mm(pbank[pk][0:4, h * 128:(h + 1) * 128], CTs[:, h, :], wk5[:, h, :], True, True, ["CTs", "w5s"],
                 ["pb%d" % pk])
        P.ts("dve", K4[0:4], pbank[pk][0:4, :], 128.0 ** -0.5, ALU.mult, ["pb%d" % pk], ["K4"])
        A_ = {n: v[0:4] for n, v in S4.items()}
        P.tt("dve", A_["IG"], ZS[0:4, O_I:O_I + 4], GB4[0:4, 0:4], ALU.add, ["ZS", "s5in"], ["IG"])
        P.tt("dve", A_["XF"], ZS[0:4, O_F:O_F + 4], GB4[0:4, 4:8], ALU.add, ["ZS", "s5in"], ["XF"])
        P.act(A_["LF"], A_["XF"], AF.Exp, ["XF"], ["LF"], scale=-1.0)
        P.act(A_["LF"], A_["LF"], AF.Ln, ["LF"], ["LF"], bias=1.0, scale=1.0)
        P.stt("dve", A_["MI"], A_["LF"], -1.0, M0[0:4], ALU.mult, ALU.add, ["LF", "s5in"], ["MI"])
        P.tt("dve", A_["MT"], A_["MI"], A_["IG"], ALU.max, ["MI", "IG"], ["MT"])
        P.tt("dve", A_["T"], A_["IG"], A_["MT"], ALU.subtract, ["IG", "MT"], ["T"])
        P.act(A_["SWS"], A_["T"], AF.Exp, ["T"], ["SWS"])
        P.tt("dve", A_["T"], A_["MI"], A_["MT"], ALU.subtract, ["MI", "MT", "SWS"], ["T"])
        P.act(A_["SCI"], A_["T"], AF.Exp, ["T"], ["SCI"])
        P.act(A_["EMT"], A_["MT"], AF.Exp, ["MT"], ["EMT"], scale=-1.0)
        P.tt("dve", T4[0:4], Q4[0:4], K4[0:4], ALU.mult, ["Q4", "K4"], ["T4"])
        P.op("dve", lambda e: e.reduce_sum(out=A_["QK"], in_=T4[0:4].rearrange("p (h e) -> p h e", h=4), axis=AX.X),
             ["T4"], ["QK"])
        P.tt("dve", T4[0:4], Q4[0:4], N0[0:4], ALU.mult, ["Q4", "s5in", "QK"], ["T4"])
        P.op("dve", lambda e: e.reduce_sum(out=A_["NQ"], in_=T4[0:4].rearrange("p (h e) -> p h e", h=4), axis=AX.X),
             ["T4"], ["NQ"])
        P.tt("dve", A_["SW"], A_["QK"], A_["SWS"], ALU.mult, ["QK", "SWS"], ["SW"])
        P.tt("dve", A_["DEN"], A_["SCI"], A_["NQ"], ALU.mult, ["SCI", "NQ"], ["DEN"])
        P.tt("dve", A_["DEN"], A_["DEN"], A_["SW"], ALU.add, ["DEN", "SW"], ["DEN"])
        P.act(A_["DEN"], A_["DEN"], AF.Abs, ["DEN"], ["DEN"])
        P.tt("dve", A_["DEN"], A_["DEN"], A_["EMT"], ALU.max, ["DEN", "EMT"], ["DEN"])
        P.op("dve", lambda e: e.reciprocal(out=A_["RD"], in_=A_["DEN"]), ["DEN"], ["RD"])
        P.cp("dve", PACK[0:4, :, 0:128], Q4[0:4].rearrange("p (h e) -> p h e", h=4), ["Q4"], ["PACK"])
        P.cp("dve", PACK[0:4, :, 128:256], K4[0:4].rearrange("p (h e) -> p h e", h=4), ["K4"], ["PACK"])
        for i, n in enumerate(("SCI", "SWS", "SW", "RD")):
            P.cp("dve", PACK[0:4, :, 256 + i:257 + i], A_[n].unsqueeze(2), [n], ["PACK"])
        P.st(SCRB.rearrange("(b h) x -> b h x", h=4), PACK[0:4], ["PACK"], ["SCRB"], eng="sp")
        P.ld(BCT, bass.AP(SCRB.tensor, 0, [[0, 128], [264, 16], [1, 264]]), ["BCT"], r=["SCRB"])
        pb = P.bank()
        for h in range(4):
            P.tr(pbank[pb][:, h * 4:(h + 1) * 4], ZS[0:4, O_V + h * 128:O_V + (h + 1) * 128], ident[0:4, 0:4],
                 ["ZS", "ident"], ["pb%d" % pb])
        P.cp("act", VTs, pbank[pb][:, 0:16].rearrange("p (h b) -> p h b", h=4), ["pb%d" % pb], ["VTs"])
        VTbh = VTs.rearrange("p h b -> p b h")
        sc = lambda i: BCT[:, :, 256 + i].rearrange("p (b h) -> p b h", h=4)
        P.tt("dve", T1, C0s, BCT[:, :, 0:128], ALU.mult, ["C0s", "BCT"], ["T1"])
        P.op("dve", lambda e: e.reduce_sum(out=CQ, in_=T1, axis=AX.X), ["T1"], ["CQ"])
        CQ3 = CQ.rearrange("p (b h) -> p b h", h=4)
        NUM3 = NUM.rearrange("p (b h) -> p b h", h=4)
        WV3 = WV.rearrange("p (b h) -> p b h", h=4)
        HT3 = HTs.rearrange("p (b h) -> p b h", h=4)
        P.tt("dve", NUM3, CQ3, sc(0), ALU.mult, ["CQ", "BCT"], ["NUM"])
        P.tt("dve", WV3, VTbh, sc(2), ALU.mult, ["VTs", "BCT"], ["WV"])
        P.tt("dve", NUM3, NUM3, WV3, ALU.add, ["NUM", "WV"], ["NUM"])
        P.tt("dve", HT3, NUM3, sc(3), ALU.mult, ["NUM", "BCT"], ["HTs"])
        pb = P.bank()
        P.tr(pbank[pb][0:16, 0:128], HTs, ident, ["HTs", "ident"], ["pb%d" % pb])
        P.cp("act", HB[0:16], pbank[pb][0:16, 0:128], ["pb%d" % pb], ["HB"])
        st5 = alloc(8)
        ag5 = alloc(4)
        rs5 = alloc(1)
        P.op("dve", lambda e: e.bn_stats(out=st5[0:16, 0:6], in_=HB[0:16]), ["HB"], ["st5"])
        P.op("dve", lambda e: e.bn_aggr(out=ag5[0:16, 0:2], in_=st5[0:16, 0:6]), ["st5"], ["ag5"])
        P.act(rs5[0:16], ag5[0:16, 1:2], AF.Sqrt, ["ag5"], ["rs5"], bias=1e-5, scale=1.0)
        P.op("dve", lambda e: e.reciprocal(out=rs5[0:16], in_=rs5[0:16]), ["rs5"], ["rs5"])
        P.ts("dve", HB[0:16], HB[0:16], ag5[0:16, 0:1], ALU.subtract, ["HB", "ag5", "rs5"], ["HB"], s2=rs5[0:16],
             op1=ALU.mult)
        P.tt("dve", HB[0:16], HB[0:16], NG16[0:16], ALU.mult, ["HB", "s5in"], ["HB"])
        P.act(OPS[0:16], OPS[0:16], AF.Sigmoid, ["OPS"], ["OPS"])
        P.tt("dve", HB[0:16], HB[0:16], OPS[0:16], ALU.mult, ["HB", "OPS"], ["HB"])
        for b in range(SSC):
            P.st(MOUT[TP + b, :].rearrange("(h v) -> h v", h=4), HB[4 * b:4 * b + 4], ["HB"], ["MOUT_s"], eng="sp")
        P.tt("dve", WV3, VTbh, sc(1), ALU.mult, ["VTs", "BCT", "NUM"], ["WV"])
        P.tt("dve", T1, BCT[:, :, 128:256], WV.unsqueeze(2).to_broadcast([128, 16, 128]), ALU.mult, ["BCT", "WV", "CQ"],
             ["T1"])
        P.tt("dve", C0s, C0s, BCT[:, :, 256:257].to_broadcast([128, 16, 128]), ALU.mult, ["C0s", "BCT"], ["C0s"])
        P.tt("dve", C0s, C0s, T1, ALU.add, ["C0s", "T1"], ["C0s"])
        P.st(o_C_s.rearrange("a v k -> v a k"), C0s, ["C0s"], eng="sp")
        N03 = N0[0:4].rearrange("p (h e) -> p h e", h=4)
        NN3 = NN[0:4].rearrange("p (h e) -> p h e", h=4)
        K43 = K4[0:4].rearrange("p (h e) -> p h e", h=4)
        P.tt("dve", NN3, N03, A_["SCI"].unsqueeze(2).to_broadcast([4, 4, 128]), ALU.mult, ["s5in", "SCI"], ["NN"])
        P.tt("dve", T4[0:4].rearrange("p (h e) -> p h e", h=4), K43, A_["SWS"].unsqueeze(2).to_broadcast([4, 4, 128]),
             ALU.mult, ["K4", "SWS", "NQ"], ["T4"])
        P.tt("dve", NN[0:4], NN[0:4], T4[0:4], ALU.add, ["NN", "T4"], ["NN"])
        P.st(o_n_s[:, :], NN[0:4], ["NN"], eng="sp")
        P.st(o_m_s[:, :], A_["MT"], ["MT"], eng="sp")

        P.barrier()
        apos[0] = persist_mark
        K6 = int(os.environ.get("K6STOP", "9"))
        W1Ls = alloc(2 * 16 * 128).rearrange("p (c r o) -> p c r o", c=2, r=16)
        W1Ns = alloc(2 * 16 * 64).rearrange("p (c j o) -> p c j o", c=2, j=16)
        PELs = alloc3(2, 16)
        W2Ns = alloc3(2, 64)
        W2KT = alloc(64)
        SHMs = alloc(128)
        SHD = alloc(128)
        ONESM = alloc(128)
        CBS = alloc3(8, 8)
        BWt = alloc3(4, 8)
        T25 = alloc(2 * 8 * 64).rearrange("p (a h t) -> p a h t", a=2, h=8)
        CBR6 = alloc(8)
        FL255 = alloc(1)
        IND = alloc(4)
        RB0 = alloc(8)
        IOTA8 = alloc(8)
        IOT128 = alloc3(2, 128)
        PETs = alloc(128)
        PETB = alloc3(2, 64)
        PT4i = alloc(4).bitcast(I32)
        PT8i = alloc(128).bitcast(I32)
        PTF = alloc(4)
        PTROW = alloc(128)
        IDXf = alloc3(4, 8)
        P.ld(W1Ls, w1l_d[:, :, :, :], ["c6"])
        P.ld(W1Ns, w1n_d[:, :, :, :], ["c6"])
        P.ld(PELs, pel_d[:, :, :], ["c6"])
        P.ld(W2Ns[0:64], w2n_d[:, :, :], ["c6"])
        P.ld(W2KT[0:64], w2kt_d[:, :], ["c6"])
        P.ld(SHMs, sh_d[:, :], ["c6"])
        P.ld(SHD, shd_d[:, :], ["c6"])
        P.ld(CBS, cbs_d[:, :, :], ["c6"])
        P.ld(BWt, bw_d[:, :, :], ["c6"])
        P.ld(T25[0:60], t25_d[:, :, :, :], ["c6"])
        P.ld(CBR6, cbr_d[:, :], ["c6"])
        P.ld(FL255[0:60], fl255_d[:, :], ["c6"])
        P.ld(IND[0:60], ind_d[:, :], ["c6"])
        P.ld(RB0[0:4], rb0_d[:, :], ["c6"])
        P.ld(IOTA8, iota8_d[:, :], ["c6"])
        P.ld(IOT128[0:8], iot128_d[:, :, :], ["c6"])
        P.ld(PT4i, ptl_d[:, :], ["c6"])
        P.ld(PT8i[0:8], pt8_d[:, :], ["c6"])
        P.memset("dve", ONESM, 1.0, ["ONESM"])
        P.cp("dve", PTF, PT4i, ["c6"], ["PTF"])
        P.cp("dve", PTROW[0:8], PT8i[0:8], ["c6"], ["PTROW"])
        P.stt("dve", IDXf, PTF.unsqueeze(2).to_broadcast([128, 4, 8]), 8.0,
              IOTA8.unsqueeze(1).to_broadcast([128, 4, 8]), ALU.mult, ALU.add, ["PTF", "c6"], ["IDXf"])
        P.cp("dve", IDXC.rearrange("p (b n) -> p b n", b=4), IDXf, ["IDXf"], ["IDXC"])
        for c in range(2):
            pb = P.bank()
            for j in range(16):
                P.mm(pbank[pb][0:1, 0:64], PELs[:, c, j:j + 1], W1Ns[:, c, j, :], j == 0, j == 15, ["c6"], ["pb%d" % pb])
            P.cp("act", PETs[0:1, c * 64:(c + 1) * 64], pbank[pb][0:1, 0:64], ["pb%d" % pb], ["PETs"])
        pb = P.bank()
        P.mm(pbank[pb][:, 0:128], ONESM[0:1, :], PETs[0:1, :], True, True, ["ONESM", "PETs"], ["pb%d" % pb])
        P.cp("act", PETB, pbank[pb][:, 0:128].rearrange("p (c o) -> p c o", c=2), ["pb%d" % pb], ["PETB"])
        ZN = alloc(1304)
        P.ld(ZN[0:4], Z[TP:TP + SSC, O_Q:O_Q + 1304], ["ZN"])
        QS = alloc(512)
        P.ts("dve", QS[0:4], ZN[0:4, 0:512], 0.125, ALU.mult, ["ZN"], ["QS"])
        QTS = alloc3(4, 8)
        pb = P.bank()
        for h in range(8):
            P.tr(pbank[pb][0:64, h * 4:(h + 1) * 4], QS[0:4, h * 64:(h + 1) * 64], ident[0:4, 0:4], ["QS", "ident"],
                 ["pb%d" % pb])
        P.cp("act", QTS[0:64].rearrange("p b h -> p h b"), pbank[pb][0:64, 0:32].rearrange("p (h b) -> p h b", h=8),
             ["pb%d" % pb], ["QTS"])
        QW8 = alloc(64)
        for b in range(SSC):
            pb = P.bank()
            P.mm(pbank[pb][0:8, 0:64], QTS[0:64, b, :], W2KT[0:64, :], True, True, ["QTS", "c6"], ["pb%d" % pb])
            P.cp("act", QW8[0:8], pbank[pb][0:8, 0:64], ["pb%d" % pb], ["QW8"])
            P.st(SCRQ[b, :].rearrange("(h i) -> h i", h=8), QW8[0:8], ["QW8"], ["SCRQ"], eng="sp")
        QWB = alloc3(4, 512)
        P.ld(QWB, bass.AP(SCRQ.tensor, 0, [[0, 128], [512, 4], [1, 512]]), ["QWB"], r=["SCRQ"])
        G = [alloc(4096) for _ in range(2)]
        XTs = alloc(32 * 128).rearrange("p (q n) -> p q n", q=32)
        PABs = alloc(8 * 4 * 128).rearrange("p (n k o) -> p n k o", n=8, k=4)
        HPRE = alloc(8 * 4 * 64).rearrange("p (n k o) -> p n k o", n=8, k=4)
        HTMP = alloc(8 * 4 * 64).rearrange("p (n k o) -> p n k o", n=8, k=4)
        HIDs = alloc(8 * 4 * 64).rearrange("p (n k o) -> p n k o", n=8, k=4)
        TMPs = alloc(1024)
        SCs = alloc3(8, 8)
        RS = alloc(8)
        RT = alloc(8)
        PGs = alloc3(8, 2)
        S3 = alloc(2)
        PSB = alloc3(2, 2)
        U4 = alloc(64)
        UT = alloc(4)
        OC = alloc(65)
        WK = alloc3(4, 256)
        QB128 = alloc(512)
        SWs = alloc3(4, 8)
        gcnt6 = 0
        for b in range(SSC if K6 >= 1 else 0):
            for n_ in range(8):
                gb = gcnt6 % 2
                gcnt6 += 1
                P.dma("pool", lambda e, gb=gb, col=b * 8 + n_: e.indirect_dma_start(
                    out=G[gb], out_offset=None, in_=pool_cmp_d[:, :],
                    in_offset=bass.IndirectOffsetOnAxis(ap=IDXC[:, col:col + 1], axis=0)), ["IDXC"], ["G%d" % gb])
                G3 = G[gb].rearrange("p (r x) -> p r x", r=16)
                for g8 in range(8):
                    pb = P.bank()
                    for j in range(4):
                        q_ = g8 * 4 + j
                        r, c = q_ // 2, q_ % 2
                        P.tr(pbank[pb][:, j * 128:(j + 1) * 128], G3[:, r, c * 128:(c + 1) * 128], ident,
                             ["G%d" % gb, "ident"], ["pb%d" % pb])
                    P.cp("act" if g8 % 2 else "dve", XTs[:, g8 * 4:(g8 + 1) * 4, :],
                         pbank[pb][:, :].rearrange("p (q n) -> p q n", q=4), ["pb%d" % pb], ["XTs"])
                for c in range(2):
                    for k in range(2):
                        rows = slice(k * 64, (k + 1) * 64)
                        pb = P.bank()
                        for r in range(16):
                            P.mm(pbank[pb][:, 0:128], XTs[rows, r * 2 + c, :], W1Ls[rows, c, r, :], r == 0, r == 15,
                                 ["XTs", "c6"], ["pb%d" % pb])
                        P.cp("act", PABs[:, n_, c * 2 + k, :], pbank[pb][:, 0:128], ["pb%d" % pb], ["PABs"])
            P.tt("dve", HPRE[:, 0:7], PABs[:, 0:7, :, 0:64], PABs[:, 1:8, :, 64:128], ALU.add, ["PABs"], ["HPRE"])
            pb = P.bank()
            P.mm(pbank[pb][:, 0:256], SHMs, PABs[:, 0, :, 64:128], True, True, ["c6", "PABs"], ["pb%d" % pb])
            P.tt("dve", HPRE[:, 7], PABs[:, 7, :, 0:64], pbank[pb][:, 0:256].rearrange("p (k o) -> p k o", k=4), ALU.add,
                 ["PABs", "pb%d" % pb], ["HPRE"])
            for c in range(2):
                P.tt("dve", HPRE[:, :, 2 * c:2 * c + 2, :], HPRE[:, :, 2 * c:2 * c + 2, :],
                     PETB[:, c, :].unsqueeze(1).unsqueeze(1).to_broadcast([128, 8, 2, 64]), ALU.add, ["HPRE", "PETB"],
                     ["HPRE"])
            gelu_from(HPRE, HTMP, HIDs, ["HPRE"], "HIDs")
            for h in range(8):
                kvh = h // 4
                P.tt("dve", TMPs[:, 0:512].rearrange("p (n i) -> p n i", n=8), HIDs[:, :, kvh, :],
                     QWB[:, b, h * 64:(h + 1) * 64].unsqueeze(1).to_broadcast([128, 8, 64]), ALU.mult,
                     ["HIDs", "QWB"], ["TMPs"])
                P.op("dve", lambda e, h=h: e.reduce_sum(out=SCs[:, :, h], in_=TMPs[:, 0:512].rearrange(
                    "p (n i) -> p n i", n=8), axis=AX.X), ["TMPs"], ["SCs"])
            P.tt("dve", SCs, SCs, CBS, ALU.add, ["SCs", "c6"], ["SCs"])
            P.act(SCs, SCs, AF.Exp, ["SCs"], ["SCs"])
            P.op("dve", lambda e: e.reduce_sum(out=RS, in_=SCs.rearrange("p n h -> p h n"), axis=AX.X), ["SCs"], ["RS"])
            pb = P.bank()
            P.mm(pbank[pb][:, 0:8], ONESM, RS, True, True, ["ONESM", "RS"], ["pb%d" % pb])
            P.op("dve", lambda e, pb=pb: e.reciprocal(out=RT, in_=pbank[pb][:, 0:8]), ["pb%d" % pb], ["RT"])
            P.tt("dve", SCs, SCs, RT.unsqueeze(1).to_broadcast([128, 8, 8]), ALU.mult, ["SCs", "RT"], ["SCs"])
            for kvh in range(2):
                pb = P.bank()
                for n_ in range(8):
                    P.mm(pbank[pb][0:4, 0:64], SCs[:, n_, kvh * 4:(kvh + 1) * 4], HIDs[:, n_, 2 + kvh, :], n_ == 0,
                         n_ == 7, ["SCs", "HIDs"], ["pb%d" % pb])
                P.cp("act", U4[0:4], pbank[pb][0:4, 0:64], ["pb%d" % pb], ["U4"])
                pb = P.bank()
                P.tr(pbank[pb][0:64, 0:4], U4[0:4, 0:64], ident[0:4, 0:4], ["U4", "ident"], ["pb%d" % pb])
                P.cp("act", UT[0:64], pbank[pb][0:64, 0:4], ["pb%d" % pb], ["UT"])
                pb = P.bank()
                P.mm(pbank[pb][0:4, 0:64], UT[0:64, 0:4], W2Ns[0:64, 1, :], True, True, ["UT", "c6"], ["pb%d" % pb])
                P.cp("act", OC[0:4, 0:64], pbank[pb][0:4, 0:64], ["pb%d" % pb], ["OC"])
                P.st(OSD[0, b, kvh * 4:(kvh + 1) * 4, 0:64], OC[0:4, 0:64], ["OC"], ["OSD"], eng="sp")
            P.op("dve", lambda e: e.tensor_reduce(out=PGs, in_=SCs.rearrange("p n (k g) -> p n k g", k=2), axis=AX.X,
                                                 op=ALU.add), ["SCs"], ["PGs"])
            pb = P.bank()
            P.mm(pbank[pb][:, 0:2], SHD, PGs[:, 7, :], True, True, ["c6", "PGs"], ["pb%d" % pb])
            for eo in range(2):
                o_ = eo * 4
                P.tt("dve", S3, PGs[:, o_ + 0, :], PGs[:, o_ + 1, :], ALU.add, ["PGs"], ["S3"])
                P.tt("dve", S3, S3, PGs[:, o_ + 2, :], ALU.add, ["S3", "PGs"], ["S3"])
                P.stt("dve", PSB[:, :, eo], S3, 2.0, PGs[:, 3, :], ALU.mult, ALU.add, ["S3", "PGs"], ["PSB"])
                if eo == 0:
                    P.tt("dve", PSB[:, :, 0], PSB[:, :, 0], pbank[pb][:, 0:2], ALU.add, ["PSB", "pb%d" % pb], ["PSB"])
                else:
                    P.tt("dve", PSB[:, :, 1], PSB[:, :, 1], PGs[:, 7, :], ALU.add, ["PSB", "PGs"], ["PSB"])
            P.st(SELD[b * 2:b * 2 + 2, :].rearrange("k (p e) -> p k e", e=2), PSB, ["PSB"], ["SELD"], eng="sp")
            P.ld(WK, st_win[b, :, :].rearrange("(c p) x -> p c x", p=128), ["WK"])
            P.ld(QB128, bass.AP(Z.tensor, (TP + b) * IN_DIM + O_Q, [[0, 128], [1, 512]]), ["QB128"])
            P.ts("dve", QB128, QB128, 0.125, ALU.mult, ["QB128"], ["QB128"])
            for h in range(8):
                kvh = h // 4
                P.tt("dve", TMPs[:, 0:256].rearrange("p (c i) -> p c i", c=4), WK[:, :, kvh * 64:(kvh + 1) * 64],
                     QB128[:, h * 64:(h + 1) * 64].unsqueeze(1).to_broadcast([128, 4, 64]), ALU.mult, ["WK", "QB128"],
                     ["TMPs"])
                P.op("dve", lambda e, h=h: e.reduce_sum(out=SWs[:, :, h], in_=TMPs[:, 0:256].rearrange(
                    "p (c i) -> p c i", c=4), axis=AX.X), ["TMPs"], ["SWs"])
            P.tt("dve", SWs, SWs, BWt, ALU.add, ["SWs", "c6"], ["SWs"])
            P.act(SWs, SWs, AF.Exp, ["SWs"], ["SWs"])
            for kvh in range(2):
                pn_ = P.bank()
                for c in range(4):
                    P.mm(pbank[pn_][0:4, 0:64], SWs[:, c, kvh * 4:(kvh + 1) * 4],
                         WK[:, c, 128 + kvh * 64:128 + (kvh + 1) * 64], c == 0, c == 3, ["SWs", "WK"], ["pb%d" % pn_])
                pd_ = P.bank()
                for c in range(4):
                    P.mm(pbank[pd_][0:4, 0:1], SWs[:, c, kvh * 4:(kvh + 1) * 4], ONESM[:, 0:1], c == 0, c == 3,
                         ["SWs", "ONESM"], ["pb%d" % pd_])
                P.cp("act", OC[0:4, 0:64], pbank[pn_][0:4, 0:64], ["pb%d" % pn_], ["OC"])
                P.cp("act", OC[0:4, 64:65], pbank[pd_][0:4, 0:1], ["pb%d" % pd_], ["OC"])
                P.st(OSD[1, b, kvh * 4:(kvh + 1) * 4, :], OC[0:4, :], ["OC"], ["OSD"], eng="sp")

        if K6 >= 2:
            SELIN = alloc(256)
            SELW = alloc(256)
            V16s = alloc(16)
            I16s = alloc(16).bitcast(U32)
            BLK = alloc(16)
            HALF = alloc(16)
            PAR = alloc(16)
            PGID = alloc(16)
            PHYS = alloc(16)
            OH6 = alloc3(15, 128)
            PQ5 = alloc3(15, 5)
            P.ld(SELIN[0:8], SELD[:, :], ["SELIN"], r=["SELD"])
            P.memset("dve", SELIN[0:8, 0:1], -1e9, ["SELIN"])
            P.memset("dve", SELIN[0:8, 255:256], -1e9, ["SELIN"])
            top16(SELIN[0:8], SELW[0:8], V16s[0:8], I16s[0:8], 256, "SELIN")
            P.cp("dve", BLK[0:8], I16s[0:8], ["SELINi"], ["BLK"])
            P.memset("dve", BLK[0:8, 13:14], 0.0, ["BLK"])
            P.memset("dve", BLK[0:8, 14:15], 255.0, ["BLK"])
            B15 = BLK[0:8, 0:15]
            P.tt("dve", OH6[0:8], B15.unsqueeze(2).to_broadcast([8, 15, 128]),
                 IOT128[0:8, 1, :].unsqueeze(1).to_broadcast([8, 15, 128]), ALU.is_ge, ["BLK", "c6"], ["OH6"])
            P.op("dve", lambda e: e.tensor_reduce(out=HALF[0:8, 0:15], in_=OH6[0:8], axis=AX.X, op=ALU.add), ["OH6"],
                 ["HALF"])
            P.stt("dve", PAR[0:8, 0:15], HALF[0:8, 0:15], -2.0, B15, ALU.mult, ALU.add, ["HALF", "BLK"], ["PAR"])
            P.tt("dve", OH6[0:8], HALF[0:8, 0:15].unsqueeze(2).to_broadcast([8, 15, 128]),
                 IOT128[0:8, 0, :].unsqueeze(1).to_broadcast([8, 15, 128]), ALU.is_equal, ["HALF", "c6"], ["OH6"])
            P.tt("dve", OH6[0:8], OH6[0:8], PTROW[0:8].unsqueeze(1).to_broadcast([8, 15, 128]), ALU.mult,
                 ["OH6", "PTROW"], ["OH6"])
            P.op("dve", lambda e: e.tensor_reduce(out=PGID[0:8, 0:15], in_=OH6[0:8], axis=AX.X, op=ALU.add), ["OH6"],
                 ["PGID"])
            P.stt("dve", PHYS[0:8, 0:15], PGID[0:8, 0:15], 2.0, PAR[0:8, 0:15], ALU.mult, ALU.add, ["PGID", "PAR"],
                  ["PHYS"])
            for qd in range(4):
                P.ts("dve", PQ5[0:8, :, qd], PHYS[0:8, 0:15], 4.0, ALU.mult, ["PHYS"], ["PQ5"], s2=float(qd),
                     op1=ALU.add)
            P.cp("dve", PQ5[0:8, :, 4], B15, ["BLK"], ["PQ5"])
            P.st(SCRP[:, :, :], PQ5[0:8], ["PQ5"], ["SCRP"], eng="sp")
            IQf = [alloc(5) for _ in range(2)]
            P.memset("dve", IDXS, 0, ["IDXS"])
            for kvh in range(2):
                for b in range(SSC):
                    P.ld(IQf[kvh][15 * b:15 * b + 15], SCRP[b * 2 + kvh, :, :], ["IQf%d" % kvh], r=["SCRP"])
                P.cp("dve", IDXS[0:60, kvh * 4:(kvh + 1) * 4], IQf[kvh][0:60, 0:4], ["IQf%d" % kvh], ["IDXS"])
            QB60 = alloc(512)
            for b in range(SSC):
                P.ld(QB60[15 * b:15 * b + 15], bass.AP(Z.tensor, (TP + b) * IN_DIM + O_Q, [[0, 15], [1, 512]]), ["QB60"])
            P.ts("dve", QB60[0:60], QB60[0:60], 0.125, ALU.mult, ["QB60"], ["QB60"])
            FL254 = alloc(1)
            W0 = alloc(1)
            BIAS = alloc3(4, 64)
            SS6 = alloc3(4, 64)
            PVD = alloc(260)
            PVt = alloc(64)
            DNt = alloc(4)
            SLCR = [alloc(260) for _ in range(2)]
            for kvh in range(2):
                hs = slice(kvh * 4, (kvh + 1) * 4)
                P.ts("dve", FL254[0:60], IQf[kvh][0:60, 4:5], 254.0, ALU.is_equal, ["IQf%d" % kvh], ["FL254"])
                P.tt("dve", W0[0:60], FL254[0:60], FL255[0:60], ALU.add, ["FL254", "c6"], ["W0"])
                P.ts("dve", W0[0:60], W0[0:60], -1.0, ALU.mult, ["W0"], ["W0"], s2=1.0, op1=ALU.add)
                P.ts("dve", BIAS[0:60], T25[0:60, 0, hs, :], FL254[0:60], ALU.mult, ["c6", "FL254"], ["BIAS"])
                P.stt("dve", BIAS[0:60], T25[0:60, 1, hs, :], FL255[0:60], BIAS[0:60], ALU.mult, ALU.add,
                      ["c6", "BIAS"], ["BIAS"])
                P.stt("dve", BIAS[0:60], CBR6[0:60, hs].unsqueeze(2).to_broadcast([60, 4, 64]), W0[0:60], BIAS[0:60],
                      ALU.mult, ALU.add, ["c6", "W0", "BIAS"], ["BIAS"])
                P.memset("dve", PVD[0:60], 0.0, ["PVD"])
                for qd in range(4):
                    gb = gcnt6 % 2
                    gcnt6 += 1
                    P.dma("pool", lambda e, gb=gb, col=kvh * 4 + qd: e.indirect_dma_start(
                        out=G[gb], out_offset=None, in_=pool_slc_d[:, :],
                        in_offset=bass.IndirectOffsetOnAxis(ap=IDXS[:, col:col + 1], axis=0)), ["IDXS"],
                        ["G%d" % gb])
                    GQ3 = G[gb][0:60].rearrange("p (t x) -> p t x", t=16)
                    ts_ = slice(qd * 16, (qd + 1) * 16)
                    for g in range(4):
                        h = kvh * 4 + g
                        P.tt("dve", TMPs[0:60].rearrange("p (t d) -> p t d", t=16), GQ3[:, :, kvh * 64:(kvh + 1) * 64],
                             QB60[0:60, h * 64:(h + 1) * 64].unsqueeze(1).to_broadcast([60, 16, 64]), ALU.mult,
                             ["G%d" % gb, "QB60"], ["TMPs"])
                        P.op("dve", lambda e, g=g, ts_=ts_: e.reduce_sum(out=SS6[0:60, g, ts_], in_=TMPs[0:60].rearrange(
                            "p (t d) -> p t d", t=16), axis=AX.X), ["TMPs"], ["SS6"])
                    P.tt("dve", SS6[0:60, :, ts_], SS6[0:60, :, ts_], BIAS[0:60, :, ts_], ALU.add, ["SS6", "BIAS"], ["SS6"])
                    P.act(SS6[0:60, :, ts_], SS6[0:60, :, ts_], AF.Exp, ["SS6"], ["SS6"])
                    P.op("dve", lambda e, ts_=ts_: e.reduce_sum(out=DNt[0:60], in_=SS6[0:60, :, ts_], axis=AX.X), ["SS6"],
                         ["DNt"])
                    P.tt("dve", PVD[0:60, 256:260], PVD[0:60, 256:260], DNt[0:60], ALU.add, ["PVD", "DNt"], ["PVD"])
                    for g in range(4):
                        P.tt("dve", TMPs[0:60].rearrange("p (d t) -> p d t", d=64),
                             GQ3[:, :, 128 + kvh * 64:128 + (kvh + 1) * 64].rearrange("p t d -> p d t"),
                             SS6[0:60, g, ts_].unsqueeze(1).to_broadcast([60, 64, 16]), ALU.mult, ["G%d" % gb, "SS6"],
                             ["TMPs"])
                        P.op("dve", lambda e: e.reduce_sum(out=PVt[0:60], in_=TMPs[0:60].rearrange(
                            "p (d t) -> p d t", d=64), axis=AX.X), ["TMPs"], ["PVt"])
                        P.tt("dve", PVD[0:60, g * 64:(g + 1) * 64], PVD[0:60, g * 64:(g + 1) * 64], PVt[0:60], ALU.add,
                             ["PVD", "PVt"], ["PVD"])
                pb = P.bank()
                P.mm(pbank[pb][0:4, 0:260], IND[0:60, :], PVD[0:60, :], True, True, ["c6", "PVD"], ["pb%d" % pb])
                P.cp("act", SLCR[kvh][0:4], pbank[pb][0:4, 0:260], ["pb%d" % pb], ["SLCR%d" % kvh])
            OSB = alloc(2 * 8 * 65).rearrange("p (a h x) -> p a h x", a=2, h=8)
            for a in range(2):
                P.ld(OSB[0:4, a], OSD[a, :, :, :], ["OSB"], r=["OSD"])
            GS4 = alloc(24)
            P.act(GS4[0:4], ZN[0:4, 1280:1304], AF.Sigmoid, ["ZN"], ["GS4"])
            QS3 = QS[0:4].rearrange("p (h d) -> p h d", h=8)
            NOS = alloc(512)
            NOS3 = NOS[0:4].rearrange("p (h d) -> p h d", h=8)
            TL = alloc(512)
            TL3 = TL[0:4].rearrange("p (h d) -> p h d", h=8)
            PTL = alloc(8)
            DEN8 = alloc(8)
            P.tt("dve", NOS3, OSB[0:4, 0, :, 0:64], GS4[0:4, 0:8].unsqueeze(2).to_broadcast([4, 8, 64]), ALU.mult,
                 ["OSB", "GS4"], ["NOS"])
            for br, kbase in ((1, 768), (2, 1024)):
                for kvh in range(2):
                    P.tt("dve", TL3[:, kvh * 4:(kvh + 1) * 4, :], QS3[:, kvh * 4:(kvh + 1) * 4, :],
                         ZN[0:4, kbase + kvh * 64:kbase + (kvh + 1) * 64].unsqueeze(1).to_broadcast([4, 4, 64]),
                         ALU.mult, ["QS", "ZN", "TLr"], ["TL"])
                P.op("dve", lambda e: e.reduce_sum(out=PTL[0:4], in_=TL3, axis=AX.X), ["TL"], ["PTL"])
                P.tt("dve", PTL[0:4], PTL[0:4], RB0[0:4], ALU.add, ["PTL", "c6"], ["PTL"])
                P.act(PTL[0:4], PTL[0:4], AF.Exp, ["PTL"], ["PTL"])
                for kvh in range(2):
                    hs = slice(kvh * 4, (kvh + 1) * 4)
                    if br == 1:
                        num_src = SLCR[kvh][0:4, 0:256].rearrange("p (g d) -> p g d", g=4)
                        den_src = SLCR[kvh][0:4, 256:260]
                        rk = ["SLCR%d" % kvh]
                    else:
                        num_src = OSB[0:4, 1, hs, 0:64]
                        den_src = OSB[0:4, 1, hs, 64]
                        rk = ["OSB"]
                    P.tt("dve", DEN8[0:4, hs], den_src, PTL[0:4, hs], ALU.add, rk + ["PTL"], ["DEN8"])
                    P.tt("dve", TL3[:, hs, :], PTL[0:4, hs].unsqueeze(2).to_broadcast([4, 4, 64]),
                         ZN[0:4, kbase + 128 + kvh * 64:kbase + 128 + (kvh + 1) * 64].unsqueeze(1).to_broadcast([4, 4, 64]),
                         ALU.mult, ["PTL", "ZN", "PTL"], ["TL"])
                    P.tt("dve", TL3[:, hs, :], TL3[:, hs, :], num_src, ALU.add, ["TL"] + rk, ["TL"])
                P.ts("dve", DEN8[0:4], DEN8[0:4], 1e-30, ALU.max, ["DEN8"], ["DEN8"])
                P.op("dve", lambda e: e.reciprocal(out=DEN8[0:4], in_=DEN8[0:4]), ["DEN8"], ["DEN8"])
                P.tt("dve", DEN8[0:4], DEN8[0:4], GS4[0:4, br * 8:(br + 1) * 8], ALU.mult, ["DEN8", "GS4"], ["DEN8"])
                P.tt("dve", TL3, TL3, DEN8[0:4].unsqueeze(2).to_broadcast([4, 8, 64]), ALU.mult, ["TL", "DEN8"], ["TL"])
                P.tt("dve", NOS[0:4], NOS[0:4], TL[0:4], ALU.add, ["NOS", "TL"], ["NOS", "TLr"])
            P.st(NOUTD[TP:TP + SSC, :], NOS[0:4], ["NOS"], ["NOUTD_s"], eng="sp")

        P.barrier()
        apos[0] = persist_mark
        K5 = int(os.environ.get("K5STOP", "9"))
        ALPHA = 2.0 ** 0.25
        WUM = alloc3(4, D)
        WUN = alloc3(4, D)
        WO = alloc3(8, D)
        LNP = alloc3(4, D)
        P.ld(WUM, wupm_d.rearrange("(c p) n -> p c n", p=128), ["w4"])
        P.ld(WUN, wupn_d.rearrange("(c p) n -> p c n", p=128), ["w4"])
        P.ld(WO, wout_d.rearrange("(c p) n -> p c n", p=128), ["w4"])
        P.ld(LNP, lnp_d[:, :, :], ["w4"])
        NB = 2
        XB = [alloc(D) for _ in range(NB)]
        MOB = [alloc(512) for _ in range(NB)]
        NOB = [alloc(512) for _ in range(NB)]
        GMB = [alloc(D) for _ in range(NB)]
        GNB = [alloc(D) for _ in range(NB)]
        MOT = alloc3(4, 128)
        NOTt = alloc3(4, 128)
        MIX = alloc(D)
        MIXT = alloc3(8, 128)
        RB = alloc(D)
        X1B = [alloc(D) for _ in range(NB)]
        STT = alloc(16)
        AGG = alloc(4)
        RSTD = alloc(1)

        def layer_norm(src, dst, gi, rk, wk_):
            for i in range(2):
                P.op("dve", lambda e, i=i: e.bn_stats(out=STT[:, i * 6:(i + 1) * 6], in_=src[:, i * 512:(i + 1) * 512]),
                     rk, ["STT"])
            P.op("dve", lambda e: e.bn_aggr(out=AGG[:, 0:2], in_=STT[:, 0:12].rearrange("p (a b) -> p a b", a=2)),
                 ["STT"], ["AGG"])
            P.act(RSTD, AGG[:, 1:2], AF.Sqrt, ["AGG"], ["RSTD"], bias=1e-5, scale=1.0)
            P.op("dve", lambda e: e.reciprocal(out=RSTD, in_=RSTD), ["RSTD"], ["RSTD"])
            P.ts("dve", dst, src, AGG[:, 0:1], ALU.subtract, rk + ["AGG", "RSTD"], wk_, s2=RSTD, op1=ALU.mult)
            P.tt("pool", dst, dst, LNP[:, gi, :], ALU.mult, wk_ + ["w4"], wk_)
            P.tt("pool", dst, dst, LNP[:, gi + 1, :], ALU.add, wk_ + ["w4"], wk_)

        for ti in range(NT + 1 if K5 >= 1 else 0):
            b_ = ti % NB
            rows = slice(ti * 128, (ti + 1) * 128)
            xsrc = x_p[rows, :] if ti < NT else x_s[:, :]
            P.ld(XB[b_], xsrc, ["XB%d" % b_])
            P.ld(MOB[b_], MOUT[rows, :], ["MOB%d" % b_], r=["MOUT", "MOUT_s"])
            P.ld(NOB[b_], NOUTD[rows, :], ["NOB%d" % b_], r=["NOUTD", "NOUTD_s"])
            P.ld(GMB[b_], Z[rows, O_GM:O_GM + D], ["GMB%d" % b_])
            P.ld(GNB[b_], Z[rows, O_GNN:O_GNN + D], ["GNB%d" % b_])
            for src, skey, dst, dkey in ((MOB[b_], "MOB%d" % b_, MOT, "MOT"), (NOB[b_], "NOB%d" % b_, NOTt, "NOT")):
                pb = P.bank()
                for c in range(4):
                    P.tr(pbank[pb][:, c * 128:(c + 1) * 128], src[:, c * 128:(c + 1) * 128], ident, [skey, "ident"],
                         ["pb%d" % pb])
                P.cp("act", dst, pbank[pb][:, :].rearrange("p (c t) -> p c t", c=4), ["pb%d" % pb], [dkey])
            P.act(GMB[b_], GMB[b_], AF.Sigmoid, ["GMB%d" % b_], ["GMB%d" % b_])
            P.act(GNB[b_], GNB[b_], AF.Sigmoid, ["GNB%d" % b_], ["GNB%d" % b_])
            for nb in range(2):
                ns = slice(nb * 512, (nb + 1) * 512)
                pa = P.bank()
                for c in range(4):
                    P.mm(pbank[pa][:, :], MOT[:, c, :], WUM[:, c, ns], c == 0, c == 3, ["MOT", "w4"], ["pb%d" % pa])
                pn = P.bank()
                for c in range(4):
                    P.mm(pbank[pn][:, :], NOTt[:, c, :], WUN[:, c, ns], c == 0, c == 3, ["NOT", "w4"], ["pb%d" % pn])
                P.tt("dve", MIX[:, ns], pbank[pa][:, :], GMB[b_][:, ns], ALU.mult, ["pb%d" % pa, "GMB%d" % b_], ["MIX"])
                P.tt("dve", GNB[b_][:, ns], pbank[pn][:, :], GNB[b_][:, ns], ALU.mult, ["pb%d" % pn, "GNB%d" % b_],
                     ["GNB%d" % b_])
                P.tt("pool", MIX[:, ns], MIX[:, ns], GNB[b_][:, ns], ALU.add, ["MIX", "GNB%d" % b_], ["MIX"])
            for g in range(2):
                pb = P.bank()
                for j in range(4):
                    c = g * 4 + j
                    P.tr(pbank[pb][:, j * 128:(j + 1) * 128], MIX[:, c * 128:(c + 1) * 128], ident, ["MIX", "ident"],
                         ["pb%d" % pb])
                P.cp("act", MIXT[:, g * 4:(g + 1) * 4, :], pbank[pb][:, :].rearrange("p (c t) -> p c t", c=4),
                     ["pb%d" % pb], ["MIXT"])
            for nb in range(2):
                ns = slice(nb * 512, (nb + 1) * 512)
                pb = P.bank()
                for c in range(8):
                    P.mm(pbank[pb][:, :], MIXT[:, c, :], WO[:, c, ns], c == 0, c == 7, ["MIXT", "w4"], ["pb%d" % pb])
                P.stt("dve", RB[:, ns], XB[b_][:, ns], ALPHA, pbank[pb][:, :], ALU.mult, ALU.add,
                      ["XB%d" % b_, "pb%d" % pb], ["RB"])
            layer_norm(RB, X1B[b_], 0, ["RB"], ["X1B%d" % b_])
            P.st(X1D[rows, :], X1B[b_], ["X1B%d" % b_], ["X1D_%d" % ti])

        P.barrier()
        apos[0] = persist_mark
        LN2 = alloc3(2, D)
        WPR = alloc3(8, 2048)
        IOTA = alloc(16)
        P.ld(LN2, lnp_d[:, 2:4, :], ["w5"])
        IOTA2 = alloc(16)
        P.ld(IOTA, iota_d[:, :], ["w5"])
        P.ts("dve", IOTA2, IOTA, 16.0, ALU.mult, ["w5"], ["w5"], s2=16.0, op1=ALU.add)
        cmark = apos[0]
        CV32 = [alloc(4096) for _ in range(3)]
        CV16 = [alloc(2048).bitcast(BF16) for _ in range(3)]
        ccnt = 0
        for src_d, dst_d in ((peer_u_d, U16D), (peer_v_d, V16D)):
            for ch in range(32):
                cb_ = ccnt % 3
                ccnt += 1
                rs_ = slice(ch * 512, (ch + 1) * 512)
                P.ld(CV32[cb_], src_d[rs_, :].rearrange("(p r) n -> p (r n)", p=128), ["CV32_%d" % cb_])
                eng = ("dve", "act", "pool")[cb_]
                P.cp(eng, CV16[cb_], CV32[cb_], ["CV32_%d" % cb_], ["CV16_%d" % cb_])
                P.st(dst_d[rs_, :].rearrange("(p r) n -> p (r n)", p=128), CV16[cb_], ["CV16_%d" % cb_], ["T16"], eng="sp")
        P.barrier()
        apos[0] = cmark
        PWT = alloc(16 * 8 * 128).rearrange("p (a c m) -> p a c m", a=16, c=8)
        SKT = alloc3(2, 128)
        P.ld(PWT, pwqt_d[:, :, :, :], ["PWT"])
        P.ld(SKT, skt_d[:, :, :], ["SKT"])
        for c in range(8):
            for g in range(4):
                pb = P.bank()
                for j in range(4):
                    hp = g * 4 + j
                    P.mm(pbank[pb][:, j * 128:(j + 1) * 128], PWT[:, hp, c, :], SKT[:, hp % 2, :], True, True,
                         ["PWT", "SKT"], ["pb%d" % pb])
                P.cp("act" if g % 2 else "dve", WPR[:, c, g * 512:(g + 1) * 512], pbank[pb][:, :], ["pb%d" % pb], ["w5"])
        P.barrier()
        apos[0] -= 16 * 8 * 128 + 256
        X1 = [alloc(D) for _ in range(2)]
        X1T = alloc3(8, 128)
        SS = alloc3(16, 128)
        SS2 = alloc3(16, 128)
        V16 = alloc3(16, 16)
        I16 = alloc3(16, 16)
        I16u = I16.bitcast(U32)
        I16f = alloc3(16, 16)
        CAND = alloc3(8, 256)
        CAND2 = alloc3(8, 256)
        VC = alloc3(8, 16)
        ICu = alloc3(8, 16).bitcast(U32)
        ICf = alloc3(8, 16)
        AIX = alloc3(8, 16)
        BIX = alloc3(8, 16)
        OH = alloc(8 * 16 * 16).rearrange("p (h k a) -> p h k a", h=8, k=16)
        I1S = alloc3(8, 16)
        I2S = alloc3(8, 16)
        EF = alloc(128)
        GWs = [alloc3(8, 16) for _ in range(2)]
        sm8b = alloc(8)
        APREs = [alloc(128) for _ in range(2)]
        GT = alloc(128)
        WCOLs = [alloc(128) for _ in range(2)]
        NGU = 8
        NGV = 8
        UBu = [alloc(D // 2).bitcast(BF16) for _ in range(NGU)]
        UBv = [alloc(D // 2).bitcast(BF16) for _ in range(NGV)]
        JUNK = alloc(D)
        RB2 = alloc(D)
        EIs = [EI, EI2]
        DG = [alloc(64).bitcast(BF16) for _ in range(4)]
        YB = alloc(D)
        gcnt = 0
        dcnt = 0

        def stage_a(ti, parts=(0, 1, 2, 3, 4, 5)):
            for part in parts:
                stage_a_part(ti, part)

        def stage_a_part(ti, part):
            b_ = ti % 2
            rows = slice(ti * 128, (ti + 1) * 128)
            if part == 0:
                P.ld(X1[b_], X1D[rows, :], ["X1_%d" % b_], r=["X1D_%d" % ti])
                for g in range(2):
                    pb = P.bank()
                    for j in range(4):
                        c = g * 4 + j
                        P.tr(pbank[pb][:, j * 128:(j + 1) * 128], X1[b_][:, c * 128:(c + 1) * 128], ident,
                             ["X1_%d" % b_, "ident"], ["pb%d" % pb])
                    P.cp("act", X1T[:, g * 4:(g + 1) * 4, :], pbank[pb][:, :].rearrange("p (c t) -> p c t", c=4),
                         ["pb%d" % pb], ["X1T"])
                return
            if part in (1, 2, 3, 4):
                g = part - 1
                pb = P.bank()
                for c in range(8):
                    P.mm(pbank[pb][:, :], X1T[:, c, :], WPR[:, c, g * 512:(g + 1) * 512], c == 0, c == 7,
                         ["X1T", "w5"], ["pb%d" % pb])
                P.cp("act", SS[:, g * 4:(g + 1) * 4, :], pbank[pb][:, :].rearrange("p (a k) -> p a k", a=4),
                     ["pb%d" % pb], ["SS%d" % g])
                for hp in range(g * 4, g * 4 + 4):
                    top16(SS[:, hp, :], SS2[:, hp, :], V16[:, hp, :], I16u[:, hp, :], 128, "SS%d" % (hp // 4))
                return
            VK = ["SS%dv" % g for g in range(4)]
            IK = ["SS%di" % g for g in range(4)]
            V4 = V16.rearrange("p (h q) k -> p h q k", q=2)
            P.tt("dve", CAND.rearrange("p h (a b) -> p h a b", a=16),
                 V4[:, :, 0, :].unsqueeze(3).to_broadcast([128, 8, 16, 16]),
                 V4[:, :, 1, :].unsqueeze(2).to_broadcast([128, 8, 16, 16]), ALU.add, VK, ["CAND"])
            for h in range(8):
                top16(CAND[:, h, :], CAND2[:, h, :], VC[:, h, :], ICu[:, h, :], 256, "CAND")
            P.cp("dve", ICf, ICu, ["CANDi"], ["ICf"])
            P.cp("dve", I16f, I16u, IK, ["I16f"])
            P.tt("dve", OH, ICf.unsqueeze(3).to_broadcast([128, 8, 16, 16]),
                 IOTA2.unsqueeze(1).unsqueeze(1).to_broadcast([128, 8, 16, 16]), ALU.is_ge, ["ICf", "w5"], ["OH"])
            P.op("dve", lambda e: e.tensor_reduce(out=AIX, in_=OH, axis=AX.X, op=ALU.add), ["OH"], ["AIX"])
            P.stt("dve", BIX, AIX, -16.0, ICf, ALU.mult, ALU.add, ["AIX", "ICf"], ["BIX"])
            I4 = I16f.rearrange("p (h q) k -> p h q k", q=2)
            iota_b = IOTA.unsqueeze(1).unsqueeze(1).to_broadcast([128, 8, 16, 16])
            for sel_ix, src_i, dst in ((AIX, I4[:, :, 0, :], I1S), (BIX, I4[:, :, 1, :], I2S)):
                P.tt("dve", OH, sel_ix.unsqueeze(3).to_broadcast([128, 8, 16, 16]), iota_b, ALU.is_equal,
                     ["AIX", "BIX", "w5"], ["OH"])
                P.tt("dve", OH, OH, src_i.unsqueeze(2).to_broadcast([128, 8, 16, 16]), ALU.mult, ["OH", "I16f"], ["OH"])
                P.op("dve", lambda e, dst=dst: e.tensor_reduce(out=dst, in_=OH, axis=AX.X, op=ALU.add), ["OH"],
                     ["I12S"])
            P.stt("dve", EF.rearrange("p (h k) -> p h k", h=8), I1S, 128.0, I2S, ALU.mult, ALU.add, ["I12S"], ["EF"])
            P.cp("dve", EIs[b_], EF, ["EF"], ["EI%d" % b_])
            gw = GWs[b_]
            P.tt("dve", gw, VC, VC[:, :, 0:1].to_broadcast([128, 8, 16]), ALU.subtract, ["CANDv"], ["GW%d" % b_])

        ucnt = [0]
        vcnt = [0]
        dgc = [0]

        def col_u(ti, j):
            b_ = ti % 2
            ub = ucnt[0] % NGU
            ucnt[0] += 1
            P.dma("pool", lambda e: e.indirect_dma_start(
                out=UBu[ub], out_offset=None, in_=U16D[:, :],
                in_offset=bass.IndirectOffsetOnAxis(ap=EIs[b_][:, j:j + 1], axis=0)), ["EI%d" % b_], ["UBu%d" % ub])
            P.op("dve", lambda e: e.scalar_tensor_tensor(
                out=JUNK, in0=UBu[ub], scalar=1.0, in1=X1[b_], op0=ALU.mult, op1=ALU.mult,
                accum_out=APREs[b_][:, j:j + 1]), ["UBu%d" % ub, "X1_%d" % b_], ["JUNK", "APRE%d" % b_])

        def finish_u(ti):
            b_ = ti % 2
            gw = GWs[b_]
            P.act(gw, gw, AF.Exp, ["GW%d" % b_], ["GW%d" % b_])
            P.op("dve", lambda e: e.reduce_sum(out=sm8b, in_=gw, axis=AX.X), ["GW%d" % b_], ["sm8b"])
            P.op("dve", lambda e: e.reciprocal(out=sm8b, in_=sm8b), ["sm8b"], ["sm8b"])
            P.tt("dve", gw, gw, sm8b.unsqueeze(2).to_broadcast([128, 8, 16]), ALU.mult, ["GW%d" % b_, "sm8b"],
                 ["GW%d" % b_])
            gelu_from(APREs[b_], GT, WCOLs[b_], ["APRE%d" % b_], "WCOL%d" % b_)
            P.tt("dve", WCOLs[b_], WCOLs[b_], GWs[b_].rearrange("p h k -> p (h k)"), ALU.mult,
                 ["WCOL%d" % b_, "GW%d" % b_], ["WCOL%d" % b_])

        def col_v(ti, j, p0, p1):
            b_ = ti % 2
            vb = vcnt[0] % NGV
            vcnt[0] += 1
            P.dma("pool", lambda e: e.indirect_dma_start(
                out=UBv[vb], out_offset=None, in_=V16D[:, :],
                in_offset=bass.IndirectOffsetOnAxis(ap=EIs[b_][:, j:j + 1], axis=0)), ["EI%d" % b_], ["UBv%d" % vb])
            dg = dgc[0] % 4
            dgc[0] += 1
            P.act(DG[dg], ident, AF.Copy, ["ident", "WCOL%d" % b_], ["DG%d" % dg], scale=WCOLs[b_][:, j:j + 1])
            P.mm(pbank[p0][:, :], DG[dg], UBv[vb][:, 0:512], j == 0, j == 127, ["DG%d" % dg, "UBv%d" % vb],
                 ["pb%d" % p0])
            P.mm(pbank[p1][:, :], DG[dg], UBv[vb][:, 512:1024], j == 0, j == 127, ["DG%d" % dg, "UBv%d" % vb],
                 ["pb%d" % p1])

        def finish_v(ti, p0, p1):
            b_ = ti % 2
            rows = slice(ti * 128, (ti + 1) * 128)
            P.stt("dve", RB2[:, 0:512], X1[b_][:, 0:512], ALPHA, pbank[p0][:, :], ALU.mult, ALU.add,
                  ["X1_%d" % b_, "pb%d" % p0], ["RB2"])
            P.stt("dve", RB2[:, 512:1024], X1[b_][:, 512:1024], ALPHA, pbank[p1][:, :], ALU.mult, ALU.add,
                  ["X1_%d" % b_, "pb%d" % p1], ["RB2"])
            layer_norm(RB2, YB, 0, ["RB2"], ["YB"])
            P.st(y_out[rows, :], YB, ["YB"], eng="sp")

        LNP = LN2
        ntile = NT + 1 if K5 >= 2 else 0
        if ntile:
            stage_a(0)
            for j in range(128):
                col_u(0, j)
            finish_u(0)
        for ti in range(ntile):
            nxt = ti + 1 < ntile
            p0 = P.bank()
            p1 = P.bank()
            LEAD = 112
            SLICES = {0: (0, 1), 10: (2,), 20: (3,), 30: (4, 5)}
            for k in range(128 + LEAD):
                if nxt and k in SLICES:
                    stage_a(ti + 1, SLICES[k])
                if k < 128:
                    col_v(ti, k, p0, p1)
                if nxt and k >= LEAD:
                    col_u(ti + 1, k - LEAD)
            finish_v(ti, p0, p1)
            if nxt:
                finish_u(ti + 1)

        with nc.Block() as block:
            @block.tensor
            def _(e):
                P.emit("pe", e)

            @block.scalar
            def _(e):
                P.emit("act", e)

            @block.vector
            def _(e):
                P.emit("dve", e)

            @block.gpsimd
            def _(e):
                P.emit("pool", e)

            @block.sync
            def _(e):
                P.emit("sp", e)
    return nc


_NC_CACHE = {}


def kernel(x_prompt, x_sample, cache_cmp_kv, cache_slc_kv, page_table, state_win_kv, state_mlstm_C,
           state_mlstm_n, state_mlstm_m, state_mlstm_conv, w_in, m_conv_w, m_conv_b, m_wq, m_wk,
           m_gate_bias, m_norm_g, cmp_pe, cmp_w1, cmp_w2, rel_bias, w_up_m, w_up_n, w_out, ln1_g, ln1_b,
           ln2_g, ln2_b, peer_wq, peer_subkeys, peer_u, peer_v):
    f32 = np.float32
    if "nc" not in _NC_CACHE:
        _NC_CACHE["nc"] = build_program()
    nc = _NC_CACHE["nc"]
    xp = np.ascontiguousarray(np.asarray(x_prompt, f32)).reshape(BP * SEQ, D)
    xs = np.asarray(x_sample, f32).reshape(BS, D)
    ident = np.eye(128, dtype=f32)
    w_in0 = np.ascontiguousarray(np.asarray(w_in, f32)[0])
    stw = np.asarray(state_win_kv, f32)[0].reshape(BS, 512, 256)
    stc = np.asarray(state_mlstm_conv, f32)[0]
    cw = np.asarray(m_conv_w, f32)[0]
    convw_l = np.ascontiguousarray(np.transpose(cw.reshape(4, 4, 128), (2, 1, 0)))
    convb_l = np.ascontiguousarray(np.asarray(m_conv_b, f32)[0].reshape(4, 128).T)
    wq_l = np.ascontiguousarray(np.transpose(np.asarray(m_wq, f32)[0], (1, 0, 2)))
    wk_l = np.ascontiguousarray(np.transpose(np.asarray(m_wk, f32)[0], (1, 0, 2)))
    gb_l = np.ascontiguousarray(np.asarray(m_gate_bias, f32)[0].T)
    normg_rep = np.ascontiguousarray(np.broadcast_to(np.asarray(m_norm_g, f32)[0][None, :], (128, 512)))
    sel_c = np.zeros((4, 4, 128), f32)
    for h in range(4):
        sel_c[h, h, :] = 1.0
    tri_c = np.triu(np.ones((128, 128), f32))
    relb = np.asarray(rel_bias, f32)
    dist = np.arange(0, 4096)
    nf = np.maximum(dist, 16).astype(f32)
    large = 16 + (np.log(nf / f32(16)) / f32(np.log(8.0)) * f32(16)).astype(np.int32)
    bucket = np.where(dist < 16, dist, np.minimum(large, 31)).astype(np.int64)
    NEGM = f32(-30000.0)
    tt_ = np.arange(SEQ)[:, None]
    nn_ = np.arange(128)[None, :]
    dcm = tt_ - 16 * nn_ - 31
    vcm = (dcm >= 0) & (nn_ <= 126)
    cbt = np.where(vcm[:, None, :], relb[bucket[np.maximum(dcm, 0)]].transpose(0, 2, 1), NEGM).astype(f32)
    ii = np.arange(128)[:, None]
    jj = np.arange(128)[None, :]
    tz = np.empty((128, 8, 2, 128), f32)
    d0 = jj - ii
    tz[:, :, 0, :] = np.where((d0 >= 0)[:, None, :], relb[bucket[np.maximum(d0, 0)]].transpose(0, 2, 1), NEGM)
    tz[:, :, 1, :] = relb[bucket[128 + d0]].transpose(0, 2, 1)
    wz = np.where(jj < ii, f32(0), NEGM).astype(f32)
    ebig = (np.arange(SEQ)[None, :] // 64 == np.arange(32)[:, None]).astype(f32) * f32(30000.0)
    tq_ = (np.arange(16)[None, :, None] * 128 + np.arange(128)[:, None, None])
    bb_ = np.arange(32)[None, None, :]
    cur = tq_ // 64
    fvnc = np.empty((128, 16, 2, 32), f32)
    fvnc[:, :, 0, :] = np.where((bb_ == 0) | (bb_ == cur) | (bb_ == cur - 1), f32(1e4), f32(0))
    fvnc[:, :, 1, :] = np.where(bb_ * 64 <= tq_, f32(0), NEGM)
    rowv = (np.arange(128) >= 31).astype(f32).reshape(128, 1)
    shm = (ii == jj + 1).astype(f32)
    cbr = np.ascontiguousarray(np.broadcast_to(relb[31][None, :], (128, 8))).astype(f32)
    w1 = np.asarray(cmp_w1, f32)[0]
    w1r = w1.reshape(2, 2, 16, 64, 64)
    w1l_h = np.transpose(w1r, (3, 0, 2, 1, 4)).reshape(64, 2, 16, 128)
    w1l = np.ascontiguousarray(np.concatenate([w1l_h, w1l_h], axis=0))
    w1n = np.ascontiguousarray(np.transpose(w1.reshape(2, 16, 128, 64), (2, 0, 1, 3)))
    pel = np.ascontiguousarray(np.transpose(np.asarray(cmp_pe, f32)[0].reshape(2, 16, 128), (2, 0, 1)))
    w2 = np.asarray(cmp_w2, f32)[0]
    w2n = np.ascontiguousarray(np.transpose(w2, (1, 0, 2)))
    w2d = np.zeros((64, 2, 128), f32)
    w2d[:, 0, 0:64] = w2[0]
    w2d[:, 1, 64:128] = w2[0]
    lnp = np.ascontiguousarray(np.broadcast_to(np.stack([np.asarray(a, f32)[0] for a in (ln1_g, ln1_b, ln2_g, ln2_b)])[None],
                                               (128, 4, D)))
    pwq_t = np.ascontiguousarray(np.asarray(peer_wq, f32)[0].reshape(8, 128, 16, 128).transpose(3, 2, 0, 1))
    sk_t = np.ascontiguousarray(np.asarray(peer_subkeys, f32)[0].transpose(2, 0, 1))
    iota16 = np.ascontiguousarray(np.broadcast_to(np.arange(16, dtype=f32)[None, :], (128, 16)))
    cw4 = np.ascontiguousarray(np.broadcast_to(cw[None], (4, 4, 512)))
    cb4 = np.ascontiguousarray(np.broadcast_to(np.asarray(m_conv_b, f32)[0][None], (4, 512)))
    gb4 = np.ascontiguousarray(np.broadcast_to(np.asarray(m_gate_bias, f32)[0].reshape(1, 8), (4, 8)))
    ng16 = np.ascontiguousarray(np.tile(np.asarray(m_norm_g, f32)[0].reshape(4, 128), (4, 1)))
    stC = np.asarray(state_mlstm_C, f32)[0].reshape(BS * 4, 128, 128)
    stn = np.asarray(state_mlstm_n, f32)[0].reshape(BS, 512)
    stm = np.asarray(state_mlstm_m, f32)[0]
    n6 = np.arange(128)[:, None] * 8 + np.arange(8)[None, :]
    d6 = 16353 - 16 * n6
    cbs = np.where((n6 <= 1022)[:, :, None], relb[bucket[np.clip(d6, 0, 4095)]], NEGM).astype(f32)
    i6 = np.arange(4)[None, :] * 128 + np.arange(128)[:, None]
    bw = np.where((i6 >= 1)[:, :, None], relb[bucket[np.clip(512 - i6, 0, 4095)]], NEGM).astype(f32)
    tok = np.arange(64)
    t25 = np.empty((60, 2, 8, 64), f32)
    t25[:, 0] = relb[bucket[128 - tok]].T[None]
    t25[:, 1] = relb[bucket[64 - tok]].T[None]
    fl255 = (np.arange(60) % 15 == 14).astype(f32).reshape(60, 1)
    ind60 = (np.arange(60)[:, None] // 15 == np.arange(4)[None, :]).astype(f32)
    rb0 = np.ascontiguousarray(np.broadcast_to(relb[0][None, :], (4, 8))).astype(f32)
    shd = np.ascontiguousarray(shm.T)
    w2kt = np.ascontiguousarray(w2[0].T)
    iota8 = np.ascontiguousarray(np.broadcast_to(np.arange(8, dtype=f32)[None, :], (128, 8)))
    iot128 = np.empty((8, 2, 128), f32)
    iot128[:, 0] = np.arange(128, dtype=f32)[None]
    iot128[:, 1] = 2.0 * (np.arange(128, dtype=f32)[None] + 1.0)
    pool_c = np.asarray(cache_cmp_kv, f32)[0].reshape(5120 * 8, 4096)
    pool_s = np.asarray(cache_slc_kv, f32)[0].reshape(5120 * 8, 4096)
    ptab = np.asarray(page_table, np.int32)
    shared = {"pool_cmp": pool_c, "pool_slc": pool_s, "cbs": cbs, "bw": bw, "t25": t25, "fl255": fl255, "ind60": ind60,
              "rb0": rb0, "shd": shd, "w2kt": w2kt, "iota8": iota8, "iot128": iot128, "cw4": cw4, "cb4": cb4, "gb4": gb4, "ng16": ng16, "w_up_m": np.ascontiguousarray(np.asarray(w_up_m, f32)[0]), "w_up_n": np.ascontiguousarray(np.asarray(w_up_n, f32)[0]),
              "w_out": np.ascontiguousarray(np.asarray(w_out, f32)[0]), "lnp": lnp, "pwq_t": pwq_t, "sk_t": sk_t,
              "iota16": iota16, "peer_u": np.ascontiguousarray(np.asarray(peer_u, f32)[0]),
              "peer_v": np.ascontiguousarray(np.asarray(peer_v, f32)[0]),
              "cbt": cbt, "tz": tz, "wz": wz, "ebig": ebig, "fvnc": fvnc, "rowv": rowv, "shm": shm, "cbr": cbr,
              "w1l": w1l, "w1n": w1n, "pel": pel, "w2d": w2d, "w2n": w2n,
              "w_in": w_in0, "ident": ident, "convw_l": convw_l, "convb_l": convb_l, "wq_l": wq_l, "wk_l": wk_l,
              "gb_l": gb_l, "normg_rep": normg_rep, "sel_c": sel_c, "tri_c": tri_c}
    in_maps = []
    for c in range(NCORES):
        xs_pad = np.zeros((128, D), f32)
        xs_pad[:SSC] = xs[c * SSC:(c + 1) * SSC]
        in_maps.append({
            **shared,
            "x_p": xp[c * TP:(c + 1) * TP],
            "x_s": xs_pad,
            "st_win": np.ascontiguousarray(stw[c * SSC:(c + 1) * SSC]),
            "st_conv": np.ascontiguousarray(stc[c * SSC:(c + 1) * SSC]),
            "st_C": np.ascontiguousarray(stC[c * SSC * 4:(c + 1) * SSC * 4]),
            "st_n": np.ascontiguousarray(stn[c * SSC:(c + 1) * SSC]),
            "st_m": np.ascontiguousarray(stm[c * SSC:(c + 1) * SSC]),
            "pt_l": np.ascontiguousarray(ptab[c * SSC:(c + 1) * SSC].T),
            "pt8": np.ascontiguousarray(np.repeat(ptab[c * SSC:(c + 1) * SSC], 2, axis=0)),
        })
    res = run_bass_kernel_spmd(nc, in_maps, core_ids=list(range(NCORES)))
    R = res.results
    if DEBUG:
        DBG["R"] = R

    def cat(name):
        return np.concatenate([np.asarray(r[name]) for r in R], axis=0)

    kvt = (2, 2, 64)
    y_p = np.concatenate([np.asarray(r["y_out"])[:TP] for r in R], axis=0).reshape(BP, SEQ, D)
    y_s = np.concatenate([np.asarray(r["y_out"])[TP:TP + SSC] for r in R], axis=0).reshape(BS, 1, D)
    cmp_p = cat("o_cmp_p").reshape((1, BP, SEQ) + kvt)
    cmp_s = cat("o_cmp_s").reshape((1, BS, 1) + kvt)
    slc_p = cat("o_slc_p").reshape((1, BP, SEQ) + kvt)
    slc_s = cat("o_slc_s").reshape((1, BS, 1) + kvt)
    win_p = cat("o_win_p").reshape((1, BP, 512) + kvt)
    win_s = cat("o_win_s").reshape((1, BS, 512) + kvt)
    C_p = cat("o_C_p").reshape(1, BP, 4, 128, 128)
    C_s = cat("o_C_s").reshape(1, BS, 4, 128, 128)
    n_p = cat("o_n_p").reshape(1, BP, 4, 128)
    n_s = cat("o_n_s").reshape(1, BS, 4, 128)
    m_p = cat("o_m_p").reshape(1, BP, 4)
    m_s = cat("o_m_s").reshape(1, BS, 4)
    conv_p = cat("o_conv_p").reshape(1, BP, 3, 512)
    conv_s = cat("o_conv_s").reshape(1, BS, 3, 512)
    return (y_p, y_s, cmp_p, cmp_s, slc_p, slc_s, win_p, win_s, C_p, C_s, n_p, n_s, m_p, m_s, conv_p, conv_s)
```

```python
import contextlib
import os
import numpy as np
import concourse.bass as bass
import concourse.mybir as mybir
from concourse.bass_utils import run_bass_kernel_spmd

F32 = mybir.dt.float32
I32 = mybir.dt.int32
U32 = mybir.dt.uint32
BF16 = mybir.dt.bfloat16
AF = mybir.ActivationFunctionType
ALU = mybir.AluOpType
AX = mybir.AxisListType

NCORES = 8
D = 1024
SEQ = 2048
BP = 16
BS = 32
SPC = BP // NCORES
SSC = BS // NCORES
TP = SPC * SEQ
NT = TP // 128
IN_DIM = 4896
O_U, O_V, O_O, O_I, O_F, O_Q, O_KC, O_KS, O_KW, O_GN, O_GM, O_GNN = (
    0, 512, 1024, 1536, 1540, 1544, 2056, 2312, 2568, 2824, 2848, 3872)
COL_GROUPS = [(0, 512), (512, 512), (1024, 512), (1536, 8), (1544, 512), (2056, 512),
              (2568, 280), (2848, 512), (3360, 512), (3872, 512), (4384, 512)]

DEBUG = False
DBG = {}
ENGS = ("pe", "act", "dve", "pool", "sp")
N_DMA_SEMS = 12


class Prog:
    def __init__(self, nc, stack):
        self.nc = nc
        self.ops = {e: [] for e in ENGS}
        self.cnt = {e: 0 for e in ENGS}
        self.esem = {e: stack.enter_context(nc.semaphore("es_" + e)) for e in ENGS}
        self.dsem = {e: [stack.enter_context(nc.semaphore("ds_%s%d" % (e, i))) for i in range(N_DMA_SEMS)]
                     for e in ("sp", "pool", "act")}
        self.dval = {e: [0] * N_DMA_SEMS for e in ("sp", "pool", "act")}
        self.dnext = {e: 0 for e in ("sp", "pool", "act")}
        self.semobj = {}
        for e in ENGS:
            self.semobj["es_" + e] = self.esem[e]
        for e in self.dsem:
            for i, s in enumerate(self.dsem[e]):
                self.semobj["ds_%s%d" % (e, i)] = s
        self.lastw = {}
        self.readers = {}
        self.waited = {e: {} for e in ENGS}
        self.final_tokens = []

    def _deps(self, eng, reads, writes):
        deps = {}

        def add(tok, same_ok):
            if tok is None:
                return
            s, v = tok
            if s == "es_" + eng and not same_ok:
                return
            if deps.get(s, 0) < v:
                deps[s] = v

        for k in reads:
            add(self.lastw.get(k), eng != "pe")
        for k in writes:
            add(self.lastw.get(k), False)
            for s, v in self.readers.get(k, {}).items():
                add((s, v), False)
        out = []
        for s, v in deps.items():
            if self.waited[eng].get(s, 0) < v:
                self.waited[eng][s] = v
                out.append((s, v))
        return out

    def _commit(self, tok, reads, writes):
        for k in writes:
            self.lastw[k] = tok
            self.readers[k] = {}
        for k in reads:
            r = self.readers.setdefault(k, {})
            if r.get(tok[0], 0) < tok[1]:
                r[tok[0]] = tok[1]

    def op(self, eng, fn, reads=(), writes=()):
        waits = self._deps(eng, reads, writes)
        self.cnt[eng] += 1
        tok = ("es_" + eng, self.cnt[eng])
        self.ops[eng].append((waits, fn, ("es_" + eng, 1)))
        self._commit(tok, reads, writes)
        return tok

    def dma(self, eng, fn, reads=(), writes=(), final=False):
        i = self.dnext[eng]
        self.dnext[eng] = (i + 1) % N_DMA_SEMS
        sname = "ds_%s%d" % (eng, i)
        waits = self._deps(eng, reads, writes)
        prev = self.dval[eng][i]
        if prev > 0 and self.waited[eng].get(sname, 0) < prev:
            self.waited[eng][sname] = prev
            waits.append((sname, prev))
        self.dval[eng][i] += 16
        tok = (sname, self.dval[eng][i])
        self.ops[eng].append((waits, fn, (sname, 16)))
        self._commit(tok, reads, writes)
        if final:
            self.final_tokens.append(tok)
        return tok

    def bank(self):
        b = self._bank = (getattr(self, "_bank", -1) + 1) % 8
        return b

    def mm(self, out, lhsT, rhs, start, stop, r, w):
        return self.op("pe", lambda e: e.matmul(out=out, lhsT=lhsT, rhs=rhs, start=start, stop=stop), r, w)

    def tr(self, out, in_, ident, r, w):
        return self.op("pe", lambda e: e.transpose(out=out, in_=in_, identity=ident), r, w)

    def act(self, out, in_, func, r, w, bias=None, scale=None):
        kw = {}
        if bias is not None:
            kw["bias"] = bias
        if scale is not None:
            kw["scale"] = scale
        return self.op("act", lambda e: e.activation(out=out, in_=in_, func=func, **kw), r, w)

    def tt(self, eng, out, in0, in1, op, r, w):
        return self.op(eng, lambda e: e.tensor_tensor(out=out, in0=in0, in1=in1, op=op), r, w)

    def ts(self, eng, out, in0, s1, op0, r, w, s2=None, op1=None):
        if op1 is None:
            return self.op(eng, lambda e: e.tensor_scalar(out=out, in0=in0, scalar1=s1, scalar2=None, op0=op0), r, w)
        return self.op(eng, lambda e: e.tensor_scalar(out=out, in0=in0, scalar1=s1, scalar2=s2, op0=op0, op1=op1), r, w)

    def stt(self, eng, out, in0, scalar, in1, op0, op1, r, w):
        return self.op(eng, lambda e: e.scalar_tensor_tensor(out=out, in0=in0, scalar=scalar, in1=in1, op0=op0, op1=op1), r, w)

    def cp(self, eng, out, in_, r, w):
        if eng == "act":
            return self.op("act", lambda e: e.copy(out=out, in_=in_), r, w)
        return self.op(eng, lambda e: e.tensor_copy(out=out, in_=in_), r, w)

    def memset(self, eng, out, val, w):
        return self.op(eng, lambda e: e.memset(out, val), (), w)

    def ld(self, out, in_, w, r=(), eng="sp"):
        return self.dma(eng, lambda e: e.dma_start(out=out, in_=in_), r, w)

    def st(self, out, in_, r, w=(), eng="pool"):
        return self.dma(eng, lambda e: e.dma_start(out=out, in_=in_), r, w)

    def barrier(self):
        toks = []
        for e in ENGS:
            if self.cnt[e] > 0:
                toks.append(("es_" + e, self.cnt[e]))
        for e in self.dsem:
            for i in range(N_DMA_SEMS):
                if self.dval[e][i] > 0:
                    toks.append(("ds_%s%d" % (e, i), self.dval[e][i]))
        for e in ENGS:
            waits = []
            for s_, v in toks:
                if s_ == "es_" + e and e in ("pe", "sp"):
                    continue
                if self.waited[e].get(s_, 0) < v:
                    self.waited[e][s_] = v
                    waits.append((s_, v))
            if waits:
                self.ops[e].append((waits, None, None))
        self.lastw = {}
        self.readers = {}

    def emit(self, eng, eobj):
        for waits, fn, inc in self.ops[eng]:
            sname, amt = inc if inc is not None else (None, None)
            for s, v in waits:
                eobj.wait_ge(self.semobj[s], v)
            if fn is None:
                continue
            ins = fn(eobj)
            ins.then_inc(self.semobj[sname], amt)
        if eng == "sp":
            for e in self.dsem:
                for i in range(N_DMA_SEMS):
                    if self.dval[e][i] > 0:
                        eobj.wait_ge(self.dsem[e][i], self.dval[e][i])


def build_program():
    nc = bass.Bass("TRN2", target_bir_lowering=False)
    stack = contextlib.ExitStack()
    with stack:
        def din(name, shape, dt=F32):
            return nc.dram_tensor(name, list(shape), dt, kind="ExternalInput").ap()

        def dout(name, shape, dt=F32):
            return nc.dram_tensor(name, list(shape), dt, kind="ExternalOutput").ap()

        def dscr(name, shape, dt=F32):
            return nc.dram_tensor(name, list(shape), dt, kind="Internal").ap()

        def sb(name, shape, dt=F32):
            return stack.enter_context(nc.sbuf_tensor(name, list(shape), dt))

        def ps(name, shape, dt=F32):
            return stack.enter_context(nc.psum_tensor(name, list(shape), dt))

        x_p = din("x_p", [TP, D])
        x_s = din("x_s", [128, D])
        w_in = din("w_in", [D, IN_DIM])
        ident_d = din("ident", [128, 128])
        st_win = din("st_win", [SSC, 512, 256])
        st_conv = din("st_conv", [SSC, 3, 512])

        o_cmp_p = dout("o_cmp_p", [TP, 256])
        o_slc_p = dout("o_slc_p", [TP, 256])
        o_win_p = dout("o_win_p", [SPC, 512, 256])
        o_conv_p = dout("o_conv_p", [SPC, 3, 512])
        o_cmp_s = dout("o_cmp_s", [SSC, 256])
        o_slc_s = dout("o_slc_s", [SSC, 256])
        o_win_s = dout("o_win_s", [SSC, 512, 256])
        o_conv_s = dout("o_conv_s", [SSC, 3, 512])

        convw_d = din("convw_l", [128, 4, 4])
        convb_d = din("convb_l", [128, 4])
        wq_d = din("wq_l", [128, 4, 128])
        wk_d = din("wk_l", [128, 4, 128])
        gb_d = din("gb_l", [4, 2])
        normg_d = din("normg_rep", [128, 512])
        sel_d = din("sel_c", [4, 4, 128])
        tri_d = din("tri_c", [128, 128])
        o_C_p = dout("o_C_p", [SPC, 4, 128, 128])
        o_n_p = dout("o_n_p", [SPC, 4, 128])
        o_m_p = dout("o_m_p", [SPC, 4])
        MOUT = (dout if DEBUG else dscr)("MOUT", [TP + 128, 512])
        cbt_d = din("cbt", [SEQ, 8, 128])
        tz_d = din("tz", [128, 8, 2, 128])
        wz_d = din("wz", [128, 128])
        ebig_d = din("ebig", [32, SEQ])
        fvnc_d = din("fvnc", [128, 16, 2, 32])
        rowv_d = din("rowv", [128, 1])
        sh_d = din("shm", [128, 128])
        cbr_d = din("cbr", [128, 8])
        w1l_d = din("w1l", [128, 2, 16, 128])
        w1n_d = din("w1n", [128, 2, 16, 64])
        pel_d = din("pel", [128, 2, 16])
        w2d_d = din("w2d", [64, 2, 128])
        w2n_d = din("w2n", [64, 2, 64])
        NOUTD = (dout if DEBUG else dscr)("NOUTD", [TP + 128, 512])
        wupm_d = din("w_up_m", [512, D])
        wupn_d = din("w_up_n", [512, D])
        wout_d = din("w_out", [D, D])
        lnp_d = din("lnp", [128, 4, D])
        pwqt_d = din("pwq_t", [128, 16, 8, 128])
        skt_d = din("sk_t", [128, 2, 128])
        iota_d = din("iota16", [128, 16])
        peer_u_d = din("peer_u", [16384, D])
        peer_v_d = din("peer_v", [16384, D])
        X1D = (dout if DEBUG else dscr)("X1D", [TP + 128, D])
        y_out = dout("y_out", [TP + 128, D])
        U16D = dscr("U16D", [16384, D], BF16)
        V16D = dscr("V16D", [16384, D], BF16)
        stC_d = din("st_C", [SSC * 4, 128, 128])
        stn_d = din("st_n", [SSC, 512])
        stm_d = din("st_m", [SSC, 4])
        cw4_d = din("cw4", [4, 4, 512])
        cb4_d = din("cb4", [4, 512])
        gb4_d = din("gb4", [4, 8])
        ng16_d = din("ng16", [16, 128])
        o_C_s = dout("o_C_s", [SSC * 4, 128, 128])
        o_n_s = dout("o_n_s", [SSC, 512])
        o_m_s = dout("o_m_s", [SSC, 4])
        SCRB = dscr("SCRB", [16, 264])
        pool_cmp_d = din("pool_cmp", [5120 * 8, 4096])
        pool_slc_d = din("pool_slc", [5120 * 8, 4096])
        ptl_d = din("pt_l", [128, 4], I32)
        pt8_d = din("pt8", [8, 128], I32)
        cbs_d = din("cbs", [128, 8, 8])
        bw_d = din("bw", [128, 4, 8])
        t25_d = din("t25", [60, 2, 8, 64])
        fl255_d = din("fl255", [60, 1])
        ind_d = din("ind60", [60, 4])
        rb0_d = din("rb0", [4, 8])
        shd_d = din("shd", [128, 128])
        w2kt_d = din("w2kt", [64, 64])
        iota8_d = din("iota8", [128, 8])
        iot128_d = din("iot128", [8, 2, 128])
        SCRQ = dscr("SCRQ", [4, 512])
        OSD = dscr("OSD", [2, 4, 8, 65])
        SELD = dscr("SELD", [8, 256])
        SCRP = dscr("SCRP", [8, 15, 5])
        Z = dscr("Z", [TP + 128, IN_DIM])

        P = Prog(nc, stack)

        ARENA_F = 52800
        EI = sb("EI_i32", [128, 128], I32)[:, :]
        EI2 = sb("EI2_i32", [128, 128], I32)[:, :]
        IDXC = sb("IDXC_i32", [128, 32], I32)[:, :]
        IDXS = sb("IDXS_i32", [128, 8], I32)[:, :]
        arena = sb("arena", [128, ARENA_F])
        apos = [0]

        def alloc(n):
            o = apos[0]
            apos[0] += n
            assert apos[0] <= ARENA_F, ("arena overflow", apos[0])
            return arena[:, o:o + n]

        def alloc3(a, b):
            return alloc(a * b).rearrange("p (a b) -> p a b", a=a)

        ident = alloc(128)
        pbank = [ps("pb%d" % i, [128, 512]) for i in range(8)]
        tri = alloc(128)
        selc = alloc3(4, 128)
        persist_mark = apos[0]

        QT = 8
        xt = [alloc(D) for i in range(2)]
        xT = alloc3(8, QT * 128)
        wg = [alloc3(8, 512) for i in range(2)]
        zs = [alloc(512) for i in range(2)]

        P.dma("sp", lambda e: e.dma_start(out=ident, in_=ident_d[:, :]), writes=["ident"])
        P.ld(tri, tri_d[:, :], ["tri"])
        P.ld(selc[0:4], sel_d[:, :, :], ["selc"])

        n_tiles_all = NT + 1
        groups = [list(range(g, min(g + QT, n_tiles_all))) for g in range(0, n_tiles_all, QT)]
        xcnt = 0
        wcnt = 0
        zcnt = 0
        pcnt = 0
        for grp in groups:
            for li, ti in enumerate(grp):
                b = xcnt % 2
                xcnt += 1
                src = x_p[ti * 128:(ti + 1) * 128, :] if ti < NT else x_s[:, :]
                P.dma("sp", lambda e, b=b, src=src: e.dma_start(out=xt[b], in_=src),
                      writes=["xt%d" % b])
                for c in range(8):
                    pb = pcnt % 8
                    pcnt += 1
                    P.op("pe", lambda e, pb=pb, b=b, c=c: e.transpose(
                        out=pbank[pb][:, 0:128], in_=xt[b][:, c * 128:(c + 1) * 128], identity=ident),
                        reads=["xt%d" % b, "ident"], writes=["pb%d" % pb])
                    eng = "dve" if c % 2 == 0 else "act"
                    if eng == "dve":
                        P.op("dve", lambda e, pb=pb, c=c, li=li: e.tensor_copy(
                            out=xT[:, c, li * 128:(li + 1) * 128], in_=pbank[pb][:, 0:128]),
                            reads=["pb%d" % pb], writes=["xT_%d_%d" % (c, li)])
                    else:
                        P.op("act", lambda e, pb=pb, c=c, li=li: e.copy(
                            out=xT[:, c, li * 128:(li + 1) * 128], in_=pbank[pb][:, 0:128]),
                            reads=["pb%d" % pb], writes=["xT_%d_%d" % (c, li)])
            for (c0, cw) in COL_GROUPS:
                wb = wcnt % 2
                wcnt += 1
                P.dma("sp", lambda e, wb=wb, c0=c0, cw=cw: e.dma_start(
                    out=wg[wb][:, :, 0:cw],
                    in_=w_in[:, c0:c0 + cw].rearrange("(c p) n -> p c n", p=128)),
                    writes=["wg%d" % wb])
                for li, ti in enumerate(grp):
                    pb = pcnt % 8
                    pcnt += 1
                    for c in range(8):
                        P.op("pe", lambda e, pb=pb, c=c, li=li, wb=wb, cw=cw: e.matmul(
                            out=pbank[pb][:, 0:cw], lhsT=xT[:, c, li * 128:(li + 1) * 128],
                            rhs=wg[wb][:, c, 0:cw], start=(c == 0), stop=(c == 7)),
                            reads=["xT_%d_%d" % (c, li), "wg%d" % wb], writes=["pb%d" % pb])
                    zb = zcnt % 2
                    zcnt += 1
                    if zcnt % 2 == 0:
                        P.op("dve", lambda e, pb=pb, zb=zb, cw=cw: e.tensor_copy(
                            out=zs[zb][:, 0:cw], in_=pbank[pb][:, 0:cw]),
                            reads=["pb%d" % pb], writes=["zs%d" % zb])
                    else:
                        P.op("act", lambda e, pb=pb, zb=zb, cw=cw: e.copy(
                            out=zs[zb][:, 0:cw], in_=pbank[pb][:, 0:cw]),
                            reads=["pb%d" % pb], writes=["zs%d" % zb])
                    P.dma("pool", lambda e, zb=zb, ti=ti, c0=c0, cw=cw: e.dma_start(
                        out=Z[ti * 128:(ti + 1) * 128, c0:c0 + cw], in_=zs[zb][:, 0:cw]),
                        reads=["zs%d" % zb], writes=["Z_%d_%d" % (ti, c0)])

        ZALL = ["Z_%d_%d" % (ti, c0) for ti in range(NT + 1) for (c0, _) in COL_GROUPS]

        def d2d(dst, src, final=True):
            P.dma("sp", lambda e: e.dma_start(out=dst, in_=src), reads=ZALL, writes=[], final=final)

        d2d(o_cmp_p[:, :], Z[0:TP, O_KC:O_KC + 256])
        d2d(o_slc_p[:, :], Z[0:TP, O_KS:O_KS + 256])
        for s in range(SPC):
            d2d(o_win_p[s, :, :], Z[s * SEQ + SEQ - 512:(s + 1) * SEQ, O_KW:O_KW + 256])
            d2d(o_conv_p[s, :, :], Z[(s + 1) * SEQ - 3:(s + 1) * SEQ, O_U:O_U + 512])
        d2d(o_cmp_s[:, :], Z[TP:TP + SSC, O_KC:O_KC + 256])
        d2d(o_slc_s[:, :], Z[TP:TP + SSC, O_KS:O_KS + 256])
        for s in range(SSC):
            d2d(o_win_s[s, 0:511, :], st_win[s, 1:512, :])
            d2d(o_win_s[s, 511:512, :], Z[TP + s:TP + s + 1, O_KW:O_KW + 256])
            d2d(o_conv_s[s, 0:2, :], st_conv[s, 1:3, :])
            d2d(o_conv_s[s, 2:3, :], Z[TP + s:TP + s + 1, O_U:O_U + 512])


        P.barrier()
        apos[0] = persist_mark
        CONST = ["ident", "tri", "selc", "m_w"]
        convw = alloc3(4, 4)
        convb = alloc(4)
        wq = alloc3(4, 128)
        wk = alloc3(4, 128)
        gb = alloc(2)
        ngbf = alloc(1)
        normg = alloc(512)
        zero_row = alloc(SEQ)
        P.ld(convw, convw_d[:, :, :], ["m_w"])
        P.ld(convb, convb_d[:, :], ["m_w"])
        P.ld(wq, wq_d[:, :, :], ["m_w"])
        P.ld(wk, wk_d[:, :, :], ["m_w"])
        P.ld(gb[0:4], gb_d[:, :], ["gb"])
        P.ld(normg, normg_d[:, :], ["m_w"])
        P.memset("dve", zero_row, 0.0, ["zero_row"])
        P.ts("dve", ngbf[0:4], gb[0:4, 1:2], -1.0, ALU.mult, ["gb"], ["ngbf"])

        ZIF = alloc3(16, 8)
        IT = alloc(SEQ)
        FT = alloc(SEQ)
        Bc = alloc(SEQ)
        Ac = IT
        CMc = FT
        Mc = Bc
        NRc = alloc(SEQ)
        EMc = alloc(SEQ)
        aT = alloc3(16, 4)
        emT = alloc3(16, 4)
        Uh = alloc3(16, 128)
        UT = alloc(3 + SEQ)
        CT = alloc(SEQ)
        ACC = CT
        QTh = alloc(SEQ)
        KTh = alloc(SEQ)
        KTOK = alloc3(16, 128)
        V1 = alloc3(16, 129)
        OP = alloc3(16, 128)
        SIG = alloc3(16, 128)
        Rb = alloc(SEQ)
        WT = alloc3(16, 512)
        Eb = [alloc(512) for _ in range(2)]
        MO = alloc3(16, 128)
        wts = alloc(16)
        VW = alloc3(16, 128)
        Csb = alloc(128)
        nsb = alloc(128)
        ep = [dict(dd=alloc(1), rec=alloc(1), hq=alloc(128), st=alloc(8), ag=alloc(4), rstd=alloc(1),
                   hn=alloc(128)) for _ in range(2)]
        BN_S = 6

        P.memset("dve", UT[:, 0:3], 0.0, ["UT"])
        P.memset("dve", V1[:, :, 128:129], 1.0, ["V1ones"])
        ecnt = 0
        epc = 0
        for sq in range(SPC):
            r0 = sq * SEQ
            zrows = Z[r0:r0 + SEQ, :]
            P.ld(ZIF, zrows[:, O_I:O_I + 8].rearrange("(n p) c -> p n c", p=128), ["ZIF"])
            for which, dst, dkey in ((0, IT, "ITb"), (1, FT, "FTb")):
                for g in range(4):
                    pb = P.bank()
                    for j in range(4):
                        n = g * 4 + j
                        P.tr(pbank[pb][0:4, j * 128:(j + 1) * 128], ZIF[:, n, which * 4:which * 4 + 4], ident,
                             ["ZIF", "ident"], ["pb%d" % pb])
                    P.cp("dve", dst[0:4, g * 512:(g + 1) * 512], pbank[pb][0:4, :], ["pb%d" % pb], [dkey])
            ITk = ["ITb"]
            FTk = ["FTb"]
            P.ts("dve", IT[0:4], IT[0:4], gb[0:4, 0:1], ALU.add, ITk + ["gb"], ["ITb"])
            P.act(FT[0:4], FT[0:4], AF.Exp, FTk + ["ngbf"], ["FTb"], bias=ngbf[0:4], scale=-1.0)
            P.act(FT[0:4], FT[0:4], AF.Ln, ["FTb"], ["FTb"], bias=1.0, scale=1.0)
            P.ts("dve", FT[0:4], FT[0:4], -1.0, ALU.mult, ["FTb"], ["FTb"])
            P.op("dve", lambda e: e.tensor_tensor_scan(out=Bc[0:4], data0=FT[0:4], data1=zero_row[0:4], initial=0.0,
                                                      op0=ALU.add, op1=ALU.add), ["FTb", "zero_row"], ["Bb"])
            P.tt("dve", Ac[0:4], IT[0:4], Bc[0:4], ALU.subtract, ["ITb", "Bb"], ["ITb", "ITb"] + ITk)
            P.op("dve", lambda e: e.tensor_tensor_scan(out=CMc[0:4], data0=Ac[0:4], data1=Ac[0:4], initial=0.0,
                                                      op0=ALU.max, op1=ALU.max), ["ITb"], ["FTb", "FTb", "FTb", "FTb"] + FTk)
            P.ts("dve", NRc[0:4], CMc[0:4], -1.0, ALU.mult, ["FTb"], ["NRc"])
            P.tt("dve", Mc[0:4], Bc[0:4], CMc[0:4], ALU.add, ["Bb", "FTb", "ITb"], ["Bb", "Bb"])
            P.act(EMc[0:4], Mc[0:4], AF.Exp, ["Bb"], ["EMc"], scale=-1.0)
            P.st(o_m_p[sq, :].rearrange("(h o) -> h o", o=1), Mc[0:4, SEQ - 1:SEQ], ["Bb"])
            for src, skey, dst, dkey in ((Ac, "ITb", aT, "aT"), (EMc, "EMc", emT, "emT")):
                pb = P.bank()
                for n in range(16):
                    P.tr(pbank[pb][:, n * 4:(n + 1) * 4], src[0:4, n * 128:(n + 1) * 128], ident[0:4, 0:4],
                         [skey, "ident"], ["pb%d" % pb])
                P.cp("dve", dst, pbank[pb][:, 0:64].rearrange("p (n h) -> p n h", h=4), ["pb%d" % pb], [dkey])
            for h in range(4):
                hs = slice(h * 128, (h + 1) * 128)
                P.ld(Uh, zrows[:, O_U + h * 128:O_U + (h + 1) * 128].rearrange("(n p) c -> p n c", p=128), ["Uh"])
                P.ld(V1[:, :, 0:128], zrows[:, O_V + h * 128:O_V + (h + 1) * 128].rearrange("(n p) c -> p n c", p=128),
                     ["V1"])
                P.ld(OP, zrows[:, O_O + h * 128:O_O + (h + 1) * 128].rearrange("(n p) c -> p n c", p=128), ["OP"])
                P.act(SIG, OP, AF.Sigmoid, ["OP"], ["SIG"])
                for g in range(4):
                    pb = P.bank()
                    P.mm(pbank[pb][:, :], selc[0:4, h, :], NRc[0:4, g * 512:(g + 1) * 512], True, True,
                         ["selc", "NRc"], ["pb%d" % pb])
                    P.cp("act", Rb[:, g * 512:(g + 1) * 512], pbank[pb][:, :], ["pb%d" % pb], ["Rb%d" % g])
                for g in range(4):
                    pb = P.bank()
                    for j in range(4):
                        n = g * 4 + j
                        P.tr(pbank[pb][:, j * 128:(j + 1) * 128], Uh[:, n, :], ident, ["Uh", "ident"], ["pb%d" % pb])
                    P.cp("dve", UT[:, 3 + g * 512:3 + (g + 1) * 512], pbank[pb][:, :], ["pb%d" % pb], ["UT"])
                P.ts("dve", ACC, UT[:, 0:SEQ], convw[:, h, 0:1], ALU.mult, ["UT", "m_w"], ["CT", "CT"])
                for j in range(1, 4):
                    P.stt("dve", ACC, UT[:, j:j + SEQ], convw[:, h, j:j + 1], ACC, ALU.mult, ALU.add,
                          ["UT", "CT", "m_w"], ["CT"])
                P.act(CT, ACC, AF.Silu, ["CT", "m_w"], ["CT", "CT"], bias=convb[:, h:h + 1])
                for g in range(4):
                    pb = P.bank()
                    P.mm(pbank[pb][:, :], wq[:, h, :], CT[:, g * 512:(g + 1) * 512], True, True, ["m_w", "CT"],
                         ["pb%d" % pb])
                    P.cp("act", QTh[:, g * 512:(g + 1) * 512], pbank[pb][:, :], ["pb%d" % pb], ["QT%d" % g])
                    pb = P.bank()
                    P.mm(pbank[pb][:, :], wk[:, h, :], CT[:, g * 512:(g + 1) * 512], True, True, ["m_w", "CT"],
                         ["pb%d" % pb])
                    P.ts("dve", KTh[:, g * 512:(g + 1) * 512], pbank[pb][:, :], 128.0 ** -0.5, ALU.mult,
                         ["pb%d" % pb], ["KT"])
                    pb = P.bank()
                    for j in range(4):
                        n = g * 4 + j
                        P.mm(pbank[pb][:, j * 128:(j + 1) * 128], CT[:, n * 128:(n + 1) * 128], wk[:, h, :], True, True,
                             ["m_w", "CT"], ["pb%d" % pb])
                    P.ts("dve", KTOK[:, g * 4:(g + 1) * 4, :], pbank[pb][:, :].rearrange("p (j e) -> p j e", j=4),
                         128.0 ** -0.5, ALU.mult, ["pb%d" % pb], ["KTOK"])
                for tb in range(4):
                    nst = 4 * tb + 4
                    for st_ in range(nst):
                        p_ = max(0, st_ - 4 * tb)
                        c0 = p_ * 128
                        cs = slice(c0, 512)
                        gs = slice(tb * 512 + c0, (tb + 1) * 512)
                        pb = P.bank()
                        P.mm(pbank[pb][:, cs], KTh[:, st_ * 128:(st_ + 1) * 128], QTh[:, gs], True, True,
                             ["KT", "QT%d" % tb], ["pb%d" % pb])
                        eb = ecnt % 2
                        ecnt += 1
                        P.ts("dve", Eb[eb][:, cs], Rb[:, gs], aT[:, st_, h:h + 1], ALU.add, ["Rb%d" % tb, "aT"],
                             ["Eb%d" % eb], s2=0.0, op1=ALU.min)
                        P.act(Eb[eb][:, cs], Eb[eb][:, cs], AF.Exp, ["Eb%d" % eb], ["Eb%d" % eb])
                        if st_ >= 4 * tb:
                            P.tt("pool", Eb[eb][:, c0:c0 + 128], Eb[eb][:, c0:c0 + 128], tri, ALU.mult,
                                 ["Eb%d" % eb, "tri"], ["Eb%d" % eb])
                        P.tt("dve", WT[:, st_, cs], pbank[pb][:, cs], Eb[eb][:, cs], ALU.mult,
                             ["pb%d" % pb, "Eb%d" % eb], ["WT%d" % st_])
                    for sub in range(4):
                        tq = 4 * tb + sub
                        pb = P.bank()
                        for st_ in range(tq + 1):
                            P.mm(pbank[pb][:, 0:129], WT[:, st_, sub * 128:(sub + 1) * 128], V1[:, st_, :],
                                 st_ == 0, st_ == tq, ["WT%d" % st_, "V1", "V1ones"], ["pb%d" % pb])
                        E = ep[epc % 2]
                        ek = "ep%d" % (epc % 2)
                        epc += 1
                        P.act(E["dd"], pbank[pb][:, 128:129], AF.Abs, ["pb%d" % pb], [ek + "dd"])
                        P.ts("dve", E["dd"], E["dd"], emT[:, tq, h:h + 1], ALU.max, [ek + "dd", "emT"], [ek + "dd"])
                        P.op("dve", lambda e, E=E: e.reciprocal(out=E["rec"], in_=E["dd"]), [ek + "dd"], [ek + "rec"])
                        P.ts("dve", E["hq"], pbank[pb][:, 0:128], E["rec"], ALU.mult, ["pb%d" % pb, ek + "rec"],
                             [ek + "hq"])
                        P.op("dve", lambda e, E=E: e.bn_stats(out=E["st"][:, 0:BN_S], in_=E["hq"]), [ek + "hq"],
                             [ek + "st"])
                        P.op("dve", lambda e, E=E: e.bn_aggr(out=E["ag"][:, 0:2], in_=E["st"][:, 0:BN_S]), [ek + "st"],
                             [ek + "ag"])
                        P.act(E["rstd"], E["ag"][:, 1:2], AF.Sqrt, [ek + "ag"], [ek + "rstd"], bias=1e-5, scale=1.0)
                        P.op("dve", lambda e, E=E: e.reciprocal(out=E["rstd"], in_=E["rstd"]), [ek + "rstd"],
                             [ek + "rstd"])
                        P.ts("dve", E["hn"], E["hq"], E["ag"][:, 0:1], ALU.subtract, [ek + "hq", ek + "ag", ek + "rstd"],
                             [ek + "hn"], s2=E["rstd"], op1=ALU.mult)
                        P.tt("pool", E["hn"], E["hn"], normg[:, hs], ALU.mult, [ek + "hn", "m_w"], [ek + "hn"])
                        P.tt("pool", MO[:, tq, :], E["hn"], SIG[:, tq, :], ALU.mult, [ek + "hn", "SIG"], ["MO"])
                P.st(MOUT[r0:r0 + SEQ, hs].rearrange("(n p) c -> p n c", p=128), MO, ["MO"], ["MOUT"])
                P.act(wts, aT[:, :, h], AF.Exp, ["aT", "Rb3"], ["wts"], bias=Rb[:, SEQ - 1:SEQ])
                P.tt("dve", VW, V1[:, :, 0:128], wts.unsqueeze(2).to_broadcast([128, 16, 128]), ALU.mult,
                     ["V1", "wts"], ["VW"])
                pb = P.bank()
                for st_ in range(16):
                    P.mm(pbank[pb][:, 0:128], VW[:, st_, :], KTOK[:, st_, :], st_ == 0, st_ == 15, ["VW", "KTOK"],
                         ["pb%d" % pb])
                P.cp("act", Csb, pbank[pb][:, 0:128], ["pb%d" % pb], ["Csb"])
                P.st(o_C_p[sq, h, :, :], Csb, ["Csb"])
                pb = P.bank()
                for st_ in range(16):
                    P.mm(pbank[pb][0:1, 0:128], wts[:, st_:st_ + 1], KTOK[:, st_, :], st_ == 0, st_ == 15,
                         ["wts", "KTOK"], ["pb%d" % pb])
                P.cp("act", nsb[0:1], pbank[pb][0:1, 0:128], ["pb%d" % pb], ["nsb"])
                P.st(o_n_p[sq:sq + 1, h, :], nsb[0:1], ["nsb"])


        P.barrier()
        apos[0] = persist_mark
        BIGM = 30000.0
        QT8 = alloc3(4, SEQ)
        KTz = [[alloc(SEQ) for _ in range(2)] for _ in range(2)]
        V1n = [alloc3(16, 65) for _ in range(2)]
        KCTz = alloc(2 * 2 * 128).rearrange("p (k f n) -> p k f n", k=2, f=2)
        KCV = alloc3(2, 64)
        TZs = alloc(8 * 2 * 128).rearrange("p (h k j) -> p h k j", h=8, k=2)
        WZ = alloc(128)
        SHM = alloc(128)
        EBIG = alloc(SEQ)
        SELM1 = [alloc(SEQ) for _ in range(2)]
        XTc = SELM1
        FVNC = alloc(16 * 2 * 32).rearrange("p (n k b) -> p n k b", n=16, k=2)
        ROWV = alloc(1)
        CBR = alloc(8)
        W2Z = alloc3(2, 128)
        W2N = alloc3(2, 64)
        PET = alloc(128)
        ONES = alloc(128)
        GS = alloc3(16, 24)
        NOUT = alloc3(4, 512)
        stg = [alloc3(16, 128) for _ in range(2)]
        CB = [alloc3(8, 128) for _ in range(2)]
        Sx = alloc3(8, 128)
        Pn = alloc3(8, 128)
        PnT = Sx
        sm8 = alloc(8)
        rs8 = alloc(8)
        PG = alloc3(2, 128)
        PSs = alloc3(2, 32)
        S012 = alloc3(2, 32)
        SC = alloc3(2, 32)
        M8 = alloc(8)
        SCW = alloc(32)
        SEL = alloc3(2, 32)
        PAB = alloc(128)
        XG = alloc(64)
        TG = alloc(64)
        HID = alloc(64)
        HT = alloc(128)
        epn = [dict(r=alloc(1)) for _ in range(4)]
        PTb = alloc3(16, 512)
        W1L = PTb[:, 0:8, :].rearrange("p a b -> p (a b)").rearrange("p (c r o) -> p c r o", c=2, r=16)
        W1N = PTb[:, 8:12, :].rearrange("p a b -> p (a b)").rearrange("p (c j o) -> p c j o", c=2, j=16)
        PEL = PTb[:, 12, 0:32].rearrange("p (c j) -> p c j", c=2)
        PTK = ["PT%d" % i for i in range(16)]

        P.ld(TZs, tz_d[:, :, :, :], ["TZ"])
        P.ld(WZ, wz_d[:, :], ["ncst"])
        P.ld(SHM, sh_d[:, :], ["ncst"])
        P.ld(EBIG[0:32], ebig_d[:, :], ["ncst"])
        P.ld(FVNC, fvnc_d[:, :, :, :], ["ncst"])
        P.ld(ROWV, rowv_d[:, :], ["ncst"])
        P.ld(CBR, cbr_d[:, :], ["CBR"])
        P.ld(W2Z[0:64], w2d_d[:, :, :], ["ncst"])
        P.ld(W2N[0:64], w2n_d[:, :, :], ["ncst"])
        P.memset("dve", ONES, 1.0, ["ONES"])
        for h in range(8):
            P.ts("dve", TZs[:, h, :, :], TZs[:, h, :, :], CBR[:, h:h + 1], ALU.subtract, ["TZ", "CBR"], ["TZ"])
        for br in range(2):
            P.memset("dve", V1n[br][:, :, 64:65], 1.0, ["V1n1"])
        cbcnt = 0
        encnt = 0
        K3 = int(os.environ.get("K3STOP", "9"))

        def gelu_from(xsb, tmp, out, rk, wk_):
            P.tt("dve", tmp, xsb, xsb, ALU.mult, rk, [wk_ + "t"])
            P.ts("dve", tmp, tmp, 0.044715, ALU.mult, [wk_ + "t"], [wk_ + "t"], s2=1.0, op1=ALU.add)
            P.tt("dve", tmp, tmp, xsb, ALU.mult, [wk_ + "t"] + rk, [wk_ + "t"])
            P.act(tmp, tmp, AF.Sigmoid, [wk_ + "t"], [wk_ + "t"], scale=1.5957691216057308)
            P.tt("dve", out, tmp, xsb, ALU.mult, [wk_ + "t"] + rk, [wk_])

        def tr_block(src3, dst, wkeys, scale=None, rmajor=False):
            for g in range(4):
                pb = P.bank()
                for j in range(4):
                    P.tr(pbank[pb][:, j * 128:(j + 1) * 128], src3[:, g * 4 + j, :], ident, ["stgX", "ident"],
                         ["pb%d" % pb])
                if rmajor:
                    P.cp("act", dst.rearrange("p (r n) -> p r n", r=16)[:, :, g * 32:(g + 1) * 32],
                         pbank[pb][:, :].rearrange("p (n r) -> p r n", r=16), ["pb%d" % pb], wkeys)
                elif scale is not None:
                    P.ts("dve", dst[:, g * 512:(g + 1) * 512], pbank[pb][:, :], scale, ALU.mult, ["pb%d" % pb], wkeys)
                else:
                    P.cp("act", dst[:, g * 512:(g + 1) * 512], pbank[pb][:, :], ["pb%d" % pb], wkeys)

        for sq in range(SPC if K3 >= 2 else 0):
            r0 = sq * SEQ
            zrows = Z[r0:r0 + SEQ, :]

            def zt(c0, w):
                return zrows[:, c0:c0 + w].rearrange("(n p) c -> p n c", p=128)

            for pr in range(4):
                P.ld(stg[0], zt(O_Q + pr * 128, 128), ["stgX"])
                tr_block(stg[0], QT8[:, pr, :], ["QT8"], scale=0.125)
            for c in range(2):
                P.ld(stg[0], zt(O_KC + c * 128, 128), ["stgX"])
                tr_block(stg[0], XTc[c], ["XTc", "SELM1_0", "SELM1_1"], rmajor=True)
            P.ld(GS, zt(O_GN, 24), ["GSraw"])
            P.act(GS, GS, AF.Sigmoid, ["GSraw"], ["GS", "GSraw"])
            P.ld(W1L, w1l_d[:, :, :, :], ["W1L"] + PTK)
            P.ld(W1N, w1n_d[:, :, :, :], ["W1N"] + PTK)
            P.ld(PEL, pel_d[:, :, :], ["PEL"] + PTK)
            for c in range(2):
                pb = P.bank()
                for j in range(16):
                    P.mm(pbank[pb][0:1, 0:64], PEL[:, c, j:j + 1], W1N[:, c, j, :], j == 0, j == 15, ["PEL", "W1N"],
                         ["pb%d" % pb])
                P.cp("act", PET[0:1, c * 64:(c + 1) * 64], pbank[pb][0:1, 0:64], ["pb%d" % pb], ["PET"])
            for c in range(2):
                for k in range(2):
                    rows = slice(k * 64, (k + 1) * 64)
                    xv = XTc[c][rows, :].rearrange("p (r n) -> p r n", r=16)
                    pb = P.bank()
                    for r in range(16):
                        P.mm(pbank[pb][:, 0:128], xv[:, r, :], W1L[rows, c, r, :], r == 0, r == 15, ["XTc", "W1L"],
                             ["pb%d" % pb])
                    P.cp("act", PAB, pbank[pb][:, 0:128], ["pb%d" % pb], ["PAB"])
                    pb = P.bank()
                    P.mm(pbank[pb][:, 0:64], ident, PAB[:, 0:64], True, False, ["ident", "PAB"], ["pb%d" % pb])
                    P.mm(pbank[pb][:, 0:64], SHM, PAB[:, 64:128], False, False, ["ncst", "PAB"], ["pb%d" % pb])
                    P.mm(pbank[pb][:, 0:64], ONES[0:1, :], PET[0:1, c * 64:(c + 1) * 64], False, True, ["ONES", "PET"],
                         ["pb%d" % pb])
                    P.cp("act", XG, pbank[pb][:, 0:64], ["pb%d" % pb], ["XG"])
                    gelu_from(XG, TG, HID, ["XG"], "HID")
                    pb = P.bank()
                    P.tr(pbank[pb][0:64, 0:128], HID, ident, ["HID", "ident"], ["pb%d" % pb])
                    P.cp("act", HT[0:64], pbank[pb][0:64, 0:128], ["pb%d" % pb], ["HT"])
                    if c == 0:
                        for f in range(2):
                            pb = P.bank()
                            P.mm(pbank[pb][:, 0:128], W2Z[0:64, f, :], HT[0:64], True, True, ["ncst", "HT"],
                                 ["pb%d" % pb])
                            P.cp("act", KCTz[:, k, f, :], pbank[pb][:, 0:128], ["pb%d" % pb], ["KCT"])
                    else:
                        pb = P.bank()
                        P.mm(pbank[pb][:, 0:64], HT[0:64], W2N[0:64, c, :], True, True, ["ncst", "HT"], ["pb%d" % pb])
                        P.cp("act", KCV[:, k, :], pbank[pb][:, 0:64], ["pb%d" % pb], ["KCV"])

            def attn_block(br, h, tb, st_list):
                nonlocal encnt
                pr, half, kvh = h // 2, h % 2, h // 4
                hl = h % 4
                active = {}
                for st_ in st_list:
                    subs = [sub for sub in range(4) if 0 <= (4 * tb + sub - st_) <= (4 if br == 1 else 10 ** 6)]
                    if not subs:
                        continue
                    lo, hi = subs[0], subs[-1] + 1
                    active[st_] = (lo, hi)
                    cs = slice(lo * 128, hi * 128)
                    gs = slice(tb * 512 + lo * 128, tb * 512 + hi * 128)
                    pb = P.bank()
                    extra = []
                    for sub in subs:
                        dd = 4 * tb + sub - st_
                        if dd == 0:
                            extra.append((sub, TZs[:, h, 0, :], "TZ"))
                        elif dd == 1:
                            extra.append((sub, TZs[:, h, 1, :], "TZ"))
                        elif dd == 4 and br == 1:
                            extra.append((sub, WZ, "ncst"))
                    nmm = 1 + (1 if br == 0 else 0) + len(extra)
                    i_mm = 0
                    P.mm(pbank[pb][:, cs], KTz[br][half][:, st_ * 128:(st_ + 1) * 128], QT8[:, pr, gs],
                         True, nmm == 1, ["KTz", "QT8"], ["pb%d" % pb])
                    i_mm += 1
                    if br == 0:
                        P.mm(pbank[pb][:, cs], EBIG[0:32, st_ * 128:(st_ + 1) * 128], SELM1[kvh][0:32, gs],
                             False, i_mm == nmm - 1, ["ncst", "SELM1_%d" % kvh], ["pb%d" % pb])
                        i_mm += 1
                    for (sub, tile_, key) in extra:
                        P.mm(pbank[pb][:, sub * 128:(sub + 1) * 128], ident, tile_, False, i_mm == nmm - 1,
                             ["ident", key], ["pb%d" % pb])
                        i_mm += 1
                    P.act(PTb[:, st_, cs], pbank[pb][:, cs], AF.Exp, ["pb%d" % pb, "CBR"], ["PT%d" % st_],
                          bias=CBR[:, h:h + 1])
                for sub in range(4):
                    tq = 4 * tb + sub
                    sts = [st_ for st_ in st_list if st_ in active and active[st_][0] <= sub < active[st_][1]]
                    pb = P.bank()
                    for i, st_ in enumerate(sts):
                        P.mm(pbank[pb][:, 0:65], PTb[:, st_, sub * 128:(sub + 1) * 128], V1n[br][:, st_, :],
                             i == 0, i == len(sts) - 1, ["PT%d" % st_, "V1n", "V1n1"], ["pb%d" % pb])
                    E = epn[encnt % 4]
                    ek = "epn%d" % (encnt % 4)
                    encnt += 1
                    P.ts("dve", E["r"], pbank[pb][:, 64:65], 1e-30, ALU.max, ["pb%d" % pb], [ek])
                    P.op("dve", lambda e, E=E: e.reciprocal(out=E["r"], in_=E["r"]), [ek], [ek])
                    gcol = (1 + br) * 8 + h
                    P.tt("dve", E["r"], E["r"], GS[:, tq, gcol:gcol + 1], ALU.mult, [ek, "GS"], [ek])
                    P.stt("dve", NOUT[:, sub, hl * 64:(hl + 1) * 64], pbank[pb][:, 0:64], E["r"],
                          NOUT[:, sub, hl * 64:(hl + 1) * 64], ALU.mult, ALU.add, ["pb%d" % pb, ek, "NOUT"], ["NOUT"])

            for tb in range(4 if K3 >= 4 else 0):
                for sub in range(4):
                    tq = 4 * tb + sub
                    cb_ = cbcnt % 2
                    cbcnt += 1
                    P.ld(CB[cb_], cbt_d[tq * 128:(tq + 1) * 128, :, :], ["CB%d" % cb_])
                    for hg in range(2):
                        pb = P.bank()
                        for hh in range(4):
                            h = hg * 4 + hh
                            pr, half, kvh = h // 2, h % 2, h // 4
                            P.mm(pbank[pb][:, hh * 128:(hh + 1) * 128], QT8[:, pr, tq * 128:(tq + 1) * 128],
                                 KCTz[:, kvh, half, :], True, True, ["QT8", "KCT"], ["pb%d" % pb])
                        P.tt("dve", Sx[:, hg * 4:(hg + 1) * 4, :],
                             pbank[pb][:, :].rearrange("p (h n) -> p h n", h=4), CB[cb_][:, hg * 4:(hg + 1) * 4, :],
                             ALU.add, ["pb%d" % pb, "CB%d" % cb_], ["Sx", "PnT"])
                    P.op("dve", lambda e: e.reduce_max(out=sm8, in_=Sx, axis=AX.X), ["Sx"], ["sm8"])
                    P.tt("dve", Sx, Sx, sm8.unsqueeze(2).to_broadcast([128, 8, 128]), ALU.subtract, ["Sx", "sm8"], ["Sx"])
                    P.act(Pn, Sx, AF.Exp, ["Sx"], ["Pn"])
                    P.op("dve", lambda e: e.reduce_sum(out=sm8, in_=Pn, axis=AX.X), ["Pn"], ["sm8"])
                    P.op("dve", lambda e: e.reciprocal(out=rs8, in_=sm8), ["sm8"], ["rs8"])
                    if tq == 0:
                        P.ts("dve", rs8, rs8, ROWV[:, 0:1], ALU.mult, ["rs8", "ncst"], ["rs8"])
                    P.tt("dve", Pn, Pn, rs8.unsqueeze(2).to_broadcast([128, 8, 128]), ALU.mult, ["Pn", "rs8"], ["Pn"])
                    for hg in range(2):
                        pb = P.bank()
                        for hh in range(4):
                            h = hg * 4 + hh
                            P.tr(pbank[pb][:, hh * 128:(hh + 1) * 128], Pn[:, h, :], ident, ["Pn", "ident"],
                                 ["pb%d" % pb])
                        P.cp("act", PnT[:, hg * 4:(hg + 1) * 4, :], pbank[pb][:, :].rearrange("p (h n) -> p h n", h=4),
                             ["pb%d" % pb], ["PnT", "Sx"])
                    pb = P.bank()
                    for h in range(8):
                        P.mm(pbank[pb][:, h * 64:(h + 1) * 64], PnT[:, h, :], KCV[:, h // 4, :], True, True,
                             ["PnT", "KCV"], ["pb%d" % pb])
                    P.tt("dve", NOUT[:, sub, :].rearrange("p (h d) -> p h d", h=8),
                         pbank[pb][:, :].rearrange("p (h d) -> p h d", h=8),
                         GS[:, tq, 0:8].unsqueeze(2).to_broadcast([128, 8, 64]), ALU.mult, ["pb%d" % pb, "GS"], ["NOUT"])
                    P.op("dve", lambda e: e.tensor_reduce(out=PG, in_=Pn.rearrange("p (k g) n -> p k n g", k=2),
                                                         axis=AX.X, op=ALU.add), ["Pn"], ["PG"])
                    pg4 = PG.rearrange("p k (b r) -> p k b r", r=4)
                    P.op("dve", lambda e, pg4=pg4: e.tensor_reduce(out=S012, in_=pg4[:, :, :, 0:3], axis=AX.X, op=ALU.add),
                         ["PG"], ["S012"])
                    P.stt("dve", PSs, S012, 2.0, pg4[:, :, :, 3], ALU.mult, ALU.add, ["S012", "PG"], ["PSs"])
                    P.tt("dve", PSs[:, :, 1:32], PSs[:, :, 1:32], pg4[:, :, 0:31, 3], ALU.add, ["PSs", "PG"], ["PSs"])
                    fv = FVNC[:, tq, 0, :]
                    ncm = FVNC[:, tq, 1, :]
                    for k in range(2):
                        P.tt("dve", SC[:, k, :], PSs[:, k, :], fv, ALU.max, ["PSs", "ncst"], ["SC"])
                        P.tt("dve", SC[:, k, :], SC[:, k, :], ncm, ALU.add, ["SC", "ncst"], ["SC"])
                        P.op("dve", lambda e, k=k: e.max(out=M8, in_=SC[:, k, :]), ["SC"], ["M8"])
                        P.op("dve", lambda e, k=k: e.match_replace(out=SCW, in_to_replace=M8, in_values=SC[:, k, :],
                                                                  imm_value=-1e9), ["SC", "M8"], ["SCW"])
                        P.op("dve", lambda e: e.max(out=M8, in_=SCW), ["SCW"], ["M8"])
                        P.ts("dve", SEL[:, k, :], SC[:, k, :], M8[:, 7:8], ALU.is_ge, ["SC", "M8"], ["SEL"], s2=-1.0,
                             op1=ALU.add)
                        pb = P.bank()
                        P.tr(pbank[pb][0:32, 0:128], SEL[:, k, :], ident, ["SEL", "ident"], ["pb%d" % pb])
                        P.cp("act", SELM1[k][0:32, tq * 128:(tq + 1) * 128], pbank[pb][0:32, 0:128], ["pb%d" % pb],
                             ["SELM1_%d" % k, "XTc"])
                P.st(NOUTD[r0 + tb * 512:r0 + (tb + 1) * 512, :].rearrange("(n p) c -> p n c", p=128), NOUT,
                     ["NOUT"], ["NOUTD"])
            for kvh in range(2 if K3 >= 5 else 0):
                for br, obase in ((0, O_KS), (1, O_KW)):
                    for half in range(2):
                        P.memset("dve", stg[half], 0.0, ["stgX"])
                        P.ld(stg[half][:, :, half * 64:(half + 1) * 64], zt(obase + kvh * 64, 64), ["stgX"])
                        tr_block(stg[half], KTz[br][half], ["KTz"])
                    P.ld(V1n[br][:, :, 0:64], zt(obase + 128 + kvh * 64, 64), ["V1n"])
                for tb in range(4):
                    nsl = NOUT[:, :, 0:256]
                    dsl = NOUTD[r0 + tb * 512:r0 + (tb + 1) * 512, kvh * 256:(kvh + 1) * 256].rearrange(
                        "(n p) c -> p n c", p=128)
                    P.ld(nsl, dsl, ["NOUT"], r=["NOUTD"])
                    for g in range(4):
                        h = kvh * 4 + g
                        if K3 != 7:
                            attn_block(0, h, tb, list(range(0, 4 * tb + 4)))
                        if K3 != 6:
                            attn_block(1, h, tb, list(range(max(0, 4 * tb - 4), 4 * tb + 4)))
                    P.st(dsl, nsl, ["NOUT"], ["NOUTD"])

        def top16(src, scr, vals, idx_u, nelem, key):
            P.op("dve", lambda e: e.max(out=vals[:, 0:8], in_=src), [key], [key + "v"])
            P.op("dve", lambda e: e.max_index(out=idx_u[:, 0:8], in_max=vals[:, 0:8], in_values=src),
                 [key, key + "v"], [key + "i"])
            P.op("dve", lambda e: e.match_replace(out=scr, in_to_replace=vals[:, 0:8], in_values=src,
                                                  imm_value=-1e30), [key, key + "v"], [key + "s"])
            P.op("dve", lambda e: e.max(out=vals[:, 8:16], in_=scr), [key + "s"], [key + "v"])
            P.op("dve", lambda e: e.max_index(out=idx_u[:, 8:16], in_max=vals[:, 8:16], in_values=scr),
                 [key + "s", key + "v"], [key + "i"])

        P.barrier()
        apos[0] = persist_mark
        zt0 = alloc(512)
        P.memset("dve", zt0, 0.0, ["zt0"])
        P.st(MOUT[TP:TP + 128, :], zt0, ["zt0"], ["MOUT_s"])
        P.st(NOUTD[TP:TP + 128, :], zt0, ["zt0"], ["NOUTD_s"])
        wq5 = alloc3(4, 128)
        wk5 = alloc3(4, 128)
        P.ld(wq5, wq_d[:, :, :], ["w5s"])
        P.ld(wk5, wk_d[:, :, :], ["w5s"])
        ZS = alloc(1544)
        CB3 = alloc3(3, 512)
        N0 = alloc(512)
        M0 = alloc(4)
        CW4 = alloc3(4, 512)
        CB4 = alloc(512)
        GB4 = alloc(8)
        NG16 = alloc(128)
        OPS = alloc(128)
        C4 = alloc(512)
        T4 = alloc(512)
        CTs = alloc3(4, 4)
        Q4 = alloc(512)
        K4 = alloc(512)
        S4 = {n: alloc(4) for n in ("IG", "XF", "LF", "MI", "MT", "SWS", "SCI", "EMT", "QK", "NQ", "SW", "DEN", "RD", "T")}
        PACK = alloc3(4, 264)
        C0s = alloc3(16, 128)
        BCT = alloc3(16, 264)
        VTs = alloc3(4, 4)
        T1 = alloc3(16, 128)
        CQ = alloc(16)
        NUM = alloc(16)
        WV = alloc(16)
        HTs = alloc(16)
        HB = alloc(128)
        NN = alloc(512)
        zs4 = Z[TP:TP + SSC, :]
        P.ld(ZS[0:4], zs4[:, 0:1544], ["ZS"])
        P.ld(CB3[0:4], st_conv[:, :, :], ["s5in"])
        P.ld(N0[0:4], stn_d[:, :], ["s5in"])
        P.ld(M0[0:4], stm_d[:, :], ["s5in"])
        P.ld(CW4[0:4], cw4_d[:, :, :], ["s5in"])
        P.ld(CB4[0:4], cb4_d[:, :], ["s5in"])
        P.ld(GB4[0:4], gb4_d[:, :], ["s5in"])
        P.ld(NG16[0:16], ng16_d[:, :], ["s5in"])
        for b in range(SSC):
            P.ld(OPS[4 * b:4 * b + 4], Z[TP + b, O_O:O_O + 512].rearrange("(h v) -> h v", h=4), ["OPS"])
        P.ld(C0s, stC_d.rearrange("a v k -> v a k"), ["C0s"])
        P.tt("dve", C4[0:4], CW4[0:4, 3, :], ZS[0:4, O_U:O_U + 512], ALU.mult, ["s5in", "ZS"], ["C4"])
        for j in range(3):
            P.tt("dve", T4[0:4], CW4[0:4, j, :], CB3[0:4, j, :], ALU.mult, ["s5in"], ["T4"])
            P.tt("dve", C4[0:4], C4[0:4], T4[0:4], ALU.add, ["C4", "T4"], ["C4"])
        P.tt("dve", C4[0:4], C4[0:4], CB4[0:4], ALU.add, ["C4", "s5in"], ["C4"])
        P.act(C4[0:4], C4[0:4], AF.Silu, ["C4"], ["C4"])
        pb = P.bank()
        for h in range(4):
            P.tr(pbank[pb][:, h * 4:(h + 1) * 4], C4[0:4, h * 128:(h + 1) * 128], ident[0:4, 0:4], ["C4", "ident"],
                 ["pb%d" % pb])
        P.cp("act", CTs, pbank[pb][:, 0:16].rearrange("p (h b) -> p h b", h=4), ["pb%d" % pb], ["CTs"])
        pq = P.bank()
        for h in range(4):
            P.mm(pbank[pq][0:4, h * 128:(h + 1) * 128], CTs[:, h, :], wq5[:, h, :], True, True, ["CTs", "w5s"],
                 ["pb%d" % pq])
        P.cp("act", Q4[0:4], pbank[pq][0:4, :], ["pb%d" % pq], ["Q4"])
        pk = P.bank()
        for h in range(4):
            P.mm(pbank[pk][0:4, h * 128:(h + 1) * 128], CTs[:, h, :], wk5[:, h, :], True, True, ["CTs", "w5s"],
                 ["pb%d" % pk])
        P.ts("dve", K4[0:4], pbank[pk][0:4, :], 128.0 ** -0.5, ALU.mult, ["pb%d" % pk], ["K4"])
        A_ = {n: v[0:4] for n, v in S4.items()}
        P.tt("dve", A_["IG"], ZS[0:4, O_I:O_I + 4], GB4[0:4, 0:4], ALU.add, ["ZS", "s5in"], ["IG"])
        P.tt("dve", A_["XF"], ZS[0:4, O_F:O_F + 4], GB4[0:4, 4:8], ALU.add, ["ZS", "s5in"], ["XF"])
        P.act(A_["LF"], A_["XF"], AF.Exp, ["XF"], ["LF"], scale=-1.0)
        P.act(A_["LF"], A_["LF"], AF.Ln, ["LF"], ["LF"], bias=1.0, scale=1.0)
        P.stt("dve", A_["MI"], A_["LF"], -1.0, M0[0:4], ALU.mult, ALU.add, ["LF", "s5in"], ["MI"])
        P.tt("dve", A_["MT"], A_["MI"], A_["IG"], ALU.max, ["MI", "IG"], ["MT"])
        P.tt("dve", A_["T"], A_["IG"], A_["MT"], ALU.subtract, ["IG", "MT"], ["T"])
        P.act(A_["SWS"], A_["T"], AF.Exp, ["T"], ["SWS"])
        P.tt("dve", A_["T"], A_["MI"], A_["MT"], ALU.subtract, ["MI", "MT", "SWS"], ["T"])
        P.act(A_["SCI"], A_["T"], AF.Exp, ["T"], ["SCI"])
        P.act(A_["EMT"], A_["MT"], AF.Exp, ["MT"], ["EMT"], scale=-1.0)
        P.tt("dve", T4[0:4], Q4[0:4], K4[0:4], ALU.mult, ["Q4", "K4"], ["T4"])
        P.op("dve", lambda e: e.reduce_sum(out=A_["QK"], in_=T4[0:4].rearrange("p (h e) -> p h e", h=4), axis=AX.X),
             ["T4"], ["QK"])
        P.tt("dve", T4[0:4], Q4[0:4], N0[0:4], ALU.mult, ["Q4", "s5in", "QK"], ["T4"])
        P.op("dve", lambda e: e.reduce_sum(out=A_["NQ"], in_=T4[0:4].rearrange("p (h e) -> p h e", h=4), axis=AX.X),
             ["T4"], ["NQ"])
        P.tt("dve", A_["SW"], A_["QK"], A_["SWS"], ALU.mult, ["QK", "SWS"], ["SW"])
        P.tt("dve", A_["DEN"], A_["SCI"], A_["NQ"], ALU.mult, ["SCI", "NQ"], ["DEN"])
        P.tt("dve", A_["DEN"], A_["DEN"], A_["SW"], ALU.add, ["DEN", "SW"], ["DEN"])
        P.act(A_["DEN"], A_["DEN"], AF.Abs, ["DEN"], ["DEN"])
        P.tt("dve", A_["DEN"], A_["DEN"], A_["EMT"], ALU.max, ["DEN", "EMT"], ["DEN"])
        P.op("dve", lambda e: e.reciprocal(out=A_["RD"], in_=A_["DEN"]), ["DEN"], ["RD"])
        P.cp("dve", PACK[0:4, :, 0:128], Q4[0:4].rearrange("p (h e) -> p h e", h=4), ["Q4"], ["PACK"])
        P.cp("dve", PACK[0:4, :, 128:256], K4[0:4].rearrange("p (h e) -> p h e", h=4), ["K4"], ["PACK"])
        for i, n in enumerate(("SCI", "SWS", "SW", "RD")):
            P.cp("dve", PACK[0:4, :, 256 + i:257 + i], A_[n].unsqueeze(2), [n], ["PACK"])
        P.st(SCRB.rearrange("(b h) x -> b h x", h=4), PACK[0:4], ["PACK"], ["SCRB"], eng="sp")
        P.ld(BCT, bass.AP(SCRB.tensor, 0, [[0, 128], [264, 16], [1, 264]]), ["BCT"], r=["SCRB"])
        pb = P.bank()
        for h in range(4):
            P.tr(pbank[pb][:, h * 4:(h + 1) * 4], ZS[0:4, O_V + h * 128:O_V + (h + 1) * 128], ident[0:4, 0:4],
                 ["ZS", "ident"], ["pb%d" % pb])
        P.cp("act", VTs, pbank[pb][:, 0:16].rearrange("p (h b) -> p h b", h=4), ["pb%d" % pb], ["VTs"])
        VTbh = VTs.rearrange("p h b -> p b h")
        sc = lambda i: BCT[:, :, 256 + i].rearrange("p (b h) -> p b h", h=4)
        P.tt("dve", T1, C0s, BCT[:, :, 0:128], ALU.mult, ["C0s", "BCT"], ["T1"])
        P.op("dve", lambda e: e.reduce_sum(out=CQ, in_=T1, axis=AX.X), ["T1"], ["CQ"])
        CQ3 = CQ.rearrange("p (b h) -> p b h", h=4)
        NUM3 = NUM.rearrange("p (b h) -> p b h", h=4)
        WV3 = WV.rearrange("p (b h) -> p b h", h=4)
        HT3 = HTs.rearrange("p (b h) -> p b h", h=4)
        P.tt("dve", NUM3, CQ3, sc(0), ALU.mult, ["CQ", "BCT"], ["NUM"])
        P.tt("dve", WV3, VTbh, sc(2), ALU.mult, ["VTs", "BCT"], ["WV"])
        P.tt("dve", NUM3, NUM3, WV3, ALU.add, ["NUM", "WV"], ["NUM"])
        P.tt("dve", HT3, NUM3, sc(3), ALU.mult, ["NUM", "BCT"], ["HTs"])
        pb = P.bank()
        P.tr(pbank[pb][0:16, 0:128], HTs, ident, ["HTs", "ident"], ["pb%d" % pb])
        P.cp("act", HB[0:16], pbank[pb][0:16, 0:128], ["pb%d" % pb], ["HB"])
        st5 = alloc(8)
        ag5 = alloc(4)
        rs5 = alloc(1)
        P.op("dve", lambda e: e.bn_stats(out=st5[0:16, 0:6], in_=HB[0:16]), ["HB"], ["st5"])
        P.op("dve", lambda e: e.bn_aggr(out=ag5[0:16, 0:2], in_=st5[0:16, 0:6]), ["st5"], ["ag5"])
        P.act(rs5[0:16], ag5[0:16, 1:2], AF.Sqrt, ["ag5"], ["rs5"], bias=1e-5, scale=1.0)
        P.op("dve", lambda e: e.reciprocal(out=rs5[0:16], in_=rs5[0:16]), ["rs5"], ["rs5"])
        P.ts("dve", HB[0:16], HB[0:16], ag5[0:16, 0:1], ALU.subtract, ["HB", "ag5", "rs5"], ["HB"], s2=rs5[0:16],
             op1=ALU.mult)
        P.tt("dve", HB[0:16], HB[0:16], NG16[0:16], ALU.mult, ["HB", "s5in"], ["HB"])
        P.act(OPS[0:16], OPS[0:16], AF.Sigmoid, ["OPS"], ["OPS"])
        P.tt("dve", HB[0:16], HB[0:16], OPS[0:16], ALU.mult, ["HB", "OPS"], ["HB"])
        for b in range(SSC):
            P.st(MOUT[TP + b, :].rearrange("(h v) -> h v", h=4), HB[4 * b:4 * b + 4], ["HB"], ["MOUT_s"], eng="sp")
        P.tt("dve", WV3, VTbh, sc(1), ALU.mult, ["VTs", "BCT", "NUM"], ["WV"])
        P.tt("dve", T1, BCT[:, :, 128:256], WV.unsqueeze(2).to_broadcast([128, 16, 128]), ALU.mult, ["BCT", "WV", "CQ"],
             ["T1"])
        P.tt("dve", C0s, C0s, BCT[:, :, 256:257].to_broadcast([128, 16, 128]), ALU.mult, ["C0s", "BCT"], ["C0s"])
        P.tt("dve", C0s, C0s, T1, ALU.add, ["C0s", "T1"], ["C0s"])
        P.st(o_C_s.rearrange("a v k -> v a k"), C0s, ["C0s"], eng="sp")
        N03 = N0[0:4].rearrange("p (h e) -> p h e", h=4)
        NN3 = NN[0:4].rearrange("p (h e) -> p h e", h=4)
        K43 = K4[0:4].rearrange("p (h e) -> p h e", h=4)
        P.tt("dve", NN3, N03, A_["SCI"].unsqueeze(2).to_broadcast([4, 4, 128]), ALU.mult, ["s5in", "SCI"], ["NN"])
        P.tt("dve", T4[0:4].rearrange("p (h e) -> p h e", h=4), K43, A_["SWS"].unsqueeze(2).to_broadcast([4, 4, 128]),
             ALU.mult, ["K4", "SWS", "NQ"], ["T4"])
        P.tt("dve", NN[0:4], NN[0:4], T4[0:4], ALU.add, ["NN", "T4"], ["NN"])
        P.st(o_n_s[:, :], NN[0:4], ["NN"], eng="sp")
        P.st(o_m_s[:, :], A_["MT"], ["MT"], eng="sp")

        P.barrier()
        apos[0] = persist_mark
        K6 = int(os.environ.get("K6STOP", "9"))
        W1Ls = alloc(2 * 16 * 128).rearrange("p (c r o) -> p c r o", c=2, r=16)
        W1Ns = alloc(2 * 16 * 64).rearrange("p (c j o) -> p c j o", c=2, j=16)
        PELs = alloc3(2, 16)
        W2Ns = alloc3(2, 64)
        W2KT = alloc(64)
        SHMs = alloc(128)
        SHD = alloc(128)
        ONESM = alloc(128)
        CBS = alloc3(8, 8)
        BWt = alloc3(4, 8)
        T25 = alloc(2 * 8 * 64).rearrange("p (a h t) -> p a h t", a=2, h=8)
        CBR6 = alloc(8)
        FL255 = alloc(1)
        IND = alloc(4)
        RB0 = alloc(8)
        IOTA8 = alloc(8)
        IOT128 = alloc3(2, 128)
        PETs = alloc(128)
        PETB = alloc3(2, 64)
        PT4i = alloc(4).bitcast(I32)
        PT8i = alloc(128).bitcast(I32)
        PTF = alloc(4)
        PTROW = alloc(128)
        IDXf = alloc3(4, 8)
        P.ld(W1Ls, w1l_d[:, :, :, :], ["c6"])
        P.ld(W1Ns, w1n_d[:, :, :, :], ["c6"])
        P.ld(PELs, pel_d[:, :, :], ["c6"])
        P.ld(W2Ns[0:64], w2n_d[:, :, :], ["c6"])
        P.ld(W2KT[0:64], w2kt_d[:, :], ["c6"])
        P.ld(SHMs, sh_d[:, :], ["c6"])
        P.ld(SHD, shd_d[:, :], ["c6"])
        P.ld(CBS, cbs_d[:, :, :], ["c6"])
        P.ld(BWt, bw_d[:, :, :], ["c6"])
        P.ld(T25[0:60], t25_d[:, :, :, :], ["c6"])
        P.ld(CBR6, cbr_d[:, :], ["c6"])
        P.ld(FL255[0:60], fl255_d[:, :], ["c6"])
        P.ld(IND[0:60], ind_d[:, :], ["c6"])
        P.ld(RB0[0:4], rb0_d[:, :], ["c6"])
        P.ld(IOTA8, iota8_d[:, :], ["c6"])
        P.ld(IOT128[0:8], iot128_d[:, :, :], ["c6"])
        P.ld(PT4i, ptl_d[:, :], ["c6"])
        P.ld(PT8i[0:8], pt8_d[:, :], ["c6"])
        P.memset("dve", ONESM, 1.0, ["ONESM"])
        P.cp("dve", PTF, PT4i, ["c6"], ["PTF"])
        P.cp("dve", PTROW[0:8], PT8i[0:8], ["c6"], ["PTROW"])
        P.stt("dve", IDXf, PTF.unsqueeze(2).to_broadcast([128, 4, 8]), 8.0,
              IOTA8.unsqueeze(1).to_broadcast([128, 4, 8]), ALU.mult, ALU.add, ["PTF", "c6"], ["IDXf"])
        P.cp("dve", IDXC.rearrange("p (b n) -> p b n", b=4), IDXf, ["IDXf"], ["IDXC"])
        for c in range(2):
            pb = P.bank()
            for j in range(16):
                P.mm(pbank[pb][0:1, 0:64], PELs[:, c, j:j + 1], W1Ns[:, c, j, :], j == 0, j == 15, ["c6"], ["pb%d" % pb])
            P.cp("act", PETs[0:1, c * 64:(c + 1) * 64], pbank[pb][0:1, 0:64], ["pb%d" % pb], ["PETs"])
        pb = P.bank()
        P.mm(pbank[pb][:, 0:128], ONESM[0:1, :], PETs[0:1, :], True, True, ["ONESM", "PETs"], ["pb%d" % pb])
        P.cp("act", PETB, pbank[pb][:, 0:128].rearrange("p (c o) -> p c o", c=2), ["pb%d" % pb], ["PETB"])
        ZN = alloc(1304)
        P.ld(ZN[0:4], Z[TP:TP + SSC, O_Q:O_Q + 1304], ["ZN"])
        QS = alloc(512)
        P.ts("dve", QS[0:4], ZN[0:4, 0:512], 0.125, ALU.mult, ["ZN"], ["QS"])
        QTS = alloc3(4, 8)
        pb = P.bank()
        for h in range(8):
            P.tr(pbank[pb][0:64, h * 4:(h + 1) * 4], QS[0:4, h * 64:(h + 1) * 64], ident[0:4, 0:4], ["QS", "ident"],
                 ["pb%d" % pb])
        P.cp("act", QTS[0:64].rearrange("p b h -> p h b"), pbank[pb][0:64, 0:32].rearrange("p (h b) -> p h b", h=8),
             ["pb%d" % pb], ["QTS"])
        QW8 = alloc(64)
        for b in range(SSC):
            pb = P.bank()
            P.mm(pbank[pb][0:8, 0:64], QTS[0:64, b, :], W2KT[0:64, :], True, True, ["QTS", "c6"], ["pb%d" % pb])
            P.cp("act", QW8[0:8], pbank[pb][0:8, 0:64], ["pb%d" % pb], ["QW8"])
            P.st(SCRQ[b, :].rearrange("(h i) -> h i", h=8), QW8[0:8], ["QW8"], ["SCRQ"], eng="sp")
        QWB = alloc3(4, 512)
        P.ld(QWB, bass.AP(SCRQ.tensor, 0, [[0, 128], [512, 4], [1, 512]]), ["QWB"], r=["SCRQ"])
        G = [alloc(4096) for _ in range(2)]
        XTs = alloc(32 * 128).rearrange("p (q n) -> p q n", q=32)
        PABs = alloc(8 * 4 * 128).rearrange("p (n k o) -> p n k o", n=8, k=4)
        HPRE = alloc(8 * 4 * 64).rearrange("p (n k o) -> p n k o", n=8, k=4)
        HTMP = alloc(8 * 4 * 64).rearrange("p (n k o) -> p n k o", n=8, k=4)
        HIDs = alloc(8 * 4 * 64).rearrange("p (n k o) -> p n k o", n=8, k=4)
        TMPs = alloc(1024)
        SCs = alloc3(8, 8)
        RS = alloc(8)
        RT = alloc(8)
        PGs = alloc3(8, 2)
        S3 = alloc(2)
        PSB = alloc3(2, 2)
        U4 = alloc(64)
        UT = alloc(4)
        OC = alloc(65)
        WK = alloc3(4, 256)
        QB128 = alloc(512)
        SWs = alloc3(4, 8)
        gcnt6 = 0
        for b in range(SSC if K6 >= 1 else 0):
            for n_ in range(8):
                gb = gcnt6 % 2
                gcnt6 += 1
                P.dma("pool", lambda e, gb=gb, col=b * 8 + n_: e.indirect_dma_start(
                    out=G[gb], out_offset=None, in_=pool_cmp_d[:, :],
                    in_offset=bass.IndirectOffsetOnAxis(ap=IDXC[:, col:col + 1], axis=0)), ["IDXC"], ["G%d" % gb])
                G3 = G[gb].rearrange("p (r x) -> p r x", r=16)
                for g8 in range(8):
                    pb = P.bank()
                    for j in range(4):
                        q_ = g8 * 4 + j
                        r, c = q_ // 2, q_ % 2
                        P.tr(pbank[pb][:, j * 128:(j + 1) * 128], G3[:, r, c * 128:(c + 1) * 128], ident,
                             ["G%d" % gb, "ident"], ["pb%d" % pb])
                    P.cp("act" if g8 % 2 else "dve", XTs[:, g8 * 4:(g8 + 1) * 4, :],
                         pbank[pb][:, :].rearrange("p (q n) -> p q n", q=4), ["pb%d" % pb], ["XTs"])
                for c in range(2):
                    for k in range(2):
                        rows = slice(k * 64, (k + 1) * 64)
                        pb = P.bank()
                        for r in range(16):
                            P.mm(pbank[pb][:, 0:128], XTs[rows, r * 2 + c, :], W1Ls[rows, c, r, :], r == 0, r == 15,
                                 ["XTs", "c6"], ["pb%d" % pb])
                        P.cp("act", PABs[:, n_, c * 2 + k, :], pbank[pb][:, 0:128], ["pb%d" % pb], ["PABs"])
            P.tt("dve", HPRE[:, 0:7], PABs[:, 0:7, :, 0:64], PABs[:, 1:8, :, 64:128], ALU.add, ["PABs"], ["HPRE"])
            pb = P.bank()
            P.mm(pbank[pb][:, 0:256], SHMs, PABs[:, 0, :, 64:128], True, True, ["c6", "PABs"], ["pb%d" % pb])
            P.tt("dve", HPRE[:, 7], PABs[:, 7, :, 0:64], pbank[pb][:, 0:256].rearrange("p (k o) -> p k o", k=4), ALU.add,
                 ["PABs", "pb%d" % pb], ["HPRE"])
            for c in range(2):
                P.tt("dve", HPRE[:, :, 2 * c:2 * c + 2, :], HPRE[:, :, 2 * c:2 * c + 2, :],
                     PETB[:, c, :].unsqueeze(1).unsqueeze(1).to_broadcast([128, 8, 2, 64]), ALU.add, ["HPRE", "PETB"],
                     ["HPRE"])
            gelu_from(HPRE, HTMP, HIDs, ["HPRE"], "HIDs")
            for h in range(8):
                kvh = h // 4
                P.tt("dve", TMPs[:, 0:512].rearrange("p (n i) -> p n i", n=8), HIDs[:, :, kvh, :],
                     QWB[:, b, h * 64:(h + 1) * 64].unsqueeze(1).to_broadcast([128, 8, 64]), ALU.mult,
                     ["HIDs", "QWB"], ["TMPs"])
                P.op("dve", lambda e, h=h: e.reduce_sum(out=SCs[:, :, h], in_=TMPs[:, 0:512].rearrange(
                    "p (n i) -> p n i", n=8), axis=AX.X), ["TMPs"], ["SCs"])
            P.tt("dve", SCs, SCs, CBS, ALU.add, ["SCs", "c6"], ["SCs"])
            P.act(SCs, SCs, AF.Exp, ["SCs"], ["SCs"])
            P.op("dve", lambda e: e.reduce_sum(out=RS, in_=SCs.rearrange("p n h -> p h n"), axis=AX.X), ["SCs"], ["RS"])
            pb = P.bank()
            P.mm(pbank[pb][:, 0:8], ONESM, RS, True, True, ["ONESM", "RS"], ["pb%d" % pb])
            P.op("dve", lambda e, pb=pb: e.reciprocal(out=RT, in_=pbank[pb][:, 0:8]), ["pb%d" % pb], ["RT"])
            P.tt("dve", SCs, SCs, RT.unsqueeze(1).to_broadcast([128, 8, 8]), ALU.mult, ["SCs", "RT"], ["SCs"])
            for kvh in range(2):
                pb = P.bank()
                for n_ in range(8):
                    P.mm(pbank[pb][0:4, 0:64], SCs[:, n_, kvh * 4:(kvh + 1) * 4], HIDs[:, n_, 2 + kvh, :], n_ == 0,
                         n_ == 7, ["SCs", "HIDs"], ["pb%d" % pb])
                P.cp("act", U4[0:4], pbank[pb][0:4, 0:64], ["pb%d" % pb], ["U4"])
                pb = P.bank()
                P.tr(pbank[pb][0:64, 0:4], U4[0:4, 0:64], ident[0:4, 0:4], ["U4", "ident"], ["pb%d" % pb])
                P.cp("act", UT[0:64], pbank[pb][0:64, 0:4], ["pb%d" % pb], ["UT"])
                pb = P.bank()
                P.mm(pbank[pb][0:4, 0:64], UT[0:64, 0:4], W2Ns[0:64, 1, :], True, True, ["UT", "c6"], ["pb%d" % pb])
                P.cp("act", OC[0:4, 0:64], pbank[pb][0:4, 0:64], ["pb%d" % pb], ["OC"])
                P.st(OSD[0, b, kvh * 4:(kvh + 1) * 4, 0:64], OC[0:4, 0:64], ["OC"], ["OSD"], eng="sp")
            P.op("dve", lambda e: e.tensor_reduce(out=PGs, in_=SCs.rearrange("p n (k g) -> p n k g", k=2), axis=AX.X,
                                                 op=ALU.add), ["SCs"], ["PGs"])
            pb = P.bank()
            P.mm(pbank[pb][:, 0:2], SHD, PGs[:, 7, :], True, True, ["c6", "PGs"], ["pb%d" % pb])
            for eo in range(2):
                o_ = eo * 4
                P.tt("dve", S3, PGs[:, o_ + 0, :], PGs[:, o_ + 1, :], ALU.add, ["PGs"], ["S3"])
                P.tt("dve", S3, S3, PGs[:, o_ + 2, :], ALU.add, ["S3", "PGs"], ["S3"])
                P.stt("dve", PSB[:, :, eo], S3, 2.0, PGs[:, 3, :], ALU.mult, ALU.add, ["S3", "PGs"], ["PSB"])
                if eo == 0:
                    P.tt("dve", PSB[:, :, 0], PSB[:, :, 0], pbank[pb][:, 0:2], ALU.add, ["PSB", "pb%d" % pb], ["PSB"])
                else:
                    P.tt("dve", PSB[:, :, 1], PSB[:, :, 1], PGs[:, 7, :], ALU.add, ["PSB", "PGs"], ["PSB"])
            P.st(SELD[b * 2:b * 2 + 2, :].rearrange("k (p e) -> p k e", e=2), PSB, ["PSB"], ["SELD"], eng="sp")
            P.ld(WK, st_win[b, :, :].rearrange("(c p) x -> p c x", p=128), ["WK"])
            P.ld(QB128, bass.AP(Z.tensor, (TP + b) * IN_DIM + O_Q, [[0, 128], [1, 512]]), ["QB128"])
            P.ts("dve", QB128, QB128, 0.125, ALU.mult, ["QB128"], ["QB128"])
            for h in range(8):
                kvh = h // 4
                P.tt("dve", TMPs[:, 0:256].rearrange("p (c i) -> p c i", c=4), WK[:, :, kvh * 64:(kvh + 1) * 64],
                     QB128[:, h * 64:(h + 1) * 64].unsqueeze(1).to_broadcast([128, 4, 64]), ALU.mult, ["WK", "QB128"],
                     ["TMPs"])
                P.op("dve", lambda e, h=h: e.reduce_sum(out=SWs[:, :, h], in_=TMPs[:, 0:256].rearrange(
                    "p (c i) -> p c i", c=4), axis=AX.X), ["TMPs"], ["SWs"])
            P.tt("dve", SWs, SWs, BWt, ALU.add, ["SWs", "c6"], ["SWs"])
            P.act(SWs, SWs, AF.Exp, ["SWs"], ["SWs"])
            for kvh in range(2):
                pn_ = P.bank()
                for c in range(4):
                    P.mm(pbank[pn_][0:4, 0:64], SWs[:, c, kvh * 4:(kvh + 1) * 4],
                         WK[:, c, 128 + kvh * 64:128 + (kvh + 1) * 64], c == 0, c == 3, ["SWs", "WK"], ["pb%d" % pn_])
                pd_ = P.bank()
                for c in range(4):
                    P.mm(pbank[pd_][0:4, 0:1], SWs[:, c, kvh * 4:(kvh + 1) * 4], ONESM[:, 0:1], c == 0, c == 3,
                         ["SWs", "ONESM"], ["pb%d" % pd_])
                P.cp("act", OC[0:4, 0:64], pbank[pn_][0:4, 0:64], ["pb%d" % pn_], ["OC"])
                P.cp("act", OC[0:4, 64:65], pbank[pd_][0:4, 0:1], ["pb%d" % pd_], ["OC"])
                P.st(OSD[1, b, kvh * 4:(kvh + 1) * 4, :], OC[0:4, :], ["OC"], ["OSD"], eng="sp")

        if K6 >= 2:
            SELIN = alloc(256)
            SELW = alloc(256)
            V16s = alloc(16)
            I16s = alloc(16).bitcast(U32)
            BLK = alloc(16)
            HALF = alloc(16)
            PAR = alloc(16)
            PGID = alloc(16)
            PHYS = alloc(16)
            OH6 = alloc3(15, 128)
            PQ5 = alloc3(15, 5)
            P.ld(SELIN[0:8], SELD[:, :], ["SELIN"], r=["SELD"])
            P.memset("dve", SELIN[0:8, 0:1], -1e9, ["SELIN"])
            P.memset("dve", SELIN[0:8, 255:256], -1e9, ["SELIN"])
            top16(SELIN[0:8], SELW[0:8], V16s[0:8], I16s[0:8], 256, "SELIN")
            P.cp("dve", BLK[0:8], I16s[0:8], ["SELINi"], ["BLK"])
            P.memset("dve", BLK[0:8, 13:14], 0.0, ["BLK"])
            P.memset("dve", BLK[0:8, 14:15], 255.0, ["BLK"])
            B15 = BLK[0:8, 0:15]
            P.tt("dve", OH6[0:8], B15.unsqueeze(2).to_broadcast([8, 15, 128]),
                 IOT128[0:8, 1, :].unsqueeze(1).to_broadcast([8, 15, 128]), ALU.is_ge, ["BLK", "c6"], ["OH6"])
            P.op("dve", lambda e: e.tensor_reduce(out=HALF[0:8, 0:15], in_=OH6[0:8], axis=AX.X, op=ALU.add), ["OH6"],
                 ["HALF"])
            P.stt("dve", PAR[0:8, 0:15], HALF[0:8, 0:15], -2.0, B15, ALU.mult, ALU.add, ["HALF", "BLK"], ["PAR"])
            P.tt("dve", OH6[0:8], HALF[0:8, 0:15].unsqueeze(2).to_broadcast([8, 15, 128]),
                 IOT128[0:8, 0, :].unsqueeze(1).to_broadcast([8, 15, 128]), ALU.is_equal, ["HALF", "c6"], ["OH6"])
            P.tt("dve", OH6[0:8], OH6[0:8], PTROW[0:8].unsqueeze(1).to_broadcast([8, 15, 128]), ALU.mult,
                 ["OH6", "PTROW"], ["OH6"])
            P.op("dve", lambda e: e.tensor_reduce(out=PGID[0:8, 0:15], in_=OH6[0:8], axis=AX.X, op=ALU.add), ["OH6"],
                 ["PGID"])
            P.stt("dve", PHYS[0:8, 0:15], PGID[0:8, 0:15], 2.0, PAR[0:8, 0:15], ALU.mult, ALU.add, ["PGID", "PAR"],
                  ["PHYS"])
            for qd in range(4):
                P.ts("dve", PQ5[0:8, :, qd], PHYS[0:8, 0:15], 4.0, ALU.mult, ["PHYS"], ["PQ5"], s2=float(qd),
                     op1=ALU.add)
            P.cp("dve", PQ5[0:8, :, 4], B15, ["BLK"], ["PQ5"])
            P.st(SCRP[:, :, :], PQ5[0:8], ["PQ5"], ["SCRP"], eng="sp")
            IQf = [alloc(5) for _ in range(2)]
            P.memset("dve", IDXS, 0, ["IDXS"])
            for kvh in range(2):
                for b in range(SSC):
                    P.ld(IQf[kvh][15 * b:15 * b + 15], SCRP[b * 2 + kvh, :, :], ["IQf%d" % kvh], r=["SCRP"])
                P.cp("dve", IDXS[0:60, kvh * 4:(kvh + 1) * 4], IQf[kvh][0:60, 0:4], ["IQf%d" % kvh], ["IDXS"])
            QB60 = alloc(512)
            for b in range(SSC):
                P.ld(QB60[15 * b:15 * b + 15], bass.AP(Z.tensor, (TP + b) * IN_DIM + O_Q, [[0, 15], [1, 512]]), ["QB60"])
            P.ts("dve", QB60[0:60], QB60[0:60], 0.125, ALU.mult, ["QB60"], ["QB60"])
            FL254 = alloc(1)
            W0 = alloc(1)
            BIAS = alloc3(4, 64)
            SS6 = alloc3(4, 64)
            PVD = alloc(260)
            PVt = alloc(64)
            DNt = alloc(4)
            SLCR = [alloc(260) for _ in range(2)]
            for kvh in range(2):
                hs = slice(kvh * 4, (kvh + 1) * 4)
                P.ts("dve", FL254[0:60], IQf[kvh][0:60, 4:5], 254.0, ALU.is_equal, ["IQf%d" % kvh], ["FL254"])
                P.tt("dve", W0[0:60], FL254[0:60], FL255[0:60], ALU.add, ["FL254", "c6"], ["W0"])
                P.ts("dve", W0[0:60], W0[0:60], -1.0, ALU.mult, ["W0"], ["W0"], s2=1.0, op1=ALU.add)
                P.ts("dve", BIAS[0:60], T25[0:60, 0, hs, :], FL254[0:60], ALU.mult, ["c6", "FL254"], ["BIAS"])
                P.stt("dve", BIAS[0:60], T25[0:60, 1, hs, :], FL255[0:60], BIAS[0:60], ALU.mult, ALU.add,
                      ["c6", "BIAS"], ["BIAS"])
                P.stt("dve", BIAS[0:60], CBR6[0:60, hs].unsqueeze(2).to_broadcast([60, 4, 64]), W0[0:60], BIAS[0:60],
                      ALU.mult, ALU.add, ["c6", "W0", "BIAS"], ["BIAS"])
                P.memset("dve", PVD[0:60], 0.0, ["PVD"])
                for qd in range(4):
                    gb = gcnt6 % 2
                    gcnt6 += 1
                    P.dma("pool", lambda e, gb=gb, col=kvh * 4 + qd: e.indirect_dma_start(
                        out=G[gb], out_offset=None, in_=pool_slc_d[:, :],
                        in_offset=bass.IndirectOffsetOnAxis(ap=IDXS[:, col:col + 1], axis=0)), ["IDXS"],
                        ["G%d" % gb])
                    GQ3 = G[gb][0:60].rearrange("p (t x) -> p t x", t=16)
                    ts_ = slice(qd * 16, (qd + 1) * 16)
                    for g in range(4):
                        h = kvh * 4 + g
                        P.tt("dve", TMPs[0:60].rearrange("p (t d) -> p t d", t=16), GQ3[:, :, kvh * 64:(kvh + 1) * 64],
                             QB60[0:60, h * 64:(h + 1) * 64].unsqueeze(1).to_broadcast([60, 16, 64]), ALU.mult,
                             ["G%d" % gb, "QB60"], ["TMPs"])
                        P.op("dve", lambda e, g=g, ts_=ts_: e.reduce_sum(out=SS6[0:60, g, ts_], in_=TMPs[0:60].rearrange(
                            "p (t d) -> p t d", t=16), axis=AX.X), ["TMPs"], ["SS6"])
                    P.tt("dve", SS6[0:60, :, ts_], SS6[0:60, :, ts_], BIAS[0:60, :, ts_], ALU.add, ["SS6", "BIAS"], ["SS6"])
                    P.act(SS6[0:60, :, ts_], SS6[0:60, :, ts_], AF.Exp, ["SS6"], ["SS6"])
                    P.op("dve", lambda e, ts_=ts_: e.reduce_sum(out=DNt[0:60], in_=SS6[0:60, :, ts_], axis=AX.X), ["SS6"],
                         ["DNt"])
                    P.tt("dve", PVD[0:60, 256:260], PVD[0:60, 256:260], DNt[0:60], ALU.add, ["PVD", "DNt"], ["PVD"])
                    for g in range(4):
                        P.tt("dve", TMPs[0:60].rearrange("p (d t) -> p d t", d=64),
                             GQ3[:, :, 128 + kvh * 64:128 + (kvh + 1) * 64].rearrange("p t d -> p d t"),
                             SS6[0:60, g, ts_].unsqueeze(1).to_broadcast([60, 64, 16]), ALU.mult, ["G%d" % gb, "SS6"],
                             ["TMPs"])
                        P.op("dve", lambda e: e.reduce_sum(out=PVt[0:60], in_=TMPs[0:60].rearrange(
                            "p (d t) -> p d t", d=64), axis=AX.X), ["TMPs"], ["PVt"])
                        P.tt("dve", PVD[0:60, g * 64:(g + 1) * 64], PVD[0:60, g * 64:(g + 1) * 64], PVt[0:60], ALU.add,
                             ["PVD", "PVt"], ["PVD"])
                pb = P.bank()
                P.mm(pbank[pb][0:4, 0:260], IND[0:60, :], PVD[0:60, :], True, True, ["c6", "PVD"], ["pb%d" % pb])
                P.cp("act", SLCR[kvh][0:4], pbank[pb][0:4, 0:260], ["pb%d" % pb], ["SLCR%d" % kvh])
            OSB = alloc(2 * 8 * 65).rearrange("p (a h x) -> p a h x", a=2, h=8)
            for a in range(2):
                P.ld(OSB[0:4, a], OSD[a, :, :, :], ["OSB"], r=["OSD"])
            GS4 = alloc(24)
            P.act(GS4[0:4], ZN[0:4, 1280:1304], AF.Sigmoid, ["ZN"], ["GS4"])
            QS3 = QS[0:4].rearrange("p (h d) -> p h d", h=8)
            NOS = alloc(512)
            NOS3 = NOS[0:4].rearrange("p (h d) -> p h d", h=8)
            TL = alloc(512)
            TL3 = TL[0:4].rearrange("p (h d) -> p h d", h=8)
            PTL = alloc(8)
            DEN8 = alloc(8)
            P.tt("dve", NOS3, OSB[0:4, 0, :, 0:64], GS4[0:4, 0:8].unsqueeze(2).to_broadcast([4, 8, 64]), ALU.mult,
                 ["OSB", "GS4"], ["NOS"])
            for br, kbase in ((1, 768), (2, 1024)):
                for kvh in range(2):
                    P.tt("dve", TL3[:, kvh * 4:(kvh + 1) * 4, :], QS3[:, kvh * 4:(kvh + 1) * 4, :],
                         ZN[0:4, kbase + kvh * 64:kbase + (kvh + 1) * 64].unsqueeze(1).to_broadcast([4, 4, 64]),
                         ALU.mult, ["QS", "ZN", "TLr"], ["TL"])
                P.op("dve", lambda e: e.reduce_sum(out=PTL[0:4], in_=TL3, axis=AX.X), ["TL"], ["PTL"])
                P.tt("dve", PTL[0:4], PTL[0:4], RB0[0:4], ALU.add, ["PTL", "c6"], ["PTL"])
                P.act(PTL[0:4], PTL[0:4], AF.Exp, ["PTL"], ["PTL"])
                for kvh in range(2):
                    hs = slice(kvh * 4, (kvh + 1) * 4)
                    if br == 1:
                        num_src = SLCR[kvh][0:4, 0:256].rearrange("p (g d) -> p g d", g=4)
                        den_src = SLCR[kvh][0:4, 256:260]
                        rk = ["SLCR%d" % kvh]
                    else:
                        num_src = OSB[0:4, 1, hs, 0:64]
                        den_src = OSB[0:4, 1, hs, 64]
                        rk = ["OSB"]
                    P.tt("dve", DEN8[0:4, hs], den_src, PTL[0:4, hs], ALU.add, rk + ["PTL"], ["DEN8"])
                    P.tt("dve", TL3[:, hs, :], PTL[0:4, hs].unsqueeze(2).to_broadcast([4, 4, 64]),
                         ZN[0:4, kbase + 128 + kvh * 64:kbase + 128 + (kvh + 1) * 64].unsqueeze(1).to_broadcast([4, 4, 64]),
                         ALU.mult, ["PTL", "ZN", "PTL"], ["TL"])
                    P.tt("dve", TL3[:, hs, :], TL3[:, hs, :], num_src, ALU.add, ["TL"] + rk, ["TL"])
                P.ts("dve", DEN8[0:4], DEN8[0:4], 1e-30, ALU.max, ["DEN8"], ["DEN8"])
                P.op("dve", lambda e: e.reciprocal(out=DEN8[0:4], in_=DEN8[0:4]), ["DEN8"], ["DEN8"])
                P.tt("dve", DEN8[0:4], DEN8[0:4], GS4[0:4, br * 8:(br + 1) * 8], ALU.mult, ["DEN8", "GS4"], ["DEN8"])
                P.tt("dve", TL3, TL3, DEN8[0:4].unsqueeze(2).to_broadcast([4, 8, 64]), ALU.mult, ["TL", "DEN8"], ["TL"])
                P.tt("dve", NOS[0:4], NOS[0:4], TL[0:4], ALU.add, ["NOS", "TL"], ["NOS", "TLr"])
            P.st(NOUTD[TP:TP + SSC, :], NOS[0:4], ["NOS"], ["NOUTD_s"], eng="sp")

        P.barrier()
        apos[0] = persist_mark
        K5 = int(os.environ.get("K5STOP", "9"))
        ALPHA = 2.0 ** 0.25
        WUM = alloc3(4, D)
        WUN = alloc3(4, D)
        WO = alloc3(8, D)
        LNP = alloc3(4, D)
        P.ld(WUM, wupm_d.rearrange("(c p) n -> p c n", p=128), ["w4"])
        P.ld(WUN, wupn_d.rearrange("(c p) n -> p c n", p=128), ["w4"])
        P.ld(WO, wout_d.rearrange("(c p) n -> p c n", p=128), ["w4"])
        P.ld(LNP, lnp_d[:, :, :], ["w4"])
        NB = 2
        XB = [alloc(D) for _ in range(NB)]
        MOB = [alloc(512) for _ in range(NB)]
        NOB = [alloc(512) for _ in range(NB)]
        GMB = [alloc(D) for _ in range(NB)]
        GNB = [alloc(D) for _ in range(NB)]
        MOT = alloc3(4, 128)
        NOTt = alloc3(4, 128)
        MIX = alloc(D)
        MIXT = alloc3(8, 128)
        RB = alloc(D)
        X1B = [alloc(D) for _ in range(NB)]
        STT = alloc(16)
        AGG = alloc(4)
        RSTD = alloc(1)

        def layer_norm(src, dst, gi, rk, wk_):
            for i in range(2):
                P.op("dve", lambda e, i=i: e.bn_stats(out=STT[:, i * 6:(i + 1) * 6], in_=src[:, i * 512:(i + 1) * 512]),
                     rk, ["STT"])
            P.op("dve", lambda e: e.bn_aggr(out=AGG[:, 0:2], in_=STT[:, 0:12].rearrange("p (a b) -> p a b", a=2)),
                 ["STT"], ["AGG"])
            P.act(RSTD, AGG[:, 1:2], AF.Sqrt, ["AGG"], ["RSTD"], bias=1e-5, scale=1.0)
            P.op("dve", lambda e: e.reciprocal(out=RSTD, in_=RSTD), ["RSTD"], ["RSTD"])
            P.ts("dve", dst, src, AGG[:, 0:1], ALU.subtract, rk + ["AGG", "RSTD"], wk_, s2=RSTD, op1=ALU.mult)
            P.tt("pool", dst, dst, LNP[:, gi, :], ALU.mult, wk_ + ["w4"], wk_)
            P.tt("pool", dst, dst, LNP[:, gi + 1, :], ALU.add, wk_ + ["w4"], wk_)

        for ti in range(NT + 1 if K5 >= 1 else 0):
            b_ = ti % NB
            rows = slice(ti * 128, (ti + 1) * 128)
            xsrc = x_p[rows, :] if ti < NT else x_s[:, :]
            P.ld(XB[b_], xsrc, ["XB%d" % b_])
            P.ld(MOB[b_], MOUT[rows, :], ["MOB%d" % b_], r=["MOUT", "MOUT_s"])
            P.ld(NOB[b_], NOUTD[rows, :], ["NOB%d" % b_], r=["NOUTD", "NOUTD_s"])
            P.ld(GMB[b_], Z[rows, O_GM:O_GM + D], ["GMB%d" % b_])
            P.ld(GNB[b_], Z[rows, O_GNN:O_GNN + D], ["GNB%d" % b_])
            for src, skey, dst, dkey in ((MOB[b_], "MOB%d" % b_, MOT, "MOT"), (NOB[b_], "NOB%d" % b_, NOTt, "NOT")):
                pb = P.bank()
                for c in range(4):
                    P.tr(pbank[pb][:, c * 128:(c + 1) * 128], src[:, c * 128:(c + 1) * 128], ident, [skey, "ident"],
                         ["pb%d" % pb])
                P.cp("act", dst, pbank[pb][:, :].rearrange("p (c t) -> p c t", c=4), ["pb%d" % pb], [dkey])
            P.act(GMB[b_], GMB[b_], AF.Sigmoid, ["GMB%d" % b_], ["GMB%d" % b_])
            P.act(GNB[b_], GNB[b_], AF.Sigmoid, ["GNB%d" % b_], ["GNB%d" % b_])
            for nb in range(2):
                ns = slice(nb * 512, (nb + 1) * 512)
                pa = P.bank()
                for c in range(4):
                    P.mm(pbank[pa][:, :], MOT[:, c, :], WUM[:, c, ns], c == 0, c == 3, ["MOT", "w4"], ["pb%d" % pa])
                pn = P.bank()
                for c in range(4):
                    P.mm(pbank[pn][:, :], NOTt[:, c, :], WUN[:, c, ns], c == 0, c == 3, ["NOT", "w4"], ["pb%d" % pn])
                P.tt("dve", MIX[:, ns], pbank[pa][:, :], GMB[b_][:, ns], ALU.mult, ["pb%d" % pa, "GMB%d" % b_], ["MIX"])
                P.tt("dve", GNB[b_][:, ns], pbank[pn][:, :], GNB[b_][:, ns], ALU.mult, ["pb%d" % pn, "GNB%d" % b_],
                     ["GNB%d" % b_])
                P.tt("pool", MIX[:, ns], MIX[:, ns], GNB[b_][:, ns], ALU.add, ["MIX", "GNB%d" % b_], ["MIX"])
            for g in range(2):
                pb = P.bank()
                for j in range(4):
                    c = g * 4 + j
                    P.tr(pbank[pb][:, j * 128:(j + 1) * 128], MIX[:, c * 128:(c + 1) * 128], ident, ["MIX", "ident"],
                         ["pb%d" % pb])
                P.cp("act", MIXT[:, g * 4:(g + 1) * 4, :], pbank[pb][:, :].rearrange("p (c t) -> p c t", c=4),
                     ["pb%d" % pb], ["MIXT"])
            for nb in range(2):
                ns = slice(nb * 512, (nb + 1) * 512)
                pb = P.bank()
                for c in range(8):
                    P.mm(pbank[pb][:, :], MIXT[:, c, :], WO[:, c, ns], c == 0, c == 7, ["MIXT", "w4"], ["pb%d" % pb])
                P.stt("dve", RB[:, ns], XB[b_][:, ns], ALPHA, pbank[pb][:, :], ALU.mult, ALU.add,
                      ["XB%d" % b_, "pb%d" % pb], ["RB"])
            layer_norm(RB, X1B[b_], 0, ["RB"], ["X1B%d" % b_])
            P.st(X1D[rows, :], X1B[b_], ["X1B%d" % b_], ["X1D_%d" % ti])

        P.barrier()
        apos[0] = persist_mark
        LN2 = alloc3(2, D)
        WPR = alloc3(8, 2048)
        IOTA = alloc(16)
        P.ld(LN2, lnp_d[:, 2:4, :], ["w5"])
        IOTA2 = alloc(16)
        P.ld(IOTA, iota_d[:, :], ["w5"])
        P.ts("dve", IOTA2, IOTA, 16.0, ALU.mult, ["w5"], ["w5"], s2=16.0, op1=ALU.add)
        cmark = apos[0]
        CV32 = [alloc(4096) for _ in range(3)]
        CV16 = [alloc(2048).bitcast(BF16) for _ in range(3)]
        ccnt = 0
        for src_d, dst_d in ((peer_u_d, U16D), (peer_v_d, V16D)):
            for ch in range(32):
                cb_ = ccnt % 3
                ccnt += 1
                rs_ = slice(ch * 512, (ch + 1) * 512)
                P.ld(CV32[cb_], src_d[rs_, :].rearrange("(p r) n -> p (r n)", p=128), ["CV32_%d" % cb_])
                eng = ("dve", "act", "pool")[cb_]
                P.cp(eng, CV16[cb_], CV32[cb_], ["CV32_%d" % cb_], ["CV16_%d" % cb_])
                P.st(dst_d[rs_, :].rearrange("(p r) n -> p (r n)", p=128), CV16[cb_], ["CV16_%d" % cb_], ["T16"], eng="sp")
        P.barrier()
        apos[0] = cmark
        PWT = alloc(16 * 8 * 128).rearrange("p (a c m) -> p a c m", a=16, c=8)
        SKT = alloc3(2, 128)
        P.ld(PWT, pwqt_d[:, :, :, :], ["PWT"])
        P.ld(SKT, skt_d[:, :, :], ["SKT"])
        for c in range(8):
            for g in range(4):
                pb = P.bank()
                for j in range(4):
                    hp = g * 4 + j
                    P.mm(pbank[pb][:, j * 128:(j + 1) * 128], PWT[:, hp, c, :], SKT[:, hp % 2, :], True, True,
                         ["PWT", "SKT"], ["pb%d" % pb])
                P.cp("act" if g % 2 else "dve", WPR[:, c, g * 512:(g + 1) * 512], pbank[pb][:, :], ["pb%d" % pb], ["w5"])
        P.barrier()
        apos[0] -= 16 * 8 * 128 + 256
        X1 = [alloc(D) for _ in range(2)]
        X1T = alloc3(8, 128)
        SS = alloc3(16, 128)
        SS2 = alloc3(16, 128)
        V16 = alloc3(16, 16)
        I16 = alloc3(16, 16)
        I16u = I16.bitcast(U32)
        I16f = alloc3(16, 16)
        CAND = alloc3(8, 256)
        CAND2 = alloc3(8, 256)
        VC = alloc3(8, 16)
        ICu = alloc3(8, 16).bitcast(U32)
        ICf = alloc3(8, 16)
        AIX = alloc3(8, 16)
        BIX = alloc3(8, 16)
        OH = alloc(8 * 16 * 16).rearrange("p (h k a) -> p h k a", h=8, k=16)
        I1S = alloc3(8, 16)
        I2S = alloc3(8, 16)
        EF = alloc(128)
        GWs = [alloc3(8, 16) for _ in range(2)]
        sm8b = alloc(8)
        APREs = [alloc(128) for _ in range(2)]
        GT = alloc(128)
        WCOLs = [alloc(128) for _ in range(2)]
        NGU = 8
        NGV = 12
        UBu = [alloc(D // 2).bitcast(BF16) for _ in range(NGU)]
        UBv = [alloc(D // 2).bitcast(BF16) for _ in range(NGV)]
        JUNK = alloc(D)
        RB2 = alloc(D)
        EIs = [EI, EI2]
        DG = [alloc(64).bitcast(BF16) for _ in range(4)]
        YB = alloc(D)
        gcnt = 0
        dcnt = 0

        def stage_a(ti, parts=(0, 1, 2, 3, 4, 5)):
            for part in parts:
                stage_a_part(ti, part)

        def stage_a_part(ti, part):
            b_ = ti % 2
            rows = slice(ti * 128, (ti + 1) * 128)
            if part == 0:
                P.ld(X1[b_], X1D[rows, :], ["X1_%d" % b_], r=["X1D_%d" % ti])
                for g in range(2):
                    pb = P.bank()
                    for j in range(4):
                        c = g * 4 + j
                        P.tr(pbank[pb][:, j * 128:(j + 1) * 128], X1[b_][:, c * 128:(c + 1) * 128], ident,
                             ["X1_%d" % b_, "ident"], ["pb%d" % pb])
                    P.cp("act", X1T[:, g * 4:(g + 1) * 4, :], pbank[pb][:, :].rearrange("p (c t) -> p c t", c=4),
                         ["pb%d" % pb], ["X1T"])
                return
            if part in (1, 2, 3, 4):
                g = part - 1
                pb = P.bank()
                for c in range(8):
                    P.mm(pbank[pb][:, :], X1T[:, c, :], WPR[:, c, g * 512:(g + 1) * 512], c == 0, c == 7,
                         ["X1T", "w5"], ["pb%d" % pb])
                P.cp("act", SS[:, g * 4:(g + 1) * 4, :], pbank[pb][:, :].rearrange("p (a k) -> p a k", a=4),
                     ["pb%d" % pb], ["SS%d" % g])
                for hp in range(g * 4, g * 4 + 4):
                    top16(SS[:, hp, :], SS2[:, hp, :], V16[:, hp, :], I16u[:, hp, :], 128, "SS%d" % (hp // 4))
                return
            VK = ["SS%dv" % g for g in range(4)]
            IK = ["SS%di" % g for g in range(4)]
            V4 = V16.rearrange("p (h q) k -> p h q k", q=2)
            P.tt("dve", CAND.rearrange("p h (a b) -> p h a b", a=16),
                 V4[:, :, 0, :].unsqueeze(3).to_broadcast([128, 8, 16, 16]),
                 V4[:, :, 1, :].unsqueeze(2).to_broadcast([128, 8, 16, 16]), ALU.add, VK, ["CAND"])
            for h in range(8):
                top16(CAND[:, h, :], CAND2[:, h, :], VC[:, h, :], ICu[:, h, :], 256, "CAND")
            P.cp("dve", ICf, ICu, ["CANDi"], ["ICf"])
            P.cp("dve", I16f, I16u, IK, ["I16f"])
            P.tt("dve", OH, ICf.unsqueeze(3).to_broadcast([128, 8, 16, 16]),
                 IOTA2.unsqueeze(1).unsqueeze(1).to_broadcast([128, 8, 16, 16]), ALU.is_ge, ["ICf", "w5"], ["OH"])
            P.op("dve", lambda e: e.tensor_reduce(out=AIX, in_=OH, axis=AX.X, op=ALU.add), ["OH"], ["AIX"])
            P.stt("dve", BIX, AIX, -16.0, ICf, ALU.mult, ALU.add, ["AIX", "ICf"], ["BIX"])
            I4 = I16f.rearrange("p (h q) k -> p h q k", q=2)
            iota_b = IOTA.unsqueeze(1).unsqueeze(1).to_broadcast([128, 8, 16, 16])
            for sel_ix, src_i, dst in ((AIX, I4[:, :, 0, :], I1S), (BIX, I4[:, :, 1, :], I2S)):
                P.tt("dve", OH, sel_ix.unsqueeze(3).to_broadcast([128, 8, 16, 16]), iota_b, ALU.is_equal,
                     ["AIX", "BIX", "w5"], ["OH"])
                P.tt("dve", OH, OH, src_i.unsqueeze(2).to_broadcast([128, 8, 16, 16]), ALU.mult, ["OH", "I16f"], ["OH"])
                P.op("dve", lambda e, dst=dst: e.tensor_reduce(out=dst, in_=OH, axis=AX.X, op=ALU.add), ["OH"],
                     ["I12S"])
            P.stt("dve", EF.rearrange("p (h k) -> p h k", h=8), I1S, 128.0, I2S, ALU.mult, ALU.add, ["I12S"], ["EF"])
            P.cp("dve", EIs[b_], EF, ["EF"], ["EI%d" % b_])
            gw = GWs[b_]
            P.tt("dve", gw, VC, VC[:, :, 0:1].to_broadcast([128, 8, 16]), ALU.subtract, ["CANDv"], ["GW%d" % b_])

        ucnt = [0]
        vcnt = [0]
        dgc = [0]

        def col_u(ti, j):
            b_ = ti % 2
            ub = ucnt[0] % NGU
            ucnt[0] += 1
            P.dma("pool", lambda e: e.indirect_dma_start(
                out=UBu[ub], out_offset=None, in_=U16D[:, :],
                in_offset=bass.IndirectOffsetOnAxis(ap=EIs[b_][:, j:j + 1], axis=0)), ["EI%d" % b_], ["UBu%d" % ub])
            P.op("dve", lambda e: e.scalar_tensor_tensor(
                out=JUNK, in0=UBu[ub], scalar=1.0, in1=X1[b_], op0=ALU.mult, op1=ALU.mult,
                accum_out=APREs[b_][:, j:j + 1]), ["UBu%d" % ub, "X1_%d" % b_], ["JUNK", "APRE%d" % b_])

        def finish_u(ti):
            b_ = ti % 2
            gw = GWs[b_]
            P.act(gw, gw, AF.Exp, ["GW%d" % b_], ["GW%d" % b_])
            P.op("dve", lambda e: e.reduce_sum(out=sm8b, in_=gw, axis=AX.X), ["GW%d" % b_], ["sm8b"])
            P.op("dve", lambda e: e.reciprocal(out=sm8b, in_=sm8b), ["sm8b"], ["sm8b"])
            P.tt("dve", gw, gw, sm8b.unsqueeze(2).to_broadcast([128, 8, 16]), ALU.mult, ["GW%d" % b_, "sm8b"],
                 ["GW%d" % b_])
            gelu_from(APREs[b_], GT, WCOLs[b_], ["APRE%d" % b_], "WCOL%d" % b_)
            P.tt("dve", WCOLs[b_], WCOLs[b_], GWs[b_].rearrange("p h k -> p (h k)"), ALU.mult,
                 ["WCOL%d" % b_, "GW%d" % b_], ["WCOL%d" % b_])

        def col_v(ti, j, p0, p1):
            b_ = ti % 2
            vb = vcnt[0] % NGV
            vcnt[0] += 1
            P.dma("pool", lambda e: e.indirect_dma_start(
                out=UBv[vb], out_offset=None, in_=V16D[:, :],
                in_offset=bass.IndirectOffsetOnAxis(ap=EIs[b_][:, j:j + 1], axis=0)), ["EI%d" % b_], ["UBv%d" % vb])
            dg = dgc[0] % 4
            dgc[0] += 1
            P.act(DG[dg], ident, AF.Copy, ["ident", "WCOL%d" % b_], ["DG%d" % dg], scale=WCOLs[b_][:, j:j + 1])
            P.mm(pbank[p0][:, :], DG[dg], UBv[vb][:, 0:512], j == 0, j == 127, ["DG%d" % dg, "UBv%d" % vb],
                 ["pb%d" % p0])
            P.mm(pbank[p1][:, :], DG[dg], UBv[vb][:, 512:1024], j == 0, j == 127, ["DG%d" % dg, "UBv%d" % vb],
                 ["pb%d" % p1])

        def finish_v(ti, p0, p1):
            b_ = ti % 2
            rows = slice(ti * 128, (ti + 1) * 128)
            P.stt("dve", RB2[:, 0:512], X1[b_][:, 0:512], ALPHA, pbank[p0][:, :], ALU.mult, ALU.add,
                  ["X1_%d" % b_, "pb%d" % p0], ["RB2"])
            P.stt("dve", RB2[:, 512:1024], X1[b_][:, 512:1024], ALPHA, pbank[p1][:, :], ALU.mult, ALU.add,
                  ["X1_%d" % b_, "pb%d" % p1], ["RB2"])
            layer_norm(RB2, YB, 0, ["RB2"], ["YB"])
            P.st(y_out[rows, :], YB, ["YB"], eng="sp")

        LNP = LN2
        ntile = NT + 1 if K5 >= 2 else 0
        if ntile:
            stage_a(0)
            for j in range(128):
                col_u(0, j)
            finish_u(0)
        for ti in range(ntile):
            nxt = ti + 1 < ntile
            p0 = P.bank()
            p1 = P.bank()
            LEAD = 112
            SLICES = {0: (0, 1), 10: (2,), 20: (3,), 30: (4, 5)}
            for k in range(128 + LEAD):
                if nxt and k in SLICES:
                    stage_a(ti + 1, SLICES[k])
                if k < 128:
                    col_v(ti, k, p0, p1)
                if nxt and k >= LEAD:
                    col_u(ti + 1, k - LEAD)
                if k == 136:
                    finish_v(ti, p0, p1)
            if nxt:
                finish_u(ti + 1)

        with nc.Block() as block:
            @block.tensor
            def _(e):
                P.emit("pe", e)

            @block.scalar
            def _(e):
                P.emit("act", e)

            @block.vector
            def _(e):
                P.emit("dve", e)

            @block.gpsimd
            def _(e):
                P.emit("pool", e)

            @block.sync
            def _(e):
                P.emit("sp", e)
    return nc


_NC_CACHE = {}


def kernel(x_prompt, x_sample, cache_cmp_kv, cache_slc_kv, page_table, state_win_kv, state_mlstm_C,
           state_mlstm_n, state_mlstm_m, state_mlstm_conv, w_in, m_conv_w, m_conv_b, m_wq, m_wk,
           m_gate_bias, m_norm_g, cmp_pe, cmp_w1, cmp_w2, rel_bias, w_up_m, w_up_n, w_out, ln1_g, ln1_b,
           ln2_g, ln2_b, peer_wq, peer_subkeys, peer_u, peer_v):
    f32 = np.float32
    if "nc" not in _NC_CACHE:
        _NC_CACHE["nc"] = build_program()
    nc = _NC_CACHE["nc"]
    xp = np.ascontiguousarray(np.asarray(x_prompt, f32)).reshape(BP * SEQ, D)
    xs = np.asarray(x_sample, f32).reshape(BS, D)
    ident = np.eye(128, dtype=f32)
    w_in0 = np.ascontiguousarray(np.asarray(w_in, f32)[0])
    stw = np.asarray(state_win_kv, f32)[0].reshape(BS, 512, 256)
    stc = np.asarray(state_mlstm_conv, f32)[0]
    cw = np.asarray(m_conv_w, f32)[0]
    convw_l = np.ascontiguousarray(np.transpose(cw.reshape(4, 4, 128), (2, 1, 0)))
    convb_l = np.ascontiguousarray(np.asarray(m_conv_b, f32)[0].reshape(4, 128).T)
    wq_l = np.ascontiguousarray(np.transpose(np.asarray(m_wq, f32)[0], (1, 0, 2)))
    wk_l = np.ascontiguousarray(np.transpose(np.asarray(m_wk, f32)[0], (1, 0, 2)))
    gb_l = np.ascontiguousarray(np.asarray(m_gate_bias, f32)[0].T)
    normg_rep = np.ascontiguousarray(np.broadcast_to(np.asarray(m_norm_g, f32)[0][None, :], (128, 512)))
    sel_c = np.zeros((4, 4, 128), f32)
    for h in range(4):
        sel_c[h, h, :] = 1.0
    tri_c = np.triu(np.ones((128, 128), f32))
    relb = np.asarray(rel_bias, f32)
    dist = np.arange(0, 4096)
    nf = np.maximum(dist, 16).astype(f32)
    large = 16 + (np.log(nf / f32(16)) / f32(np.log(8.0)) * f32(16)).astype(np.int32)
    bucket = np.where(dist < 16, dist, np.minimum(large, 31)).astype(np.int64)
    NEGM = f32(-30000.0)
    tt_ = np.arange(SEQ)[:, None]
    nn_ = np.arange(128)[None, :]
    dcm = tt_ - 16 * nn_ - 31
    vcm = (dcm >= 0) & (nn_ <= 126)
    cbt = np.where(vcm[:, None, :], relb[bucket[np.maximum(dcm, 0)]].transpose(0, 2, 1), NEGM).astype(f32)
    ii = np.arange(128)[:, None]
    jj = np.arange(128)[None, :]
    tz = np.empty((128, 8, 2, 128), f32)
    d0 = jj - ii
    tz[:, :, 0, :] = np.where((d0 >= 0)[:, None, :], relb[bucket[np.maximum(d0, 0)]].transpose(0, 2, 1), NEGM)
    tz[:, :, 1, :] = relb[bucket[128 + d0]].transpose(0, 2, 1)
    wz = np.where(jj < ii, f32(0), NEGM).astype(f32)
    ebig = (np.arange(SEQ)[None, :] // 64 == np.arange(32)[:, None]).astype(f32) * f32(30000.0)
    tq_ = (np.arange(16)[None, :, None] * 128 + np.arange(128)[:, None, None])
    bb_ = np.arange(32)[None, None, :]
    cur = tq_ // 64
    fvnc = np.empty((128, 16, 2, 32), f32)
    fvnc[:, :, 0, :] = np.where((bb_ == 0) | (bb_ == cur) | (bb_ == cur - 1), f32(1e4), f32(0))
    fvnc[:, :, 1, :] = np.where(bb_ * 64 <= tq_, f32(0), NEGM)
    rowv = (np.arange(128) >= 31).astype(f32).reshape(128, 1)
    shm = (ii == jj + 1).astype(f32)
    cbr = np.ascontiguousarray(np.broadcast_to(relb[31][None, :], (128, 8))).astype(f32)
    w1 = np.asarray(cmp_w1, f32)[0]
    w1r = w1.reshape(2, 2, 16, 64, 64)
    w1l_h = np.transpose(w1r, (3, 0, 2, 1, 4)).reshape(64, 2, 16, 128)
    w1l = np.ascontiguousarray(np.concatenate([w1l_h, w1l_h], axis=0))
    w1n = np.ascontiguousarray(np.transpose(w1.reshape(2, 16, 128, 64), (2, 0, 1, 3)))
    pel = np.ascontiguousarray(np.transpose(np.asarray(cmp_pe, f32)[0].reshape(2, 16, 128), (2, 0, 1)))
    w2 = np.asarray(cmp_w2, f32)[0]
    w2n = np.ascontiguousarray(np.transpose(w2, (1, 0, 2)))
    w2d = np.zeros((64, 2, 128), f32)
    w2d[:, 0, 0:64] = w2[0]
    w2d[:, 1, 64:128] = w2[0]
    lnp = np.ascontiguousarray(np.broadcast_to(np.stack([np.asarray(a, f32)[0] for a in (ln1_g, ln1_b, ln2_g, ln2_b)])[None],
                                               (128, 4, D)))
    pwq_t = np.ascontiguousarray(np.asarray(peer_wq, f32)[0].reshape(8, 128, 16, 128).transpose(3, 2, 0, 1))
    sk_t = np.ascontiguousarray(np.asarray(peer_subkeys, f32)[0].transpose(2, 0, 1))
    iota16 = np.ascontiguousarray(np.broadcast_to(np.arange(16, dtype=f32)[None, :], (128, 16)))
    cw4 = np.ascontiguousarray(np.broadcast_to(cw[None], (4, 4, 512)))
    cb4 = np.ascontiguousarray(np.broadcast_to(np.asarray(m_conv_b, f32)[0][None], (4, 512)))
    gb4 = np.ascontiguousarray(np.broadcast_to(np.asarray(m_gate_bias, f32)[0].reshape(1, 8), (4, 8)))
    ng16 = np.ascontiguousarray(np.tile(np.asarray(m_norm_g, f32)[0].reshape(4, 128), (4, 1)))
    stC = np.asarray(state_mlstm_C, f32)[0].reshape(BS * 4, 128, 128)
    stn = np.asarray(state_mlstm_n, f32)[0].reshape(BS, 512)
    stm = np.asarray(state_mlstm_m, f32)[0]
    n6 = np.arange(128)[:, None] * 8 + np.arange(8)[None, :]
    d6 = 16353 - 16 * n6
    cbs = np.where((n6 <= 1022)[:, :, None], relb[bucket[np.clip(d6, 0, 4095)]], NEGM).astype(f32)
    i6 = np.arange(4)[None, :] * 128 + np.arange(128)[:, None]
    bw = np.where((i6 >= 1)[:, :, None], relb[bucket[np.clip(512 - i6, 0, 4095)]], NEGM).astype(f32)
    tok = np.arange(64)
    t25 = np.empty((60, 2, 8, 64), f32)
    t25[:, 0] = relb[bucket[128 - tok]].T[None]
    t25[:, 1] = relb[bucket[64 - tok]].T[None]
    fl255 = (np.arange(60) % 15 == 14).astype(f32).reshape(60, 1)
    ind60 = (np.arange(60)[:, None] // 15 == np.arange(4)[None, :]).astype(f32)
    rb0 = np.ascontiguousarray(np.broadcast_to(relb[0][None, :], (4, 8))).astype(f32)
    shd = np.ascontiguousarray(shm.T)
    w2kt = np.ascontiguousarray(w2[0].T)
    iota8 = np.ascontiguousarray(np.broadcast_to(np.arange(8, dtype=f32)[None, :], (128, 8)))
    iot128 = np.empty((8, 2, 128), f32)
    iot128[:, 0] = np.arange(128, dtype=f32)[None]
    iot128[:, 1] = 2.0 * (np.arange(128, dtype=f32)[None] + 1.0)
    pool_c = np.asarray(cache_cmp_kv, f32)[0].reshape(5120 * 8, 4096)
    pool_s = np.asarray(cache_slc_kv, f32)[0].reshape(5120 * 8, 4096)
    ptab = np.asarray(page_table, np.int32)
    shared = {"pool_cmp": pool_c, "pool_slc": pool_s, "cbs": cbs, "bw": bw, "t25": t25, "fl255": fl255, "ind60": ind60,
              "rb0": rb0, "shd": shd, "w2kt": w2kt, "iota8": iota8, "iot128": iot128, "cw4": cw4, "cb4": cb4, "gb4": gb4, "ng16": ng16, "w_up_m": np.ascontiguousarray(np.asarray(w_up_m, f32)[0]), "w_up_n": np.ascontiguousarray(np.asarray(w_up_n, f32)[0]),
              "w_out": np.ascontiguousarray(np.asarray(w_out, f32)[0]), "lnp": lnp, "pwq_t": pwq_t, "sk_t": sk_t,
              "iota16": iota16, "peer_u": np.ascontiguousarray(np.asarray(peer_u, f32)[0]),
              "peer_v": np.ascontiguousarray(np.asarray(peer_v, f32)[0]),
              "cbt": cbt, "tz": tz, "wz": wz, "ebig": ebig, "fvnc": fvnc, "rowv": rowv, "shm": shm, "cbr": cbr,
              "w1l": w1l, "w1n": w1n, "pel": pel, "w2d": w2d, "w2n": w2n,
              "w_in": w_in0, "ident": ident, "convw_l": convw_l, "convb_l": convb_l, "wq_l": wq_l, "wk_l": wk_l,
              "gb_l": gb_l, "normg_rep": normg_rep, "sel_c": sel_c, "tri_c": tri_c}
    in_maps = []
    for c in range(NCORES):
        xs_pad = np.zeros((128, D), f32)
        xs_pad[:SSC] = xs[c * SSC:(c + 1) * SSC]
        in_maps.append({
            **shared,
            "x_p": xp[c * TP:(c + 1) * TP],
            "x_s": xs_pad,
            "st_win": np.ascontiguousarray(stw[c * SSC:(c + 1) * SSC]),
            "st_conv": np.ascontiguousarray(stc[c * SSC:(c + 1) * SSC]),
            "st_C": np.ascontiguousarray(stC[c * SSC * 4:(c + 1) * SSC * 4]),
            "st_n": np.ascontiguousarray(stn[c * SSC:(c + 1) * SSC]),
            "st_m": np.ascontiguousarray(stm[c * SSC:(c + 1) * SSC]),
            "pt_l": np.ascontiguousarray(ptab[c * SSC:(c + 1) * SSC].T),
            "pt8": np.ascontiguousarray(np.repeat(ptab[c * SSC:(c + 1) * SSC], 2, axis=0)),
        })
    res = run_bass_kernel_spmd(nc, in_maps, core_ids=list(range(NCORES)))
    R = res.results
    if DEBUG:
        DBG["R"] = R

    def cat(name):
        return np.concatenate([np.asarray(r[name]) for r in R], axis=0)

    kvt = (2, 2, 64)
    y_p = np.concatenate([np.asarray(r["y_out"])[:TP] for r in R], axis=0).reshape(BP, SEQ, D)
    y_s = np.concatenate([np.asarray(r["y_out"])[TP:TP + SSC] for r in R], axis=0).reshape(BS, 1, D)
    cmp_p = cat("o_cmp_p").reshape((1, BP, SEQ) + kvt)
    cmp_s = cat("o_cmp_s").reshape((1, BS, 1) + kvt)
    slc_p = cat("o_slc_p").reshape((1, BP, SEQ) + kvt)
    slc_s = cat("o_slc_s").reshape((1, BS, 1) + kvt)
    win_p = cat("o_win_p").reshape((1, BP, 512) + kvt)
    win_s = cat("o_win_s").reshape((1, BS, 512) + kvt)
    C_p = cat("o_C_p").reshape(1, BP, 4, 128, 128)
    C_s = cat("o_C_s").reshape(1, BS, 4, 128, 128)
    n_p = cat("o_n_p").reshape(1, BP, 4, 128)
    n_s = cat("o_n_s").reshape(1, BS, 4, 128)
    m_p = cat("o_m_p").reshape(1, BP, 4)
    m_s = cat("o_m_s").reshape(1, BS, 4)
    conv_p = cat("o_conv_p").reshape(1, BP, 3, 512)
    conv_s = cat("o_conv_s").reshape(1, BS, 3, 512)
    return (y_p, y_s, cmp_p, cmp_s, slc_p, slc_s, win_p, win_s, C_p, C_s, n_p, n_s, m_p, m_s, conv_p, conv_s)
```

```python
import contextlib
import os
import numpy as np
import concourse.bass as bass
import concourse.mybir as mybir
from concourse.bass_utils import run_bass_kernel_spmd

F32 = mybir.dt.float32
I32 = mybir.dt.int32
U32 = mybir.dt.uint32
BF16 = mybir.dt.bfloat16
AF = mybir.ActivationFunctionType
ALU = mybir.AluOpType
AX = mybir.AxisListType

NCORES = 8
D = 1024
SEQ = 2048
BP = 16
BS = 32
SPC = BP // NCORES
SSC = BS // NCORES
TP = SPC * SEQ
NT = TP // 128
IN_DIM = 4896
O_U, O_V, O_O, O_I, O_F, O_Q, O_KC, O_KS, O_KW, O_GN, O_GM, O_GNN = (
    0, 512, 1024, 1536, 1540, 1544, 2056, 2312, 2568, 2824, 2848, 3872)
COL_GROUPS = [(0, 512), (512, 512), (1024, 512), (1536, 8), (1544, 512), (2056, 512),
              (2568, 280), (2848, 512), (3360, 512), (3872, 512), (4384, 512)]

DEBUG = False
DBG = {}
ENGS = ("pe", "act", "dve", "pool", "sp")
N_DMA_SEMS = 12


class Prog:
    def __init__(self, nc, stack):
        self.nc = nc
        self.ops = {e: [] for e in ENGS}
        self.cnt = {e: 0 for e in ENGS}
        self.esem = {e: stack.enter_context(nc.semaphore("es_" + e)) for e in ENGS}
        self.dsem = {e: [stack.enter_context(nc.semaphore("ds_%s%d" % (e, i))) for i in range(N_DMA_SEMS)]
                     for e in ("sp", "pool", "act")}
        self.dval = {e: [0] * N_DMA_SEMS for e in ("sp", "pool", "act")}
        self.dnext = {e: 0 for e in ("sp", "pool", "act")}
        self.semobj = {}
        for e in ENGS:
            self.semobj["es_" + e] = self.esem[e]
        for e in self.dsem:
            for i, s in enumerate(self.dsem[e]):
                self.semobj["ds_%s%d" % (e, i)] = s
        self.lastw = {}
        self.readers = {}
        self.waited = {e: {} for e in ENGS}
        self.final_tokens = []

    def _deps(self, eng, reads, writes):
        deps = {}

        def add(tok, same_ok):
            if tok is None:
                return
            s, v = tok
            if s == "es_" + eng and not same_ok:
                return
            if deps.get(s, 0) < v:
                deps[s] = v

        for k in reads:
            add(self.lastw.get(k), eng != "pe")
        for k in writes:
            add(self.lastw.get(k), False)
            for s, v in self.readers.get(k, {}).items():
                add((s, v), False)
        out = []
        for s, v in deps.items():
            if self.waited[eng].get(s, 0) < v:
                self.waited[eng][s] = v
                out.append((s, v))
        return out

    def _commit(self, tok, reads, writes):
        for k in writes:
            self.lastw[k] = tok
            self.readers[k] = {}
        for k in reads:
            r = self.readers.setdefault(k, {})
            if r.get(tok[0], 0) < tok[1]:
                r[tok[0]] = tok[1]

    def op(self, eng, fn, reads=(), writes=()):
        waits = self._deps(eng, reads, writes)
        self.cnt[eng] += 1
        tok = ("es_" + eng, self.cnt[eng])
        self.ops[eng].append((waits, fn, ("es_" + eng, 1)))
        self._commit(tok, reads, writes)
        return tok

    def dma(self, eng, fn, reads=(), writes=(), final=False):
        i = self.dnext[eng]
        self.dnext[eng] = (i + 1) % N_DMA_SEMS
        sname = "ds_%s%d" % (eng, i)
        waits = self._deps(eng, reads, writes)
        prev = self.dval[eng][i]
        if prev > 0 and self.waited[eng].get(sname, 0) < prev:
            self.waited[eng][sname] = prev
            waits.append((sname, prev))
        self.dval[eng][i] += 16
        tok = (sname, self.dval[eng][i])
        self.ops[eng].append((waits, fn, (sname, 16)))
        self._commit(tok, reads, writes)
        if final:
            self.final_tokens.append(tok)
        return tok

    def bank(self):
        b = self._bank = (getattr(self, "_bank", -1) + 1) % 8
        return b

    def mm(self, out, lhsT, rhs, start, stop, r, w):
        return self.op("pe", lambda e: e.matmul(out=out, lhsT=lhsT, rhs=rhs, start=start, stop=stop), r, w)

    def tr(self, out, in_, ident, r, w):
        return self.op("pe", lambda e: e.transpose(out=out, in_=in_, identity=ident), r, w)

    def act(self, out, in_, func, r, w, bias=None, scale=None):
        kw = {}
        if bias is not None:
            kw["bias"] = bias
        if scale is not None:
            kw["scale"] = scale
        return self.op("act", lambda e: e.activation(out=out, in_=in_, func=func, **kw), r, w)

    def tt(self, eng, out, in0, in1, op, r, w):
        return self.op(eng, lambda e: e.tensor_tensor(out=out, in0=in0, in1=in1, op=op), r, w)

    def ts(self, eng, out, in0, s1, op0, r, w, s2=None, op1=None):
        if op1 is None:
            return self.op(eng, lambda e: e.tensor_scalar(out=out, in0=in0, scalar1=s1, scalar2=None, op0=op0), r, w)
        return self.op(eng, lambda e: e.tensor_scalar(out=out, in0=in0, scalar1=s1, scalar2=s2, op0=op0, op1=op1), r, w)

    def stt(self, eng, out, in0, scalar, in1, op0, op1, r, w):
        return self.op(eng, lambda e: e.scalar_tensor_tensor(out=out, in0=in0, scalar=scalar, in1=in1, op0=op0, op1=op1), r, w)

    def cp(self, eng, out, in_, r, w):
        if eng == "act":
            return self.op("act", lambda e: e.copy(out=out, in_=in_), r, w)
        return self.op(eng, lambda e: e.tensor_copy(out=out, in_=in_), r, w)

    def memset(self, eng, out, val, w):
        return self.op(eng, lambda e: e.memset(out, val), (), w)

    def ld(self, out, in_, w, r=(), eng="sp"):
        return self.dma(eng, lambda e: e.dma_start(out=out, in_=in_), r, w)

    def st(self, out, in_, r, w=(), eng="pool"):
        return self.dma(eng, lambda e: e.dma_start(out=out, in_=in_), r, w)

    def barrier(self):
        toks = []
        for e in ENGS:
            if self.cnt[e] > 0:
                toks.append(("es_" + e, self.cnt[e]))
        for e in self.dsem:
            for i in range(N_DMA_SEMS):
                if self.dval[e][i] > 0:
                    toks.append(("ds_%s%d" % (e, i), self.dval[e][i]))
        for e in ENGS:
            waits = []
            for s_, v in toks:
                if s_ == "es_" + e and e in ("pe", "sp"):
                    continue
                if self.waited[e].get(s_, 0) < v:
                    self.waited[e][s_] = v
                    waits.append((s_, v))
            if waits:
                self.ops[e].append((waits, None, None))
        self.lastw = {}
        self.readers = {}

    def emit(self, eng, eobj):
        for waits, fn, inc in self.ops[eng]:
            sname, amt = inc if inc is not None else (None, None)
            for s, v in waits:
                eobj.wait_ge(self.semobj[s], v)
            if fn is None:
                continue
            ins = fn(eobj)
            ins.then_inc(self.semobj[sname], amt)
        if eng == "sp":
            for e in self.dsem:
                for i in range(N_DMA_SEMS):
                    if self.dval[e][i] > 0:
                        eobj.wait_ge(self.dsem[e][i], self.dval[e][i])


def build_program():
    nc = bass.Bass("TRN2", target_bir_lowering=False)
    stack = contextlib.ExitStack()
    with stack:
        def din(name, shape, dt=F32):
            return nc.dram_tensor(name, list(shape), dt, kind="ExternalInput").ap()

        def dout(name, shape, dt=F32):
            return nc.dram_tensor(name, list(shape), dt, kind="ExternalOutput").ap()

        def dscr(name, shape, dt=F32):
            return nc.dram_tensor(name, list(shape), dt, kind="Internal").ap()

        def sb(name, shape, dt=F32):
            return stack.enter_context(nc.sbuf_tensor(name, list(shape), dt))

        def ps(name, shape, dt=F32):
            return stack.enter_context(nc.psum_tensor(name, list(shape), dt))

        x_p = din("x_p", [TP, D])
        x_s = din("x_s", [128, D])
        w_in = din("w_in", [D, IN_DIM])
        ident_d = din("ident", [128, 128])
        st_win = din("st_win", [SSC, 512, 256])
        st_conv = din("st_conv", [SSC, 3, 512])

        o_cmp_p = dout("o_cmp_p", [TP, 256])
        o_slc_p = dout("o_slc_p", [TP, 256])
        o_win_p = dout("o_win_p", [SPC, 512, 256])
        o_conv_p = dout("o_conv_p", [SPC, 3, 512])
        o_cmp_s = dout("o_cmp_s", [SSC, 256])
        o_slc_s = dout("o_slc_s", [SSC, 256])
        o_win_s = dout("o_win_s", [SSC, 512, 256])
        o_conv_s = dout("o_conv_s", [SSC, 3, 512])

        convw_d = din("convw_l", [128, 4, 4])
        convb_d = din("convb_l", [128, 4])
        wq_d = din("wq_l", [128, 4, 128])
        wk_d = din("wk_l", [128, 4, 128])
        gb_d = din("gb_l", [4, 2])
        normg_d = din("normg_rep", [128, 512])
        sel_d = din("sel_c", [4, 4, 128])
        tri_d = din("tri_c", [128, 128])
        o_C_p = dout("o_C_p", [SPC, 4, 128, 128])
        o_n_p = dout("o_n_p", [SPC, 4, 128])
        o_m_p = dout("o_m_p", [SPC, 4])
        MOUT = (dout if DEBUG else dscr)("MOUT", [TP + 128, 512])
        cbt_d = din("cbt", [SEQ, 8, 128])
        tz_d = din("tz", [128, 8, 2, 128])
        wz_d = din("wz", [128, 128])
        ebig_d = din("ebig", [32, SEQ])
        fvnc_d = din("fvnc", [128, 16, 2, 32])
        rowv_d = din("rowv", [128, 1])
        sh_d = din("shm", [128, 128])
        cbr_d = din("cbr", [128, 8])
        w1l_d = din("w1l", [128, 2, 16, 128])
        w1n_d = din("w1n", [128, 2, 16, 64])
        pel_d = din("pel", [128, 2, 16])
        w2d_d = din("w2d", [64, 2, 128])
        w2n_d = din("w2n", [64, 2, 64])
        NOUTD = (dout if DEBUG else dscr)("NOUTD", [TP + 128, 512])
        wupm_d = din("w_up_m", [512, D])
        wupn_d = din("w_up_n", [512, D])
        wout_d = din("w_out", [D, D])
        lnp_d = din("lnp", [128, 4, D])
        pwqt_d = din("pwq_t", [128, 16, 8, 128])
        skt_d = din("sk_t", [128, 2, 128])
        iota_d = din("iota16", [128, 16])
        peer_u_d = din("peer_u", [16384, D])
        peer_v_d = din("peer_v", [16384, D])
        X1D = (dout if DEBUG else dscr)("X1D", [TP + 128, D])
        y_out = dout("y_out", [TP + 128, D])
        U16D = dscr("U16D", [16384, D], BF16)
        V16D = dscr("V16D", [16384, D], BF16)
        stC_d = din("st_C", [SSC * 4, 128, 128])
        stn_d = din("st_n", [SSC, 512])
        stm_d = din("st_m", [SSC, 4])
        cw4_d = din("cw4", [4, 4, 512])
        cb4_d = din("cb4", [4, 512])
        gb4_d = din("gb4", [4, 8])
        ng16_d = din("ng16", [16, 128])
        o_C_s = dout("o_C_s", [SSC * 4, 128, 128])
        o_n_s = dout("o_n_s", [SSC, 512])
        o_m_s = dout("o_m_s", [SSC, 4])
        SCRB = dscr("SCRB", [16, 264])
        pool_cmp_d = din("pool_cmp", [5120 * 8, 4096])
        pool_slc_d = din("pool_slc", [5120 * 8, 4096])
        ptl_d = din("pt_l", [128, 4], I32)
        pt8_d = din("pt8", [8, 128], I32)
        cbs_d = din("cbs", [128, 8, 8])
        bw_d = din("bw", [128, 4, 8])
        t25_d = din("t25", [60, 2, 8, 64])
        fl255_d = din("fl255", [60, 1])
        ind_d = din("ind60", [60, 4])
        rb0_d = din("rb0", [4, 8])
        shd_d = din("shd", [128, 128])
        w2kt_d = din("w2kt", [64, 64])
        iota8_d = din("iota8", [128, 8])
        iot128_d = din("iot128", [8, 2, 128])
        SCRQ = dscr("SCRQ", [4, 512])
        OSD = dscr("OSD", [2, 4, 8, 65])
        SELD = dscr("SELD", [8, 256])
        SCRP = dscr("SCRP", [8, 15, 5])
        Z = dscr("Z", [TP + 128, IN_DIM])

        P = Prog(nc, stack)

        ARENA_F = 52800
        EI = sb("EI_i32", [128, 128], I32)[:, :]
        EI2 = sb("EI2_i32", [128, 128], I32)[:, :]
        IDXC = sb("IDXC_i32", [128, 32], I32)[:, :]
        IDXS = sb("IDXS_i32", [128, 8], I32)[:, :]
        arena = sb("arena", [128, ARENA_F])
        apos = [0]

        def alloc(n):
            o = apos[0]
            apos[0] += n
            assert apos[0] <= ARENA_F, ("arena overflow", apos[0])
            return arena[:, o:o + n]

        def alloc3(a, b):
            return alloc(a * b).rearrange("p (a b) -> p a b", a=a)

        ident = alloc(128)
        pbank = [ps("pb%d" % i, [128, 512]) for i in range(8)]
        tri = alloc(128)
        selc = alloc3(4, 128)
        persist_mark = apos[0]

        QT = 8
        xt = [alloc(D) for i in range(2)]
        xT = alloc3(8, QT * 128)
        wg = [alloc3(8, 512) for i in range(2)]
        zs = [alloc(512) for i in range(2)]

        P.dma("sp", lambda e: e.dma_start(out=ident, in_=ident_d[:, :]), writes=["ident"])
        P.ld(tri, tri_d[:, :], ["tri"])
        P.ld(selc[0:4], sel_d[:, :, :], ["selc"])

        n_tiles_all = NT + 1
        groups = [list(range(g, min(g + QT, n_tiles_all))) for g in range(0, n_tiles_all, QT)]
        xcnt = 0
        wcnt = 0
        zcnt = 0
        pcnt = 0
        for grp in groups:
            for li, ti in enumerate(grp):
                b = xcnt % 2
                xcnt += 1
                src = x_p[ti * 128:(ti + 1) * 128, :] if ti < NT else x_s[:, :]
                P.dma("sp", lambda e, b=b, src=src: e.dma_start(out=xt[b], in_=src),
                      writes=["xt%d" % b])
                for c in range(8):
                    pb = pcnt % 8
                    pcnt += 1
                    P.op("pe", lambda e, pb=pb, b=b, c=c: e.transpose(
                        out=pbank[pb][:, 0:128], in_=xt[b][:, c * 128:(c + 1) * 128], identity=ident),
                        reads=["xt%d" % b, "ident"], writes=["pb%d" % pb])
                    eng = "dve" if c % 2 == 0 else "act"
                    if eng == "dve":
                        P.op("dve", lambda e, pb=pb, c=c, li=li: e.tensor_copy(
                            out=xT[:, c, li * 128:(li + 1) * 128], in_=pbank[pb][:, 0:128]),
                            reads=["pb%d" % pb], writes=["xT_%d_%d" % (c, li)])
                    else:
                        P.op("act", lambda e, pb=pb, c=c, li=li: e.copy(
                            out=xT[:, c, li * 128:(li + 1) * 128], in_=pbank[pb][:, 0:128]),
                            reads=["pb%d" % pb], writes=["xT_%d_%d" % (c, li)])
            for (c0, cw) in COL_GROUPS:
                wb = wcnt % 2
                wcnt += 1
                P.dma("sp", lambda e, wb=wb, c0=c0, cw=cw: e.dma_start(
                    out=wg[wb][:, :, 0:cw],
                    in_=w_in[:, c0:c0 + cw].rearrange("(c p) n -> p c n", p=128)),
                    writes=["wg%d" % wb])
                for li, ti in enumerate(grp):
                    pb = pcnt % 8
                    pcnt += 1
                    for c in range(8):
                        P.op("pe", lambda e, pb=pb, c=c, li=li, wb=wb, cw=cw: e.matmul(
                            out=pbank[pb][:, 0:cw], lhsT=xT[:, c, li * 128:(li + 1) * 128],
                            rhs=wg[wb][:, c, 0:cw], start=(c == 0), stop=(c == 7)),
                            reads=["xT_%d_%d" % (c, li), "wg%d" % wb], writes=["pb%d" % pb])
                    zb = zcnt % 2
                    zcnt += 1
                    if zcnt % 2 == 0:
                        P.op("dve", lambda e, pb=pb, zb=zb, cw=cw: e.tensor_copy(
                            out=zs[zb][:, 0:cw], in_=pbank[pb][:, 0:cw]),
                            reads=["pb%d" % pb], writes=["zs%d" % zb])
                    else:
                        P.op("act", lambda e, pb=pb, zb=zb, cw=cw: e.copy(
                            out=zs[zb][:, 0:cw], in_=pbank[pb][:, 0:cw]),
                            reads=["pb%d" % pb], writes=["zs%d" % zb])
                    P.dma("pool", lambda e, zb=zb, ti=ti, c0=c0, cw=cw: e.dma_start(
                        out=Z[ti * 128:(ti + 1) * 128, c0:c0 + cw], in_=zs[zb][:, 0:cw]),
                        reads=["zs%d" % zb], writes=["Z_%d_%d" % (ti, c0)])

        ZALL = ["Z_%d_%d" % (ti, c0) for ti in range(NT + 1) for (c0, _) in COL_GROUPS]

        def d2d(dst, src, final=True):
            P.dma("sp", lambda e: e.dma_start(out=dst, in_=src), reads=ZALL, writes=[], final=final)

        d2d(o_cmp_p[:, :], Z[0:TP, O_KC:O_KC + 256])
        d2d(o_slc_p[:, :], Z[0:TP, O_KS:O_KS + 256])
        for s in range(SPC):
            d2d(o_win_p[s, :, :], Z[s * SEQ + SEQ - 512:(s + 1) * SEQ, O_KW:O_KW + 256])
            d2d(o_conv_p[s, :, :], Z[(s + 1) * SEQ - 3:(s + 1) * SEQ, O_U:O_U + 512])
        d2d(o_cmp_s[:, :], Z[TP:TP + SSC, O_KC:O_KC + 256])
        d2d(o_slc_s[:, :], Z[TP:TP + SSC, O_KS:O_KS + 256])
        for s in range(SSC):
            d2d(o_win_s[s, 0:511, :], st_win[s, 1:512, :])
            d2d(o_win_s[s, 511:512, :], Z[TP + s:TP + s + 1, O_KW:O_KW + 256])
            d2d(o_conv_s[s, 0:2, :], st_conv[s, 1:3, :])
            d2d(o_conv_s[s, 2:3, :], Z[TP + s:TP + s + 1, O_U:O_U + 512])


        P.barrier()
        apos[0] = persist_mark
        CONST = ["ident", "tri", "selc", "m_w"]
        convw = alloc3(4, 4)
        convb = alloc(4)
        wq = alloc3(4, 128)
        wk = alloc3(4, 128)
        gb = alloc(2)
        ngbf = alloc(1)
        normg = alloc(512)
        zero_row = alloc(SEQ)
        P.ld(convw, convw_d[:, :, :], ["m_w"])
        P.ld(convb, convb_d[:, :], ["m_w"])
        P.ld(wq, wq_d[:, :, :], ["m_w"])
        P.ld(wk, wk_d[:, :, :], ["m_w"])
        P.ld(gb[0:4], gb_d[:, :], ["gb"])
        P.ld(normg, normg_d[:, :], ["m_w"])
        P.memset("dve", zero_row, 0.0, ["zero_row"])
        P.ts("dve", ngbf[0:4], gb[0:4, 1:2], -1.0, ALU.mult, ["gb"], ["ngbf"])

        ZIF = alloc3(16, 8)
        IT = alloc(SEQ)
        FT = alloc(SEQ)
        Bc = alloc(SEQ)
        Ac = IT
        CMc = FT
        Mc = Bc
        NRc = alloc(SEQ)
        EMc = alloc(SEQ)
        aT = alloc3(16, 4)
        emT = alloc3(16, 4)
        Uh = alloc3(16, 128)
        UT = alloc(3 + SEQ)
        CT = alloc(SEQ)
        ACC = CT
        QTh = alloc(SEQ)
        KTh = alloc(SEQ)
        KTOK = alloc3(16, 128)
        V1 = alloc3(16, 129)
        OP = alloc3(16, 128)
        SIG = alloc3(16, 128)
        Rb = alloc(SEQ)
        WT = alloc3(16, 512)
        Eb = [alloc(512) for _ in range(2)]
        MO = alloc3(16, 128)
        wts = alloc(16)
        VW = alloc3(16, 128)
        Csb = alloc(128)
        nsb = alloc(128)
        ep = [dict(dd=alloc(1), rec=alloc(1), hq=alloc(128), st=alloc(8), ag=alloc(4), rstd=alloc(1),
                   hn=alloc(128)) for _ in range(2)]
        BN_S = 6

        P.memset("dve", UT[:, 0:3], 0.0, ["UT"])
        P.memset("dve", V1[:, :, 128:129], 1.0, ["V1ones"])
        ecnt = 0
        epc = 0
        for sq in range(SPC):
            r0 = sq * SEQ
            zrows = Z[r0:r0 + SEQ, :]
            P.ld(ZIF, zrows[:, O_I:O_I + 8].rearrange("(n p) c -> p n c", p=128), ["ZIF"])
            for which, dst, dkey in ((0, IT, "ITb"), (1, FT, "FTb")):
                for g in range(4):
                    pb = P.bank()
                    for j in range(4):
                        n = g * 4 + j
                        P.tr(pbank[pb][0:4, j * 128:(j + 1) * 128], ZIF[:, n, which * 4:which * 4 + 4], ident,
                             ["ZIF", "ident"], ["pb%d" % pb])
                    P.cp("dve", dst[0:4, g * 512:(g + 1) * 512], pbank[pb][0:4, :], ["pb%d" % pb], [dkey])
            ITk = ["ITb"]
            FTk = ["FTb"]
            P.ts("dve", IT[0:4], IT[0:4], gb[0:4, 0:1], ALU.add, ITk + ["gb"], ["ITb"])
            P.act(FT[0:4], FT[0:4], AF.Exp, FTk + ["ngbf"], ["FTb"], bias=ngbf[0:4], scale=-1.0)
            P.act(FT[0:4], FT[0:4], AF.Ln, ["FTb"], ["FTb"], bias=1.0, scale=1.0)
            P.ts("dve", FT[0:4], FT[0:4], -1.0, ALU.mult, ["FTb"], ["FTb"])
            P.op("dve", lambda e: e.tensor_tensor_scan(out=Bc[0:4], data0=FT[0:4], data1=zero_row[0:4], initial=0.0,
                                                      op0=ALU.add, op1=ALU.add), ["FTb", "zero_row"], ["Bb"])
            P.tt("dve", Ac[0:4], IT[0:4], Bc[0:4], ALU.subtract, ["ITb", "Bb"], ["ITb", "ITb"] + ITk)
            P.op("dve", lambda e: e.tensor_tensor_scan(out=CMc[0:4], data0=Ac[0:4], data1=Ac[0:4], initial=0.0,
                                                      op0=ALU.max, op1=ALU.max), ["ITb"], ["FTb", "FTb", "FTb", "FTb"] + FTk)
            P.ts("dve", NRc[0:4], CMc[0:4], -1.0, ALU.mult, ["FTb"], ["NRc"])
            P.tt("dve", Mc[0:4], Bc[0:4], CMc[0:4], ALU.add, ["Bb", "FTb", "ITb"], ["Bb", "Bb"])
            P.act(EMc[0:4], Mc[0:4], AF.Exp, ["Bb"], ["EMc"], scale=-1.0)
            P.st(o_m_p[sq, :].rearrange("(h o) -> h o", o=1), Mc[0:4, SEQ - 1:SEQ], ["Bb"])
            for src, skey, dst, dkey in ((Ac, "ITb", aT, "aT"), (EMc, "EMc", emT, "emT")):
                pb = P.bank()
                for n in range(16):
                    P.tr(pbank[pb][:, n * 4:(n + 1) * 4], src[0:4, n * 128:(n + 1) * 128], ident[0:4, 0:4],
                         [skey, "ident"], ["pb%d" % pb])
                P.cp("dve", dst, pbank[pb][:, 0:64].rearrange("p (n h) -> p n h", h=4), ["pb%d" % pb], [dkey])
            for h in range(4):
                hs = slice(h * 128, (h + 1) * 128)
                P.ld(Uh, zrows[:, O_U + h * 128:O_U + (h + 1) * 128].rearrange("(n p) c -> p n c", p=128), ["Uh"])
                P.ld(V1[:, :, 0:128], zrows[:, O_V + h * 128:O_V + (h + 1) * 128].rearrange("(n p) c -> p n c", p=128),
                     ["V1"])
                P.ld(OP, zrows[:, O_O + h * 128:O_O + (h + 1) * 128].rearrange("(n p) c -> p n c", p=128), ["OP"])
                P.act(SIG, OP, AF.Sigmoid, ["OP"], ["SIG"])
                for g in range(4):
                    pb = P.bank()
                    P.mm(pbank[pb][:, :], selc[0:4, h, :], NRc[0:4, g * 512:(g + 1) * 512], True, True,
                         ["selc", "NRc"], ["pb%d" % pb])
                    P.cp("act", Rb[:, g * 512:(g + 1) * 512], pbank[pb][:, :], ["pb%d" % pb], ["Rb%d" % g])
                for g in range(4):
                    pb = P.bank()
                    for j in range(4):
                        n = g * 4 + j
                        P.tr(pbank[pb][:, j * 128:(j + 1) * 128], Uh[:, n, :], ident, ["Uh", "ident"], ["pb%d" % pb])
                    P.cp("dve", UT[:, 3 + g * 512:3 + (g + 1) * 512], pbank[pb][:, :], ["pb%d" % pb], ["UT"])
                P.ts("dve", ACC, UT[:, 0:SEQ], convw[:, h, 0:1], ALU.mult, ["UT", "m_w"], ["CT", "CT"])
                for j in range(1, 4):
                    P.stt("dve", ACC, UT[:, j:j + SEQ], convw[:, h, j:j + 1], ACC, ALU.mult, ALU.add,
                          ["UT", "CT", "m_w"], ["CT"])
                P.act(CT, ACC, AF.Silu, ["CT", "m_w"], ["CT", "CT"], bias=convb[:, h:h + 1])
                for g in range(4):
                    pb = P.bank()
                    P.mm(pbank[pb][:, :], wq[:, h, :], CT[:, g * 512:(g + 1) * 512], True, True, ["m_w", "CT"],
                         ["pb%d" % pb])
                    P.cp("act", QTh[:, g * 512:(g + 1) * 512], pbank[pb][:, :], ["pb%d" % pb], ["QT%d" % g])
                    pb = P.bank()
                    P.mm(pbank[pb][:, :], wk[:, h, :], CT[:, g * 512:(g + 1) * 512], True, True, ["m_w", "CT"],
                         ["pb%d" % pb])
                    P.ts("dve", KTh[:, g * 512:(g + 1) * 512], pbank[pb][:, :], 128.0 ** -0.5, ALU.mult,
                         ["pb%d" % pb], ["KT"])
                    pb = P.bank()
                    for j in range(4):
                        n = g * 4 + j
                        P.mm(pbank[pb][:, j * 128:(j + 1) * 128], CT[:, n * 128:(n + 1) * 128], wk[:, h, :], True, True,
                             ["m_w", "CT"], ["pb%d" % pb])
                    P.ts("dve", KTOK[:, g * 4:(g + 1) * 4, :], pbank[pb][:, :].rearrange("p (j e) -> p j e", j=4),
                         128.0 ** -0.5, ALU.mult, ["pb%d" % pb], ["KTOK"])
                for tb in range(4):
                    nst = 4 * tb + 4
                    for st_ in range(nst):
                        p_ = max(0, st_ - 4 * tb)
                        c0 = p_ * 128
                        cs = slice(c0, 512)
                        gs = slice(tb * 512 + c0, (tb + 1) * 512)
                        pb = P.bank()
                        P.mm(pbank[pb][:, cs], KTh[:, st_ * 128:(st_ + 1) * 128], QTh[:, gs], True, True,
                             ["KT", "QT%d" % tb], ["pb%d" % pb])
                        eb = ecnt % 2
                        ecnt += 1
                        P.ts("dve", Eb[eb][:, cs], Rb[:, gs], aT[:, st_, h:h + 1], ALU.add, ["Rb%d" % tb, "aT"],
                             ["Eb%d" % eb], s2=0.0, op1=ALU.min)
                        P.act(Eb[eb][:, cs], Eb[eb][:, cs], AF.Exp, ["Eb%d" % eb], ["Eb%d" % eb])
                        if st_ >= 4 * tb:
                            P.tt("pool", Eb[eb][:, c0:c0 + 128], Eb[eb][:, c0:c0 + 128], tri, ALU.mult,
                                 ["Eb%d" % eb, "tri"], ["Eb%d" % eb])
                        P.tt("dve", WT[:, st_, cs], pbank[pb][:, cs], Eb[eb][:, cs], ALU.mult,
                             ["pb%d" % pb, "Eb%d" % eb], ["WT%d" % st_])
                    for sub in range(4):
                        tq = 4 * tb + sub
                        pb = P.bank()
                        for st_ in range(tq + 1):
                            P.mm(pbank[pb][:, 0:129], WT[:, st_, sub * 128:(sub + 1) * 128], V1[:, st_, :],
                                 st_ == 0, st_ == tq, ["WT%d" % st_, "V1", "V1ones"], ["pb%d" % pb])
                        E = ep[epc % 2]
                        ek = "ep%d" % (epc % 2)
                        epc += 1
                        P.act(E["dd"], pbank[pb][:, 128:129], AF.Abs, ["pb%d" % pb], [ek + "dd"])
                        P.ts("dve", E["dd"], E["dd"], emT[:, tq, h:h + 1], ALU.max, [ek + "dd", "emT"], [ek + "dd"])
                        P.op("dve", lambda e, E=E: e.reciprocal(out=E["rec"], in_=E["dd"]), [ek + "dd"], [ek + "rec"])
                        P.ts("dve", E["hq"], pbank[pb][:, 0:128], E["rec"], ALU.mult, ["pb%d" % pb, ek + "rec"],
                             [ek + "hq"])
                        P.op("dve", lambda e, E=E: e.bn_stats(out=E["st"][:, 0:BN_S], in_=E["hq"]), [ek + "hq"],
                             [ek + "st"])
                        P.op("dve", lambda e, E=E: e.bn_aggr(out=E["ag"][:, 0:2], in_=E["st"][:, 0:BN_S]), [ek + "st"],
                             [ek + "ag"])
                        P.act(E["rstd"], E["ag"][:, 1:2], AF.Sqrt, [ek + "ag"], [ek + "rstd"], bias=1e-5, scale=1.0)
                        P.op("dve", lambda e, E=E: e.reciprocal(out=E["rstd"], in_=E["rstd"]), [ek + "rstd"],
                             [ek + "rstd"])
                        P.ts("dve", E["hn"], E["hq"], E["ag"][:, 0:1], ALU.subtract, [ek + "hq", ek + "ag", ek + "rstd"],
                             [ek + "hn"], s2=E["rstd"], op1=ALU.mult)
                        P.tt("pool", E["hn"], E["hn"], normg[:, hs], ALU.mult, [ek + "hn", "m_w"], [ek + "hn"])
                        P.tt("pool", MO[:, tq, :], E["hn"], SIG[:, tq, :], ALU.mult, [ek + "hn", "SIG"], ["MO"])
                P.st(MOUT[r0:r0 + SEQ, hs].rearrange("(n p) c -> p n c", p=128), MO, ["MO"], ["MOUT"])
                P.act(wts, aT[:, :, h], AF.Exp, ["aT", "Rb3"], ["wts"], bias=Rb[:, SEQ - 1:SEQ])
                P.tt("dve", VW, V1[:, :, 0:128], wts.unsqueeze(2).to_broadcast([128, 16, 128]), ALU.mult,
                     ["V1", "wts"], ["VW"])
                pb = P.bank()
                for st_ in range(16):
                    P.mm(pbank[pb][:, 0:128], VW[:, st_, :], KTOK[:, st_, :], st_ == 0, st_ == 15, ["VW", "KTOK"],
                         ["pb%d" % pb])
                P.cp("act", Csb, pbank[pb][:, 0:128], ["pb%d" % pb], ["Csb"])
                P.st(o_C_p[sq, h, :, :], Csb, ["Csb"])
                pb = P.bank()
                for st_ in range(16):
                    P.mm(pbank[pb][0:1, 0:128], wts[:, st_:st_ + 1], KTOK[:, st_, :], st_ == 0, st_ == 15,
                         ["wts", "KTOK"], ["pb%d" % pb])
                P.cp("act", nsb[0:1], pbank[pb][0:1, 0:128], ["pb%d" % pb], ["nsb"])
                P.st(o_n_p[sq:sq + 1, h, :], nsb[0:1], ["nsb"])


        P.barrier()
        apos[0] = persist_mark
        BIGM = 30000.0
        QT8 = alloc3(4, SEQ)
        KTz = [[alloc(SEQ) for _ in range(2)] for _ in range(2)]
        V1n = [alloc3(16, 65) for _ in range(2)]
        KCTz = alloc(2 * 2 * 128).rearrange("p (k f n) -> p k f n", k=2, f=2)
        KCV = alloc3(2, 64)
        TZs = alloc(8 * 2 * 128).rearrange("p (h k j) -> p h k j", h=8, k=2)
        WZ = alloc(128)
        SHM = alloc(128)
        EBIG = alloc(SEQ)
        SELM1 = [alloc(SEQ) for _ in range(2)]
        XTc = SELM1
        FVNC = alloc(16 * 2 * 32).rearrange("p (n k b) -> p n k b", n=16, k=2)
        ROWV = alloc(1)
        CBR = alloc(8)
        W2Z = alloc3(2, 128)
        W2N = alloc3(2, 64)
        PET = alloc(128)
        ONES = alloc(128)
        GS = alloc3(16, 24)
        NOUT = alloc3(4, 512)
        stg = [alloc3(16, 128) for _ in range(2)]
        CB = [alloc3(8, 128) for _ in range(2)]
        Sx = alloc3(8, 128)
        Pn = alloc3(8, 128)
        PnT = Sx
        sm8 = alloc(8)
        rs8 = alloc(8)
        PG = alloc3(2, 128)
        PSs = alloc3(2, 32)
        S012 = alloc3(2, 32)
        SC = alloc3(2, 32)
        M8 = alloc(8)
        SCW = alloc(32)
        SEL = alloc3(2, 32)
        PAB = alloc(128)
        XG = alloc(64)
        TG = alloc(64)
        HID = alloc(64)
        HT = alloc(128)
        epn = [dict(r=alloc(1)) for _ in range(4)]
        PTb = alloc3(16, 512)
        W1L = PTb[:, 0:8, :].rearrange("p a b -> p (a b)").rearrange("p (c r o) -> p c r o", c=2, r=16)
        W1N = PTb[:, 8:12, :].rearrange("p a b -> p (a b)").rearrange("p (c j o) -> p c j o", c=2, j=16)
        PEL = PTb[:, 12, 0:32].rearrange("p (c j) -> p c j", c=2)
        PTK = ["PT%d" % i for i in range(16)]

        P.ld(TZs, tz_d[:, :, :, :], ["TZ"])
        P.ld(WZ, wz_d[:, :], ["ncst"])
        P.ld(SHM, sh_d[:, :], ["ncst"])
        P.ld(EBIG[0:32], ebig_d[:, :], ["ncst"])
        P.ld(FVNC, fvnc_d[:, :, :, :], ["ncst"])
        P.ld(ROWV, rowv_d[:, :], ["ncst"])
        P.ld(CBR, cbr_d[:, :], ["CBR"])
        P.ld(W2Z[0:64], w2d_d[:, :, :], ["ncst"])
        P.ld(W2N[0:64], w2n_d[:, :, :], ["ncst"])
        P.memset("dve", ONES, 1.0, ["ONES"])
        for h in range(8):
            P.ts("dve", TZs[:, h, :, :], TZs[:, h, :, :], CBR[:, h:h + 1], ALU.subtract, ["TZ", "CBR"], ["TZ"])
        for br in range(2):
            P.memset("dve", V1n[br][:, :, 64:65], 1.0, ["V1n1"])
        cbcnt = 0
        encnt = 0
        K3 = int(os.environ.get("K3STOP", "9"))

        def gelu_from(xsb, tmp, out, rk, wk_):
            P.tt("dve", tmp, xsb, xsb, ALU.mult, rk, [wk_ + "t"])
            P.ts("dve", tmp, tmp, 0.044715, ALU.mult, [wk_ + "t"], [wk_ + "t"], s2=1.0, op1=ALU.add)
            P.tt("dve", tmp, tmp, xsb, ALU.mult, [wk_ + "t"] + rk, [wk_ + "t"])
            P.act(tmp, tmp, AF.Sigmoid, [wk_ + "t"], [wk_ + "t"], scale=1.5957691216057308)
            P.tt("dve", out, tmp, xsb, ALU.mult, [wk_ + "t"] + rk, [wk_])

        def tr_block(src3, dst, wkeys, scale=None, rmajor=False):
            for g in range(4):
                pb = P.bank()
                for j in range(4):
                    P.tr(pbank[pb][:, j * 128:(j + 1) * 128], src3[:, g * 4 + j, :], ident, ["stgX", "ident"],
                         ["pb%d" % pb])
                if rmajor:
                    P.cp("act", dst.rearrange("p (r n) -> p r n", r=16)[:, :, g * 32:(g + 1) * 32],
                         pbank[pb][:, :].rearrange("p (n r) -> p r n", r=16), ["pb%d" % pb], wkeys)
                elif scale is not None:
                    P.ts("dve", dst[:, g * 512:(g + 1) * 512], pbank[pb][:, :], scale, ALU.mult, ["pb%d" % pb], wkeys)
                else:
                    P.cp("act", dst[:, g * 512:(g + 1) * 512], pbank[pb][:, :], ["pb%d" % pb], wkeys)

        for sq in range(SPC if K3 >= 2 else 0):
            r0 = sq * SEQ
            zrows = Z[r0:r0 + SEQ, :]

            def zt(c0, w):
                return zrows[:, c0:c0 + w].rearrange("(n p) c -> p n c", p=128)

            for pr in range(4):
                P.ld(stg[0], zt(O_Q + pr * 128, 128), ["stgX"])
                tr_block(stg[0], QT8[:, pr, :], ["QT8"], scale=0.125)
            for c in range(2):
                P.ld(stg[0], zt(O_KC + c * 128, 128), ["stgX"])
                tr_block(stg[0], XTc[c], ["XTc", "SELM1_0", "SELM1_1"], rmajor=True)
            P.ld(GS, zt(O_GN, 24), ["GSraw"])
            P.act(GS, GS, AF.Sigmoid, ["GSraw"], ["GS", "GSraw"])
            P.ld(W1L, w1l_d[:, :, :, :], ["W1L"] + PTK)
            P.ld(W1N, w1n_d[:, :, :, :], ["W1N"] + PTK)
            P.ld(PEL, pel_d[:, :, :], ["PEL"] + PTK)
            for c in range(2):
                pb = P.bank()
                for j in range(16):
                    P.mm(pbank[pb][0:1, 0:64], PEL[:, c, j:j + 1], W1N[:, c, j, :], j == 0, j == 15, ["PEL", "W1N"],
                         ["pb%d" % pb])
                P.cp("act", PET[0:1, c * 64:(c + 1) * 64], pbank[pb][0:1, 0:64], ["pb%d" % pb], ["PET"])
            for c in range(2):
                for k in range(2):
                    rows = slice(k * 64, (k + 1) * 64)
                    xv = XTc[c][rows, :].rearrange("p (r n) -> p r n", r=16)
                    pb = P.bank()
                    for r in range(16):
                        P.mm(pbank[pb][:, 0:128], xv[:, r, :], W1L[rows, c, r, :], r == 0, r == 15, ["XTc", "W1L"],
                             ["pb%d" % pb])
                    P.cp("act", PAB, pbank[pb][:, 0:128], ["pb%d" % pb], ["PAB"])
                    pb = P.bank()
                    P.mm(pbank[pb][:, 0:64], ident, PAB[:, 0:64], True, False, ["ident", "PAB"], ["pb%d" % pb])
                    P.mm(pbank[pb][:, 0:64], SHM, PAB[:, 64:128], False, False, ["ncst", "PAB"], ["pb%d" % pb])
                    P.mm(pbank[pb][:, 0:64], ONES[0:1, :], PET[0:1, c * 64:(c + 1) * 64], False, True, ["ONES", "PET"],
                         ["pb%d" % pb])
                    P.cp("act", XG, pbank[pb][:, 0:64], ["pb%d" % pb], ["XG"])
                    gelu_from(XG, TG, HID, ["XG"], "HID")
                    pb = P.bank()
                    P.tr(pbank[pb][0:64, 0:128], HID, ident, ["HID", "ident"], ["pb%d" % pb])
                    P.cp("act", HT[0:64], pbank[pb][0:64, 0:128], ["pb%d" % pb], ["HT"])
                    if c == 0:
                        for f in range(2):
                            pb = P.bank()
                            P.mm(pbank[pb][:, 0:128], W2Z[0:64, f, :], HT[0:64], True, True, ["ncst", "HT"],
                                 ["pb%d" % pb])
                            P.cp("act", KCTz[:, k, f, :], pbank[pb][:, 0:128], ["pb%d" % pb], ["KCT"])
                    else:
                        pb = P.bank()
                        P.mm(pbank[pb][:, 0:64], HT[0:64], W2N[0:64, c, :], True, True, ["ncst", "HT"], ["pb%d" % pb])
                        P.cp("act", KCV[:, k, :], pbank[pb][:, 0:64], ["pb%d" % pb], ["KCV"])

            def attn_block(br, h, tb, st_list):
                nonlocal encnt
                pr, half, kvh = h // 2, h % 2, h // 4
                hl = h % 4
                active = {}
                for st_ in st_list:
                    subs = [sub for sub in range(4) if 0 <= (4 * tb + sub - st_) <= (4 if br == 1 else 10 ** 6)]
                    if not subs:
                        continue
                    lo, hi = subs[0], subs[-1] + 1
                    active[st_] = (lo, hi)
                    cs = slice(lo * 128, hi * 128)
                    gs = slice(tb * 512 + lo * 128, tb * 512 + hi * 128)
                    pb = P.bank()
                    extra = []
                    for sub in subs:
                        dd = 4 * tb + sub - st_
                        if dd == 0:
                            extra.append((sub, TZs[:, h, 0, :], "TZ"))
                        elif dd == 1:
                            extra.append((sub, TZs[:, h, 1, :], "TZ"))
                        elif dd == 4 and br == 1:
                            extra.append((sub, WZ, "ncst"))
                    nmm = 1 + (1 if br == 0 else 0) + len(extra)
                    i_mm = 0
                    P.mm(pbank[pb][:, cs], KTz[br][half][:, st_ * 128:(st_ + 1) * 128], QT8[:, pr, gs],
                         True, nmm == 1, ["KTz", "QT8"], ["pb%d" % pb])
                    i_mm += 1
                    if br == 0:
                        P.mm(pbank[pb][:, cs], EBIG[0:32, st_ * 128:(st_ + 1) * 128], SELM1[kvh][0:32, gs],
                             False, i_mm == nmm - 1, ["ncst", "SELM1_%d" % kvh], ["pb%d" % pb])
                        i_mm += 1
                    for (sub, tile_, key) in extra:
                        P.mm(pbank[pb][:, sub * 128:(sub + 1) * 128], ident, tile_, False, i_mm == nmm - 1,
                             ["ident", key], ["pb%d" % pb])
                        i_mm += 1
                    P.act(PTb[:, st_, cs], pbank[pb][:, cs], AF.Exp, ["pb%d" % pb, "CBR"], ["PT%d" % st_],
                          bias=CBR[:, h:h + 1])
                for sub in range(4):
                    tq = 4 * tb + sub
                    sts = [st_ for st_ in st_list if st_ in active and active[st_][0] <= sub < active[st_][1]]
                    pb = P.bank()
                    for i, st_ in enumerate(sts):
                        P.mm(pbank[pb][:, 0:65], PTb[:, st_, sub * 128:(sub + 1) * 128], V1n[br][:, st_, :],
                             i == 0, i == len(sts) - 1, ["PT%d" % st_, "V1n", "V1n1"], ["pb%d" % pb])
                    E = epn[encnt % 4]
                    ek = "epn%d" % (encnt % 4)
                    encnt += 1
                    P.ts("dve", E["r"], pbank[pb][:, 64:65], 1e-30, ALU.max, ["pb%d" % pb], [ek])
                    P.op("dve", lambda e, E=E: e.reciprocal(out=E["r"], in_=E["r"]), [ek], [ek])
                    gcol = (1 + br) * 8 + h
                    P.tt("dve", E["r"], E["r"], GS[:, tq, gcol:gcol + 1], ALU.mult, [ek, "GS"], [ek])
                    P.stt("dve", NOUT[:, sub, hl * 64:(hl + 1) * 64], pbank[pb][:, 0:64], E["r"],
                          NOUT[:, sub, hl * 64:(hl + 1) * 64], ALU.mult, ALU.add, ["pb%d" % pb, ek, "NOUT"], ["NOUT"])

            for tb in range(4 if K3 >= 4 else 0):
                for sub in range(4):
                    tq = 4 * tb + sub
                    cb_ = cbcnt % 2
                    cbcnt += 1
                    P.ld(CB[cb_], cbt_d[tq * 128:(tq + 1) * 128, :, :], ["CB%d" % cb_])
                    for hg in range(2):
                        pb = P.bank()
                        for hh in range(4):
                            h = hg * 4 + hh
                            pr, half, kvh = h // 2, h % 2, h // 4
                            P.mm(pbank[pb][:, hh * 128:(hh + 1) * 128], QT8[:, pr, tq * 128:(tq + 1) * 128],
                                 KCTz[:, kvh, half, :], True, True, ["QT8", "KCT"], ["pb%d" % pb])
                        P.tt("dve", Sx[:, hg * 4:(hg + 1) * 4, :],
                             pbank[pb][:, :].rearrange("p (h n) -> p h n", h=4), CB[cb_][:, hg * 4:(hg + 1) * 4, :],
                             ALU.add, ["pb%d" % pb, "CB%d" % cb_], ["Sx", "PnT"])
                    P.op("dve", lambda e: e.reduce_max(out=sm8, in_=Sx, axis=AX.X), ["Sx"], ["sm8"])
                    P.tt("dve", Sx, Sx, sm8.unsqueeze(2).to_broadcast([128, 8, 128]), ALU.subtract, ["Sx", "sm8"], ["Sx"])
                    P.act(Pn, Sx, AF.Exp, ["Sx"], ["Pn"])
                    P.op("dve", lambda e: e.reduce_sum(out=sm8, in_=Pn, axis=AX.X), ["Pn"], ["sm8"])
                    P.op("dve", lambda e: e.reciprocal(out=rs8, in_=sm8), ["sm8"], ["rs8"])
                    if tq == 0:
                        P.ts("dve", rs8, rs8, ROWV[:, 0:1], ALU.mult, ["rs8", "ncst"], ["rs8"])
                    P.tt("dve", Pn, Pn, rs8.unsqueeze(2).to_broadcast([128, 8, 128]), ALU.mult, ["Pn", "rs8"], ["Pn"])
                    for hg in range(2):
                        pb = P.bank()
                        for hh in range(4):
                            h = hg * 4 + hh
                            P.tr(pbank[pb][:, hh * 128:(hh + 1) * 128], Pn[:, h, :], ident, ["Pn", "ident"],
                                 ["pb%d" % pb])
                        P.cp("act", PnT[:, hg * 4:(hg + 1) * 4, :], pbank[pb][:, :].rearrange("p (h n) -> p h n", h=4),
                             ["pb%d" % pb], ["PnT", "Sx"])
                    pb = P.bank()
                    for h in range(8):
                        P.mm(pbank[pb][:, h * 64:(h + 1) * 64], PnT[:, h, :], KCV[:, h // 4, :], True, True,
                             ["PnT", "KCV"], ["pb%d" % pb])
                    P.tt("dve", NOUT[:, sub, :].rearrange("p (h d) -> p h d", h=8),
                         pbank[pb][:, :].rearrange("p (h d) -> p h d", h=8),
                         GS[:, tq, 0:8].unsqueeze(2).to_broadcast([128, 8, 64]), ALU.mult, ["pb%d" % pb, "GS"], ["NOUT"])
                    P.op("dve", lambda e: e.tensor_reduce(out=PG, in_=Pn.rearrange("p (k g) n -> p k n g", k=2),
                                                         axis=AX.X, op=ALU.add), ["Pn"], ["PG"])
                    pg4 = PG.rearrange("p k (b r) -> p k b r", r=4)
                    P.op("dve", lambda e, pg4=pg4: e.tensor_reduce(out=S012, in_=pg4[:, :, :, 0:3], axis=AX.X, op=ALU.add),
                         ["PG"], ["S012"])
                    P.stt("dve", PSs, S012, 2.0, pg4[:, :, :, 3], ALU.mult, ALU.add, ["S012", "PG"], ["PSs"])
                    P.tt("dve", PSs[:, :, 1:32], PSs[:, :, 1:32], pg4[:, :, 0:31, 3], ALU.add, ["PSs", "PG"], ["PSs"])
                    fv = FVNC[:, tq, 0, :]
                    ncm = FVNC[:, tq, 1, :]
                    for k in range(2):
                        P.tt("dve", SC[:, k, :], PSs[:, k, :], fv, ALU.max, ["PSs", "ncst"], ["SC"])
                        P.tt("dve", SC[:, k, :], SC[:, k, :], ncm, ALU.add, ["SC", "ncst"], ["SC"])
                        P.op("dve", lambda e, k=k: e.max(out=M8, in_=SC[:, k, :]), ["SC"], ["M8"])
                        P.op("dve", lambda e, k=k: e.match_replace(out=SCW, in_to_replace=M8, in_values=SC[:, k, :],
                                                                  imm_value=-1e9), ["SC", "M8"], ["SCW"])
                        P.op("dve", lambda e: e.max(out=M8, in_=SCW), ["SCW"], ["M8"])
                        P.ts("dve", SEL[:, k, :], SC[:, k, :], M8[:, 7:8], ALU.is_ge, ["SC", "M8"], ["SEL"], s2=-1.0,
                             op1=ALU.add)
                        pb = P.bank()
                        P.tr(pbank[pb][0:32, 0:128], SEL[:, k, :], ident, ["SEL", "ident"], ["pb%d" % pb])
                        P.cp("act", SELM1[k][0:32, tq * 128:(tq + 1) * 128], pbank[pb][0:32, 0:128], ["pb%d" % pb],
                             ["SELM1_%d" % k, "XTc"])
                P.st(NOUTD[r0 + tb * 512:r0 + (tb + 1) * 512, :].rearrange("(n p) c -> p n c", p=128), NOUT,
                     ["NOUT"], ["NOUTD"])
            for kvh in range(2 if K3 >= 5 else 0):
                for br, obase in ((0, O_KS), (1, O_KW)):
                    for half in range(2):
                        P.memset("dve", stg[half], 0.0, ["stgX"])
                        P.ld(stg[half][:, :, half * 64:(half + 1) * 64], zt(obase + kvh * 64, 64), ["stgX"])
                        tr_block(stg[half], KTz[br][half], ["KTz"])
                    P.ld(V1n[br][:, :, 0:64], zt(obase + 128 + kvh * 64, 64), ["V1n"])
                for tb in range(4):
                    nsl = NOUT[:, :, 0:256]
                    dsl = NOUTD[r0 + tb * 512:r0 + (tb + 1) * 512, kvh * 256:(kvh + 1) * 256].rearrange(
                        "(n p) c -> p n c", p=128)
                    P.ld(nsl, dsl, ["NOUT"], r=["NOUTD"])
                    for g in range(4):
                        h = kvh * 4 + g
                        if K3 != 7:
                            attn_block(0, h, tb, list(range(0, 4 * tb + 4)))
                        if K3 != 6:
                            attn_block(1, h, tb, list(range(max(0, 4 * tb - 4), 4 * tb + 4)))
                    P.st(dsl, nsl, ["NOUT"], ["NOUTD"])

        def top16(src, scr, vals, idx_u, nelem, key):
            P.op("dve", lambda e: e.max(out=vals[:, 0:8], in_=src), [key], [key + "v"])
            P.op("dve", lambda e: e.max_index(out=idx_u[:, 0:8], in_max=vals[:, 0:8], in_values=src),
                 [key, key + "v"], [key + "i"])
            P.op("dve", lambda e: e.match_replace(out=scr, in_to_replace=vals[:, 0:8], in_values=src,
                                                  imm_value=-1e30), [key, key + "v"], [key + "s"])
            P.op("dve", lambda e: e.max(out=vals[:, 8:16], in_=scr), [key + "s"], [key + "v"])
            P.op("dve", lambda e: e.max_index(out=idx_u[:, 8:16], in_max=vals[:, 8:16], in_values=scr),
                 [key + "s", key + "v"], [key + "i"])

        P.barrier()
        apos[0] = persist_mark
        zt0 = alloc(512)
        P.memset("dve", zt0, 0.0, ["zt0"])
        P.st(MOUT[TP:TP + 128, :], zt0, ["zt0"], ["MOUT_s"])
        P.st(NOUTD[TP:TP + 128, :], zt0, ["zt0"], ["NOUTD_s"])
        wq5 = alloc3(4, 128)
        wk5 = alloc3(4, 128)
        P.ld(wq5, wq_d[:, :, :], ["w5s"])
        P.ld(wk5, wk_d[:, :, :], ["w5s"])
        ZS = alloc(1544)
        CB3 = alloc3(3, 512)
        N0 = alloc(512)
        M0 = alloc(4)
        CW4 = alloc3(4, 512)
        CB4 = alloc(512)
        GB4 = alloc(8)
        NG16 = alloc(128)
        OPS = alloc(128)
        C4 = alloc(512)
        T4 = alloc(512)
        CTs = alloc3(4, 4)
        Q4 = alloc(512)
        K4 = alloc(512)
        S4 = {n: alloc(4) for n in ("IG", "XF", "LF", "MI", "MT", "SWS", "SCI", "EMT", "QK", "NQ", "SW", "DEN", "RD", "T")}
        PACK = alloc3(4, 264)
        C0s = alloc3(16, 128)
        BCT = alloc3(16, 264)
        VTs = alloc3(4, 4)
        T1 = alloc3(16, 128)
        CQ = alloc(16)
        NUM = alloc(16)
        WV = alloc(16)
        HTs = alloc(16)
        HB = alloc(128)
        NN = alloc(512)
        zs4 = Z[TP:TP + SSC, :]
        P.ld(ZS[0:4], zs4[:, 0:1544], ["ZS"])
        P.ld(CB3[0:4], st_conv[:, :, :], ["s5in"])
        P.ld(N0[0:4], stn_d[:, :], ["s5in"])
        P.ld(M0[0:4], stm_d[:, :], ["s5in"])
        P.ld(CW4[0:4], cw4_d[:, :, :], ["s5in"])
        P.ld(CB4[0:4], cb4_d[:, :], ["s5in"])
        P.ld(GB4[0:4], gb4_d[:, :], ["s5in"])
        P.ld(NG16[0:16], ng16_d[:, :], ["s5in"])
        for b in range(SSC):
            P.ld(OPS[4 * b:4 * b + 4], Z[TP + b, O_O:O_O + 512].rearrange("(h v) -> h v", h=4), ["OPS"])
        P.ld(C0s, stC_d.rearrange("a v k -> v a k"), ["C0s"])
        P.tt("dve", C4[0:4], CW4[0:4, 3, :], ZS[0:4, O_U:O_U + 512], ALU.mult, ["s5in", "ZS"], ["C4"])
        for j in range(3):
            P.tt("dve", T4[0:4], CW4[0:4, j, :], CB3[0:4, j, :], ALU.mult, ["s5in"], ["T4"])
            P.tt("dve", C4[0:4], C4[0:4], T4[0:4], ALU.add, ["C4", "T4"], ["C4"])
        P.tt("dve", C4[0:4], C4[0:4], CB4[0:4], ALU.add, ["C4", "s5in"], ["C4"])
        P.act(C4[0:4], C4[0:4], AF.Silu, ["C4"], ["C4"])
        pb = P.bank()
        for h in range(4):
            P.tr(pbank[pb][:, h * 4:(h + 1) * 4], C4[0:4, h * 128:(h + 1) * 128], ident[0:4, 0:4], ["C4", "ident"],
                 ["pb%d" % pb])
        P.cp("act", CTs, pbank[pb][:, 0:16].rearrange("p (h b) -> p h b", h=4), ["pb%d" % pb], ["CTs"])
        pq = P.bank()
        for h in range(4):
            P.mm(pbank[pq][0:4, h * 128:(h + 1) * 128], CTs[:, h, :], wq5[:, h, :], True, True, ["CTs", "w5s"],
                 ["pb%d" % pq])
        P.cp("act", Q4[0:4], pbank[pq][0:4, :], ["pb%d" % pq], ["Q4"])
        pk = P.bank()
        for h in range(4):
            P.mm(pbank[pk][0:4, h * 128:(h + 1) * 128], CTs[:, h, :], wk5[:, h, :], True, True, ["CTs", "w5s"],
                 ["pb%d" % pk])
        P.ts("dve", K4[0:4], pbank[pk][0:4, :], 128.0 ** -0.5, ALU.mult, ["pb%d" % pk], ["K4"])
        A_ = {n: v[0:4] for n, v in S4.items()}
        P.tt("dve", A_["IG"], ZS[0:4, O_I:O_I + 4], GB4[0:4, 0:4], ALU.add, ["ZS", "s5in"], ["IG"])
        P.tt("dve", A_["XF"], ZS[0:4, O_F:O_F + 4], GB4[0:4, 4:8], ALU.add, ["ZS", "s5in"], ["XF"])
        P.act(A_["LF"], A_["XF"], AF.Exp, ["XF"], ["LF"], scale=-1.0)
        P.act(A_["LF"], A_["LF"], AF.Ln, ["LF"], ["LF"], bias=1.0, scale=1.0)
        P.stt("dve", A_["MI"], A_["LF"], -1.0, M0[0:4], ALU.mult, ALU.add, ["LF", "s5in"], ["MI"])
        P.tt("dve", A_["MT"], A_["MI"], A_["IG"], ALU.max, ["MI", "IG"], ["MT"])
        P.tt("dve", A_["T"], A_["IG"], A_["MT"], ALU.subtract, ["IG", "MT"], ["T"])
        P.act(A_["SWS"], A_["T"], AF.Exp, ["T"], ["SWS"])
        P.tt("dve", A_["T"], A_["MI"], A_["MT"], ALU.subtract, ["MI", "MT", "SWS"], ["T"])
        P.act(A_["SCI"], A_["T"], AF.Exp, ["T"], ["SCI"])
        P.act(A_["EMT"], A_["MT"], AF.Exp, ["MT"], ["EMT"], scale=-1.0)
        P.tt("dve", T4[0:4], Q4[0:4], K4[0:4], ALU.mult, ["Q4", "K4"], ["T4"])
        P.op("dve", lambda e: e.reduce_sum(out=A_["QK"], in_=T4[0:4].rearrange("p (h e) -> p h e", h=4), axis=AX.X),
             ["T4"], ["QK"])
        P.tt("dve", T4[0:4], Q4[0:4], N0[0:4], ALU.mult, ["Q4", "s5in", "QK"], ["T4"])
        P.op("dve", lambda e: e.reduce_sum(out=A_["NQ"], in_=T4[0:4].rearrange("p (h e) -> p h e", h=4), axis=AX.X),
             ["T4"], ["NQ"])
        P.tt("dve", A_["SW"], A_["QK"], A_["SWS"], ALU.mult, ["QK", "SWS"], ["SW"])
        P.tt("dve", A_["DEN"], A_["SCI"], A_["NQ"], ALU.mult, ["SCI", "NQ"], ["DEN"])
        P.tt("dve", A_["DEN"], A_["DEN"], A_["SW"], ALU.add, ["DEN", "SW"], ["DEN"])
        P.act(A_["DEN"], A_["DEN"], AF.Abs, ["DEN"], ["DEN"])
        P.tt("dve", A_["DEN"], A_["DEN"], A_["EMT"], ALU.max, ["DEN", "EMT"], ["DEN"])
        P.op("dve", lambda e: e.reciprocal(out=A_["RD"], in_=A_["DEN"]), ["DEN"], ["RD"])
        P.cp("dve", PACK[0:4, :, 0:128], Q4[0:4].rearrange("p (h e) -> p h e", h=4), ["Q4"], ["PACK"])
        P.cp("dve", PACK[0:4, :, 128:256], K4[0:4].rearrange("p (h e) -> p h e", h=4), ["K4"], ["PACK"])
        for i, n in enumerate(("SCI", "SWS", "SW", "RD")):
            P.cp("dve", PACK[0:4, :, 256 + i:257 + i], A_[n].unsqueeze(2), [n], ["PACK"])
        P.st(SCRB.rearrange("(b h) x -> b h x", h=4), PACK[0:4], ["PACK"], ["SCRB"], eng="sp")
        P.ld(BCT, bass.AP(SCRB.tensor, 0, [[0, 128], [264, 16], [1, 264]]), ["BCT"], r=["SCRB"])
        pb = P.bank()
        for h in range(4):
            P.tr(pbank[pb][:, h * 4:(h + 1) * 4], ZS[0:4, O_V + h * 128:O_V + (h + 1) * 128], ident[0:4, 0:4],
                 ["ZS", "ident"], ["pb%d" % pb])
        P.cp("act", VTs, pbank[pb][:, 0:16].rearrange("p (h b) -> p h b", h=4), ["pb%d" % pb], ["VTs"])
        VTbh = VTs.rearrange("p h b -> p b h")
        sc = lambda i: BCT[:, :, 256 + i].rearrange("p (b h) -> p b h", h=4)
        P.tt("dve", T1, C0s, BCT[:, :, 0:128], ALU.mult, ["C0s", "BCT"], ["T1"])
        P.op("dve", lambda e: e.reduce_sum(out=CQ, in_=T1, axis=AX.X), ["T1"], ["CQ"])
        CQ3 = CQ.rearrange("p (b h) -> p b h", h=4)
        NUM3 = NUM.rearrange("p (b h) -> p b h", h=4)
        WV3 = WV.rearrange("p (b h) -> p b h", h=4)
        HT3 = HTs.rearrange("p (b h) -> p b h", h=4)
        P.tt("dve", NUM3, CQ3, sc(0), ALU.mult, ["CQ", "BCT"], ["NUM"])
        P.tt("dve", WV3, VTbh, sc(2), ALU.mult, ["VTs", "BCT"], ["WV"])
        P.tt("dve", NUM3, NUM3, WV3, ALU.add, ["NUM", "WV"], ["NUM"])
        P.tt("dve", HT3, NUM3, sc(3), ALU.mult, ["NUM", "BCT"], ["HTs"])
        pb = P.bank()
        P.tr(pbank[pb][0:16, 0:128], HTs, ident, ["HTs", "ident"], ["pb%d" % pb])
        P.cp("act", HB[0:16], pbank[pb][0:16, 0:128], ["pb%d" % pb], ["HB"])
        st5 = alloc(8)
        ag5 = alloc(4)
        rs5 = alloc(1)
        P.op("dve", lambda e: e.bn_stats(out=st5[0:16, 0:6], in_=HB[0:16]), ["HB"], ["st5"])
        P.op("dve", lambda e: e.bn_aggr(out=ag5[0:16, 0:2], in_=st5[0:16, 0:6]), ["st5"], ["ag5"])
        P.act(rs5[0:16], ag5[0:16, 1:2], AF.Sqrt, ["ag5"], ["rs5"], bias=1e-5, scale=1.0)
        P.op("dve", lambda e: e.reciprocal(out=rs5[0:16], in_=rs5[0:16]), ["rs5"], ["rs5"])
        P.ts("dve", HB[0:16], HB[0:16], ag5[0:16, 0:1], ALU.subtract, ["HB", "ag5", "rs5"], ["HB"], s2=rs5[0:16],
             op1=ALU.mult)
        P.tt("dve", HB[0:16], HB[0:16], NG16[0:16], ALU.mult, ["HB", "s5in"], ["HB"])
        P.act(OPS[0:16], OPS[0:16], AF.Sigmoid, ["OPS"], ["OPS"])
        P.tt("dve", HB[0:16], HB[0:16], OPS[0:16], ALU.mult, ["HB", "OPS"], ["HB"])
        for b in range(SSC):
            P.st(MOUT[TP + b, :].rearrange("(h v) -> h v", h=4), HB[4 * b:4 * b + 4], ["HB"], ["MOUT_s"], eng="sp")
        P.tt("dve", WV3, VTbh, sc(1), ALU.mult, ["VTs", "BCT", "NUM"], ["WV"])
        P.tt("dve", T1, BCT[:, :, 128:256], WV.unsqueeze(2).to_broadcast([128, 16, 128]), ALU.mult, ["BCT", "WV", "CQ"],
             ["T1"])
        P.tt("dve", C0s, C0s, BCT[:, :, 256:257].to_broadcast([128, 16, 128]), ALU.mult, ["C0s", "BCT"], ["C0s"])
        P.tt("dve", C0s, C0s, T1, ALU.add, ["C0s", "T1"], ["C0s"])
        P.st(o_C_s.rearrange("a v k -> v a k"), C0s, ["C0s"], eng="sp")
        N03 = N0[0:4].rearrange("p (h e) -> p h e", h=4)
        NN3 = NN[0:4].rearrange("p (h e) -> p h e", h=4)
        K43 = K4[0:4].rearrange("p (h e) -> p h e", h=4)
        P.tt("dve", NN3, N03, A_["SCI"].unsqueeze(2).to_broadcast([4, 4, 128]), ALU.mult, ["s5in", "SCI"], ["NN"])
        P.tt("dve", T4[0:4].rearrange("p (h e) -> p h e", h=4), K43, A_["SWS"].unsqueeze(2).to_broadcast([4, 4, 128]),
             ALU.mult, ["K4", "SWS", "NQ"], ["T4"])
        P.tt("dve", NN[0:4], NN[0:4], T4[0:4], ALU.add, ["NN", "T4"], ["NN"])
        P.st(o_n_s[:, :], NN[0:4], ["NN"], eng="sp")
        P.st(o_m_s[:, :], A_["MT"], ["MT"], eng="sp")

        P.barrier()
        apos[0] = persist_mark
        K6 = int(os.environ.get("K6STOP", "9"))
        W1Ls = alloc(2 * 16 * 128).rearrange("p (c r o) -> p c r o", c=2, r=16)
        W1Ns = alloc(2 * 16 * 64).rearrange("p (c j o) -> p c j o", c=2, j=16)
        PELs = alloc3(2, 16)
        W2Ns = alloc3(2, 64)
        W2KT = alloc(64)
        SHMs = alloc(128)
        SHD = alloc(128)
        ONESM = alloc(128)
        CBS = alloc3(8, 8)
        BWt = alloc3(4, 8)
        T25 = alloc(2 * 8 * 64).rearrange("p (a h t) -> p a h t", a=2, h=8)
        CBR6 = alloc(8)
        FL255 = alloc(1)
        IND = alloc(4)
        RB0 = alloc(8)
        IOTA8 = alloc(8)
        IOT128 = alloc3(2, 128)
        PETs = alloc(128)
        PETB = alloc3(2, 64)
        PT4i = alloc(4).bitcast(I32)
        PT8i = alloc(128).bitcast(I32)
        PTF = alloc(4)
        PTROW = alloc(128)
        IDXf = alloc3(4, 8)
        P.ld(W1Ls, w1l_d[:, :, :, :], ["c6"])
        P.ld(W1Ns, w1n_d[:, :, :, :], ["c6"])
        P.ld(PELs, pel_d[:, :, :], ["c6"])
        P.ld(W2Ns[0:64], w2n_d[:, :, :], ["c6"])
        P.ld(W2KT[0:64], w2kt_d[:, :], ["c6"])
        P.ld(SHMs, sh_d[:, :], ["c6"])
        P.ld(SHD, shd_d[:, :], ["c6"])
        P.ld(CBS, cbs_d[:, :, :], ["c6"])
        P.ld(BWt, bw_d[:, :, :], ["c6"])
        P.ld(T25[0:60], t25_d[:, :, :, :], ["c6"])
        P.ld(CBR6, cbr_d[:, :], ["c6"])
        P.ld(FL255[0:60], fl255_d[:, :], ["c6"])
        P.ld(IND[0:60], ind_d[:, :], ["c6"])
        P.ld(RB0[0:4], rb0_d[:, :], ["c6"])
        P.ld(IOTA8, iota8_d[:, :], ["c6"])
        P.ld(IOT128[0:8], iot128_d[:, :, :], ["c6"])
        P.ld(PT4i, ptl_d[:, :], ["c6"])
        P.ld(PT8i[0:8], pt8_d[:, :], ["c6"])
        P.memset("dve", ONESM, 1.0, ["ONESM"])
        P.cp("dve", PTF, PT4i, ["c6"], ["PTF"])
        P.cp("dve", PTROW[0:8], PT8i[0:8], ["c6"], ["PTROW"])
        P.stt("dve", IDXf, PTF.unsqueeze(2).to_broadcast([128, 4, 8]), 8.0,
              IOTA8.unsqueeze(1).to_broadcast([128, 4, 8]), ALU.mult, ALU.add, ["PTF", "c6"], ["IDXf"])
        P.cp("dve", IDXC.rearrange("p (b n) -> p b n", b=4), IDXf, ["IDXf"], ["IDXC"])
        for c in range(2):
            pb = P.bank()
            for j in range(16):
                P.mm(pbank[pb][0:1, 0:64], PELs[:, c, j:j + 1], W1Ns[:, c, j, :], j == 0, j == 15, ["c6"], ["pb%d" % pb])
            P.cp("act", PETs[0:1, c * 64:(c + 1) * 64], pbank[pb][0:1, 0:64], ["pb%d" % pb], ["PETs"])
        pb = P.bank()
        P.mm(pbank[pb][:, 0:128], ONESM[0:1, :], PETs[0:1, :], True, True, ["ONESM", "PETs"], ["pb%d" % pb])
        P.cp("act", PETB, pbank[pb][:, 0:128].rearrange("p (c o) -> p c o", c=2), ["pb%d" % pb], ["PETB"])
        ZN = alloc(1304)
        P.ld(ZN[0:4], Z[TP:TP + SSC, O_Q:O_Q + 1304], ["ZN"])
        QS = alloc(512)
        P.ts("dve", QS[0:4], ZN[0:4, 0:512], 0.125, ALU.mult, ["ZN"], ["QS"])
        QTS = alloc3(4, 8)
        pb = P.bank()
        for h in range(8):
            P.tr(pbank[pb][0:64, h * 4:(h + 1) * 4], QS[0:4, h * 64:(h + 1) * 64], ident[0:4, 0:4], ["QS", "ident"],
                 ["pb%d" % pb])
        P.cp("act", QTS[0:64].rearrange("p b h -> p h b"), pbank[pb][0:64, 0:32].rearrange("p (h b) -> p h b", h=8),
             ["pb%d" % pb], ["QTS"])
        QW8 = alloc(64)
        for b in range(SSC):
            pb = P.bank()
            P.mm(pbank[pb][0:8, 0:64], QTS[0:64, b, :], W2KT[0:64, :], True, True, ["QTS", "c6"], ["pb%d" % pb])
            P.cp("act", QW8[0:8], pbank[pb][0:8, 0:64], ["pb%d" % pb], ["QW8"])
            P.st(SCRQ[b, :].rearrange("(h i) -> h i", h=8), QW8[0:8], ["QW8"], ["SCRQ"], eng="sp")
        QWB = alloc3(4, 512)
        P.ld(QWB, bass.AP(SCRQ.tensor, 0, [[0, 128], [512, 4], [1, 512]]), ["QWB"], r=["SCRQ"])
        G = [alloc(4096) for _ in range(2)]
        XTs = alloc(32 * 128).rearrange("p (q n) -> p q n", q=32)
        PABs = alloc(8 * 4 * 128).rearrange("p (n k o) -> p n k o", n=8, k=4)
        HPRE = alloc(8 * 4 * 64).rearrange("p (n k o) -> p n k o", n=8, k=4)
        HTMP = alloc(8 * 4 * 64).rearrange("p (n k o) -> p n k o", n=8, k=4)
        HIDs = alloc(8 * 4 * 64).rearrange("p (n k o) -> p n k o", n=8, k=4)
        TMPs = alloc(1024)
        SCs = alloc3(8, 8)
        RS = alloc(8)
        RT = alloc(8)
        PGs = alloc3(8, 2)
        S3 = alloc(2)
        PSB = alloc3(2, 2)
        U4 = alloc(64)
        UT = alloc(4)
        OC = alloc(65)
        WK = alloc3(4, 256)
        QB128 = alloc(512)
        SWs = alloc3(4, 8)
        gcnt6 = 0
        for b in range(SSC if K6 >= 1 else 0):
            for n_ in range(8):
                gb = gcnt6 % 2
                gcnt6 += 1
                P.dma("pool", lambda e, gb=gb, col=b * 8 + n_: e.indirect_dma_start(
                    out=G[gb], out_offset=None, in_=pool_cmp_d[:, :],
                    in_offset=bass.IndirectOffsetOnAxis(ap=IDXC[:, col:col + 1], axis=0)), ["IDXC"], ["G%d" % gb])
                G3 = G[gb].rearrange("p (r x) -> p r x", r=16)
                for g8 in range(8):
                    pb = P.bank()
                    for j in range(4):
                        q_ = g8 * 4 + j
                        r, c = q_ // 2, q_ % 2
                        P.tr(pbank[pb][:, j * 128:(j + 1) * 128], G3[:, r, c * 128:(c + 1) * 128], ident,
                             ["G%d" % gb, "ident"], ["pb%d" % pb])
                    P.cp("act" if g8 % 2 else "dve", XTs[:, g8 * 4:(g8 + 1) * 4, :],
                         pbank[pb][:, :].rearrange("p (q n) -> p q n", q=4), ["pb%d" % pb], ["XTs"])
                for c in range(2):
                    for k in range(2):
                        rows = slice(k * 64, (k + 1) * 64)
                        pb = P.bank()
                        for r in range(16):
                            P.mm(pbank[pb][:, 0:128], XTs[rows, r * 2 + c, :], W1Ls[rows, c, r, :], r == 0, r == 15,
                                 ["XTs", "c6"], ["pb%d" % pb])
                        P.cp("act", PABs[:, n_, c * 2 + k, :], pbank[pb][:, 0:128], ["pb%d" % pb], ["PABs"])
            P.tt("dve", HPRE[:, 0:7], PABs[:, 0:7, :, 0:64], PABs[:, 1:8, :, 64:128], ALU.add, ["PABs"], ["HPRE"])
            pb = P.bank()
            P.mm(pbank[pb][:, 0:256], SHMs, PABs[:, 0, :, 64:128], True, True, ["c6", "PABs"], ["pb%d" % pb])
            P.tt("dve", HPRE[:, 7], PABs[:, 7, :, 0:64], pbank[pb][:, 0:256].rearrange("p (k o) -> p k o", k=4), ALU.add,
                 ["PABs", "pb%d" % pb], ["HPRE"])
            for c in range(2):
                P.tt("dve", HPRE[:, :, 2 * c:2 * c + 2, :], HPRE[:, :, 2 * c:2 * c + 2, :],
                     PETB[:, c, :].unsqueeze(1).unsqueeze(1).to_broadcast([128, 8, 2, 64]), ALU.add, ["HPRE", "PETB"],
                     ["HPRE"])
            gelu_from(HPRE, HTMP, HIDs, ["HPRE"], "HIDs")
            for h in range(8):
                kvh = h // 4
                P.tt("dve", TMPs[:, 0:512].rearrange("p (n i) -> p n i", n=8), HIDs[:, :, kvh, :],
                     QWB[:, b, h * 64:(h + 1) * 64].unsqueeze(1).to_broadcast([128, 8, 64]), ALU.mult,
                     ["HIDs", "QWB"], ["TMPs"])
                P.op("dve", lambda e, h=h: e.reduce_sum(out=SCs[:, :, h], in_=TMPs[:, 0:512].rearrange(
                    "p (n i) -> p n i", n=8), axis=AX.X), ["TMPs"], ["SCs"])
            P.tt("dve", SCs, SCs, CBS, ALU.add, ["SCs", "c6"], ["SCs"])
            P.act(SCs, SCs, AF.Exp, ["SCs"], ["SCs"])
            P.op("dve", lambda e: e.reduce_sum(out=RS, in_=SCs.rearrange("p n h -> p h n"), axis=AX.X), ["SCs"], ["RS"])
            pb = P.bank()
            P.mm(pbank[pb][:, 0:8], ONESM, RS, True, True, ["ONESM", "RS"], ["pb%d" % pb])
            P.op("dve", lambda e, pb=pb: e.reciprocal(out=RT, in_=pbank[pb][:, 0:8]), ["pb%d" % pb], ["RT"])
            P.tt("dve", SCs, SCs, RT.unsqueeze(1).to_broadcast([128, 8, 8]), ALU.mult, ["SCs", "RT"], ["SCs"])
            for kvh in range(2):
                pb = P.bank()
                for n_ in range(8):
                    P.mm(pbank[pb][0:4, 0:64], SCs[:, n_, kvh * 4:(kvh + 1) * 4], HIDs[:, n_, 2 + kvh, :], n_ == 0,
                         n_ == 7, ["SCs", "HIDs"], ["pb%d" % pb])
                P.cp("act", U4[0:4], pbank[pb][0:4, 0:64], ["pb%d" % pb], ["U4"])
                pb = P.bank()
                P.tr(pbank[pb][0:64, 0:4], U4[0:4, 0:64], ident[0:4, 0:4], ["U4", "ident"], ["pb%d" % pb])
                P.cp("act", UT[0:64], pbank[pb][0:64, 0:4], ["pb%d" % pb], ["UT"])
                pb = P.bank()
                P.mm(pbank[pb][0:4, 0:64], UT[0:64, 0:4], W2Ns[0:64, 1, :], True, True, ["UT", "c6"], ["pb%d" % pb])
                P.cp("act", OC[0:4, 0:64], pbank[pb][0:4, 0:64], ["pb%d" % pb], ["OC"])
                P.st(OSD[0, b, kvh * 4:(kvh + 1) * 4, 0:64], OC[0:4, 0:64], ["OC"], ["OSD"], eng="sp")
            P.op("dve", lambda e: e.tensor_reduce(out=PGs, in_=SCs.rearrange("p n (k g) -> p n k g", k=2), axis=AX.X,
                                                 op=ALU.add), ["SCs"], ["PGs"])
            pb = P.bank()
            P.mm(pbank[pb][:, 0:2], SHD, PGs[:, 7, :], True, True, ["c6", "PGs"], ["pb%d" % pb])
            for eo in range(2):
                o_ = eo * 4
                P.tt("dve", S3, PGs[:, o_ + 0, :], PGs[:, o_ + 1, :], ALU.add, ["PGs"], ["S3"])
                P.tt("dve", S3, S3, PGs[:, o_ + 2, :], ALU.add, ["S3", "PGs"], ["S3"])
                P.stt("dve", PSB[:, :, eo], S3, 2.0, PGs[:, 3, :], ALU.mult, ALU.add, ["S3", "PGs"], ["PSB"])
                if eo == 0:
                    P.tt("dve", PSB[:, :, 0], PSB[:, :, 0], pbank[pb][:, 0:2], ALU.add, ["PSB", "pb%d" % pb], ["PSB"])
                else:
                    P.tt("dve", PSB[:, :, 1], PSB[:, :, 1], PGs[:, 7, :], ALU.add, ["PSB", "PGs"], ["PSB"])
            P.st(SELD[b * 2:b * 2 + 2, :].rearrange("k (p e) -> p k e", e=2), PSB, ["PSB"], ["SELD"], eng="sp")
            P.ld(WK, st_win[b, :, :].rearrange("(c p) x -> p c x", p=128), ["WK"])
            P.ld(QB128, bass.AP(Z.tensor, (TP + b) * IN_DIM + O_Q, [[0, 128], [1, 512]]), ["QB128"])
            P.ts("dve", QB128, QB128, 0.125, ALU.mult, ["QB128"], ["QB128"])
            for h in range(8):
                kvh = h // 4
                P.tt("dve", TMPs[:, 0:256].rearrange("p (c i) -> p c i", c=4), WK[:, :, kvh * 64:(kvh + 1) * 64],
                     QB128[:, h * 64:(h + 1) * 64].unsqueeze(1).to_broadcast([128, 4, 64]), ALU.mult, ["WK", "QB128"],
                     ["TMPs"])
                P.op("dve", lambda e, h=h: e.reduce_sum(out=SWs[:, :, h], in_=TMPs[:, 0:256].rearrange(
                    "p (c i) -> p c i", c=4), axis=AX.X), ["TMPs"], ["SWs"])
            P.tt("dve", SWs, SWs, BWt, ALU.add, ["SWs", "c6"], ["SWs"])
            P.act(SWs, SWs, AF.Exp, ["SWs"], ["SWs"])
            for kvh in range(2):
                pn_ = P.bank()
                for c in range(4):
                    P.mm(pbank[pn_][0:4, 0:64], SWs[:, c, kvh * 4:(kvh + 1) * 4],
                         WK[:, c, 128 + kvh * 64:128 + (kvh + 1) * 64], c == 0, c == 3, ["SWs", "WK"], ["pb%d" % pn_])
                pd_ = P.bank()
                for c in range(4):
                    P.mm(pbank[pd_][0:4, 0:1], SWs[:, c, kvh * 4:(kvh + 1) * 4], ONESM[:, 0:1], c == 0, c == 3,
                         ["SWs", "ONESM"], ["pb%d" % pd_])
                P.cp("act", OC[0:4, 0:64], pbank[pn_][0:4, 0:64], ["pb%d" % pn_], ["OC"])
                P.cp("act", OC[0:4, 64:65], pbank[pd_][0:4, 0:1], ["pb%d" % pd_], ["OC"])
                P.st(OSD[1, b, kvh * 4:(kvh + 1) * 4, :], OC[0:4, :], ["OC"], ["OSD"], eng="sp")

        if K6 >= 2:
            SELIN = alloc(256)
            SELW = alloc(256)
            V16s = alloc(16)
            I16s = alloc(16).bitcast(U32)
            BLK = alloc(16)
            HALF = alloc(16)
            PAR = alloc(16)
            PGID = alloc(16)
            PHYS = alloc(16)
            OH6 = alloc3(15, 128)
            PQ5 = alloc3(15, 5)
            P.ld(SELIN[0:8], SELD[:, :], ["SELIN"], r=["SELD"])
            P.memset("dve", SELIN[0:8, 0:1], -1e9, ["SELIN"])
            P.memset("dve", SELIN[0:8, 255:256], -1e9, ["SELIN"])
            top16(SELIN[0:8], SELW[0:8], V16s[0:8], I16s[0:8], 256, "SELIN")
            P.cp("dve", BLK[0:8], I16s[0:8], ["SELINi"], ["BLK"])
            P.memset("dve", BLK[0:8, 13:14], 0.0, ["BLK"])
            P.memset("dve", BLK[0:8, 14:15], 255.0, ["BLK"])
            B15 = BLK[0:8, 0:15]
            P.tt("dve", OH6[0:8], B15.unsqueeze(2).to_broadcast([8, 15, 128]),
                 IOT128[0:8, 1, :].unsqueeze(1).to_broadcast([8, 15, 128]), ALU.is_ge, ["BLK", "c6"], ["OH6"])
            P.op("dve", lambda e: e.tensor_reduce(out=HALF[0:8, 0:15], in_=OH6[0:8], axis=AX.X, op=ALU.add), ["OH6"],
                 ["HALF"])
            P.stt("dve", PAR[0:8, 0:15], HALF[0:8, 0:15], -2.0, B15, ALU.mult, ALU.add, ["HALF", "BLK"], ["PAR"])
            P.tt("dve", OH6[0:8], HALF[0:8, 0:15].unsqueeze(2).to_broadcast([8, 15, 128]),
                 IOT128[0:8, 0, :].unsqueeze(1).to_broadcast([8, 15, 128]), ALU.is_equal, ["HALF", "c6"], ["OH6"])
            P.tt("dve", OH6[0:8], OH6[0:8], PTROW[0:8].unsqueeze(1).to_broadcast([8, 15, 128]), ALU.mult,
                 ["OH6", "PTROW"], ["OH6"])
            P.op("dve", lambda e: e.tensor_reduce(out=PGID[0:8, 0:15], in_=OH6[0:8], axis=AX.X, op=ALU.add), ["OH6"],
                 ["PGID"])
            P.stt("dve", PHYS[0:8, 0:15], PGID[0:8, 0:15], 2.0, PAR[0:8, 0:15], ALU.mult, ALU.add, ["PGID", "PAR"],
                  ["PHYS"])
            for qd in range(4):
                P.ts("dve", PQ5[0:8, :, qd], PHYS[0:8, 0:15], 4.0, ALU.mult, ["PHYS"], ["PQ5"], s2=float(qd),
                     op1=ALU.add)
            P.cp("dve", PQ5[0:8, :, 4], B15, ["BLK"], ["PQ5"])
            P.st(SCRP[:, :, :], PQ5[0:8], ["PQ5"], ["SCRP"], eng="sp")
            IQf = [alloc(5) for _ in range(2)]
            P.memset("dve", IDXS, 0, ["IDXS"])
            for kvh in range(2):
                for b in range(SSC):
                    P.ld(IQf[kvh][15 * b:15 * b + 15], SCRP[b * 2 + kvh, :, :], ["IQf%d" % kvh], r=["SCRP"])
                P.cp("dve", IDXS[0:60, kvh * 4:(kvh + 1) * 4], IQf[kvh][0:60, 0:4], ["IQf%d" % kvh], ["IDXS"])
            QB60 = alloc(512)
            for b in range(SSC):
                P.ld(QB60[15 * b:15 * b + 15], bass.AP(Z.tensor, (TP + b) * IN_DIM + O_Q, [[0, 15], [1, 512]]), ["QB60"])
            P.ts("dve", QB60[0:60], QB60[0:60], 0.125, ALU.mult, ["QB60"], ["QB60"])
            FL254 = alloc(1)
            W0 = alloc(1)
            BIAS = alloc3(4, 64)
            SS6 = alloc3(4, 64)
            PVD = alloc(260)
            PVt = alloc(64)
            DNt = alloc(4)
            SLCR = [alloc(260) for _ in range(2)]
            for kvh in range(2):
                hs = slice(kvh * 4, (kvh + 1) * 4)
                P.ts("dve", FL254[0:60], IQf[kvh][0:60, 4:5], 254.0, ALU.is_equal, ["IQf%d" % kvh], ["FL254"])
                P.tt("dve", W0[0:60], FL254[0:60], FL255[0:60], ALU.add, ["FL254", "c6"], ["W0"])
                P.ts("dve", W0[0:60], W0[0:60], -1.0, ALU.mult, ["W0"], ["W0"], s2=1.0, op1=ALU.add)
                P.ts("dve", BIAS[0:60], T25[0:60, 0, hs, :], FL254[0:60], ALU.mult, ["c6", "FL254"], ["BIAS"])
                P.stt("dve", BIAS[0:60], T25[0:60, 1, hs, :], FL255[0:60], BIAS[0:60], ALU.mult, ALU.add,
                      ["c6", "BIAS"], ["BIAS"])
                P.stt("dve", BIAS[0:60], CBR6[0:60, hs].unsqueeze(2).to_broadcast([60, 4, 64]), W0[0:60], BIAS[0:60],
                      ALU.mult, ALU.add, ["c6", "W0", "BIAS"], ["BIAS"])
                P.memset("dve", PVD[0:60], 0.0, ["PVD"])
                for qd in range(4):
                    gb = gcnt6 % 2
                    gcnt6 += 1
                    P.dma("pool", lambda e, gb=gb, col=kvh * 4 + qd: e.indirect_dma_start(
                        out=G[gb], out_offset=None, in_=pool_slc_d[:, :],
                        in_offset=bass.IndirectOffsetOnAxis(ap=IDXS[:, col:col + 1], axis=0)), ["IDXS"],
                        ["G%d" % gb])
                    GQ3 = G[gb][0:60].rearrange("p (t x) -> p t x", t=16)
                    ts_ = slice(qd * 16, (qd + 1) * 16)
                    for g in range(4):
                        h = kvh * 4 + g
                        P.tt("dve", TMPs[0:60].rearrange("p (t d) -> p t d", t=16), GQ3[:, :, kvh * 64:(kvh + 1) * 64],
                             QB60[0:60, h * 64:(h + 1) * 64].unsqueeze(1).to_broadcast([60, 16, 64]), ALU.mult,
                             ["G%d" % gb, "QB60"], ["TMPs"])
                        P.op("dve", lambda e, g=g, ts_=ts_: e.reduce_sum(out=SS6[0:60, g, ts_], in_=TMPs[0:60].rearrange(
                            "p (t d) -> p t d", t=16), axis=AX.X), ["TMPs"], ["SS6"])
                    P.tt("dve", SS6[0:60, :, ts_], SS6[0:60, :, ts_], BIAS[0:60, :, ts_], ALU.add, ["SS6", "BIAS"], ["SS6"])
                    P.act(SS6[0:60, :, ts_], SS6[0:60, :, ts_], AF.Exp, ["SS6"], ["SS6"])
                    P.op("dve", lambda e, ts_=ts_: e.reduce_sum(out=DNt[0:60], in_=SS6[0:60, :, ts_], axis=AX.X), ["SS6"],
                         ["DNt"])
                    P.tt("dve", PVD[0:60, 256:260], PVD[0:60, 256:260], DNt[0:60], ALU.add, ["PVD", "DNt"], ["PVD"])
                    for g in range(4):
                        P.tt("dve", TMPs[0:60].rearrange("p (d t) -> p d t", d=64),
                             GQ3[:, :, 128 + kvh * 64:128 + (kvh + 1) * 64].rearrange("p t d -> p d t"),
                             SS6[0:60, g, ts_].unsqueeze(1).to_broadcast([60, 64, 16]), ALU.mult, ["G%d" % gb, "SS6"],
                             ["TMPs"])
                        P.op("dve", lambda e: e.reduce_sum(out=PVt[0:60], in_=TMPs[0:60].rearrange(
                            "p (d t) -> p d t", d=64), axis=AX.X), ["TMPs"], ["PVt"])
                        P.tt("dve", PVD[0:60, g * 64:(g + 1) * 64], PVD[0:60, g * 64:(g + 1) * 64], PVt[0:60], ALU.add,
                             ["PVD", "PVt"], ["PVD"])
                pb = P.bank()
                P.mm(pbank[pb][0:4, 0:260], IND[0:60, :], PVD[0:60, :], True, True, ["c6", "PVD"], ["pb%d" % pb])
                P.cp("act", SLCR[kvh][0:4], pbank[pb][0:4, 0:260], ["pb%d" % pb], ["SLCR%d" % kvh])
            OSB = alloc(2 * 8 * 65).rearrange("p (a h x) -> p a h x", a=2, h=8)
            for a in range(2):
                P.ld(OSB[0:4, a], OSD[a, :, :, :], ["OSB"], r=["OSD"])
            GS4 = alloc(24)
            P.act(GS4[0:4], ZN[0:4, 1280:1304], AF.Sigmoid, ["ZN"], ["GS4"])
            QS3 = QS[0:4].rearrange("p (h d) -> p h d", h=8)
            NOS = alloc(512)
            NOS3 = NOS[0:4].rearrange("p (h d) -> p h d", h=8)
            TL = alloc(512)
            TL3 = TL[0:4].rearrange("p (h d) -> p h d", h=8)
            PTL = alloc(8)
            DEN8 = alloc(8)
            P.tt("dve", NOS3, OSB[0:4, 0, :, 0:64], GS4[0:4, 0:8].unsqueeze(2).to_broadcast([4, 8, 64]), ALU.mult,
                 ["OSB", "GS4"], ["NOS"])
            for br, kbase in ((1, 768), (2, 1024)):
                for kvh in range(2):
                    P.tt("dve", TL3[:, kvh * 4:(kvh + 1) * 4, :], QS3[:, kvh * 4:(kvh + 1) * 4, :],
                         ZN[0:4, kbase + kvh * 64:kbase + (kvh + 1) * 64].unsqueeze(1).to_broadcast([4, 4, 64]),
                         ALU.mult, ["QS", "ZN", "TLr"], ["TL"])
                P.op("dve", lambda e: e.reduce_sum(out=PTL[0:4], in_=TL3, axis=AX.X), ["TL"], ["PTL"])
                P.tt("dve", PTL[0:4], PTL[0:4], RB0[0:4], ALU.add, ["PTL", "c6"], ["PTL"])
                P.act(PTL[0:4], PTL[0:4], AF.Exp, ["PTL"], ["PTL"])
                for kvh in range(2):
                    hs = slice(kvh * 4, (kvh + 1) * 4)
                    if br == 1:
                        num_src = SLCR[kvh][0:4, 0:256].rearrange("p (g d) -> p g d", g=4)
                        den_src = SLCR[kvh][0:4, 256:260]
                        rk = ["SLCR%d" % kvh]
                    else:
                        num_src = OSB[0:4, 1, hs, 0:64]
                        den_src = OSB[0:4, 1, hs, 64]
                        rk = ["OSB"]
                    P.tt("dve", DEN8[0:4, hs], den_src, PTL[0:4, hs], ALU.add, rk + ["PTL"], ["DEN8"])
                    P.tt("dve", TL3[:, hs, :], PTL[0:4, hs].unsqueeze(2).to_broadcast([4, 4, 64]),
                         ZN[0:4, kbase + 128 + kvh * 64:kbase + 128 + (kvh + 1) * 64].unsqueeze(1).to_broadcast([4, 4, 64]),
                         ALU.mult, ["PTL", "ZN", "PTL"], ["TL"])
                    P.tt("dve", TL3[:, hs, :], TL3[:, hs, :], num_src, ALU.add, ["TL"] + rk, ["TL"])
                P.ts("dve", DEN8[0:4], DEN8[0:4], 1e-30, ALU.max, ["DEN8"], ["DEN8"])
                P.op("dve", lambda e: e.reciprocal(out=DEN8[0:4], in_=DEN8[0:4]), ["DEN8"], ["DEN8"])
                P.tt("dve", DEN8[0:4], DEN8[0:4], GS4[0:4, br * 8:(br + 1) * 8], ALU.mult, ["DEN8", "GS4"], ["DEN8"])
                P.tt("dve", TL3, TL3, DEN8[0:4].unsqueeze(2).to_broadcast([4, 8, 64]), ALU.mult, ["TL", "DEN8"], ["TL"])
                P.tt("dve", NOS[0:4], NOS[0:4], TL[0:4], ALU.add, ["NOS", "TL"], ["NOS", "TLr"])
            P.st(NOUTD[TP:TP + SSC, :], NOS[0:4], ["NOS"], ["NOUTD_s"], eng="sp")

        P.barrier()
        apos[0] = persist_mark
        K5 = int(os.environ.get("K5STOP", "9"))
        ALPHA = 2.0 ** 0.25
        WUM = alloc3(4, D)
        WUN = alloc3(4, D)
        WO = alloc3(8, D)
        LNP = alloc3(4, D)
        P.ld(WUM, wupm_d.rearrange("(c p) n -> p c n", p=128), ["w4"])
        P.ld(WUN, wupn_d.rearrange("(c p) n -> p c n", p=128), ["w4"])
        P.ld(WO, wout_d.rearrange("(c p) n -> p c n", p=128), ["w4"])
        P.ld(LNP, lnp_d[:, :, :], ["w4"])
        NB = 2
        XB = [alloc(D) for _ in range(NB)]
        MOB = [alloc(512) for _ in range(NB)]
        NOB = [alloc(512) for _ in range(NB)]
        GMB = [alloc(D) for _ in range(NB)]
        GNB = [alloc(D) for _ in range(NB)]
        MOT = alloc3(4, 128)
        NOTt = alloc3(4, 128)
        MIX = alloc(D)
        MIXT = alloc3(8, 128)
        RB = alloc(D)
        X1B = [alloc(D) for _ in range(NB)]
        STT = alloc(16)
        AGG = alloc(4)
        RSTD = alloc(1)

        def layer_norm(src, dst, gi, rk, wk_):
            for i in range(2):
                P.op("dve", lambda e, i=i: e.bn_stats(out=STT[:, i * 6:(i + 1) * 6], in_=src[:, i * 512:(i + 1) * 512]),
                     rk, ["STT"])
            P.op("dve", lambda e: e.bn_aggr(out=AGG[:, 0:2], in_=STT[:, 0:12].rearrange("p (a b) -> p a b", a=2)),
                 ["STT"], ["AGG"])
            P.act(RSTD, AGG[:, 1:2], AF.Sqrt, ["AGG"], ["RSTD"], bias=1e-5, scale=1.0)
            P.op("dve", lambda e: e.reciprocal(out=RSTD, in_=RSTD), ["RSTD"], ["RSTD"])
            P.ts("dve", dst, src, AGG[:, 0:1], ALU.subtract, rk + ["AGG", "RSTD"], wk_, s2=RSTD, op1=ALU.mult)
            P.tt("pool", dst, dst, LNP[:, gi, :], ALU.mult, wk_ + ["w4"], wk_)
            P.tt("pool", dst, dst, LNP[:, gi + 1, :], ALU.add, wk_ + ["w4"], wk_)

        for ti in range(NT + 1 if K5 >= 1 else 0):
            b_ = ti % NB
            rows = slice(ti * 128, (ti + 1) * 128)
            xsrc = x_p[rows, :] if ti < NT else x_s[:, :]
            P.ld(XB[b_], xsrc, ["XB%d" % b_])
            P.ld(MOB[b_], MOUT[rows, :], ["MOB%d" % b_], r=["MOUT", "MOUT_s"])
            P.ld(NOB[b_], NOUTD[rows, :], ["NOB%d" % b_], r=["NOUTD", "NOUTD_s"])
            P.ld(GMB[b_], Z[rows, O_GM:O_GM + D], ["GMB%d" % b_])
            P.ld(GNB[b_], Z[rows, O_GNN:O_GNN + D], ["GNB%d" % b_])
            for src, skey, dst, dkey in ((MOB[b_], "MOB%d" % b_, MOT, "MOT"), (NOB[b_], "NOB%d" % b_, NOTt, "NOT")):
                pb = P.bank()
                for c in range(4):
                    P.tr(pbank[pb][:, c * 128:(c + 1) * 128], src[:, c * 128:(c + 1) * 128], ident, [skey, "ident"],
                         ["pb%d" % pb])
                P.cp("act", dst, pbank[pb][:, :].rearrange("p (c t) -> p c t", c=4), ["pb%d" % pb], [dkey])
            P.act(GMB[b_], GMB[b_], AF.Sigmoid, ["GMB%d" % b_], ["GMB%d" % b_])
            P.act(GNB[b_], GNB[b_], AF.Sigmoid, ["GNB%d" % b_], ["GNB%d" % b_])
            for nb in range(2):
                ns = slice(nb * 512, (nb + 1) * 512)
                pa = P.bank()
                for c in range(4):
                    P.mm(pbank[pa][:, :], MOT[:, c, :], WUM[:, c, ns], c == 0, c == 3, ["MOT", "w4"], ["pb%d" % pa])
                pn = P.bank()
                for c in range(4):
                    P.mm(pbank[pn][:, :], NOTt[:, c, :], WUN[:, c, ns], c == 0, c == 3, ["NOT", "w4"], ["pb%d" % pn])
                P.tt("dve", MIX[:, ns], pbank[pa][:, :], GMB[b_][:, ns], ALU.mult, ["pb%d" % pa, "GMB%d" % b_], ["MIX"])
                P.tt("dve", GNB[b_][:, ns], pbank[pn][:, :], GNB[b_][:, ns], ALU.mult, ["pb%d" % pn, "GNB%d" % b_],
                     ["GNB%d" % b_])
                P.tt("pool", MIX[:, ns], MIX[:, ns], GNB[b_][:, ns], ALU.add, ["MIX", "GNB%d" % b_], ["MIX"])
            for g in range(2):
                pb = P.bank()
                for j in range(4):
                    c = g * 4 + j
                    P.tr(pbank[pb][:, j * 128:(j + 1) * 128], MIX[:, c * 128:(c + 1) * 128], ident, ["MIX", "ident"],
                         ["pb%d" % pb])
                P.cp("act", MIXT[:, g * 4:(g + 1) * 4, :], pbank[pb][:, :].rearrange("p (c t) -> p c t", c=4),
                     ["pb%d" % pb], ["MIXT"])
            for nb in range(2):
                ns = slice(nb * 512, (nb + 1) * 512)
                pb = P.bank()
                for c in range(8):
                    P.mm(pbank[pb][:, :], MIXT[:, c, :], WO[:, c, ns], c == 0, c == 7, ["MIXT", "w4"], ["pb%d" % pb])
                P.stt("dve", RB[:, ns], XB[b_][:, ns], ALPHA, pbank[pb][:, :], ALU.mult, ALU.add,
                      ["XB%d" % b_, "pb%d" % pb], ["RB"])
            layer_norm(RB, X1B[b_], 0, ["RB"], ["X1B%d" % b_])
            P.st(X1D[rows, :], X1B[b_], ["X1B%d" % b_], ["X1D_%d" % ti])

        P.barrier()
        apos[0] = persist_mark
        LN2 = alloc3(2, D)
        WPR = alloc3(8, 2048)
        IOTA = alloc(16)
        P.ld(LN2, lnp_d[:, 2:4, :], ["w5"])
        IOTA2 = alloc(16)
        P.ld(IOTA, iota_d[:, :], ["w5"])
        P.ts("dve", IOTA2, IOTA, 16.0, ALU.mult, ["w5"], ["w5"], s2=16.0, op1=ALU.add)
        cmark = apos[0]
        CV32 = [alloc(4096) for _ in range(3)]
        CV16 = [alloc(2048).bitcast(BF16) for _ in range(3)]
        ccnt = 0
        for src_d, dst_d in ((peer_u_d, U16D), (peer_v_d, V16D)):
            for ch in range(32):
                cb_ = ccnt % 3
                ccnt += 1
                rs_ = slice(ch * 512, (ch + 1) * 512)
                P.ld(CV32[cb_], src_d[rs_, :].rearrange("(p r) n -> p (r n)", p=128), ["CV32_%d" % cb_])
                eng = ("dve", "act", "pool")[cb_]
                P.cp(eng, CV16[cb_], CV32[cb_], ["CV32_%d" % cb_], ["CV16_%d" % cb_])
                P.st(dst_d[rs_, :].rearrange("(p r) n -> p (r n)", p=128), CV16[cb_], ["CV16_%d" % cb_], ["T16"], eng="sp")
        P.barrier()
        apos[0] = cmark
        PWT = alloc(16 * 8 * 128).rearrange("p (a c m) -> p a c m", a=16, c=8)
        SKT = alloc3(2, 128)
        P.ld(PWT, pwqt_d[:, :, :, :], ["PWT"])
        P.ld(SKT, skt_d[:, :, :], ["SKT"])
        for c in range(8):
            for g in range(4):
                pb = P.bank()
                for j in range(4):
                    hp = g * 4 + j
                    P.mm(pbank[pb][:, j * 128:(j + 1) * 128], PWT[:, hp, c, :], SKT[:, hp % 2, :], True, True,
                         ["PWT", "SKT"], ["pb%d" % pb])
                P.cp("act" if g % 2 else "dve", WPR[:, c, g * 512:(g + 1) * 512], pbank[pb][:, :], ["pb%d" % pb], ["w5"])
        P.barrier()
        apos[0] -= 16 * 8 * 128 + 256
        X1 = [alloc(D) for _ in range(2)]
        X1T = alloc3(8, 128)
        SS = alloc3(16, 128)
        SS2 = alloc3(16, 128)
        V16 = alloc3(16, 16)
        I16 = alloc3(16, 16)
        I16u = I16.bitcast(U32)
        I16f = alloc3(16, 16)
        CAND = alloc3(8, 256)
        CAND2 = alloc3(8, 256)
        VC = alloc3(8, 16)
        ICu = alloc3(8, 16).bitcast(U32)
        ICf = alloc3(8, 16)
        AIX = alloc3(8, 16)
        BIX = alloc3(8, 16)
        OH = alloc(8 * 16 * 16).rearrange("p (h k a) -> p h k a", h=8, k=16)
        I1S = alloc3(8, 16)
        I2S = alloc3(8, 16)
        EF = alloc(128)
        GWs = [alloc3(8, 16) for _ in range(2)]
        sm8b = alloc(8)
        APREs = [alloc(128) for _ in range(2)]
        GT = alloc(128)
        WCOLs = [alloc(128) for _ in range(2)]
        NGU = 8
        NGV = 12
        UBu = [alloc(D // 2).bitcast(BF16) for _ in range(NGU)]
        UBv = [alloc(D // 2).bitcast(BF16) for _ in range(NGV)]
        JUNK = alloc(D)
        JUNKb = JUNK[:, 0:D // 2].bitcast(BF16)
        X1b = [alloc(D // 2).bitcast(BF16) for _ in range(2)]
        RB2 = alloc(D)
        EIs = [EI, EI2]
        DG = [alloc(64).bitcast(BF16) for _ in range(4)]
        YB = alloc(D)
        gcnt = 0
        dcnt = 0

        def stage_a(ti, parts=(0, 1, 2, 3, 4, 5)):
            for part in parts:
                stage_a_part(ti, part)

        def stage_a_part(ti, part):
            b_ = ti % 2
            rows = slice(ti * 128, (ti + 1) * 128)
            if part == 0:
                P.ld(X1[b_], X1D[rows, :], ["X1_%d" % b_], r=["X1D_%d" % ti])
                P.cp("act", X1b[b_], X1[b_], ["X1_%d" % b_], ["X1b_%d" % b_])
                for g in range(2):
                    pb = P.bank()
                    for j in range(4):
                        c = g * 4 + j
                        P.tr(pbank[pb][:, j * 128:(j + 1) * 128], X1[b_][:, c * 128:(c + 1) * 128], ident,
                             ["X1_%d" % b_, "ident"], ["pb%d" % pb])
                    P.cp("act", X1T[:, g * 4:(g + 1) * 4, :], pbank[pb][:, :].rearrange("p (c t) -> p c t", c=4),
                         ["pb%d" % pb], ["X1T"])
                return
            if part in (1, 2, 3, 4):
                g = part - 1
                pb = P.bank()
                for c in range(8):
                    P.mm(pbank[pb][:, :], X1T[:, c, :], WPR[:, c, g * 512:(g + 1) * 512], c == 0, c == 7,
                         ["X1T", "w5"], ["pb%d" % pb])
                P.cp("act", SS[:, g * 4:(g + 1) * 4, :], pbank[pb][:, :].rearrange("p (a k) -> p a k", a=4),
                     ["pb%d" % pb], ["SS%d" % g])
                for hp in range(g * 4, g * 4 + 4):
                    top16(SS[:, hp, :], SS2[:, hp, :], V16[:, hp, :], I16u[:, hp, :], 128, "SS%d" % (hp // 4))
                return
            VK = ["SS%dv" % g for g in range(4)]
            IK = ["SS%di" % g for g in range(4)]
            V4 = V16.rearrange("p (h q) k -> p h q k", q=2)
            P.tt("dve", CAND.rearrange("p h (a b) -> p h a b", a=16),
                 V4[:, :, 0, :].unsqueeze(3).to_broadcast([128, 8, 16, 16]),
                 V4[:, :, 1, :].unsqueeze(2).to_broadcast([128, 8, 16, 16]), ALU.add, VK, ["CAND"])
            for h in range(8):
                top16(CAND[:, h, :], CAND2[:, h, :], VC[:, h, :], ICu[:, h, :], 256, "CAND")
            P.cp("dve", ICf, ICu, ["CANDi"], ["ICf"])
            P.cp("dve", I16f, I16u, IK, ["I16f"])
            P.tt("dve", OH, ICf.unsqueeze(3).to_broadcast([128, 8, 16, 16]),
                 IOTA2.unsqueeze(1).unsqueeze(1).to_broadcast([128, 8, 16, 16]), ALU.is_ge, ["ICf", "w5"], ["OH"])
            P.op("dve", lambda e: e.tensor_reduce(out=AIX, in_=OH, axis=AX.X, op=ALU.add), ["OH"], ["AIX"])
            P.stt("dve", BIX, AIX, -16.0, ICf, ALU.mult, ALU.add, ["AIX", "ICf"], ["BIX"])
            I4 = I16f.rearrange("p (h q) k -> p h q k", q=2)
            iota_b = IOTA.unsqueeze(1).unsqueeze(1).to_broadcast([128, 8, 16, 16])
            for sel_ix, src_i, dst in ((AIX, I4[:, :, 0, :], I1S), (BIX, I4[:, :, 1, :], I2S)):
                P.tt("dve", OH, sel_ix.unsqueeze(3).to_broadcast([128, 8, 16, 16]), iota_b, ALU.is_equal,
                     ["AIX", "BIX", "w5"], ["OH"])
                P.tt("dve", OH, OH, src_i.unsqueeze(2).to_broadcast([128, 8, 16, 16]), ALU.mult, ["OH", "I16f"], ["OH"])
                P.op("dve", lambda e, dst=dst: e.tensor_reduce(out=dst, in_=OH, axis=AX.X, op=ALU.add), ["OH"],
                     ["I12S"])
            P.stt("dve", EF.rearrange("p (h k) -> p h k", h=8), I1S, 128.0, I2S, ALU.mult, ALU.add, ["I12S"], ["EF"])
            P.cp("dve", EIs[b_], EF, ["EF"], ["EI%d" % b_])
            gw = GWs[b_]
            P.tt("dve", gw, VC, VC[:, :, 0:1].to_broadcast([128, 8, 16]), ALU.subtract, ["CANDv"], ["GW%d" % b_])

        ucnt = [0]
        vcnt = [0]
        dgc = [0]

        def col_u(ti, j):
            b_ = ti % 2
            ub = ucnt[0] % NGU
            ucnt[0] += 1
            P.dma("pool", lambda e: e.indirect_dma_start(
                out=UBu[ub], out_offset=None, in_=U16D[:, :],
                in_offset=bass.IndirectOffsetOnAxis(ap=EIs[b_][:, j:j + 1], axis=0)), ["EI%d" % b_], ["UBu%d" % ub])
            P.op("dve", lambda e: e.scalar_tensor_tensor(
                out=JUNKb, in0=UBu[ub], scalar=1.0, in1=X1b[b_], op0=ALU.mult, op1=ALU.mult,
                accum_out=APREs[b_][:, j:j + 1]), ["UBu%d" % ub, "X1b_%d" % b_], ["JUNK", "APRE%d" % b_])

        def finish_u(ti):
            b_ = ti % 2
            gw = GWs[b_]
            P.act(gw, gw, AF.Exp, ["GW%d" % b_], ["GW%d" % b_])
            P.op("dve", lambda e: e.reduce_sum(out=sm8b, in_=gw, axis=AX.X), ["GW%d" % b_], ["sm8b"])
            P.op("dve", lambda e: e.reciprocal(out=sm8b, in_=sm8b), ["sm8b"], ["sm8b"])
            P.tt("dve", gw, gw, sm8b.unsqueeze(2).to_broadcast([128, 8, 16]), ALU.mult, ["GW%d" % b_, "sm8b"],
                 ["GW%d" % b_])
            gelu_from(APREs[b_], GT, WCOLs[b_], ["APRE%d" % b_], "WCOL%d" % b_)
            P.tt("dve", WCOLs[b_], WCOLs[b_], GWs[b_].rearrange("p h k -> p (h k)"), ALU.mult,
                 ["WCOL%d" % b_, "GW%d" % b_], ["WCOL%d" % b_])

        def col_v(ti, j, p0, p1):
            b_ = ti % 2
            vb = vcnt[0] % NGV
            vcnt[0] += 1
            P.dma("pool", lambda e: e.indirect_dma_start(
                out=UBv[vb], out_offset=None, in_=V16D[:, :],
                in_offset=bass.IndirectOffsetOnAxis(ap=EIs[b_][:, j:j + 1], axis=0)), ["EI%d" % b_], ["UBv%d" % vb])
            dg = dgc[0] % 4
            dgc[0] += 1
            P.act(DG[dg], ident, AF.Copy, ["ident", "WCOL%d" % b_], ["DG%d" % dg], scale=WCOLs[b_][:, j:j + 1])
            P.mm(pbank[p0][:, :], DG[dg], UBv[vb][:, 0:512], j == 0, j == 127, ["DG%d" % dg, "UBv%d" % vb],
                 ["pb%d" % p0])
            P.mm(pbank[p1][:, :], DG[dg], UBv[vb][:, 512:1024], j == 0, j == 127, ["DG%d" % dg, "UBv%d" % vb],
                 ["pb%d" % p1])

        def finish_v(ti, p0, p1):
            b_ = ti % 2
            rows = slice(ti * 128, (ti + 1) * 128)
            P.stt("dve", RB2[:, 0:512], X1[b_][:, 0:512], ALPHA, pbank[p0][:, :], ALU.mult, ALU.add,
                  ["X1_%d" % b_, "pb%d" % p0], ["RB2"])
            P.stt("dve", RB2[:, 512:1024], X1[b_][:, 512:1024], ALPHA, pbank[p1][:, :], ALU.mult, ALU.add,
                  ["X1_%d" % b_, "pb%d" % p1], ["RB2"])
            layer_norm(RB2, YB, 0, ["RB2"], ["YB"])
            P.st(y_out[rows, :], YB, ["YB"], eng="sp")

        LNP = LN2
        ntile = NT + 1 if K5 >= 2 else 0
        if ntile:
            stage_a(0)
            for j in range(128):
                col_u(0, j)
            finish_u(0)
        for ti in range(ntile):
            nxt = ti + 1 < ntile
            p0 = P.bank()
            p1 = P.bank()
            LEAD = 112
            SLICES = {0: (0, 1), 10: (2,), 20: (3,), 30: (4, 5)}
            for k in range(128 + LEAD):
                if nxt and k in SLICES:
                    stage_a(ti + 1, SLICES[k])
                if k < 128:
                    col_v(ti, k, p0, p1)
                if nxt and k >= LEAD:
                    col_u(ti + 1, k - LEAD)
                if k == 136:
                    finish_v(ti, p0, p1)
            if nxt:
                finish_u(ti + 1)

        with nc.Block() as block:
            @block.tensor
            def _(e):
                P.emit("pe", e)

            @block.scalar
            def _(e):
                P.emit("act", e)

            @block.vector
            def _(e):
                P.emit("dve", e)

            @block.gpsimd
            def _(e):
                P.emit("pool", e)

            @block.sync
            def _(e):
                P.emit("sp", e)
    return nc


_NC_CACHE = {}


def kernel(x_prompt, x_sample, cache_cmp_kv, cache_slc_kv, page_table, state_win_kv, state_mlstm_C,
           state_mlstm_n, state_mlstm_m, state_mlstm_conv, w_in, m_conv_w, m_conv_b, m_wq, m_wk,
           m_gate_bias, m_norm_g, cmp_pe, cmp_w1, cmp_w2, rel_bias, w_up_m, w_up_n, w_out, ln1_g, ln1_b,
           ln2_g, ln2_b, peer_wq, peer_subkeys, peer_u, peer_v):
    f32 = np.float32
    if "nc" not in _NC_CACHE:
        _NC_CACHE["nc"] = build_program()
    nc = _NC_CACHE["nc"]
    xp = np.ascontiguousarray(np.asarray(x_prompt, f32)).reshape(BP * SEQ, D)
    xs = np.asarray(x_sample, f32).reshape(BS, D)
    ident = np.eye(128, dtype=f32)
    w_in0 = np.ascontiguousarray(np.asarray(w_in, f32)[0])
    stw = np.asarray(state_win_kv, f32)[0].reshape(BS, 512, 256)
    stc = np.asarray(state_mlstm_conv, f32)[0]
    cw = np.asarray(m_conv_w, f32)[0]
    convw_l = np.ascontiguousarray(np.transpose(cw.reshape(4, 4, 128), (2, 1, 0)))
    convb_l = np.ascontiguousarray(np.asarray(m_conv_b, f32)[0].reshape(4, 128).T)
    wq_l = np.ascontiguousarray(np.transpose(np.asarray(m_wq, f32)[0], (1, 0, 2)))
    wk_l = np.ascontiguousarray(np.transpose(np.asarray(m_wk, f32)[0], (1, 0, 2)))
    gb_l = np.ascontiguousarray(np.asarray(m_gate_bias, f32)[0].T)
    normg_rep = np.ascontiguousarray(np.broadcast_to(np.asarray(m_norm_g, f32)[0][None, :], (128, 512)))
    sel_c = np.zeros((4, 4, 128), f32)
    for h in range(4):
        sel_c[h, h, :] = 1.0
    tri_c = np.triu(np.ones((128, 128), f32))
    relb = np.asarray(rel_bias, f32)
    dist = np.arange(0, 4096)
    nf = np.maximum(dist, 16).astype(f32)
    large = 16 + (np.log(nf / f32(16)) / f32(np.log(8.0)) * f32(16)).astype(np.int32)
    bucket = np.where(dist < 16, dist, np.minimum(large, 31)).astype(np.int64)
    NEGM = f32(-30000.0)
    tt_ = np.arange(SEQ)[:, None]
    nn_ = np.arange(128)[None, :]
    dcm = tt_ - 16 * nn_ - 31
    vcm = (dcm >= 0) & (nn_ <= 126)
    cbt = np.where(vcm[:, None, :], relb[bucket[np.maximum(dcm, 0)]].transpose(0, 2, 1), NEGM).astype(f32)
    ii = np.arange(128)[:, None]
    jj = np.arange(128)[None, :]
    tz = np.empty((128, 8, 2, 128), f32)
    d0 = jj - ii
    tz[:, :, 0, :] = np.where((d0 >= 0)[:, None, :], relb[bucket[np.maximum(d0, 0)]].transpose(0, 2, 1), NEGM)
    tz[:, :, 1, :] = relb[bucket[128 + d0]].transpose(0, 2, 1)
    wz = np.where(jj < ii, f32(0), NEGM).astype(f32)
    ebig = (np.arange(SEQ)[None, :] // 64 == np.arange(32)[:, None]).astype(f32) * f32(30000.0)
    tq_ = (np.arange(16)[None, :, None] * 128 + np.arange(128)[:, None, None])
    bb_ = np.arange(32)[None, None, :]
    cur = tq_ // 64
    fvnc = np.empty((128, 16, 2, 32), f32)
    fvnc[:, :, 0, :] = np.where((bb_ == 0) | (bb_ == cur) | (bb_ == cur - 1), f32(1e4), f32(0))
    fvnc[:, :, 1, :] = np.where(bb_ * 64 <= tq_, f32(0), NEGM)
    rowv = (np.arange(128) >= 31).astype(f32).reshape(128, 1)
    shm = (ii == jj + 1).astype(f32)
    cbr = np.ascontiguousarray(np.broadcast_to(relb[31][None, :], (128, 8))).astype(f32)
    w1 = np.asarray(cmp_w1, f32)[0]
    w1r = w1.reshape(2, 2, 16, 64, 64)
    w1l_h = np.transpose(w1r, (3, 0, 2, 1, 4)).reshape(64, 2, 16, 128)
    w1l = np.ascontiguousarray(np.concatenate([w1l_h, w1l_h], axis=0))
    w1n = np.ascontiguousarray(np.transpose(w1.reshape(2, 16, 128, 64), (2, 0, 1, 3)))
    pel = np.ascontiguousarray(np.transpose(np.asarray(cmp_pe, f32)[0].reshape(2, 16, 128), (2, 0, 1)))
    w2 = np.asarray(cmp_w2, f32)[0]
    w2n = np.ascontiguousarray(np.transpose(w2, (1, 0, 2)))
    w2d = np.zeros((64, 2, 128), f32)
    w2d[:, 0, 0:64] = w2[0]
    w2d[:, 1, 64:128] = w2[0]
    lnp = np.ascontiguousarray(np.broadcast_to(np.stack([np.asarray(a, f32)[0] for a in (ln1_g, ln1_b, ln2_g, ln2_b)])[None],
                                               (128, 4, D)))
    pwq_t = np.ascontiguousarray(np.asarray(peer_wq, f32)[0].reshape(8, 128, 16, 128).transpose(3, 2, 0, 1))
    sk_t = np.ascontiguousarray(np.asarray(peer_subkeys, f32)[0].transpose(2, 0, 1))
    iota16 = np.ascontiguousarray(np.broadcast_to(np.arange(16, dtype=f32)[None, :], (128, 16)))
    cw4 = np.ascontiguousarray(np.broadcast_to(cw[None], (4, 4, 512)))
    cb4 = np.ascontiguousarray(np.broadcast_to(np.asarray(m_conv_b, f32)[0][None], (4, 512)))
    gb4 = np.ascontiguousarray(np.broadcast_to(np.asarray(m_gate_bias, f32)[0].reshape(1, 8), (4, 8)))
    ng16 = np.ascontiguousarray(np.tile(np.asarray(m_norm_g, f32)[0].reshape(4, 128), (4, 1)))
    stC = np.asarray(state_mlstm_C, f32)[0].reshape(BS * 4, 128, 128)
    stn = np.asarray(state_mlstm_n, f32)[0].reshape(BS, 512)
    stm = np.asarray(state_mlstm_m, f32)[0]
    n6 = np.arange(128)[:, None] * 8 + np.arange(8)[None, :]
    d6 = 16353 - 16 * n6
    cbs = np.where((n6 <= 1022)[:, :, None], relb[bucket[np.clip(d6, 0, 4095)]], NEGM).astype(f32)
    i6 = np.arange(4)[None, :] * 128 + np.arange(128)[:, None]
    bw = np.where((i6 >= 1)[:, :, None], relb[bucket[np.clip(512 - i6, 0, 4095)]], NEGM).astype(f32)
    tok = np.arange(64)
    t25 = np.empty((60, 2, 8, 64), f32)
    t25[:, 0] = relb[bucket[128 - tok]].T[None]
    t25[:, 1] = relb[bucket[64 - tok]].T[None]
    fl255 = (np.arange(60) % 15 == 14).astype(f32).reshape(60, 1)
    ind60 = (np.arange(60)[:, None] // 15 == np.arange(4)[None, :]).astype(f32)
    rb0 = np.ascontiguousarray(np.broadcast_to(relb[0][None, :], (4, 8))).astype(f32)
    shd = np.ascontiguousarray(shm.T)
    w2kt = np.ascontiguousarray(w2[0].T)
    iota8 = np.ascontiguousarray(np.broadcast_to(np.arange(8, dtype=f32)[None, :], (128, 8)))
    iot128 = np.empty((8, 2, 128), f32)
    iot128[:, 0] = np.arange(128, dtype=f32)[None]
    iot128[:, 1] = 2.0 * (np.arange(128, dtype=f32)[None] + 1.0)
    pool_c = np.asarray(cache_cmp_kv, f32)[0].reshape(5120 * 8, 4096)
    pool_s = np.asarray(cache_slc_kv, f32)[0].reshape(5120 * 8, 4096)
    ptab = np.asarray(page_table, np.int32)
    shared = {"pool_cmp": pool_c, "pool_slc": pool_s, "cbs": cbs, "bw": bw, "t25": t25, "fl255": fl255, "ind60": ind60,
              "rb0": rb0, "shd": shd, "w2kt": w2kt, "iota8": iota8, "iot128": iot128, "cw4": cw4, "cb4": cb4, "gb4": gb4, "ng16": ng16, "w_up_m": np.ascontiguousarray(np.asarray(w_up_m, f32)[0]), "w_up_n": np.ascontiguousarray(np.asarray(w_up_n, f32)[0]),
              "w_out": np.ascontiguousarray(np.asarray(w_out, f32)[0]), "lnp": lnp, "pwq_t": pwq_t, "sk_t": sk_t,
              "iota16": iota16, "peer_u": np.ascontiguousarray(np.asarray(peer_u, f32)[0]),
              "peer_v": np.ascontiguousarray(np.asarray(peer_v, f32)[0]),
              "cbt": cbt, "tz": tz, "wz": wz, "ebig": ebig, "fvnc": fvnc, "rowv": rowv, "shm": shm, "cbr": cbr,
              "w1l": w1l, "w1n": w1n, "pel": pel, "w2d": w2d, "w2n": w2n,
              "w_in": w_in0, "ident": ident, "convw_l": convw_l, "convb_l": convb_l, "wq_l": wq_l, "wk_l": wk_l,
              "gb_l": gb_l, "normg_rep": normg_rep, "sel_c": sel_c, "tri_c": tri_c}
    in_maps = []
    for c in range(NCORES):
        xs_pad = np.zeros((128, D), f32)
        xs_pad[:SSC] = xs[c * SSC:(c + 1) * SSC]
        in_maps.append({
            **shared,
            "x_p": xp[c * TP:(c + 1) * TP],
            "x_s": xs_pad,
            "st_win": np.ascontiguousarray(stw[c * SSC:(c + 1) * SSC]),
            "st_conv": np.ascontiguousarray(stc[c * SSC:(c + 1) * SSC]),
            "st_C": np.ascontiguousarray(stC[c * SSC * 4:(c + 1) * SSC * 4]),
            "st_n": np.ascontiguousarray(stn[c * SSC:(c + 1) * SSC]),
            "st_m": np.ascontiguousarray(stm[c * SSC:(c + 1) * SSC]),
            "pt_l": np.ascontiguousarray(ptab[c * SSC:(c + 1) * SSC].T),
            "pt8": np.ascontiguousarray(np.repeat(ptab[c * SSC:(c + 1) * SSC], 2, axis=0)),
        })
    res = run_bass_kernel_spmd(nc, in_maps, core_ids=list(range(NCORES)))
    R = res.results
    if DEBUG:
        DBG["R"] = R

    def cat(name):
        return np.concatenate([np.asarray(r[name]) for r in R], axis=0)

    kvt = (2, 2, 64)
    y_p = np.concatenate([np.asarray(r["y_out"])[:TP] for r in R], axis=0).reshape(BP, SEQ, D)
    y_s = np.concatenate([np.asarray(r["y_out"])[TP:TP + SSC] for r in R], axis=0).reshape(BS, 1, D)
    cmp_p = cat("o_cmp_p").reshape((1, BP, SEQ) + kvt)
    cmp_s = cat("o_cmp_s").reshape((1, BS, 1) + kvt)
    slc_p = cat("o_slc_p").reshape((1, BP, SEQ) + kvt)
    slc_s = cat("o_slc_s").reshape((1, BS, 1) + kvt)
    win_p = cat("o_win_p").reshape((1, BP, 512) + kvt)
    win_s = cat("o_win_s").reshape((1, BS, 512) + kvt)
    C_p = cat("o_C_p").reshape(1, BP, 4, 128, 128)
    C_s = cat("o_C_s").reshape(1, BS, 4, 128, 128)
    n_p = cat("o_n_p").reshape(1, BP, 4, 128)
    n_s = cat("o_n_s").reshape(1, BS, 4, 128)
    m_p = cat("o_m_p").reshape(1, BP, 4)
    m_s = cat("o_m_s").reshape(1, BS, 4)
    conv_p = cat("o_conv_p").reshape(1, BP, 3, 512)
    conv_s = cat("o_conv_s").reshape(1, BS, 3, 512)
    return (y_p, y_s, cmp_p, cmp_s, slc_p, slc_s, win_p, win_s, C_p, C_s, n_p, n_s, m_p, m_s, conv_p, conv_s)
```
